# Optimizing a Trainium2 kernel written in Bass

```python
import math
import jax
import jax.numpy as jnp
from jax import lax
import numpy as np


D_MODEL = 2048
BATCH = 2
SEQ = 4096
DEPTH = 4

GRID_W = 64
CTX_LEN = 256
N_EVEN = (DEPTH + 1) // 2
N_ODD = DEPTH // 2
EPS = 1e-6

GLA_HEADS = 4
GLA_DK = 128
GLA_DV = 256
GLA_GATE_RANK = 16
GLA_TAU = 16.0
GLA_CHUNK = 64
SC_WIDTH = 1024
DA_HEADS = 8
DA_DQK = 64
DA_DV = 128
ROPE_BASE = 10000.0
ROPE_PAIRS = DA_DQK // 4
Q_BLOCK = 128
ML_HEADS = 4
ML_DK = 128
ML_DV = 256
ML_CHUNK = 64
D_FF = 5632

EVEN_SIZES = (GLA_HEADS * GLA_DK, GLA_HEADS * GLA_DK, GLA_HEADS * GLA_DV, GLA_HEADS * GLA_DV,
              2 * GLA_GATE_RANK, SC_WIDTH, SC_WIDTH, SC_WIDTH)
ODD_SIZES = (DA_HEADS * 2 * DA_DQK, DA_HEADS * 2 * DA_DQK, DA_HEADS * DA_DV,
             ML_HEADS * ML_DK, ML_HEADS * ML_DK, ML_HEADS * ML_DV, ML_HEADS * ML_DV, 4 * ML_HEADS)
EVEN_IN = sum(EVEN_SIZES)
ODD_IN = sum(ODD_SIZES)
EVEN_MIX = GLA_HEADS * GLA_DV + SC_WIDTH
ODD_MIX = DA_HEADS * DA_DV + ML_HEADS * ML_DV

kernel_name = 'hybrid_prefix_diffusion_trunk'


def _rmsnorm(x, g):
    xf = x.astype(jnp.float32)
    y = xf * lax.rsqrt(jnp.mean(xf * xf, axis=-1, keepdims=True) + EPS)
    return (y * g.astype(jnp.float32)).astype(x.dtype)


def _dwconv3(x, w):
    xp = jnp.pad(x, ((0, 0), (1, 1), (0, 0)))
    return xp[:, :-2] * w[0] + xp[:, 1:-1] * w[1] + xp[:, 2:] * w[2]


def _split(a, sizes):
    idx = [int(i) for i in np.cumsum(sizes)[:-1]]
    return jnp.split(a, idx, axis=-1)


def _to_heads(a, h):
    b, n, _ = a.shape
    return a.reshape(b, n, h, -1).transpose(0, 2, 1, 3)


def _from_heads(a):
    b, h, n, d = a.shape
    return a.transpose(0, 2, 1, 3).reshape(b, n, h * d)


def _to_chunks(a, size):
    n = a.shape[2]
    a = a.reshape(a.shape[:2] + (n // size, size) + a.shape[3:])
    return jnp.moveaxis(a, 2, 0).astype(jnp.float32)


def _from_chunks(o):
    o = jnp.moveaxis(o, 0, 2)
    return o.reshape(o.shape[:2] + (-1, o.shape[-1]))


def _gla_scan(q, k, v, g, s0):
    L = GLA_CHUNK
    mask = jnp.tril(jnp.ones((L, L), dtype=bool))[:, :, None]

    def step(S, inp):
        qc, kc, vc, gc = inp
        b = jnp.cumsum(gc, axis=2)
        o_inter = jnp.einsum('bhld,bhde->bhle', qc * jnp.exp(b), S)
        rel = jnp.where(mask, b[:, :, :, None, :] - b[:, :, None, :, :], -jnp.inf)
        A = jnp.einsum('bhijd,bhjd->bhij', qc[:, :, :, None, :] * jnp.exp(rel), kc)
        o_intra = jnp.einsum('bhij,bhje->bhie', A, vc)
        b_last = b[:, :, -1:, :]
        S_new = jnp.exp(b_last[:, :, 0, :])[..., None] * S + jnp.einsum(
            'bhld,bhle->bhde', kc * jnp.exp(b_last - b), vc)
        return S_new, o_inter + o_intra

    S, o = lax.scan(step, s0, tuple(_to_chunks(a, L) for a in (q, k, v, g)))
    return _from_chunks(o), S


def _mlstm_scan(q, k, v, ig, fg, state):
    L = ML_CHUNK
    mask = jnp.tril(jnp.ones((L, L), dtype=bool))

    def step(carry, inp):
        C, n, m = carry
        qc, kc, vc, ic, fc = inp
        b = jnp.cumsum(jax.nn.log_sigmoid(fc), axis=-1)
        a = b + m[..., None]
        dmat = jnp.where(mask, b[..., :, None] - b[..., None, :] + ic[..., None, :], -jnp.inf)
        m_t = jnp.maximum(a, jnp.max(dmat, axis=-1))
        w_inter = jnp.exp(a - m_t)
        s = jnp.einsum('bhtd,bhsd->bhts', qc, kc) * jnp.exp(dmat - m_t[..., None])
        num = w_inter[..., None] * jnp.einsum('bhtd,bhde->bhte', qc, C) + jnp.einsum('bhts,bhse->bhte', s, vc)
        den = w_inter * jnp.einsum('bhtd,bhd->bht', qc, n) + jnp.sum(s, axis=-1)
        h = num / jnp.maximum(jnp.abs(den), jnp.exp(-m_t))[..., None]
        b_last = b[..., -1]
        g_s = b_last[..., None] - b + ic
        m_new = jnp.maximum(b_last + m, jnp.max(g_s, axis=-1))
        carry_decay = jnp.exp(b_last + m - m_new)
        w_s = jnp.exp(g_s - m_new[..., None])
        C_new = carry_decay[..., None, None] * C + jnp.einsum('bhs,bhsd,bhse->bhde', w_s, kc, vc)
        n_new = carry_decay[..., None] * n + jnp.einsum('bhs,bhsd->bhd', w_s, kc)
        return (C_new, n_new, m_new), h

    state, h = lax.scan(step, state, tuple(_to_chunks(a, L) for a in (q, k, v, ig, fg)))
    return _from_chunks(h), state


def _bidir_prefix_scan(scan_fn, init, ctx_fwd, lat_fwd, ctx_bwd, lat_bwd):
    flip = lambda t: tuple(jnp.flip(a, axis=2) for a in t)
    oc_f, st_f = scan_fn(*ctx_fwd, init)
    ol_f, _ = scan_fn(*lat_fwd, st_f)
    oc_b, st_b = scan_fn(*flip(ctx_bwd), init)
    ol_b, _ = scan_fn(*flip(lat_bwd), st_b)
    return oc_f + jnp.flip(oc_b, axis=2), ol_f + jnp.flip(ol_b, axis=2)


def _axial_rope_tables(n_lat):
    rows = n_lat // GRID_W
    row = jnp.repeat(jnp.arange(rows, dtype=jnp.float32), GRID_W)
    col = jnp.tile(jnp.arange(GRID_W, dtype=jnp.float32), rows)
    inv = jnp.power(ROPE_BASE, -jnp.arange(ROPE_PAIRS, dtype=jnp.float32) / ROPE_PAIRS)
    ang_r = row[:, None] * inv
    ang_c = col[:, None] * inv
    return (jnp.cos(ang_r), jnp.sin(ang_r), jnp.cos(ang_c), jnp.sin(ang_c))


def _rope_half(x, cos, sin):
    x1, x2 = jnp.split(x, 2, axis=-1)
    return jnp.concatenate([x1 * cos - x2 * sin, x1 * sin + x2 * cos], axis=-1)


def _axial_rope(x, tabs):
    cr, sr, cc, sc = (t[None, :, None, None, :] for t in tabs)
    xr, xc = jnp.split(x, 2, axis=-1)
    return jnp.concatenate([_rope_half(xr, cr, sr), _rope_half(xc, cc, sc)], axis=-1).astype(x.dtype)


def _diff_softmax(q, k, v, lam):
    s = jnp.einsum('bqhmd,bkhmd->bhmqk', q, k).astype(jnp.float32) * (DA_DQK ** -0.5)
    p = jax.nn.softmax(s, axis=-1)
    w = p[:, :, 0] - lam * p[:, :, 1]
    return jnp.einsum('bhqk,bkhe->bqhe', w.astype(v.dtype), v)


def _even_mixer(hc, hl, w_in, w_out, gate_w2, gate_b, gla_norm_g, sc_conv_w):
    def project(h):
        q, k, v, r, glr, sx, sb, scg = _split(h @ w_in, EVEN_SIZES)
        glr_f, glr_b = jnp.split(glr, 2, axis=-1)
        g_f = jax.nn.log_sigmoid(glr_f @ gate_w2[0] + gate_b[0]) / GLA_TAU
        g_b = jax.nn.log_sigmoid(glr_b @ gate_w2[1] + gate_b[1]) / GLA_TAU
        qkv = (_to_heads(q, GLA_HEADS) * (GLA_DK ** -0.5), _to_heads(k, GLA_HEADS), _to_heads(v, GLA_HEADS))
        conv_out = sb * _dwconv3(scg * sx, sc_conv_w)
        return qkv + (_to_heads(g_f, GLA_HEADS),), qkv + (_to_heads(g_b, GLA_HEADS),), r, conv_out

    fc, bc, rc, cc = project(hc)
    fl, bl, rl, cl = project(hl)
    bsz = hl.shape[0]
    init = jnp.zeros((bsz, GLA_HEADS, GLA_DK, GLA_DV), jnp.float32)
    oc, ol = _bidir_prefix_scan(_gla_scan, init, fc, fl, bc, bl)

    def finish(o, r, conv_out):
        o = _from_heads(_rmsnorm(o.astype(r.dtype), gla_norm_g)) * jax.nn.silu(r)
        return jnp.concatenate([o, conv_out], axis=-1) @ w_out

    return finish(oc, rc, cc), finish(ol, rl, cl)


def _odd_mixer(hc, hl, rope_tabs, layer, w_in, w_out, qn_g, kn_g, lam_p, subln_g,
               ml_conv_w, ml_gate_b, ml_norm_g):
    lam_init = 0.8 - 0.6 * math.exp(-0.3 * layer)
    lam = (jnp.exp(jnp.sum(lam_p[0] * lam_p[1]).astype(jnp.float32))
           - jnp.exp(jnp.sum(lam_p[2] * lam_p[3]).astype(jnp.float32)) + lam_init)

    def project(h, tabs):
        bsz, n, _ = h.shape
        dq, dk, dv, mq, mk, mv, mo, mg = _split(h @ w_in, ODD_SIZES)
        q = _rmsnorm(dq.reshape(bsz, n, DA_HEADS, 2, DA_DQK), qn_g)
        k = _rmsnorm(dk.reshape(bsz, n, DA_HEADS, 2, DA_DQK), kn_g)
        if tabs is not None:
            q = _axial_rope(q, tabs)
            k = _axial_rope(k, tabs)
        v = dv.reshape(bsz, n, DA_HEADS, DA_DV)
        mqk = jax.nn.silu(_dwconv3(jnp.concatenate([mq, mk], axis=-1), ml_conv_w))
        mq, mk = jnp.split(mqk, 2, axis=-1)
        gates = (mg + ml_gate_b).reshape(bsz, n, 2, 2, ML_HEADS)
        gates = jnp.moveaxis(gates, 1, -1).astype(jnp.float32)
        hq, hk, hv = _to_heads(mq, ML_HEADS), _to_heads(mk, ML_HEADS) * (ML_DK ** -0.5), _to_heads(mv, ML_HEADS)
        fwd = (hq, hk, hv, gates[:, 0, 0], gates[:, 0, 1])
        bwd = (hq, hk, hv, gates[:, 1, 0], gates[:, 1, 1])
        return (q, k, v), fwd, bwd, mo

    (qc, kc, vc), fc, bc, moc = project(hc, None)
    (ql, kl, vl), fl, bl, mol = project(hl, rope_tabs)

    da_c = _diff_softmax(qc, kc, vc, lam)
    bsz, n_lat = ql.shape[:2]
    k_all = jnp.concatenate([kc, kl], axis=1)
    v_all = jnp.concatenate([vc, vl], axis=1)
    qb = jnp.moveaxis(ql.reshape((bsz, n_lat // Q_BLOCK, Q_BLOCK) + ql.shape[2:]), 1, 0)
    da_l = lax.map(lambda qq: _diff_softmax(qq, k_all, v_all, lam), qb)
    da_l = jnp.moveaxis(da_l, 0, 1).reshape(bsz, n_lat, DA_HEADS, DA_DV)

    init = (jnp.zeros((bsz, ML_HEADS, ML_DK, ML_DV), jnp.float32),
            jnp.zeros((bsz, ML_HEADS, ML_DK), jnp.float32),
            jnp.zeros((bsz, ML_HEADS), jnp.float32))
    mc, ml = _bidir_prefix_scan(_mlstm_scan, init, fc, fl, bc, bl)

    def finish(da, m, mo):
        b_, n_ = da.shape[:2]
        da = (_rmsnorm(da, subln_g) * (1.0 - lam_init)).reshape(b_, n_, DA_HEADS * DA_DV)
        m = _from_heads(_rmsnorm(m.astype(mo.dtype), ml_norm_g)) * jax.nn.sigmoid(mo)
        return jnp.concatenate([da, m], axis=-1) @ w_out

    return finish(da_c, mc, moc), finish(da_l, ml, mol)


def _conv_ffn(h, w_up, conv_w, conv_b, w_down):
    gate, val = jnp.split(h @ w_up, 2, axis=-1)
    return (jax.nn.silu(_dwconv3(gate, conv_w) + conv_b) * val) @ w_down


def setup_inputs(seed: int = 0) -> dict:
    key = jax.random.key(seed)
    ks = iter(jax.random.split(key, 40))
    f32 = jnp.float32
    D = D_MODEL
    nrm = lambda shape, scale: scale * jax.random.normal(next(ks), shape, f32)
    gain = lambda shape: 1.0 + 0.02 * jax.random.normal(next(ks), shape, f32)
    ib = 0.1 * jax.random.normal(next(ks), (N_ODD, 2, 1, ML_HEADS), f32)
    fb = jax.random.uniform(next(ks), (N_ODD, 2, 1, ML_HEADS), f32, 3.0, 6.0)
    return {
        'x': nrm((BATCH, SEQ, D), 1.0),
        'c': nrm((BATCH, D), 1.0),
        'ctx': nrm((BATCH, CTX_LEN, D), 1.0),
        'c_ctx': nrm((D,), 1.0),
        'ada_w': nrm((DEPTH, D, 6 * D), 0.3 * D ** -0.5),
        'ada_b': nrm((DEPTH, 6 * D), 0.02),
        'norm1_g': gain((DEPTH, D)),
        'norm2_g': gain((DEPTH, D)),
        'ev_w_in': nrm((N_EVEN, D, EVEN_IN), D ** -0.5),
        'ev_w_out': nrm((N_EVEN, EVEN_MIX, D), EVEN_MIX ** -0.5),
        'gla_gate_w2': nrm((N_EVEN, 2, GLA_GATE_RANK, GLA_HEADS * GLA_DK), GLA_GATE_RANK ** -0.5),
        'gla_gate_b': nrm((N_EVEN, 2, GLA_HEADS * GLA_DK), 0.1),
        'gla_norm_g': gain((N_EVEN, GLA_DV)),
        'sc_conv_w': nrm((N_EVEN, 3, SC_WIDTH), 3 ** -0.5),
        'od_w_in': nrm((N_ODD, D, ODD_IN), D ** -0.5),
        'od_w_out': nrm((N_ODD, ODD_MIX, D), ODD_MIX ** -0.5),
        'da_qnorm_g': gain((N_ODD, DA_DQK)),
        'da_knorm_g': gain((N_ODD, DA_DQK)),
        'da_lambda': nrm((N_ODD, 4, DA_DQK), 0.1),
        'da_subln_g': gain((N_ODD, DA_DV)),
        'ml_conv_w': nrm((N_ODD, 3, 2 * ML_HEADS * ML_DK), 3 ** -0.5),
        'ml_gate_b': jnp.concatenate([ib, fb], axis=2).reshape(N_ODD, 4 * ML_HEADS),
        'ml_norm_g': gain((N_ODD, ML_DV)),
        'ffn_w_up': nrm((DEPTH, D, 2 * D_FF), D ** -0.5),
        'ffn_conv_w': nrm((DEPTH, 3, D_FF), 3 ** -0.5),
        'ffn_conv_b': nrm((DEPTH, D_FF), 0.02),
        'ffn_w_down': nrm((DEPTH, D_FF, D), D_FF ** -0.5),
    }


def reference(x, c, ctx, c_ctx, ada_w, ada_b, norm1_g, norm2_g, ev_w_in, ev_w_out, gla_gate_w2,
              gla_gate_b, gla_norm_g, sc_conv_w, od_w_in, od_w_out, da_qnorm_g, da_knorm_g, da_lambda,
              da_subln_g, ml_conv_w, ml_gate_b, ml_norm_g, ffn_w_up, ffn_conv_w, ffn_conv_b, ffn_w_down):
    n_lat = x.shape[1]
    rope_tabs = _axial_rope_tables(n_lat)
    xl, xc = x, ctx
    for layer in range(DEPTH):
        last = layer == DEPTH - 1
        mod_l = (jax.nn.silu(c) @ ada_w[layer] + ada_b[layer])[:, None, :]
        mod_c = (jax.nn.silu(c_ctx) @ ada_w[layer] + ada_b[layer])[None, None, :]
        sh1_l, sc1_l, g1_l, sh2_l, sc2_l, g2_l = jnp.split(mod_l, 6, axis=-1)
        sh1_c, sc1_c, g1_c, sh2_c, sc2_c, g2_c = jnp.split(mod_c, 6, axis=-1)

        hl = _rmsnorm(xl, norm1_g[layer]) * (1.0 + sc1_l) + sh1_l
        hc = _rmsnorm(xc, norm1_g[layer]) * (1.0 + sc1_c) + sh1_c
        if layer % 2 == 0:
            i = layer // 2
            oc, ol = _even_mixer(hc, hl, ev_w_in[i], ev_w_out[i], gla_gate_w2[i], gla_gate_b[i],
                                 gla_norm_g[i], sc_conv_w[i])
        else:
            i = layer // 2
            oc, ol = _odd_mixer(hc, hl, rope_tabs, layer, od_w_in[i], od_w_out[i], da_qnorm_g[i],
                                da_knorm_g[i], da_lambda[i], da_subln_g[i], ml_conv_w[i],
                                ml_gate_b[i], ml_norm_g[i])
        xl = xl + g1_l * ol
        hl = _rmsnorm(xl, norm2_g[layer]) * (1.0 + sc2_l) + sh2_l
        xl = xl + g2_l * _conv_ffn(hl, ffn_w_up[layer], ffn_conv_w[layer], ffn_conv_b[layer], ffn_w_down[layer])
        if not last:
            xc = xc + g1_c * oc
            hc = _rmsnorm(xc, norm2_g[layer]) * (1.0 + sc2_c) + sh2_c
            xc = xc + g2_c * _conv_ffn(hc, ffn_w_up[layer], ffn_conv_w[layer], ffn_conv_b[layer], ffn_w_down[layer])
    return xl
```

```python
import numpy as np
import ml_dtypes
import concourse.bass as bass
import concourse.mybir as mybir
from concourse.bass_utils import run_bass_kernel_spmd

F32 = mybir.dt.float32
BF16 = mybir.dt.bfloat16
AF = mybir.ActivationFunctionType
ALU = mybir.AluOpType

D = 2048
NCH = 16
DFF = 5632
FCH = 44
DEPTH = 4
SEQ = 4096
CTX = 256
NTOK = 1088
HW_ = 1092
EPS = 1e-6
NTILE = 34
TB = 4352


class Tok:
    __slots__ = ("w", "r", "name")

    def __init__(self, name=""):
        self.w = None
        self.r = {}
        self.name = name


class KB:
    def __init__(self):
        self.nc = bass.Bass("TRN2", target_bir_lowering=False)
        nc = self.nc
        self.engs = {"pe": nc.tensor, "dve": nc.vector, "act": nc.scalar, "pool": nc.gpsimd, "sp": nc.sync}
        self.sem = {}
        self.cnt = {}
        self.seen = {e: {} for e in self.engs}
        for e in ("pe", "dve", "act", "pool"):
            self.sem[e] = nc.alloc_semaphore(f"s_{e}")
            self.cnt[e] = 0
        self.semobj = {("eng", e): self.sem[e] for e in self.sem}
        self.dpool = {}
        for q, n in (("sp", 12), ("pool", 12), ("act", 4)):
            lst = []
            for i in range(n):
                s = nc.alloc_semaphore(f"d_{q}{i}")
                key = ("dma", q, i)
                self.semobj[key] = s
                lst.append([key, 0])
            self.dpool[q] = [lst, 0]
        self.out_events = []
        self._n = 0
        self.io = None
        self.scope = None
        self.uid = 0
        self._consts = None

    def sb(self, name, shape, dt):
        if self.scope is not None:
            return self.scope.enter_context(self.nc.sbuf_tensor(f"{name}_u{self.uid}", list(shape), dt)).ap()
        return self.nc.alloc_sbuf_tensor(name, list(shape), dt).ap()

    def ps(self, name, shape, dt):
        if self.scope is not None:
            return self.scope.enter_context(self.nc.psum_tensor(f"{name}_u{self.uid}", list(shape), dt)).ap()
        return self.nc.alloc_psum_tensor(name, list(shape), dt).ap()

    def dram(self, name, shape, dt, kind):
        if self.io is not None:
            ap = self.io[name]
            assert list(ap.shape) == list(shape), (name, ap.shape, shape)
            return ap
        return self.nc.dram_tensor(name, list(shape), dt, kind=kind).ap()

    def get_consts(self):
        if self._consts is None:
            sc, self.scope = self.scope, None
            self._consts = Consts(self)
            self.scope = sc
        return self._consts

    def phase(self, io):
        import contextlib
        kb = self

        @contextlib.contextmanager
        def cm():
            kb.uid += 1
            kb.io = io
            with contextlib.ExitStack() as st:
                kb.scope = st
                yield
                kb.barrier()
            kb.scope = None
            kb.io = None
        return cm()

    def end_phase(self):
        if self.io is not None:
            return None
        return self.finish()

    def _wait(self, eng, deps):
        seen = self.seen[eng]
        for key, val in deps.items():
            if eng == "pe" and key == ("eng", "pe"):
                continue
            if seen.get(key, 0) < val:
                self.engs[eng].wait_ge(self.semobj[key], val)
                seen[key] = val

    @staticmethod
    def _add(d, ev):
        if ev is None:
            return
        k, v = ev
        if d.get(k, 0) < v:
            d[k] = v

    def _deps(self, reads, writes):
        deps = {}
        for b in reads:
            self._add(deps, b.w)
        for b in writes:
            self._add(deps, b.w)
            for k, v in b.r.items():
                if deps.get(k, 0) < v:
                    deps[k] = v
        return deps

    def _commit(self, ev, reads, writes):
        for b in reads:
            self._add(b.r, ev)
        for b in writes:
            b.w = ev
            b.r = {}

    def op(self, eng, fn, reads=(), writes=()):
        self._wait(eng, self._deps(reads, writes))
        ins = fn(self.engs[eng])
        self.cnt[eng] += 1
        ins.then_inc(self.sem[eng], 1)
        ev = (("eng", eng), self.cnt[eng])
        self._commit(ev, reads, writes)
        return ev

    def group(self, eng, fns, reads=(), writes=()):
        self._wait(eng, self._deps(reads, writes))
        ins = None
        for fn in fns:
            ins = fn(self.engs[eng])
        self.cnt[eng] += 1
        ins.then_inc(self.sem[eng], 1)
        ev = (("eng", eng), self.cnt[eng])
        self._commit(ev, reads, writes)
        return ev

    def dma(self, q, out, in_, reads=(), writes=(), is_output=False, slow=False):
        lst, idx = self.dpool[q]
        ent = lst[idx]
        self.dpool[q][1] = (idx + 1) % len(lst)
        deps = self._deps(reads, writes)
        if ent[1] > 0:
            self._add(deps, (ent[0], ent[1]))
        self._wait(q, deps)
        ent[1] += 16
        kw = {"allow_slow_non_contiguous": True} if slow else {}
        self.engs[q].dma_start(out=out, in_=in_, **kw).then_inc(self.semobj[ent[0]], 16)
        ev = (ent[0], ent[1])
        self._commit(ev, reads, writes)
        if is_output:
            self.out_events.append(ev)
        return ev

    def finish(self):
        deps = {}
        for ev in self.out_events:
            self._add(deps, ev)
        self._wait("sp", deps)
        fin = {("eng", e): self.cnt[e] for e in self.cnt if self.cnt[e] > 0}
        self._wait("sp", fin)
        return self.nc


class Consts:
    def __init__(self, kb):
        self.ones = kb.sb("c_ones", [128, 128], F32)
        self.eps = kb.sb("c_eps", [128, 1], F32)
        self.t = Tok("consts")
        kb.op("pool", lambda e: e.memset(self.ones[:], 1.0), writes=[self.t])
        kb.op("pool", lambda e: e.memset(self.eps[:], EPS), writes=[self.t])


TOKBLK = [(0, 64, 0), (64, 512, 1), (576, 512, 1)]


def emit_modnorm(kb, cs, x, xt, acoef, bcoef, mt, h, ht, colmap, psA, psA_t, scr):
    sq, sq_t = scr["sq"], scr["sq_t"]
    rstd, rstd_t = scr["rstd"], scr["rstd_t"]
    tmp, tmp_t = scr["tmp"], scr["tmp_t"]
    for (t0, n, seg) in TOKBLK:
        fns = []
        for c in range(NCH):
            i = c % 2
            kb.op("act", lambda e, c=c, i=i: e.activation(out=sq[i][:, 0:n], in_=x[:, c, t0:t0 + n], func=AF.Square),
                  reads=[xt], writes=[sq_t[i]])
            kb.op("pe", lambda e, c=c, i=i: e.matmul(psA[:, 0:n], lhsT=cs.ones[:], rhs=sq[i][:, 0:n],
                                                      start=(c == 0), stop=(c == NCH - 1)),
                  reads=[sq_t[i], cs.t], writes=[psA_t])
        kb.op("act", lambda e: e.activation(out=rstd[:, 0:n], in_=psA[:, 0:n], func=AF.Sqrt,
                                             scale=1.0 / D, bias=cs.eps[:, 0:1]),
              reads=[psA_t, cs.t], writes=[rstd_t])
        kb.op("dve", lambda e: e.reciprocal(out=rstd[:, 0:n], in_=rstd[:, 0:n]), reads=[rstd_t], writes=[rstd_t])
        c0 = colmap(t0)
        for c in range(NCH):
            i = c % 2
            kb.op("dve", lambda e, c=c, i=i: e.tensor_tensor(out=tmp[i][:, 0:n], in0=x[:, c, t0:t0 + n],
                                                             in1=rstd[:, 0:n], op=ALU.mult),
                  reads=[xt, rstd_t], writes=[tmp_t[i]])
            kb.op("act", lambda e, c=c, i=i: e.activation(out=h[:, c, c0:c0 + n], in_=tmp[i][:, 0:n], func=AF.Identity,
                                                           scale=acoef[:, c, seg:seg + 1], bias=bcoef[:, c, seg:seg + 1]),
                  reads=[tmp_t[i], mt], writes=[ht])


def alloc_norm_scratch(kb):
    scr = {}
    scr["sq"] = [kb.sb(f"n_sq{i}", [128, 512], F32) for i in range(2)]
    scr["sq_t"] = [Tok() for _ in range(2)]
    scr["rstd"] = kb.sb("n_rstd", [128, 512], F32)
    scr["rstd_t"] = Tok()
    scr["tmp"] = [kb.sb(f"n_tmp{i}", [128, 512], F32) for i in range(2)]
    scr["tmp_t"] = [Tok() for _ in range(2)]
    return scr


def emit_coefs(kb, mods, mods_t, normg, normg_t, si_sh, si_sc, acoef, bcoef, ct):
    for seg in range(2):
        kb.op("dve", lambda e, seg=seg: e.scalar_tensor_tensor(out=acoef[:, :, seg], in0=mods[:, si_sc, :, seg], scalar=1.0,
                                                               in1=normg[:, :], op0=ALU.add, op1=ALU.mult),
              reads=[mods_t, normg_t], writes=[ct])
        kb.op("dve", lambda e, seg=seg: e.tensor_copy(out=bcoef[:, :, seg], in_=mods[:, si_sh, :, seg]),
              reads=[mods_t], writes=[ct])


def build_M(kb=None):
    kb = kb or KB()
    condT = kb.dram("condT", [128, NCH, 3], F32, "ExternalInput")
    adaw = kb.dram("adaw", [DEPTH, 128, NCH, 1536], F32, "ExternalInput")
    adab = kb.dram("adab", [128, DEPTH, 12], F32, "ExternalInput")
    out = kb.dram("modT", [128, DEPTH, 12, 3], F32, "ExternalOutput")
    sc = kb.sb("sc", [128, NCH, 4], F32); sc_t = Tok()
    bsb = kb.sb("bsb", [128, DEPTH, 12], F32); b_t = Tok()
    res = kb.sb("res", [128, DEPTH, 12, 3], F32); res_t = Tok()
    wb = [kb.sb(f"w{i}", [128, NCH, 768], F32) for i in range(2)]
    w_t = [Tok() for _ in range(2)]
    pst = [kb.ps(f"ps{i}", [128, 512], F32) for i in range(2)]
    ps_t = [Tok() for _ in range(2)]
    kb.dma("sp", sc[:, :, 0:3], condT, writes=[sc_t])
    kb.dma("sp", bsb[:], adab, writes=[b_t])
    kb.op("act", lambda e: e.activation(out=sc[:, :, 0:3], in_=sc[:, :, 0:3], func=AF.Silu), reads=[sc_t], writes=[sc_t])
    it = 0
    for l in range(DEPTH):
        for hf in range(2):
            i = it % 2
            it += 1
            kb.dma("sp", wb[i][:], adaw[l, :, :, hf * 768:(hf + 1) * 768], writes=[w_t[i]])
            for mm in range(6):
                m = hf * 6 + mm
                j = m % 2
                kb.group("pe", [lambda e, k=k, mm=mm, i=i, j=j: e.matmul(pst[j][:, 0:3], lhsT=wb[i][:, k, mm * 128:(mm + 1) * 128],
                                                                          rhs=sc[:, k, 0:3], start=(k == 0), stop=(k == NCH - 1))
                                 for k in range(NCH)],
                         reads=[w_t[i], sc_t], writes=[ps_t[j]])
                kb.op("dve", lambda e, l=l, m=m, j=j: e.tensor_scalar(out=res[:, l, m, :], in0=pst[j][:, 0:3],
                                                                       scalar1=bsb[:, l, m:m + 1], scalar2=None, op0=ALU.add),
                      reads=[ps_t[j], b_t], writes=[res_t])
    kb.dma("sp", out, res[:], reads=[res_t], is_output=True)
    return kb.end_phase()


def build_P0(kb=None):
    kb = kb or KB()
    xin = kb.dram("xT", [128, NCH, NTOK], F32, "ExternalInput")
    mods_d = kb.dram("mods", [128, 6, NCH, 2], F32, "ExternalInput")
    ng_d = kb.dram("normg", [128, NCH], F32, "ExternalInput")
    h_d = kb.dram("h1T", [128, NCH, NTOK], BF16, "ExternalOutput")
    cs = kb.get_consts()
    x = kb.sb("x", [128, NCH, NTOK], F32); xt = Tok()
    h = kb.sb("h", [128, NCH, NTOK], BF16); ht = Tok()
    mods = kb.sb("mods_s", [128, 6, NCH, 2], F32); mt = Tok()
    ng = kb.sb("ng", [128, NCH], F32); ngt = Tok()
    ac = kb.sb("ac", [128, NCH, 2], F32); bc = kb.sb("bc", [128, NCH, 2], F32); ct = Tok()
    psA = kb.ps("psA", [128, 512], F32); psA_t = Tok()
    scr = alloc_norm_scratch(kb)
    for c in range(0, NCH, 4):
        kb.dma("sp", x[:, c:c + 4, :], xin[:, c:c + 4, :], writes=[xt])
    kb.dma("sp", mods[:], mods_d, writes=[mt])
    kb.dma("sp", ng[:], ng_d, writes=[ngt])
    emit_coefs(kb, mods, mt, ng, ngt, 0, 1, ac, bc, ct)
    emit_modnorm(kb, cs, x, xt, ac, bc, ct, h, ht, lambda t: t, psA, psA_t, scr)
    for c in range(0, NCH, 4):
        kb.dma("sp", h_d[:, c:c + 4, :], h[:, c:c + 4, :], reads=[ht], is_output=True)
    return kb.end_phase()


def hcol(t):
    return 1 + t if t < 64 else 67 + (t - 64)


def build_B1(kb=None):
    kb = kb or KB()
    xin = kb.dram("xT", [128, NCH, NTOK], F32, "ExternalInput")
    mix_d = kb.dram("mixT", [128, NCH, NTOK], BF16, "ExternalInput")
    wout_d = kb.dram("wout", [NCH, 128, NCH, 128], F32, "ExternalInput")
    mods_d = kb.dram("mods", [128, 6, NCH, 2], F32, "ExternalInput")
    ng_d = kb.dram("normg", [128, NCH], F32, "ExternalInput")
    xo_d = kb.dram("xmidT", [128, NCH, NTOK], F32, "ExternalOutput")
    h_d = kb.dram("h2T", [128, NCH, NTOK], BF16, "ExternalOutput")
    cs = kb.get_consts()
    x = kb.sb("x", [128, NCH, NTOK], F32); xt = [Tok() for _ in range(NCH)]
    mix = kb.sb("mix", [128, NCH, NTOK], BF16); mixt = Tok()
    h = kb.sb("h", [128, NCH, NTOK], BF16); ht = Tok()
    mods = kb.sb("mods_s", [128, 6, NCH, 2], F32); mt = Tok()
    ng = kb.sb("ng", [128, NCH], F32); ngt = Tok()
    ac = kb.sb("ac", [128, NCH, 2], F32); bc = kb.sb("bc", [128, NCH, 2], F32); ct = Tok()
    wb = [kb.sb(f"w{i}", [128, NCH, 128], BF16) for i in range(3)]
    w_t = [Tok() for _ in range(3)]
    pss = [kb.ps(f"ps{i}", [128, 512], F32) for i in range(6)]
    ps_t = [Tok() for _ in range(6)]
    psA = kb.ps("psA", [128, 512], F32); psA_t = Tok()
    scr = alloc_norm_scratch(kb)
    for c in range(0, NCH, 4):
        kb.dma("sp", x[:, c:c + 4, :], xin[:, c:c + 4, :], writes=xt[c:c + 4])
        kb.dma("sp", mix[:, c:c + 4, :], mix_d[:, c:c + 4, :], writes=[mixt])
    kb.dma("sp", mods[:], mods_d, writes=[mt])
    kb.dma("sp", ng[:], ng_d, writes=[ngt])
    emit_coefs(kb, mods, mt, ng, ngt, 3, 4, ac, bc, ct)
    pi = 0
    for m in range(NCH):
        wi = m % 3
        kb.dma("pool", wb[wi][:], wout_d[m], writes=[w_t[wi]])
        for (t0, n, seg) in TOKBLK:
            j = pi % 6
            pi += 1
            kb.group("pe", [lambda e, k=k, wi=wi, j=j, t0=t0, n=n: e.matmul(pss[j][:, 0:n], lhsT=wb[wi][:, k, :], rhs=mix[:, k, t0:t0 + n],
                                                                            start=(k == 0), stop=(k == NCH - 1))
                             for k in range(NCH)],
                     reads=[w_t[wi], mixt], writes=[ps_t[j]])
            kb.op("dve", lambda e, m=m, j=j, t0=t0, n=n, seg=seg: e.scalar_tensor_tensor(
                out=x[:, m, t0:t0 + n], in0=pss[j][:, 0:n], scalar=mods[:, 2, m, seg:seg + 1],
                in1=x[:, m, t0:t0 + n], op0=ALU.mult, op1=ALU.add),
                reads=[ps_t[j], mt, xt[m]], writes=[xt[m]])
    deps_w = {}
    for m in range(NCH):
        KB._add(deps_w, xt[m].w)
    xm = Tok()
    for e in ("act", "dve", "pe", "sp"):
        kb._wait(e, deps_w)
    emit_modnorm(kb, cs, x, xm, ac, bc, ct, h, ht, lambda t: t, psA, psA_t, scr)
    for c in range(0, NCH, 4):
        kb.dma("sp", xo_d[:, c:c + 4, :], x[:, c:c + 4, :], reads=[xm], is_output=True)
        kb.dma("sp", h_d[:, c:c + 4, :], h[:, c:c + 4, :], reads=[ht], is_output=True)
    return kb.end_phase()


NQ = 4
QF = FCH // NQ
GBLK = [(0, 364), (364, 364), (728, 364)]
DBLK = [(1, 0, 64, 0), (67, 64, 512, 1), (579, 576, 512, 1)]


def build_B2(last, kb=None):
    kb = kb or KB()
    xin = kb.dram("xmidT", [128, NCH, NTOK], F32, "ExternalInput")
    h2_d = kb.dram("h2hT", [128, NCH, HW_], BF16, "ExternalInput")
    wup_d = kb.dram("wup", [FCH, 128, NCH, 256], F32, "ExternalInput")
    wdn_d = kb.dram("wdn", [NQ, NCH, 128, QF, 128], F32, "ExternalInput")
    cw_d = kb.dram("convw", [128, FCH, 4], F32, "ExternalInput")
    mods_d = kb.dram("mods", [128, 6, NCH, 2], F32, "ExternalInput")
    xo_d = kb.dram("xoT", [128, NCH, NTOK], F32, "ExternalOutput")
    cs = kb.get_consts()
    x = kb.sb("x", [128, NCH, NTOK], F32); xt = [Tok() for _ in range(NCH)]
    h2 = kb.sb("h2", [128, NCH, HW_], BF16); h2t = Tok()
    hid = kb.sb("hid", [128, QF, HW_], BF16); hid_t = [Tok() for _ in range(QF)]
    G = kb.sb("G", [128, HW_], F32); Gt = Tok()
    C = kb.sb("C", [128, HW_], F32); Ct = Tok()
    cw = kb.sb("cw", [128, FCH, 4], F32); cwt = Tok()
    mods = kb.sb("mods_s", [128, 6, NCH, 2], F32); mt = Tok()
    wu = [kb.sb(f"wu{i}", [128, NCH, 256], BF16) for i in range(2)]; wu_t = [Tok() for _ in range(2)]
    wd = [kb.sb(f"wd{i}", [128, QF, 128], BF16) for i in range(2)]; wd_t = [Tok() for _ in range(2)]
    pss = [kb.ps(f"ps{i}", [128, 512], F32) for i in range(8)]
    ps_t = [Tok() for _ in range(8)]
    for c in range(0, NCH, 4):
        kb.dma("sp", x[:, c:c + 4, :], xin[:, c:c + 4, :], writes=xt[c:c + 4])
        kb.dma("sp", h2[:, c:c + 4, :], h2_d[:, c:c + 4, :], writes=[h2t])
    kb.dma("sp", mods[:], mods_d, writes=[mt])
    kb.dma("sp", cw[:], cw_d, writes=[cwt])
    if not last:
        ng_d = kb.dram("normg", [128, NCH], F32, "ExternalInput")
        modsn_d = kb.dram("modsn", [128, 6, NCH, 2], F32, "ExternalInput")
        h1_d = kb.dram("h1T", [128, NCH, NTOK], BF16, "ExternalOutput")
        ng = kb.sb("ng", [128, NCH], F32); ngt = Tok()
        modsn = kb.sb("modsn_s", [128, 6, NCH, 2], F32); mnt = Tok()
        ac = kb.sb("ac", [128, NCH, 2], F32); bc = kb.sb("bc", [128, NCH, 2], F32); ct = Tok()
        kb.dma("sp", ng[:], ng_d, writes=[ngt])
        kb.dma("sp", modsn[:], modsn_d, writes=[mnt])
        emit_coefs(kb, modsn, mnt, ng, ngt, 0, 1, ac, bc, ct)
    wdi = 0
    for q in range(NQ):
        for mm in range(QF):
            m = q * QF + mm
            wi = m % 2
            kb.dma("pool", wu[wi][:], wup_d[m], writes=[wu_t[wi]])
            for half in range(2):
                for bi, (c0, n) in enumerate(GBLK):
                    j = half * 3 + bi
                    kb.group("pe", [lambda e, k=k, wi=wi, j=j, c0=c0, n=n, half=half: e.matmul(
                        pss[j][:, 0:n], lhsT=wu[wi][:, k, half * 128:(half + 1) * 128], rhs=h2[:, k, c0:c0 + n],
                        start=(k == 0), stop=(k == NCH - 1)) for k in range(NCH)],
                        reads=[wu_t[wi], h2t], writes=[ps_t[j]])
            for bi, (c0, n) in enumerate(GBLK):
                kb.op("act", lambda e, bi=bi, c0=c0, n=n: e.activation(out=G[:, c0:c0 + n], in_=pss[bi][:, 0:n], func=AF.Identity),
                      reads=[ps_t[bi]], writes=[Gt])
            W = HW_ - 2
            kb.op("dve", lambda e, m=m: e.tensor_scalar(out=C[:, 1:1 + W], in0=G[:, 1:1 + W], scalar1=cw[:, m, 1:2],
                                                        scalar2=cw[:, m, 3:4], op0=ALU.mult, op1=ALU.add),
                  reads=[Gt, cwt], writes=[Ct])
            kb.op("dve", lambda e, m=m: e.scalar_tensor_tensor(out=C[:, 1:1 + W], in0=G[:, 0:W], scalar=cw[:, m, 0:1],
                                                               in1=C[:, 1:1 + W], op0=ALU.mult, op1=ALU.add),
                  reads=[Gt, cwt, Ct], writes=[Ct])
            kb.op("dve", lambda e, m=m: e.scalar_tensor_tensor(out=C[:, 1:1 + W], in0=G[:, 2:2 + W], scalar=cw[:, m, 2:3],
                                                               in1=C[:, 1:1 + W], op0=ALU.mult, op1=ALU.add),
                  reads=[Gt, cwt, Ct], writes=[Ct])
            kb.op("act", lambda e: e.activation(out=C[:, 1:1 + W], in_=C[:, 1:1 + W], func=AF.Silu), reads=[Ct], writes=[Ct])
            for bi, (c0, n) in enumerate(GBLK):
                a = max(c0, 1)
                b_ = min(c0 + n, HW_ - 1)
                kb.op("dve", lambda e, bi=bi, a=a, b_=b_, c0=c0, mm=mm: e.tensor_tensor(
                    out=hid[:, mm, a:b_], in0=pss[3 + bi][:, a - c0:b_ - c0], in1=C[:, a:b_], op=ALU.mult),
                    reads=[ps_t[3 + bi], Ct], writes=[hid_t[mm]])
        for mo in range(NCH):
            wi = wdi % 2
            wdi += 1
            kb.dma("pool", wd[wi][:], wdn_d[q, mo], writes=[wd_t[wi]])
            for bi, (hc0, t0, n, seg) in enumerate(DBLK):
                j = 6 + (bi % 2)
                kb.group("pe", [lambda e, kk=kk, wi=wi, j=j, hc0=hc0, n=n: e.matmul(
                    pss[j][:, 0:n], lhsT=wd[wi][:, kk, :], rhs=hid[:, kk, hc0:hc0 + n],
                    start=(kk == 0), stop=(kk == QF - 1)) for kk in range(QF)],
                    reads=[wd_t[wi]] + hid_t, writes=[ps_t[j]])
                kb.op("dve", lambda e, mo=mo, j=j, t0=t0, n=n, seg=seg: e.scalar_tensor_tensor(
                    out=x[:, mo, t0:t0 + n], in0=pss[j][:, 0:n], scalar=mods[:, 5, mo, seg:seg + 1],
                    in1=x[:, mo, t0:t0 + n], op0=ALU.mult, op1=ALU.add),
                    reads=[ps_t[j], mt, xt[mo]], writes=[xt[mo]])
    deps_w = {}
    for m in range(NCH):
        KB._add(deps_w, xt[m].w)
    for e in ("act", "dve", "pe", "sp"):
        kb._wait(e, deps_w)
    xm = Tok()
    for c in range(0, NCH, 4):
        kb.dma("sp", xo_d[:, c:c + 4, :], x[:, c:c + 4, :], reads=[xm], is_output=True)
    if not last:
        scr = alloc_norm_scratch(kb)
        h1v = h2[:, :, 0:NTOK]
        emit_modnorm(kb, cs, x, xm, ac, bc, ct, h2, h2t, lambda t: t, pss[0], ps_t[0], scr)
        for c in range(0, NCH, 4):
            kb.dma("sp", h1_d[:, c:c + 4, :], h1v[:, c:c + 4, :], reads=[h2t], is_output=True)
    return kb.end_phase()


def fm(a):
    t, f = a.shape
    return np.ascontiguousarray(a.reshape(t, f // 128, 128).transpose(2, 1, 0))


def unfm(a):
    p, c, t = a.shape
    return np.ascontiguousarray(a.transpose(2, 1, 0).reshape(t, c * 128))


def vec_fm(v):
    return np.ascontiguousarray(v.reshape(-1, 128).T)


def core_tokens(x_b, ctx_b, j):
    return np.concatenate([ctx_b[64 * j:64 * j + 64], x_b[1024 * j:1024 * j + 1024]], axis=0)


def run(nc, in_maps):
    res = run_bass_kernel_spmd(nc, in_maps, core_ids=list(range(8)))
    return res.results


_CACHE = {}


def get_nc(name, builder, *args):
    key = (name,) + args
    if key not in _CACHE:
        _CACHE[key] = builder(*args)
    return _CACHE[key]


def host_M(c, c_ctx, ada_w, ada_b):
    cond = np.stack([c[0], c[1], c_ctx], axis=0)
    condT = np.ascontiguousarray(cond.reshape(3, NCH, 128).transpose(2, 1, 0))
    in_maps = []
    for core in range(8):
        sl = slice(core * 1536, (core + 1) * 1536)
        aw = np.ascontiguousarray(ada_w[:, :, sl].reshape(DEPTH, NCH, 128, 1536).transpose(0, 2, 1, 3))
        ab = np.ascontiguousarray(ada_b[:, sl].reshape(DEPTH, 12, 128).transpose(2, 0, 1))
        in_maps.append({"condT": condT, "adaw": aw, "adab": ab})
    res = run(get_nc("M", build_M), in_maps)
    full = np.zeros((DEPTH, 6 * D, 3), np.float32)
    for core in range(8):
        r = res[core]["modT"]
        full[:, core * 1536:(core + 1) * 1536, :] = r.transpose(1, 2, 0, 3).reshape(DEPTH, 1536, 3)
    mods = [[None, None] for _ in range(DEPTH)]
    for l in range(DEPTH):
        for b in range(2):
            m = full[l][:, [2, b]]
            mods[l][b] = np.ascontiguousarray(m.reshape(6, NCH, 128, 2).transpose(2, 0, 1, 3))
    return mods, full


def host_wout(w):
    rows = []
    for i in range(4):
        for cc in range(4):
            r0 = 256 * i + 128 * cc if cc < 2 else 1024 + 256 * i + 128 * (cc - 2)
            rows.append(w[r0:r0 + 128])
    wp = np.stack(rows, axis=0)
    return np.ascontiguousarray(wp.reshape(NCH, 128, NCH, 128).transpose(2, 1, 0, 3))


def host_wup(w):
    wk = w.reshape(NCH, 128, 2, FCH, 128)
    return np.ascontiguousarray(wk.transpose(3, 1, 0, 2, 4).reshape(FCH, 128, NCH, 256))


def host_wdn(w):
    wk = w.reshape(NQ, QF, 128, NCH, 128)
    return np.ascontiguousarray(wk.transpose(0, 3, 2, 1, 4))


def host_convw(cw, cb):
    a = np.concatenate([cw, cb[None]], axis=0)
    return np.ascontiguousarray(a.reshape(4, FCH, 128).transpose(2, 1, 0))


def host_halo(h2_cores):
    outs = []
    for core in range(8):
        b, j = divmod(core, 4)
        h = h2_cores[core]
        o = np.zeros((128, NCH, HW_), h.dtype)
        o[:, :, 1:65] = h[:, :, 0:64]
        o[:, :, 67:1091] = h[:, :, 64:1088]
        if j > 0:
            hp = h2_cores[core - 1]
            o[:, :, 0] = hp[:, :, 63]
            o[:, :, 66] = hp[:, :, 1087]
        if j < 3:
            hn = h2_cores[core + 1]
            o[:, :, 65] = hn[:, :, 0]
            o[:, :, 1091] = hn[:, :, 64]
        outs.append(o)
    return outs


def kb_barrier(kb):
    deps = {("eng", e): kb.cnt[e] for e in kb.cnt if kb.cnt[e] > 0}
    for q in kb.dpool:
        for key, val in kb.dpool[q][0]:
            if val > 0:
                deps[key] = val
    for e in kb.engs:
        kb._wait(e, dict(deps))


KB.barrier = kb_barrier

ABLK = [(0, 2)] + [(2 + 4 * i, 4) for i in range(8)]
BWD_ORDER = [1, 0] + list(range(33, 1, -1))
QSCALE = 128 ** -0.5


def build_Aeven(kb=None):
    import contextlib
    kb = kb or KB()
    nc = kb.nc
    h1_d = kb.dram("h1T", [128, NCH, TB], BF16, "ExternalInput")
    wA_d = kb.dram("wA", [128, NCH, 896], F32, "ExternalInput")
    wC_d = kb.dram("wC", [128, NCH, 768], F32, "ExternalInput")
    w2_d = kb.dram("w2", [128, 2, 128], F32, "ExternalInput")
    gb_d = kb.dram("gbias", [128, 256], F32, "ExternalInput")
    gn_d = kb.dram("gnorm", [128, 256], F32, "ExternalInput")
    scw_d = kb.dram("scw", [128, 2, 3], F32, "ExternalInput")
    msk_d = kb.dram("masks", [128, 4, 128], F32, "ExternalInput")
    id_d = kb.dram("ident", [128, 128], BF16, "ExternalInput")
    out_d = kb.dram("mixT", [128, 4, TB], BF16, "ExternalOutput")
    cs = kb.get_consts()
    pss = [kb.ps(f"ps{i}", [128, 512], F32) for i in range(7)]
    ps_t = [Tok() for _ in range(7)]
    psT = kb.ps("psT", [128, 1024], BF16); psT_t = Tok()
    hb = [kb.sb(f"hb{i}", [128, NCH, 512], BF16) for i in range(1)]; hb_t = [Tok()]

    with contextlib.ExitStack() as st:
        def sbx(name, shape, dt):
            return st.enter_context(nc.sbuf_tensor(f"{name}_u{kb.uid}", list(shape), dt)).ap()
        wC = sbx("wC_s", [128, NCH, 768], BF16); wC_t = Tok()
        U = sbx("u_s", [128, 2, TB + 4], F32); U_t = Tok()
        SBf = sbx("sb_s", [128, 2, TB], F32); SB_t = Tok()
        SX = sbx("sx_s", [128, 512], F32); SX_t = Tok()
        CO = sbx("co_s", [128, 2, TB], F32); CO_t = Tok()
        COb = sbx("cob_s", [128, 2, TB], BF16); COb_t = Tok()
        scw = sbx("scw_s", [128, 2, 3], F32); scw_t = Tok()
        kb.dma("pool", wC[:], wC_d, writes=[wC_t])
        kb.dma("sp", scw[:], scw_d, writes=[scw_t])
        kb.op("pool", lambda e: e.memset(U[:], 0.0), writes=[U_t])

        def ucol(tok):
            return 1 + tok if tok < CTX else 3 + tok
        pi = 0
        for (t0, nt) in ABLK:
            n = nt * 128
            g0 = t0 * 128
            kb.dma("sp", hb[0][:, :, 0:n], h1_d[:, :, g0:g0 + n], writes=[hb_t[0]])
            for c2 in range(2):
                for which in range(3):
                    j = pi % 4
                    pi += 1
                    col = which * 256 + c2 * 128
                    kb.group("pe", [lambda e, k=k, j=j, col=col, n=n: e.matmul(
                        pss[j][:, 0:n], lhsT=wC[:, k, col:col + 128], rhs=hb[0][:, k, 0:n],
                        start=(k == 0), stop=(k == NCH - 1)) for k in range(NCH)],
                        reads=[wC_t, hb_t[0]], writes=[ps_t[j]])
                    if which == 0:
                        kb.op("act", lambda e, j=j, n=n: e.activation(out=SX[:, 0:n], in_=pss[j][:, 0:n], func=AF.Identity),
                              reads=[ps_t[j]], writes=[SX_t])
                    elif which == 1:
                        kb.op("act", lambda e, j=j, n=n, c2=c2, g0=g0: e.activation(out=SBf[:, c2, g0:g0 + n], in_=pss[j][:, 0:n],
                                                                                  func=AF.Identity),
                              reads=[ps_t[j]], writes=[SB_t])
                    else:
                        uc = ucol(g0)
                        kb.op("dve", lambda e, j=j, n=n, c2=c2, uc=uc: e.tensor_tensor(out=U[:, c2, uc:uc + n], in0=pss[j][:, 0:n],
                                                                                     in1=SX[:, 0:n], op=ALU.mult),
                              reads=[ps_t[j], SX_t], writes=[U_t])
        for c2 in range(2):
            for (g0, n) in ((0, CTX), (CTX, SEQ)):
                uc = ucol(g0)
                kb.op("dve", lambda e, c2=c2, g0=g0, n=n, uc=uc: e.tensor_scalar(
                    out=CO[:, c2, g0:g0 + n], in0=U[:, c2, uc:uc + n], scalar1=scw[:, c2, 1:2], scalar2=None, op0=ALU.mult),
                    reads=[U_t, scw_t], writes=[CO_t])
                kb.op("dve", lambda e, c2=c2, g0=g0, n=n, uc=uc: e.scalar_tensor_tensor(
                    out=CO[:, c2, g0:g0 + n], in0=U[:, c2, uc - 1:uc - 1 + n], scalar=scw[:, c2, 0:1],
                    in1=CO[:, c2, g0:g0 + n], op0=ALU.mult, op1=ALU.add),
                    reads=[U_t, scw_t, CO_t], writes=[CO_t])
                kb.op("dve", lambda e, c2=c2, g0=g0, n=n, uc=uc: e.scalar_tensor_tensor(
                    out=CO[:, c2, g0:g0 + n], in0=U[:, c2, uc + 1:uc + 1 + n], scalar=scw[:, c2, 2:3],
                    in1=CO[:, c2, g0:g0 + n], op0=ALU.mult, op1=ALU.add),
                    reads=[U_t, scw_t, CO_t], writes=[CO_t])
                kb.op("dve", lambda e, c2=c2, g0=g0, n=n: e.tensor_tensor(
                    out=COb[:, c2, g0:g0 + n], in0=CO[:, c2, g0:g0 + n], in1=SBf[:, c2, g0:g0 + n], op=ALU.mult),
                    reads=[CO_t, SB_t], writes=[COb_t])
        kb.dma("sp", out_d[:, 2:4, :], COb[:], reads=[COb_t], is_output=True)
        kb.barrier()

    wA = kb.sb("wA_s", [128, NCH, 896], BF16); wA_t = Tok()
    QF = kb.sb("QF", [128, TB], BF16); QB = kb.sb("QB", [128, TB], BF16)
    KF = kb.sb("KF", [128, TB], BF16); KBk = kb.sb("KBk", [128, TB], BF16)
    KTF = kb.sb("KTF", [128, NTILE, 128], BF16); KTB = kb.sb("KTB", [128, NTILE, 128], BF16)
    V = kb.sb("V", [128, NTILE, 256], BF16)
    SR = kb.sb("SR", [128, NTILE, 256], BF16)
    AT = kb.sb("AT", [128, NTILE, 128], BF16)
    SBF = kb.sb("SBF", [128, NTILE, 256], BF16)
    per_t = [Tok() for _ in range(NTILE)]
    qTb = kb.sb("qTb", [128, 512], F32); kTb = kb.sb("kTb", [128, 512], F32); qk_t = Tok()
    GL = kb.sb("GL", [128, 512], F32); GL_t = Tok()
    w2 = kb.sb("w2_s", [128, 2, 128], F32); gb = kb.sb("gb_s", [128, 256], F32); gn = kb.sb("gn_s", [128, 256], F32)
    msk = kb.sb("msk_s", [128, 4, 128], F32); ident = kb.sb("id_s", [128, 128], BF16)
    k_t = Tok()
    el = kb.sb("el", [128, 2, NTILE], F32); el_t = Tok()
    tg = kb.sb("tg", [128, 256], F32); tg_t = Tok()
    sg = kb.sb("sg", [128, 256], F32); sg_t = Tok()
    E1 = kb.sb("E1", [128, 256], F32); E2 = kb.sb("E2", [128, 256], F32); E2t = kb.sb("E2t", [128, 256], F32)
    E1_t, E2_t, E2t_t = Tok(), Tok(), Tok()
    a1 = kb.sb("a1", [128, 128], F32); a2 = kb.sb("a2", [128, 128], F32); a_t = Tok()
    srt = kb.sb("srt", [128, 256], F32); srt_t = Tok()
    kb.dma("pool", wA[:], wA_d, writes=[wA_t])
    for dst, src in ((w2, w2_d), (gb, gb_d), (gn, gn_d), (msk, msk_d), (ident, id_d)):
        kb.dma("sp", dst[:], src, writes=[k_t])

    for (t0, nt) in ABLK:
        n = nt * 128
        g0 = t0 * 128
        kb.dma("sp", hb[0][:, :, 0:n], h1_d[:, :, g0:g0 + n], writes=[hb_t[0]])
        for j, (col, m) in enumerate(((0, 128), (128, 128), (768, 128))):
            kb.group("pe", [lambda e, k=k, j=j, col=col, m=m, n=n: e.matmul(
                pss[j][0:m, 0:n], lhsT=wA[:, k, col:col + m], rhs=hb[0][:, k, 0:n],
                start=(k == 0), stop=(k == NCH - 1)) for k in range(NCH)],
                reads=[wA_t, hb_t[0]], writes=[ps_t[j]])
        kb.op("act", lambda e, n=n: e.activation(out=qTb[:, 0:n], in_=pss[0][:, 0:n], func=AF.Identity, scale=QSCALE),
              reads=[ps_t[0]], writes=[qk_t])
        kb.op("act", lambda e, n=n: e.activation(out=kTb[:, 0:n], in_=pss[1][:, 0:n], func=AF.Identity),
              reads=[ps_t[1]], writes=[qk_t])
        kb.op("dve", lambda e, n=n: e.tensor_copy(out=GL[:, 0:n], in_=pss[2][:, 0:n]), reads=[ps_t[2]], writes=[GL_t])
        for ti in range(nt):
            tl = t0 + ti
            c0 = ti * 128
            gc = tl * 128
            pt = per_t[tl]
            kb.group("pe", [lambda e, k=k, c0=c0: e.matmul(pss[4][:, 0:384], lhsT=hb[0][:, k, c0:c0 + 128], rhs=wA[:, k, 128:512],
                                                          start=(k == 0), stop=(k == NCH - 1)) for k in range(NCH)],
                     reads=[wA_t, hb_t[0]], writes=[ps_t[4]])
            kb.group("pe", [lambda e, k=k, c0=c0: e.matmul(pss[5][:, 0:256], lhsT=hb[0][:, k, c0:c0 + 128], rhs=wA[:, k, 512:768],
                                                          start=(k == 0), stop=(k == NCH - 1)) for k in range(NCH)],
                     reads=[wA_t, hb_t[0]], writes=[ps_t[5]])
            kb.group("pe", [lambda e, d=d, c0=c0: e.matmul(pss[6][:, d * 128:(d + 1) * 128], lhsT=GL[:, c0:c0 + 128], rhs=w2[:, d, :],
                                                          start=True, stop=True) for d in range(2)],
                     reads=[GL_t, k_t], writes=[ps_t[6]])
            kb.op("dve", lambda e: e.tensor_tensor(out=tg[:], in0=pss[6][:, 0:256], in1=gb[:], op=ALU.add),
                  reads=[ps_t[6], k_t], writes=[tg_t])
            kb.op("act", lambda e: e.activation(out=tg[:], in_=tg[:], func=AF.Exp, scale=-1.0), reads=[tg_t], writes=[tg_t])
            kb.op("act", lambda e: e.activation(out=sg[:], in_=tg[:], func=AF.Ln, bias=1.0, scale=1.0), reads=[tg_t], writes=[sg_t])
            kb.group("pe", [lambda e, d=d: e.matmul(pss[6][:, 256 + d * 128:256 + (d + 1) * 128], lhsT=sg[:, d * 128:(d + 1) * 128],
                                                   rhs=msk[:, d, :], start=True, stop=True) for d in range(2)],
                     reads=[sg_t, k_t], writes=[ps_t[6]])
            kb.group("pe", [lambda e, d=d: e.matmul(pss[3][:, d * 128:(d + 1) * 128], lhsT=msk[:, d, :],
                                                   rhs=sg[:, d * 128:(d + 1) * 128], start=True, stop=True) for d in range(2)],
                     reads=[sg_t, k_t], writes=[ps_t[3]])
            kb.op("act", lambda e: e.activation(out=E1[:], in_=pss[6][:, 256:512], func=AF.Exp), reads=[ps_t[6]], writes=[E1_t])
            kb.op("act", lambda e: e.activation(out=E2[:], in_=pss[6][:, 256:512], func=AF.Exp, scale=-1.0),
                  reads=[ps_t[6]], writes=[E2_t])
            kb.op("act", lambda e: e.activation(out=E2t[:], in_=pss[3][:, 0:256], func=AF.Exp, scale=-1.0),
                  reads=[ps_t[3]], writes=[E2t_t])
            kb.op("pool", lambda e, tl=tl: e.tensor_copy(out=el[:, 0, tl:tl + 1], in_=E1[:, 127:128]), reads=[E1_t], writes=[el_t])
            kb.op("pool", lambda e, tl=tl: e.tensor_copy(out=el[:, 1, tl:tl + 1], in_=E1[:, 128:129]), reads=[E1_t], writes=[el_t])
            kb.op("dve", lambda e, c0=c0, gc=gc: e.tensor_tensor(out=QF[:, gc:gc + 128], in0=qTb[:, c0:c0 + 128], in1=E1[:, 0:128], op=ALU.mult),
                  reads=[qk_t, E1_t], writes=[pt])
            kb.op("dve", lambda e, c0=c0, gc=gc: e.tensor_tensor(out=QB[:, gc:gc + 128], in0=qTb[:, c0:c0 + 128], in1=E1[:, 128:256], op=ALU.mult),
                  reads=[qk_t, E1_t], writes=[pt])
            kb.op("dve", lambda e, c0=c0, gc=gc: e.tensor_tensor(out=KF[:, gc:gc + 128], in0=kTb[:, c0:c0 + 128], in1=E2[:, 0:128], op=ALU.mult),
                  reads=[qk_t, E2_t], writes=[pt])
            kb.op("dve", lambda e, c0=c0, gc=gc: e.tensor_tensor(out=KBk[:, gc:gc + 128], in0=kTb[:, c0:c0 + 128], in1=E2[:, 128:256], op=ALU.mult),
                  reads=[qk_t, E2_t], writes=[pt])
            kb.op("dve", lambda e, tl=tl: e.tensor_tensor(out=KTF[:, tl, :], in0=pss[4][:, 0:128], in1=E2t[:, 0:128], op=ALU.mult),
                  reads=[ps_t[4], E2t_t], writes=[pt])
            kb.op("dve", lambda e, tl=tl: e.tensor_tensor(out=KTB[:, tl, :], in0=pss[4][:, 0:128], in1=E2t[:, 128:256], op=ALU.mult),
                  reads=[ps_t[4], E2t_t], writes=[pt])
            kb.op("act", lambda e, tl=tl: e.activation(out=V[:, tl, :], in_=pss[4][:, 128:384], func=AF.Identity),
                  reads=[ps_t[4]], writes=[pt])
            kb.op("act", lambda e: e.activation(out=srt[:], in_=pss[5][:, 0:256], func=AF.Silu), reads=[ps_t[5]], writes=[srt_t])
            kb.op("pool", lambda e, tl=tl: e.tensor_tensor(out=SR[:, tl, :], in0=srt[:], in1=gn[:], op=ALU.mult),
                  reads=[srt_t, k_t], writes=[pt])
            kb.group("pe", [lambda e, gc=gc: e.matmul(pss[5][:, 256:384], lhsT=KF[:, gc:gc + 128], rhs=QF[:, gc:gc + 128], start=True, stop=True),
                            lambda e, gc=gc: e.matmul(pss[5][:, 384:512], lhsT=KBk[:, gc:gc + 128], rhs=QB[:, gc:gc + 128], start=True, stop=True)],
                     reads=[pt], writes=[ps_t[5]])
            kb.op("dve", lambda e: e.tensor_tensor(out=a1[:], in0=pss[5][:, 256:384], in1=msk[:, 2, :], op=ALU.mult),
                  reads=[ps_t[5], k_t], writes=[a_t])
            kb.op("dve", lambda e: e.tensor_tensor(out=a2[:], in0=pss[5][:, 384:512], in1=msk[:, 3, :], op=ALU.mult),
                  reads=[ps_t[5], k_t, a_t], writes=[a_t])
            kb.op("pool", lambda e, tl=tl: e.tensor_tensor(out=AT[:, tl, :], in0=a1[:], in1=a2[:], op=ALU.add),
                  reads=[a_t], writes=[pt])

    S = kb.sb("S", [128, 256], F32); S_t = Tok()
    S2 = kb.sb("S2", [128, 256], F32); S2_t = Tok()
    sbf_t = [Tok() for _ in range(NTILE)]
    kb.op("pool", lambda e: e.memset(S[:], 0.0), writes=[S_t])
    for c in range(NTILE):
        kb.op("act", lambda e, c=c: e.activation(out=SBF[:, c, :], in_=S[:], func=AF.Identity), reads=[S_t], writes=[sbf_t[c]])
        if c == NTILE - 1:
            break
        kb.op("pe", lambda e, c=c: e.matmul(pss[0][:, 0:256], lhsT=KTF[:, c, :], rhs=V[:, c, :], start=True, stop=True),
              reads=[per_t[c]], writes=[ps_t[0]])
        kb.op("dve", lambda e: e.tensor_tensor(out=S2[:], in0=pss[0][:, 0:256], in1=S[:], op=ALU.add),
              reads=[ps_t[0], S_t], writes=[S2_t])
        kb.op("dve", lambda e, c=c: e.tensor_scalar(out=S[:], in0=S2[:], scalar1=el[:, 0, c:c + 1], scalar2=None, op0=ALU.mult),
              reads=[S2_t, el_t], writes=[S_t])

    Sb = kb.sb("Sb", [128, 256], F32); Sb_t = Tok()
    Sbb = kb.sb("Sbb", [128, 256], BF16); Sbb_t = Tok()
    ss = kb.sb("ss", [128, 2], F32); ss_t = Tok()
    junk = kb.sb("junk", [128, 256], F32); junk_t = Tok()
    on = kb.sb("on", [128, 256], BF16); on_t = Tok()
    ost = [kb.sb(f"ost{i}", [128, 2, 128], BF16) for i in range(2)]; ost_t = [Tok(), Tok()]
    kb.op("pool", lambda e: e.memset(Sb[:], 0.0), writes=[Sb_t])
    kb.op("pool", lambda e: e.memset(Sbb[:], 0.0), writes=[Sbb_t])
    for oi, c in enumerate(BWD_ORDER):
        gc = c * 128
        kb.group("pe", [lambda e, c=c, gc=gc: e.matmul(pss[1][:, 0:256], lhsT=QF[:, gc:gc + 128], rhs=SBF[:, c, :], start=True, stop=False),
                        lambda e, c=c, gc=gc: e.matmul(pss[1][:, 0:256], lhsT=QB[:, gc:gc + 128], rhs=Sbb[:], start=False, stop=False),
                        lambda e, c=c, gc=gc: e.matmul(pss[1][:, 0:256], lhsT=AT[:, c, :], rhs=V[:, c, :], start=False, stop=True)],
                 reads=[per_t[c], sbf_t[c], Sbb_t], writes=[ps_t[1]])
        if oi < NTILE - 1 and c != 0:
            pass
        kb.op("pe", lambda e, c=c: e.matmul(pss[2][:, 0:256], lhsT=KTB[:, c, :], rhs=V[:, c, :], start=True, stop=True),
              reads=[per_t[c]], writes=[ps_t[2]])
        kb.op("dve", lambda e: e.tensor_tensor(out=S2[:], in0=pss[2][:, 0:256], in1=Sb[:], op=ALU.add),
              reads=[ps_t[2], Sb_t], writes=[S2_t])
        kb.op("dve", lambda e, c=c: e.tensor_scalar(out=Sb[:], in0=S2[:], scalar1=el[:, 1, c:c + 1], scalar2=None, op0=ALU.mult),
              reads=[S2_t, el_t], writes=[Sb_t])
        kb.op("act", lambda e: e.activation(out=Sbb[:], in_=Sb[:], func=AF.Identity), reads=[Sb_t], writes=[Sbb_t])
        kb.op("act", lambda e: e.activation(out=junk[:], in_=pss[1][:, 0:256], func=AF.Square, accum_out=ss[:, 0:1]),
              reads=[ps_t[1]], writes=[junk_t, ss_t])
        kb.op("act", lambda e: e.activation(out=ss[:, 1:2], in_=ss[:, 0:1], func=AF.Sqrt, scale=1.0 / 256, bias=cs.eps[:, 0:1]),
              reads=[ss_t, cs.t], writes=[ss_t])
        kb.op("dve", lambda e: e.reciprocal(out=ss[:, 1:2], in_=ss[:, 1:2]), reads=[ss_t], writes=[ss_t])
        kb.op("dve", lambda e, c=c: e.scalar_tensor_tensor(out=on[:], in0=pss[1][:, 0:256], scalar=ss[:, 1:2], in1=SR[:, c, :],
                                                          op0=ALU.mult, op1=ALU.mult),
              reads=[ps_t[1], ss_t, per_t[c]], writes=[on_t])
        kb.group("pe", [lambda e, hh=hh: e.transpose(psT[:, hh * 128:(hh + 1) * 128], on[:, hh * 128:(hh + 1) * 128], ident[:])
                        for hh in range(2)],
                 reads=[on_t, k_t], writes=[psT_t])
        oj = oi % 2
        kb.op("act", lambda e, oj=oj: e.activation(out=ost[oj][:].rearrange("p a b -> p (a b)"), in_=psT[:, 0:256], func=AF.Identity),
              reads=[psT_t], writes=[ost_t[oj]])
        kb.dma("sp", out_d[:, 0:2, gc:gc + 128], ost[oj][:], reads=[ost_t[oj]], is_output=True)
    return kb.end_phase()


def host_masks():
    j = np.arange(128)[:, None]
    i = np.arange(128)[None, :]
    mf = (j <= i).astype(np.float32)
    mb = (j >= i).astype(np.float32)
    return np.ascontiguousarray(np.stack([mf * (-1.0 / 16.0), mb * (-1.0 / 16.0), mf, mb], axis=1))


def wfm(w):
    return np.ascontiguousarray(w.reshape(NCH, 128, -1).transpose(1, 0, 2))


def host_Aeven_inputs(g, w_in, gate_w2, gate_b, gla_norm_g, sc_conv_w):
    o = np.cumsum((0,) + (512, 512, 1024, 1024, 32, 1024, 1024, 1024))
    qs, ks, vs, rs, gls, sxs, sbs, scgs = [int(v) for v in o[:8]]
    colsA = np.concatenate([np.arange(qs + 128 * g, qs + 128 * g + 128), np.arange(ks + 128 * g, ks + 128 * g + 128),
                            np.arange(vs + 256 * g, vs + 256 * g + 256), np.arange(rs + 256 * g, rs + 256 * g + 256),
                            np.arange(gls, gls + 32)])
    wa = w_in[:, colsA]
    pad = np.zeros((wa.shape[0], 128), wa.dtype)
    pad[:, 0:16] = wa[:, 768:784]
    pad[:, 32:48] = wa[:, 784:800]
    wa = np.concatenate([wa[:, 0:768], pad], axis=1)
    colsC = np.concatenate([np.arange(sxs + 256 * g, sxs + 256 * g + 256), np.arange(sbs + 256 * g, sbs + 256 * g + 256),
                            np.arange(scgs + 256 * g, scgs + 256 * g + 256)])
    hs = slice(128 * g, 128 * g + 128)
    w2 = np.zeros((128, 2, 128), np.float32)
    w2[0:16, 0, :] = gate_w2[0][:, hs]
    w2[32:48, 1, :] = gate_w2[1][:, hs]
    gbias = np.ascontiguousarray(np.broadcast_to(gate_b[:, hs].reshape(1, 256), (128, 256)))
    gnorm = np.ascontiguousarray(np.broadcast_to(gla_norm_g.reshape(1, 256), (128, 256)))
    scw = np.ascontiguousarray(sc_conv_w[:, 256 * g:256 * g + 256].reshape(3, 2, 128).transpose(2, 1, 0))
    return {"wA": wfm(wa), "wC": wfm(w_in[:, colsC]), "w2": w2, "gbias": gbias, "gnorm": gnorm, "scw": scw,
            "masks": host_masks(), "ident": np.eye(128, dtype=np.float32).astype(ml_dtypes.bfloat16)}


def assemble_h1(h1_cores, b):
    ctxp = [h1_cores[4 * b + j][:, :, 0:64] for j in range(4)]
    latp = [h1_cores[4 * b + j][:, :, 64:1088] for j in range(4)]
    return np.ascontiguousarray(np.concatenate(ctxp + latp, axis=2))


def scatter_mix(mix_cores, b, j):
    parts = []
    for g in range(4):
        m = mix_cores[4 * b + g]
        parts.append(np.concatenate([m[:, :, 64 * j:64 * j + 64], m[:, :, CTX + 1024 * j:CTX + 1024 * j + 1024]], axis=2))
    return np.ascontiguousarray(np.concatenate(parts, axis=1))


QBLK2 = [(0, 256, [0, 1])] + [(CTX + 512 * i, 512, list(range(NTILE))) for i in range(8)]
KSCALE = 128 ** -0.5


def build_Aodd(layer, kb=None):
    import contextlib
    import math
    lam_init = 0.8 - 0.6 * math.exp(-0.3 * layer)
    kb = kb or KB()
    nc = kb.nc
    h1_d = kb.dram("h1T", [128, NCH, TB], BF16, "ExternalInput")
    wD_d = kb.dram("wD", [128, NCH, 768], F32, "ExternalInput")
    wM_d = kb.dram("wM", [128, NCH, 772], F32, "ExternalInput")
    qkg_d = kb.dram("qkg", [128, 512], F32, "ExternalInput")
    rope_d = kb.dram("rope", [128, 32, 2, 64], F32, "ExternalInput")
    lamp_d = kb.dram("lamp", [128, 4, 64], F32, "ExternalInput")
    subg_d = kb.dram("subg", [128, 128], F32, "ExternalInput")
    subgc_d = kb.dram("subgc", [128, 1], F32, "ExternalInput")
    mlcw_d = kb.dram("mlcw", [128, 2, 3], F32, "ExternalInput")
    mlgb_d = kb.dram("mlgb", [128, 4], F32, "ExternalInput")
    mlng_d = kb.dram("mlng", [128, 256], F32, "ExternalInput")
    msk_d = kb.dram("masks", [128, 4, 128], F32, "ExternalInput")
    id_d = kb.dram("ident", [128, 128], BF16, "ExternalInput")
    out_d = kb.dram("mixT", [128, 4, TB], BF16, "ExternalOutput")
    cs = kb.get_consts()
    pss = [kb.ps(f"ps{i}", [128, 512], F32) for i in range(7)]
    ps_t = [Tok() for _ in range(7)]
    psT = kb.ps("psT", [128, 1024], BF16); psT_t = Tok()
    hb = kb.sb("hb", [128, NCH, 512], BF16); hb_t = Tok()
    ident = kb.sb("id_s", [128, 128], BF16)
    msk = kb.sb("msk_s", [128, 4, 128], F32)
    k_t = Tok()
    kb.dma("sp", ident[:], id_d, writes=[k_t])
    kb.dma("sp", msk[:], msk_d, writes=[k_t])
    ss = kb.sb("ss", [128, 4], F32); ss_t = Tok()
    junk = kb.sb("junk", [128, 256], F32); junk_t = Tok()
    ost = [kb.sb(f"ost{i}", [128, 2, 128], BF16) for i in range(2)]; ost_t = [Tok(), Tok()]
    onb = kb.sb("onb", [128, 256], BF16); onb_t = Tok()
    T1 = kb.sb("T1", [128, 256], F32); T1_t = Tok()
    T2 = kb.sb("T2", [128, 256], F32); T2_t = Tok()

    with contextlib.ExitStack() as st:
        def sbx(name, shape, dt):
            return st.enter_context(nc.sbuf_tensor(f"{name}_u{kb.uid}", list(shape), dt)).ap()
        wD = sbx("wD_s", [128, NCH, 768], BF16); wD_t = Tok()
        QKT = sbx("QKT", [128, 2, TB], BF16); qkt_t = [Tok() for _ in range(NTILE)]
        KTZ = sbx("KTZ", [128, 4, TB], BF16)
        VA = sbx("VA", [128, NTILE, 2, 132], BF16); va_t = [Tok() for _ in range(NTILE)]
        rope = sbx("rope_s", [128, 32, 2, 64], F32)
        qkg = sbx("qkg_s", [128, 512], F32)
        lamp = sbx("lamp_s", [128, 4, 64], F32)
        subg = sbx("subg_s", [128, 128], F32)
        SQ = sbx("SQ", [128, 512], F32); SQ_t = Tok()
        XN = sbx("XN", [128, 512], F32); XN_t = Tok()
        Y1 = sbx("Y1", [128, 512], F32); Y1_t = Tok()
        Y2 = sbx("Y2", [128, 512], F32); Y2_t = Tok()
        YB = sbx("YB", [128, 512], BF16); YB_t = Tok()
        rs8 = sbx("rs8", [128, 8], F32); rs8_t = Tok()
        PT = [sbx(f"PT{i}", [128, 512], BF16) for i in range(3)]; PT_t = [Tok(), Tok(), Tok()]
        lam = sbx("lam", [128, 8], F32); lam_t = Tok()
        rz = sbx("rz", [128, 4], F32); rz_t = Tok()
        kc_t = Tok()
        kb.dma("pool", wD[:], wD_d, writes=[wD_t])
        for dst, src in ((rope, rope_d), (qkg, qkg_d), (lamp, lamp_d), (subg, subg_d)):
            kb.dma("sp", dst[:], src, writes=[kc_t])
        kb.op("pool", lambda e: e.memset(VA[:], 1.0), writes=va_t)
        kb.op("pool", lambda e: e.memset(KTZ[:], 0.0), writes=qkt_t)
        kb.op("dve", lambda e: e.tensor_tensor(out=SQ[:, 0:64], in0=lamp[:, 0, :], in1=lamp[:, 1, :], op=ALU.mult),
              reads=[kc_t], writes=[SQ_t])
        kb.op("dve", lambda e: e.tensor_tensor(out=SQ[:, 64:128], in0=lamp[:, 2, :], in1=lamp[:, 3, :], op=ALU.mult),
              reads=[kc_t], writes=[SQ_t])
        kb.op("dve", lambda e: e.tensor_reduce(out=lam[:, 0:2], in_=SQ[:, 0:128].rearrange("p (a b) -> p a b", a=2),
                                               axis=mybir.AxisListType.X, op=ALU.add), reads=[SQ_t], writes=[lam_t])
        kb.op("act", lambda e: e.activation(out=lam[:, 2:4], in_=lam[:, 0:2], func=AF.Exp), reads=[lam_t], writes=[lam_t])
        kb.op("dve", lambda e: e.tensor_tensor(out=lam[:, 4:5], in0=lam[:, 3:4], in1=lam[:, 2:3], op=ALU.subtract),
              reads=[lam_t], writes=[lam_t])
        kb.op("dve", lambda e: e.tensor_scalar(out=lam[:, 4:5], in0=lam[:, 4:5], scalar1=-lam_init, scalar2=None, op0=ALU.add),
              reads=[lam_t], writes=[lam_t])
        kb.op("dve", lambda e: e.tensor_scalar(out=subg[:], in0=subg[:], scalar1=(1.0 - lam_init), scalar2=None, op0=ALU.mult),
              reads=[kc_t], writes=[kc_t])

        for (t0, nt) in ABLK:
            n = nt * 128
            g0 = t0 * 128
            kb.dma("sp", hb[:, :, 0:n], h1_d[:, :, g0:g0 + n], writes=[hb_t])
            for ti in range(nt):
                tl = t0 + ti
                c0 = ti * 128
                gc = tl * 128
                pa = 2 * (tl % 2)
                pb = pa + 1
                kb.group("pe", [lambda e, k=k, c0=c0: e.matmul(pss[pa][:, 0:512], lhsT=hb[:, k, c0:c0 + 128], rhs=wD[:, k, 0:512],
                                                              start=(k == 0), stop=(k == NCH - 1)) for k in range(NCH)],
                         reads=[wD_t, hb_t], writes=[ps_t[pa]])
                kb.group("pe", [lambda e, k=k, c0=c0: e.matmul(pss[pb][:, 0:256], lhsT=hb[:, k, c0:c0 + 128], rhs=wD[:, k, 512:768],
                                                              start=(k == 0), stop=(k == NCH - 1)) for k in range(NCH)],
                         reads=[wD_t, hb_t], writes=[ps_t[pb]])
                kb.op("act", lambda e: e.activation(out=SQ[:], in_=pss[pa][:, 0:512], func=AF.Square), reads=[ps_t[pa]], writes=[SQ_t])
                kb.op("dve", lambda e: e.tensor_reduce(out=rs8[:], in_=SQ[:].rearrange("p (a b) -> p a b", a=8),
                                                       axis=mybir.AxisListType.X, op=ALU.add), reads=[SQ_t], writes=[rs8_t])
                kb.op("act", lambda e: e.activation(out=rs8[:], in_=rs8[:], func=AF.Sqrt, scale=1.0 / 64, bias=cs.eps[:, 0:1]),
                      reads=[rs8_t, cs.t], writes=[rs8_t])
                kb.op("dve", lambda e: e.reciprocal(out=rs8[:], in_=rs8[:]), reads=[rs8_t], writes=[rs8_t])
                kb.op("dve", lambda e: e.tensor_tensor(out=XN[:].rearrange("p (a b) -> p a b", a=8),
                                                       in0=pss[pa][:, 0:512].rearrange("p (a b) -> p a b", a=8),
                                                       in1=rs8[:].unsqueeze(2).broadcast_to([128, 8, 64]), op=ALU.mult),
                      reads=[ps_t[pa], rs8_t], writes=[XN_t])
                kb.op("pool", lambda e: e.tensor_tensor(out=XN[:], in0=XN[:], in1=qkg[:], op=ALU.mult), reads=[XN_t, kc_t], writes=[XN_t])
                if tl >= 2:
                    lt = tl - 2
                    xv = XN[:].rearrange("p (g h x d) -> p g h x d", g=8, h=2, x=2)
                    yv = Y2[:].rearrange("p (g h x d) -> p g h x d", g=8, h=2, x=2)
                    sv = rope[:, lt, 1, :].rearrange("p (h x d) -> p h x d", h=2, x=2)
                    kb.op("dve", lambda e, lt=lt: e.tensor_tensor(out=Y1[:].rearrange("p (a b) -> p a b", a=8),
                                                                 in0=XN[:].rearrange("p (a b) -> p a b", a=8),
                                                                 in1=rope[:, lt:lt + 1, 0, :].broadcast_to([128, 8, 64]), op=ALU.mult),
                          reads=[XN_t, kc_t], writes=[Y1_t])
                    kb.op("dve", lambda e, xv=xv, yv=yv, sv=sv: e.tensor_tensor(
                        out=yv[:, :, :, 0, :], in0=xv[:, :, :, 1, :],
                        in1=sv[:, :, 0, :].unsqueeze(1).broadcast_to([128, 8, 2, 16]), op=ALU.mult),
                        reads=[XN_t, kc_t], writes=[Y2_t])
                    kb.op("dve", lambda e, xv=xv, yv=yv, sv=sv: e.tensor_tensor(
                        out=yv[:, :, :, 1, :], in0=xv[:, :, :, 0, :],
                        in1=sv[:, :, 1, :].unsqueeze(1).broadcast_to([128, 8, 2, 16]), op=ALU.mult),
                        reads=[XN_t, kc_t], writes=[Y2_t])
                    kb.op("pool", lambda e: e.tensor_tensor(out=YB[:], in0=Y1[:], in1=Y2[:], op=ALU.add),
                          reads=[Y1_t, Y2_t], writes=[YB_t])
                else:
                    kb.op("act", lambda e: e.activation(out=YB[:], in_=XN[:], func=AF.Identity), reads=[XN_t], writes=[YB_t])
                kb.group("pe", [lambda e, a=a: e.transpose(psT[:, a * 128:(a + 1) * 128], YB[:, a * 128:(a + 1) * 128], ident[:])
                                for a in range(4)], reads=[YB_t, k_t], writes=[psT_t])
                kb.op("act", lambda e, gc=gc: e.activation(out=QKT[:, :, gc:gc + 128],
                                                           in_=psT[:, 0:256].rearrange("p (a b) -> p a b", a=2), func=AF.Identity),
                      reads=[psT_t], writes=[qkt_t[tl]])
                for hh in range(2):
                    for mm in range(2):
                        kb.op("dve" if (hh + mm) % 2 else "act",
                              (lambda e, hh=hh, mm=mm, gc=gc: e.tensor_copy(
                                  out=KTZ[mm * 64:(mm + 1) * 64, hh * 2 + mm, gc:gc + 128],
                                  in_=psT[mm * 64:(mm + 1) * 64, (2 + hh) * 128:(3 + hh) * 128])) if (hh + mm) % 2 else
                              (lambda e, hh=hh, mm=mm, gc=gc: e.activation(
                                  out=KTZ[mm * 64:(mm + 1) * 64, hh * 2 + mm, gc:gc + 128],
                                  in_=psT[mm * 64:(mm + 1) * 64, (2 + hh) * 128:(3 + hh) * 128], func=AF.Identity)),
                              reads=[psT_t], writes=[qkt_t[tl]])
                kb.op("act", lambda e, tl=tl: e.activation(out=VA[:, tl, :, 0:128],
                                                           in_=pss[pb][:, 0:256].rearrange("p (a b) -> p a b", a=2), func=AF.Identity),
                      reads=[ps_t[pb]], writes=[va_t[tl]])

        T0 = sbx("T0", [128, 4, 128], F32); T0_t = Tok()
        OSB = [sbx(f"OSB{i}", [128, 4, 132], F32) for i in range(2)]; OSB_t = [Tok(), Tok()]
        oi = 0
        for h in range(2):
            for (q0, nq, ktiles) in QBLK2:
                nqt = nq // 128
                qtl = [q0 // 128 + i for i in range(nqt)]
                nk = len(ktiles)
                for m in range(2):
                    SB = [0, 1, 6]

                    def st_fn(ii, h=h, q0=q0, nq=nq, ktiles=ktiles, m=m):
                        kt = ktiles[ii]
                        bnk = SB[ii % 3]
                        return lambda e: e.matmul(
                            pss[bnk][:, 0:nq], lhsT=KTZ[:, h * 2 + m, kt * 128:(kt + 1) * 128],
                            rhs=QKT[:, h, q0:q0 + nq], start=True, stop=True)
                    qreads = [qkt_t[x] for x in qtl]
                    for pre in range(min(2, nk)):
                        kb.op("pe", st_fn(pre), reads=[qkt_t[ktiles[pre]]] + qreads, writes=[ps_t[SB[pre % 3]]])
                    for ii in range(nk):
                        kt = ktiles[ii]
                        bnk = SB[ii % 3]
                        pb = ii % 3
                        kb.op("act", lambda e, bnk=bnk, pb=pb, nq=nq: e.activation(out=PT[pb][:, 0:nq], in_=pss[bnk][:, 0:nq], func=AF.Exp, scale=0.125),
                              reads=[ps_t[bnk]], writes=[PT_t[pb]])
                        fns = [lambda e, qt=qt, pb=pb, kt=kt, ii=ii, nk=nk, h=h: e.matmul(
                            pss[2 + qt][:, 0:129], lhsT=PT[pb][:, qt * 128:(qt + 1) * 128],
                            rhs=VA[:, kt, h, 0:129], start=(ii == 0), stop=(ii == nk - 1)) for qt in range(nqt)]
                        rd = [PT_t[pb], va_t[kt]]
                        wr = [ps_t[2 + qt] for qt in range(nqt)]
                        if ii + 2 < nk:
                            fns.append(st_fn(ii + 2))
                            rd += [qkt_t[ktiles[ii + 2]]] + qreads
                            wr.append(ps_t[SB[(ii + 2) % 3]])
                        kb.group("pe", fns, reads=rd, writes=wr)
                    for qt in range(nqt):
                        kb.op("dve", lambda e, qt=qt, m=m: e.tensor_copy(out=OSB[m][:, qt, 0:129], in_=pss[2 + qt][:, 0:129]),
                              reads=[ps_t[2 + qt]], writes=[OSB_t[m]])
                    for qt in range(nqt):
                        kb.op("dve", lambda e, qt=qt, m=m: e.reciprocal(out=rz[:, 0:1], in_=OSB[m][:, qt, 128:129]), reads=[OSB_t[m]], writes=[rz_t])
                        if m == 0:
                            kb.op("pool", lambda e, qt=qt, m=m: e.tensor_scalar(out=T0[:, qt, :], in0=OSB[m][:, qt, 0:128], scalar1=rz[:, 0:1], scalar2=None,
                                                                                op0=ALU.mult),
                                  reads=[OSB_t[m], rz_t], writes=[T0_t])
                            continue
                        kb.op("dve", lambda e: e.tensor_tensor(out=rz[:, 2:3], in0=rz[:, 0:1], in1=lam[:, 4:5], op=ALU.mult),
                              reads=[rz_t, lam_t], writes=[rz_t])
                        kb.op("dve", lambda e, qt=qt, m=m: e.scalar_tensor_tensor(out=T2[:, 0:128], in0=OSB[m][:, qt, 0:128], scalar=rz[:, 2:3],
                                                                                 in1=T0[:, qt, :], op0=ALU.mult, op1=ALU.add),
                              reads=[OSB_t[m], rz_t, T0_t], writes=[T2_t])
                        kb.op("act", lambda e: e.activation(out=junk[:, 0:128], in_=T2[:, 0:128], func=AF.Square, accum_out=ss[:, 0:1]),
                              reads=[T2_t], writes=[junk_t, ss_t])
                        kb.op("act", lambda e: e.activation(out=ss[:, 1:2], in_=ss[:, 0:1], func=AF.Sqrt, scale=1.0 / 128, bias=cs.eps[:, 0:1]),
                              reads=[ss_t, cs.t], writes=[ss_t])
                        kb.op("dve", lambda e: e.reciprocal(out=ss[:, 1:2], in_=ss[:, 1:2]), reads=[ss_t], writes=[ss_t])
                        kb.op("dve", lambda e: e.scalar_tensor_tensor(out=onb[:, 0:128], in0=T2[:, 0:128], scalar=ss[:, 1:2], in1=subg[:],
                                                                      op0=ALU.mult, op1=ALU.mult),
                              reads=[T2_t, ss_t, kc_t], writes=[onb_t])
                        kb.op("pe", lambda e: e.transpose(psT[:, 0:128], onb[:, 0:128], ident[:]), reads=[onb_t, k_t], writes=[psT_t])
                        oj = oi % 2
                        oi += 1
                        kb.op("act", lambda e, oj=oj: e.activation(out=ost[oj][:, 0, :], in_=psT[:, 0:128], func=AF.Identity),
                              reads=[psT_t], writes=[ost_t[oj]])
                        qa = q0 + qt * 128
                        kb.dma("sp", out_d[:, h, qa:qa + 128], ost[oj][:, 0, :], reads=[ost_t[oj]], is_output=True)
        kb.barrier()

    QM = kb.sb("QM", [128, TB], BF16); KM = kb.sb("KM", [128, TB], BF16); qm_t = Tok()
    VA2 = kb.sb("VA2", [128, NTILE, 260], BF16); va2_t = [Tok() for _ in range(NTILE)]
    SO = kb.sb("SO", [128, NTILE, 256], BF16); so_t = [Tok() for _ in range(NTILE)]
    G4 = kb.sb("G4", [128, NTILE, 4], F32); g4_t = Tok()
    mlgb = kb.sb("mlgb_s", [128, 4], F32); mlng = kb.sb("mlng_s", [128, 256], F32); mlcw = kb.sb("mlcw_s", [128, 2, 3], F32)
    km_t = Tok()
    for dst, src in ((mlgb, mlgb_d), (mlng, mlng_d), (mlcw, mlcw_d)):
        kb.dma("sp", dst[:], src, writes=[km_t])
    kb.op("pool", lambda e: e.memset(VA2[:], 1.0), writes=va2_t)
    with contextlib.ExitStack() as st:
        def sbx(name, shape, dt):
            return st.enter_context(nc.sbuf_tensor(f"{name}_u{kb.uid}", list(shape), dt)).ap()
        wM = sbx("wM_s", [128, NCH, 772], BF16); wM_t = Tok()
        RQ = sbx("RQ", [128, 2, TB + 4], F32); RQ_t = Tok()
        CV = sbx("CV", [128, SEQ], F32); CV_t = Tok()
        g4s = sbx("g4s", [128, 4], F32); g4s_t = Tok()
        kb.dma("pool", wM[:], wM_d, writes=[wM_t])
        kb.op("pool", lambda e: e.memset(RQ[:], 0.0), writes=[RQ_t])

        def ucol(tok):
            return 1 + tok if tok < CTX else 3 + tok
        for (t0, nt) in ABLK:
            n = nt * 128
            g0 = t0 * 128
            kb.dma("sp", hb[:, :, 0:n], h1_d[:, :, g0:g0 + n], writes=[hb_t])
            for w in range(2):
                kb.group("pe", [lambda e, k=k, w=w, n=n: e.matmul(pss[w][:, 0:n], lhsT=wM[:, k, w * 128:(w + 1) * 128], rhs=hb[:, k, 0:n],
                                                                 start=(k == 0), stop=(k == NCH - 1)) for k in range(NCH)],
                         reads=[wM_t, hb_t], writes=[ps_t[w]])
                uc = ucol(g0)
                kb.op("act", lambda e, w=w, n=n, uc=uc: e.activation(out=RQ[:, w, uc:uc + n], in_=pss[w][:, 0:n], func=AF.Identity),
                      reads=[ps_t[w]], writes=[RQ_t])
            for ti in range(nt):
                tl = t0 + ti
                c0 = ti * 128
                kb.group("pe", [lambda e, k=k, c0=c0: e.matmul(pss[2][:, 0:512], lhsT=hb[:, k, c0:c0 + 128], rhs=wM[:, k, 256:768],
                                                              start=(k == 0), stop=(k == NCH - 1)) for k in range(NCH)],
                         reads=[wM_t, hb_t], writes=[ps_t[2]])
                kb.group("pe", [lambda e, k=k, c0=c0: e.matmul(pss[3][:, 0:4], lhsT=hb[:, k, c0:c0 + 128], rhs=wM[:, k, 768:772],
                                                              start=(k == 0), stop=(k == NCH - 1)) for k in range(NCH)],
                         reads=[wM_t, hb_t], writes=[ps_t[3]])
                kb.op("act", lambda e, tl=tl: e.activation(out=VA2[:, tl, 0:256], in_=pss[2][:, 0:256], func=AF.Identity),
                      reads=[ps_t[2]], writes=[va2_t[tl]])
                kb.op("act", lambda e: e.activation(out=T1[:], in_=pss[2][:, 256:512], func=AF.Sigmoid), reads=[ps_t[2]], writes=[T1_t])
                kb.op("pool", lambda e, tl=tl: e.tensor_tensor(out=SO[:, tl, :], in0=T1[:], in1=mlng[:], op=ALU.mult),
                      reads=[T1_t, km_t], writes=[so_t[tl]])
                kb.op("dve", lambda e, tl=tl: e.tensor_tensor(out=G4[:, tl, :], in0=pss[3][:, 0:4], in1=mlgb[:], op=ALU.add),
                      reads=[ps_t[3], km_t], writes=[g4_t])
                kb.op("act", lambda e, tl=tl: e.activation(out=g4s[:], in_=G4[:, tl, :], func=AF.Exp, scale=-1.0), reads=[g4_t], writes=[g4s_t])
                kb.op("act", lambda e: e.activation(out=g4s[:], in_=g4s[:], func=AF.Ln, bias=1.0, scale=1.0), reads=[g4s_t], writes=[g4s_t])
                for col in (1, 3):
                    kb.op("dve", lambda e, tl=tl, col=col: e.tensor_scalar(out=G4[:, tl, col:col + 1], in0=g4s[:, col:col + 1], scalar1=-1.0,
                                                                          scalar2=None, op0=ALU.mult),
                          reads=[g4s_t, g4_t], writes=[g4_t])
        for w, dst, scl in ((0, QM, 1.0), (1, KM, KSCALE)):
            for (g0, n) in ((0, CTX), (CTX, SEQ)):
                uc = ucol(g0)
                kb.op("dve", lambda e, w=w, n=n, uc=uc: e.tensor_scalar(out=CV[:, 0:n], in0=RQ[:, w, uc:uc + n], scalar1=mlcw[:, w, 1:2],
                                                                       scalar2=None, op0=ALU.mult), reads=[RQ_t, km_t], writes=[CV_t])
                kb.op("dve", lambda e, w=w, n=n, uc=uc: e.scalar_tensor_tensor(out=CV[:, 0:n], in0=RQ[:, w, uc - 1:uc - 1 + n],
                                                                              scalar=mlcw[:, w, 0:1], in1=CV[:, 0:n], op0=ALU.mult, op1=ALU.add),
                      reads=[RQ_t, km_t, CV_t], writes=[CV_t])
                kb.op("dve", lambda e, w=w, n=n, uc=uc: e.scalar_tensor_tensor(out=CV[:, 0:n], in0=RQ[:, w, uc + 1:uc + 1 + n],
                                                                              scalar=mlcw[:, w, 2:3], in1=CV[:, 0:n], op0=ALU.mult, op1=ALU.add),
                      reads=[RQ_t, km_t, CV_t], writes=[CV_t])
                kb.op("act", lambda e, n=n: e.activation(out=CV[:, 0:n], in_=CV[:, 0:n], func=AF.Silu), reads=[CV_t], writes=[CV_t])
                kb.op("dve", lambda e, dst=dst, g0=g0, n=n, scl=scl: e.tensor_scalar(out=dst[:, g0:g0 + n], in0=CV[:, 0:n], scalar1=scl,
                                                                                    scalar2=None, op0=ALU.mult), reads=[CV_t], writes=[qm_t])
        kb.barrier()

    QP = [kb.sb(f"QP{d}", [128, TB], BF16) for d in range(2)]
    STm = [kb.sb(f"STm{d}", [128, NTILE, 128], BF16) for d in range(2)]
    KP = [kb.sb(f"KP{d}", [128, NTILE, 128], BF16) for d in range(2)]
    CF = kb.sb("CF", [128, NTILE, 260], BF16); cf_t = [Tok() for _ in range(NTILE)]
    eL = kb.sb("eL", [128, 2, NTILE], F32); eL_t = Tok()
    ch_t = [Tok() for _ in range(NTILE)]
    LF = [kb.sb(f"LF{d}", [128, 128], F32) for d in range(2)]; LF_t = [Tok(), Tok()]
    TM = [kb.sb(f"TM{d}", [128, 128], F32) for d in range(2)]; TM_t = [Tok(), Tok()]
    EB = [kb.sb(f"EB{d}", [128, 128], F32) for d in range(2)]; EB_t = [Tok(), Tok()]
    cj = kb.sb("cj", [128, 8], F32); cj_t = Tok()
    for c in range(NTILE):
        gc = c * 128
        for d in range(2):
            kb.op("dve", lambda e, d=d, c=c: e.tensor_scalar(out=LF[d][:], in0=cs.ones[:], scalar1=G4[:, c, 2 * d + 1:2 * d + 2], scalar2=None,
                                                            op0=ALU.mult), reads=[g4_t, cs.t], writes=[LF_t[d]])
        kb.group("pe", [lambda e, d=d: e.matmul(pss[4][:, d * 128:(d + 1) * 128], lhsT=LF[d][:], rhs=msk[:, d, :], start=True, stop=True)
                        for d in range(2)] +
                 [lambda e, d=d, c=c: e.matmul(pss[4][:, 256 + 4 * d:260 + 4 * d], lhsT=msk[:, d, :], rhs=G4[:, c, :], start=True, stop=True)
                  for d in range(2)],
                 reads=LF_t + [k_t, g4_t], writes=[ps_t[4]])
        kb.op("pe", lambda e, gc=gc: e.matmul(pss[5][:, 0:128], lhsT=KM[:, gc:gc + 128], rhs=QM[:, gc:gc + 128], start=True, stop=True),
              reads=[qm_t], writes=[ps_t[5]])
        kb.op("pe", lambda e, gc=gc: e.transpose(psT[:, 0:128], KM[:, gc:gc + 128], ident[:]), reads=[qm_t, k_t], writes=[psT_t])
        for d in range(2):
            last = 127 if d == 0 else 0
            kb.op("dve", lambda e, d=d, c=c: e.tensor_tensor(out=cj[:, d:d + 1], in0=G4[:, c, 2 * d:2 * d + 1],
                                                            in1=pss[4][:, 256 + 4 * d + 2 * d + 1:256 + 4 * d + 2 * d + 2], op=ALU.subtract),
                  reads=[ps_t[4], g4_t], writes=[cj_t])
            kb.op("dve", lambda e, d=d, last=last: e.tensor_copy(out=cj[:, 2 + d:3 + d], in_=pss[4][:, d * 128 + last:d * 128 + last + 1]),
                  reads=[ps_t[4]], writes=[cj_t])
            kb.op("dve", lambda e, d=d: e.tensor_tensor(out=TM[d][:], in0=pss[4][:, d * 128:(d + 1) * 128], in1=msk[:, 2 + d, :], op=ALU.add),
                  reads=[ps_t[4], k_t], writes=[TM_t[d]])
            kb.op("act", lambda e, d=d: e.activation(out=TM[d][:], in_=TM[d][:], func=AF.Exp, bias=cj[:, d:d + 1], scale=1.0),
                  reads=[TM_t[d], cj_t], writes=[TM_t[d]])
            kb.op("act", lambda e, d=d: e.activation(out=EB[d][:], in_=pss[4][:, d * 128:(d + 1) * 128], func=AF.Exp),
                  reads=[ps_t[4]], writes=[EB_t[d]])
            kb.op("act", lambda e, d=d: e.activation(out=cj[:, 4 + d:5 + d], in_=cj[:, d:d + 1], func=AF.Exp, bias=cj[:, 2 + d:3 + d], scale=1.0),
                  reads=[cj_t], writes=[cj_t])
            kb.op("pool", lambda e, d=d, c=c, last=last: e.tensor_copy(out=eL[:, d, c:c + 1], in_=EB[d][:, last:last + 1]),
                  reads=[EB_t[d]], writes=[eL_t])
            kb.op("dve", lambda e, d=d, c=c: e.tensor_tensor(out=STm[d][:, c, :], in0=pss[5][:, 0:128], in1=TM[d][:], op=ALU.mult),
                  reads=[ps_t[5], TM_t[d]], writes=[ch_t[c]])
            kb.op("dve", lambda e, d=d, gc=gc: e.tensor_tensor(out=QP[d][:, gc:gc + 128], in0=QM[:, gc:gc + 128], in1=EB[d][:], op=ALU.mult),
                  reads=[qm_t, EB_t[d]], writes=[ch_t[c]])
            kb.op("dve", lambda e, d=d, c=c: e.tensor_scalar(out=KP[d][:, c, :], in0=psT[:, 0:128], scalar1=cj[:, 4 + d:5 + d], scalar2=None,
                                                            op0=ALU.mult), reads=[psT_t, cj_t], writes=[ch_t[c]])

    Cst = kb.sb("Cst", [128, 260], F32); Cst_t = Tok()
    kb.op("pool", lambda e: e.memset(Cst[:], 0.0), writes=[Cst_t])
    for c in range(NTILE):
        kb.op("act", lambda e, c=c: e.activation(out=CF[:, c, :], in_=Cst[:], func=AF.Identity), reads=[Cst_t], writes=[cf_t[c]])
        if c == NTILE - 1:
            break
        kb.op("pe", lambda e, c=c: e.matmul(pss[0][:, 0:257], lhsT=KP[0][:, c, :], rhs=VA2[:, c, 0:257], start=True, stop=True),
              reads=[ch_t[c], va2_t[c]], writes=[ps_t[0]])
        kb.op("dve", lambda e, c=c: e.scalar_tensor_tensor(out=Cst[:, 0:257], in0=Cst[:, 0:257], scalar=eL[:, 0, c:c + 1], in1=pss[0][:, 0:257],
                                                          op0=ALU.mult, op1=ALU.add), reads=[ps_t[0], eL_t, Cst_t], writes=[Cst_t])
    Cb = kb.sb("Cb", [128, 260], F32); Cb_t = Tok()
    Cbb = kb.sb("Cbb", [128, 260], BF16); Cbb_t = Tok()
    kb.op("pool", lambda e: e.memset(Cb[:], 0.0), writes=[Cb_t])
    kb.op("pool", lambda e: e.memset(Cbb[:], 0.0), writes=[Cbb_t])
    for oi, c in enumerate(BWD_ORDER):
        gc = c * 128
        kb.group("pe", [lambda e, c=c, gc=gc: e.matmul(pss[1][:, 0:257], lhsT=QP[0][:, gc:gc + 128], rhs=CF[:, c, 0:257], start=True, stop=False),
                        lambda e, c=c, gc=gc: e.matmul(pss[1][:, 0:257], lhsT=STm[0][:, c, :], rhs=VA2[:, c, 0:257], start=False, stop=True)],
                 reads=[ch_t[c], cf_t[c], va2_t[c]], writes=[ps_t[1]])
        kb.group("pe", [lambda e, c=c, gc=gc: e.matmul(pss[2][:, 0:257], lhsT=QP[1][:, gc:gc + 128], rhs=Cbb[:, 0:257], start=True, stop=False),
                        lambda e, c=c, gc=gc: e.matmul(pss[2][:, 0:257], lhsT=STm[1][:, c, :], rhs=VA2[:, c, 0:257], start=False, stop=True)],
                 reads=[ch_t[c], Cbb_t, va2_t[c]], writes=[ps_t[2]])
        kb.op("pe", lambda e, c=c: e.matmul(pss[3][:, 0:257], lhsT=KP[1][:, c, :], rhs=VA2[:, c, 0:257], start=True, stop=True),
              reads=[ch_t[c], va2_t[c]], writes=[ps_t[3]])
        kb.op("dve", lambda e, c=c: e.scalar_tensor_tensor(out=Cb[:, 0:257], in0=Cb[:, 0:257], scalar=eL[:, 1, c:c + 1], in1=pss[3][:, 0:257],
                                                          op0=ALU.mult, op1=ALU.add), reads=[ps_t[3], eL_t, Cb_t], writes=[Cb_t])
        kb.op("act", lambda e: e.activation(out=Cbb[:], in_=Cb[:], func=AF.Identity), reads=[Cb_t], writes=[Cbb_t])
        for d, bk in ((0, 1), (1, 2)):
            kb.op("act", lambda e, d=d, bk=bk: e.activation(out=ss[:, 2 + d:3 + d], in_=pss[bk][:, 256:257], func=AF.Abs),
                  reads=[ps_t[bk]], writes=[ss_t])
            kb.op("dve", lambda e, d=d: e.tensor_scalar(out=ss[:, 2 + d:3 + d], in0=ss[:, 2 + d:3 + d], scalar1=1.0, scalar2=None,
                                                       op0=ALU.max), reads=[ss_t], writes=[ss_t])
        kb.op("dve", lambda e: e.reciprocal(out=ss[:, 2:4], in_=ss[:, 2:4]), reads=[ss_t], writes=[ss_t])
        kb.op("act", lambda e: e.activation(out=T1[:], in_=pss[1][:, 0:256], func=AF.Identity, scale=ss[:, 2:3]),
              reads=[ps_t[1], ss_t], writes=[T1_t])
        kb.op("dve", lambda e: e.scalar_tensor_tensor(out=T2[:], in0=pss[2][:, 0:256], scalar=ss[:, 3:4], in1=T1[:], op0=ALU.mult, op1=ALU.add),
              reads=[ps_t[2], ss_t, T1_t], writes=[T2_t])
        kb.op("act", lambda e: e.activation(out=junk[:], in_=T2[:], func=AF.Square, accum_out=ss[:, 0:1]), reads=[T2_t], writes=[junk_t, ss_t])
        kb.op("act", lambda e: e.activation(out=ss[:, 1:2], in_=ss[:, 0:1], func=AF.Sqrt, scale=1.0 / 256, bias=cs.eps[:, 0:1]),
              reads=[ss_t, cs.t], writes=[ss_t])
        kb.op("dve", lambda e: e.reciprocal(out=ss[:, 1:2], in_=ss[:, 1:2]), reads=[ss_t], writes=[ss_t])
        kb.op("dve", lambda e, c=c: e.scalar_tensor_tensor(out=onb[:], in0=T2[:], scalar=ss[:, 1:2], in1=SO[:, c, :], op0=ALU.mult, op1=ALU.mult),
              reads=[T2_t, ss_t, so_t[c]], writes=[onb_t])
        kb.group("pe", [lambda e, hh=hh: e.transpose(psT[:, hh * 128:(hh + 1) * 128], onb[:, hh * 128:(hh + 1) * 128], ident[:])
                        for hh in range(2)], reads=[onb_t, k_t], writes=[psT_t])
        oj = oi % 2
        kb.op("act", lambda e, oj=oj: e.activation(out=ost[oj][:].rearrange("p a b -> p (a b)"), in_=psT[:, 0:256], func=AF.Identity),
              reads=[psT_t], writes=[ost_t[oj]])
        kb.dma("sp", out_d[:, 2:4, gc:gc + 128], ost[oj][:], reads=[ost_t[oj]], is_output=True)
    return kb.end_phase()


def host_rope():
    t = np.arange(SEQ)
    row = (t // 64).astype(np.float32)
    col = (t % 64).astype(np.float32)
    inv = np.power(np.float32(10000.0), -np.arange(16, dtype=np.float32) / np.float32(16)).astype(np.float32)
    ar = (row[:, None] * inv).astype(np.float32)
    ac = (col[:, None] * inv).astype(np.float32)
    cr, sr, cc, sc = np.cos(ar), np.sin(ar), np.cos(ac), np.sin(ac)
    C = np.concatenate([cr, cr, cc, cc], axis=1)
    S = np.concatenate([-sr, sr, -sc, sc], axis=1)
    tab = np.stack([C, S], axis=1).astype(np.float32)
    return np.ascontiguousarray(tab.reshape(32, 128, 2, 64).transpose(1, 0, 2, 3))


def host_masks_odd():
    j = np.arange(128)[:, None]
    i = np.arange(128)[None, :]
    mf = (j <= i).astype(np.float32)
    mb = (j >= i).astype(np.float32)
    return np.ascontiguousarray(np.stack([mf, mb, (1 - mf) * -30000.0, (1 - mb) * -30000.0], axis=1).astype(np.float32))


def bc(v, n=128):
    return np.ascontiguousarray(np.broadcast_to(np.asarray(v, np.float32).reshape(1, -1), (n, v.size)))


def host_Aodd_inputs(g, layer, w_in, qn_g, kn_g, lam_p, subln_g, ml_conv_w, ml_gate_b, ml_norm_g):
    colsD = np.concatenate([np.arange(256 * g, 256 * g + 256), np.arange(1024 + 256 * g, 1024 + 256 * g + 256),
                            np.arange(2048 + 256 * g, 2048 + 256 * g + 256)])
    gidx = [0 * 8 + 0 * 4 + g, 0 * 8 + 1 * 4 + g, 1 * 8 + 0 * 4 + g, 1 * 8 + 1 * 4 + g]
    colsM = np.concatenate([np.arange(3072 + 128 * g, 3072 + 128 * g + 128), np.arange(3584 + 128 * g, 3584 + 128 * g + 128),
                            np.arange(4096 + 256 * g, 4096 + 256 * g + 256), np.arange(5120 + 256 * g, 5120 + 256 * g + 256),
                            6144 + np.array(gidx)])
    qkg = bc(np.concatenate([np.tile(qn_g, 4), np.tile(kn_g, 4)]))
    mlcw = np.stack([ml_conv_w[:, 128 * g:128 * g + 128], ml_conv_w[:, 512 + 128 * g:512 + 128 * g + 128]], axis=0)
    mlcw = np.ascontiguousarray(mlcw.transpose(2, 0, 1))
    return {"wD": wfm(w_in[:, colsD]), "wM": wfm(w_in[:, colsM]), "qkg": qkg, "rope": host_rope(),
            "lamp": np.ascontiguousarray(np.broadcast_to(lam_p[None], (128, 4, 64))), "subg": bc(subln_g),
            "subgc": np.ascontiguousarray(subln_g.reshape(128, 1).astype(np.float32)),
            "mlcw": mlcw, "mlgb": bc(ml_gate_b[gidx]), "mlng": bc(ml_norm_g), "masks": host_masks_odd(),
            "ident": np.eye(128, dtype=np.float32).astype(ml_dtypes.bfloat16)}


def emit_Mfull(kb, condT2, adaw, adab, mods_out):
    sc = kb.sb("mf_sc", [128, NCH, 2], F32); sc_t = Tok()
    bsb = kb.sb("mf_b", [128, DEPTH, 96], F32); b_t = Tok()
    res = kb.sb("mf_res", [128, DEPTH, 96, 2], F32); res_t = Tok()
    wb = [kb.sb(f"mf_w{i}", [128, NCH, 768], F32) for i in range(2)]
    w_t = [Tok() for _ in range(2)]
    pst = [kb.ps(f"mf_ps{i}", [128, 512], F32) for i in range(2)]
    ps_t = [Tok() for _ in range(2)]
    kb.dma("sp", sc[:], condT2, writes=[sc_t])
    kb.dma("sp", bsb[:], adab, writes=[b_t])
    kb.op("act", lambda e: e.activation(out=sc[:], in_=sc[:], func=AF.Silu), reads=[sc_t], writes=[sc_t])
    it = 0
    for l in range(DEPTH):
        for blk in range(16):
            i = it % 2
            it += 1
            kb.dma("sp", wb[i][:], adaw[l, blk], writes=[w_t[i]])
            for mm in range(6):
                m = blk * 6 + mm
                j = m % 2
                kb.group("pe", [lambda e, k=k, mm=mm, i=i, j=j: e.matmul(pst[j][:, 0:2], lhsT=wb[i][:, k, mm * 128:(mm + 1) * 128],
                                                                          rhs=sc[:, k, 0:2], start=(k == 0), stop=(k == NCH - 1))
                                 for k in range(NCH)],
                         reads=[w_t[i], sc_t], writes=[ps_t[j]])
                kb.op("dve", lambda e, l=l, m=m, j=j: e.tensor_scalar(out=res[:, l, m, :], in0=pst[j][:, 0:2],
                                                                       scalar1=bsb[:, l, m:m + 1], scalar2=None, op0=ALU.add),
                      reads=[ps_t[j], b_t], writes=[res_t])
    for l in range(DEPTH):
        kb.dma("sp", mods_out[l], res[:, l, :, :].rearrange("p (s c) r -> p s c r", s=6), reads=[res_t])


def build_fused():
    kb = KB()
    nc = kb.nc

    def din(name, shape, dt=F32):
        return nc.dram_tensor(name, list(shape), dt, kind="ExternalInput").ap()

    def dint(name, shape, dt=F32):
        return nc.dram_tensor(name, list(shape), dt, kind="Internal").ap()

    xT = din("xT", [4, 128, NCH, NTOK])
    condT2 = din("condT2", [128, NCH, 2])
    adaw = din("adaw", [DEPTH, 16, 128, NCH, 768])
    adab = din("adab", [128, DEPTH, 96])
    ng1 = din("ng1", [DEPTH, 128, NCH])
    ng2 = din("ng2", [DEPTH, 128, NCH])
    zcol = din("zcol", [128, NCH, 1], BF16)
    wout = din("wout", [DEPTH, NCH, 128, NCH, 128])
    wup = din("wup", [DEPTH, FCH, 128, NCH, 256])
    wdn = din("wdn", [DEPTH, NQ, NCH, 128, QF, 128])
    convw = din("convw", [DEPTH, 128, FCH, 4])
    ident = din("ident", [128, 128], BF16)
    e_wA = din("e_wA", [2, 4, 128, NCH, 896]); e_wC = din("e_wC", [2, 4, 128, NCH, 768])
    e_w2 = din("e_w2", [2, 4, 128, 2, 128]); e_gb = din("e_gb", [2, 4, 128, 256]); e_gn = din("e_gn", [2, 128, 256])
    e_scw = din("e_scw", [2, 4, 128, 2, 3]); e_msk = din("e_msk", [128, 4, 128])
    o_wD = din("o_wD", [2, 4, 128, NCH, 768]); o_wM = din("o_wM", [2, 4, 128, NCH, 772])
    o_qkg = din("o_qkg", [2, 128, 512]); o_rope = din("o_rope", [128, 32, 2, 64]); o_lamp = din("o_lamp", [2, 128, 4, 64])
    o_subg = din("o_subg", [2, 128, 128]); o_subgc = din("o_subgc", [2, 128, 1]); o_mlcw = din("o_mlcw", [2, 4, 128, 2, 3]); o_mlgb = din("o_mlgb", [2, 4, 128, 4])
    o_mlng = din("o_mlng", [2, 128, 256]); o_msk = din("o_msk", [128, 4, 128])
    out = nc.dram_tensor("xoutT", [4, 128, NCH, NTOK], F32, kind="ExternalOutput").ap()
    MODS = dint("s_mods", [DEPTH, 128, 6, NCH, 2])
    X = dint("s_x", [4, 128, NCH, NTOK])
    XM = dint("s_xm", [4, 128, NCH, NTOK])
    H1L = dint("s_h1l", [4, 128, NCH, NTOK], BF16)
    H1G = dint("s_h1g", [128, NCH, TB], BF16)
    MIXG = dint("s_mixg", [4, 128, 4, TB], BF16)
    MIXL = dint("s_mixl", [128, NCH, NTOK], BF16)
    H2L = dint("s_h2l", [4, 128, NCH, NTOK], BF16)
    H2H = dint("s_h2h", [128, NCH, HW_], BF16)

    def copies(pairs):
        for (dst, src) in pairs:
            kb.dma("sp", dst, src, slow=(dst.shape[-1] == 1))
        kb.barrier()

    with kb.phase({}):
        emit_Mfull(kb, condT2, adaw, adab, MODS)
    for j in range(4):
        with kb.phase({"xT": xT[j], "mods": MODS[0], "normg": ng1[0], "h1T": H1L[j]}):
            build_P0(kb=kb)
    for layer in range(DEPTH):
        i2 = layer // 2
        last = layer == DEPTH - 1
        copies([(H1G[:, :, 64 * j:64 * j + 64], H1L[j][:, :, 0:64]) for j in range(4)] +
               [(H1G[:, :, CTX + 1024 * j:CTX + 1024 * j + 1024], H1L[j][:, :, 64:NTOK]) for j in range(4)])
        for g in range(4):
            if layer % 2 == 0:
                io = {"h1T": H1G, "wA": e_wA[i2, g], "wC": e_wC[i2, g], "w2": e_w2[i2, g], "gbias": e_gb[i2, g], "gnorm": e_gn[i2],
                      "scw": e_scw[i2, g], "masks": e_msk, "ident": ident, "mixT": MIXG[g]}
                with kb.phase(io):
                    build_Aeven(kb=kb)
            else:
                io = {"h1T": H1G, "wD": o_wD[i2, g], "wM": o_wM[i2, g], "qkg": o_qkg[i2], "rope": o_rope, "lamp": o_lamp[i2],
                      "subg": o_subg[i2], "subgc": o_subgc[i2], "mlcw": o_mlcw[i2, g], "mlgb": o_mlgb[i2, g], "mlng": o_mlng[i2], "masks": o_msk,
                      "ident": ident, "mixT": MIXG[g]}
                with kb.phase(io):
                    build_Aodd(layer, kb=kb)
        for j in range(4):
            copies([(MIXL[:, 4 * g:4 * g + 4, 0:64], MIXG[g][:, :, 64 * j:64 * j + 64]) for g in range(4)] +
                   [(MIXL[:, 4 * g:4 * g + 4, 64:NTOK], MIXG[g][:, :, CTX + 1024 * j:CTX + 1024 * j + 1024]) for g in range(4)])
            xin = xT[j] if layer == 0 else X[j]
            with kb.phase({"xT": xin, "mixT": MIXL, "wout": wout[layer], "mods": MODS[layer], "normg": ng2[layer],
                           "xmidT": XM[j], "h2T": H2L[j]}):
                build_B1(kb=kb)
        for j in range(4):
            cp = [(H2H[:, :, 1:65], H2L[j][:, :, 0:64]), (H2H[:, :, 67:1091], H2L[j][:, :, 64:NTOK])]
            if j > 0:
                cp += [(H2H[:, :, 0:1], H2L[j - 1][:, :, 63:64]), (H2H[:, :, 66:67], H2L[j - 1][:, :, NTOK - 1:NTOK])]
            else:
                cp += [(H2H[:, :, 0:1], zcol), (H2H[:, :, 66:67], zcol)]
            if j < 3:
                cp += [(H2H[:, :, 65:66], H2L[j + 1][:, :, 0:1]), (H2H[:, :, 1091:1092], H2L[j + 1][:, :, 64:65])]
            else:
                cp += [(H2H[:, :, 65:66], zcol), (H2H[:, :, 1091:1092], zcol)]
            copies(cp)
            io = {"xmidT": XM[j], "h2hT": H2H, "wup": wup[layer], "wdn": wdn[layer], "convw": convw[layer], "mods": MODS[layer],
                  "xoT": X[j]}
            if not last:
                io.update({"normg": ng1[layer + 1], "modsn": MODS[layer + 1], "h1T": H1L[j]})
            with kb.phase(io):
                build_B2(last, kb=kb)
    for j in range(4):
        kb.dma("sp", out[j], X[j], is_output=True)
    return kb.finish()


def kernel_fused(**inputs):
    I = {k: np.asarray(v) for k, v in inputs.items()}
    x, ctx = I["x"], I["ctx"]
    bf = ml_dtypes.bfloat16
    shared = {}
    shared["adaw"] = np.ascontiguousarray(I["ada_w"].reshape(DEPTH, NCH, 128, 16, 768).transpose(0, 3, 2, 1, 4))
    shared["adab"] = np.ascontiguousarray(I["ada_b"].reshape(DEPTH, 96, 128).transpose(2, 0, 1))
    shared["ng1"] = np.stack([vec_fm(I["norm1_g"][l]) for l in range(DEPTH)])
    shared["ng2"] = np.stack([vec_fm(I["norm2_g"][l]) for l in range(DEPTH)])
    shared["zcol"] = np.zeros((128, NCH, 1), bf)
    shared["wout"] = np.stack([host_wout((I["ev_w_out"] if l % 2 == 0 else I["od_w_out"])[l // 2]) for l in range(DEPTH)])
    shared["wup"] = np.stack([host_wup(I["ffn_w_up"][l]) for l in range(DEPTH)])
    shared["wdn"] = np.stack([host_wdn(I["ffn_w_down"][l]) for l in range(DEPTH)])
    shared["convw"] = np.stack([host_convw(I["ffn_conv_w"][l], I["ffn_conv_b"][l]) for l in range(DEPTH)])
    shared["ident"] = np.eye(128, dtype=np.float32).astype(bf)
    ev = [[host_Aeven_inputs(g, I["ev_w_in"][i], I["gla_gate_w2"][i], I["gla_gate_b"][i], I["gla_norm_g"][i], I["sc_conv_w"][i])
           for g in range(4)] for i in range(2)]
    for key, src in (("e_wA", "wA"), ("e_wC", "wC"), ("e_w2", "w2"), ("e_gb", "gbias"), ("e_scw", "scw")):
        shared[key] = np.stack([np.stack([ev[i][g][src] for g in range(4)]) for i in range(2)])
    shared["e_gn"] = np.stack([ev[i][0]["gnorm"] for i in range(2)])
    shared["e_msk"] = host_masks()
    od = [[host_Aodd_inputs(g, 2 * i + 1, I["od_w_in"][i], I["da_qnorm_g"][i], I["da_knorm_g"][i], I["da_lambda"][i],
                            I["da_subln_g"][i], I["ml_conv_w"][i], I["ml_gate_b"][i], I["ml_norm_g"][i])
           for g in range(4)] for i in range(2)]
    for key, src in (("o_wD", "wD"), ("o_wM", "wM"), ("o_mlcw", "mlcw"), ("o_mlgb", "mlgb")):
        shared[key] = np.stack([np.stack([od[i][g][src] for g in range(4)]) for i in range(2)])
    for key, src in (("o_qkg", "qkg"), ("o_lamp", "lamp"), ("o_subg", "subg"), ("o_subgc", "subgc"), ("o_mlng", "mlng")):
        shared[key] = np.stack([od[i][0][src] for i in range(2)])
    shared["o_rope"] = od[0][0]["rope"]
    shared["o_msk"] = host_masks_odd()
    in_maps = []
    for core in range(8):
        b, j = divmod(core, 4)
        d = dict(shared)
        d["xT"] = np.stack([fm(core_tokens(x[b], ctx[b], jj)) for jj in range(4)])
        cond = np.stack([I["c_ctx"], I["c"][b]], axis=0)
        d["condT2"] = np.ascontiguousarray(cond.reshape(2, NCH, 128).transpose(2, 1, 0))
        in_maps.append(d)
    res = run(get_nc("fused", build_fused), in_maps)
    out = np.zeros((2, SEQ, D), np.float32)
    for core in range(8):
        b, j = divmod(core, 4)
        out[b, 1024 * j:1024 * j + 1024, :] = unfm(res[core]["xoutT"][j])[64:, :]
    return out
def kernel_unfused(**inputs):
    I = {k: np.asarray(v) for k, v in inputs.items()}
    x, ctx = I["x"], I["ctx"]
    mods, _ = host_M(I["c"], I["c_ctx"], I["ada_w"], I["ada_b"])
    cores = [divmod(c, 4) for c in range(8)]
    x_cores = [fm(core_tokens(x[b], ctx[b], j)) for (b, j) in cores]
    res = run(get_nc("P0", build_P0), [{"xT": x_cores[c], "mods": mods[0][cores[c][0]], "normg": vec_fm(I["norm1_g"][0])}
                                       for c in range(8)])
    h1 = [res[c]["h1T"] for c in range(8)]
    for layer in range(DEPTH):
        i2 = layer // 2
        last = layer == DEPTH - 1
        h1_b = [assemble_h1(h1, b) for b in range(2)]
        in_maps = []
        for c, (b, g) in enumerate(cores):
            if layer % 2 == 0:
                d = host_Aeven_inputs(g, I["ev_w_in"][i2], I["gla_gate_w2"][i2], I["gla_gate_b"][i2], I["gla_norm_g"][i2],
                                      I["sc_conv_w"][i2])
            else:
                d = host_Aodd_inputs(g, layer, I["od_w_in"][i2], I["da_qnorm_g"][i2], I["da_knorm_g"][i2], I["da_lambda"][i2],
                                     I["da_subln_g"][i2], I["ml_conv_w"][i2], I["ml_gate_b"][i2], I["ml_norm_g"][i2])
            d["h1T"] = h1_b[b]
            in_maps.append(d)
        nc = get_nc("Ae", build_Aeven) if layer % 2 == 0 else get_nc("Ao", build_Aodd, layer)
        res = run(nc, in_maps)
        mixes = [res[c]["mixT"] for c in range(8)]
        wout = host_wout((I["ev_w_out"] if layer % 2 == 0 else I["od_w_out"])[i2])
        in_maps = [{"xT": x_cores[c], "mixT": scatter_mix(mixes, b, j), "wout": wout, "mods": mods[layer][b],
                    "normg": vec_fm(I["norm2_g"][layer])} for c, (b, j) in enumerate(cores)]
        res = run(get_nc("B1", build_B1), in_maps)
        halo = host_halo([res[c]["h2T"] for c in range(8)])
        xm = [res[c]["xmidT"] for c in range(8)]
        wup = host_wup(I["ffn_w_up"][layer])
        wdn = host_wdn(I["ffn_w_down"][layer])
        cw = host_convw(I["ffn_conv_w"][layer], I["ffn_conv_b"][layer])
        in_maps = []
        for c, (b, j) in enumerate(cores):
            d = {"xmidT": xm[c], "h2hT": halo[c], "wup": wup, "wdn": wdn, "convw": cw, "mods": mods[layer][b]}
            if not last:
                d["normg"] = vec_fm(I["norm1_g"][layer + 1])
                d["modsn"] = mods[layer + 1][b]
            in_maps.append(d)
        res = run(get_nc("B2", build_B2, last), in_maps)
        x_cores = [res[c]["xoT"] for c in range(8)]
        if not last:
            h1 = [res[c]["h1T"] for c in range(8)]
    out = np.zeros((2, SEQ, D), np.float32)
    for c, (b, j) in enumerate(cores):
        out[b, 1024 * j:1024 * j + 1024, :] = unfm(x_cores[c])[64:, :]
    return out


FUSED = True


def kernel(**inputs):
    return kernel_fused(**inputs) if FUSED else kernel_unfused(**inputs)
```

```python
import numpy as np
import ml_dtypes
import concourse.bass as bass
import concourse.mybir as mybir
from concourse.bass_utils import run_bass_kernel_spmd

F32 = mybir.dt.float32
BF16 = mybir.dt.bfloat16
AF = mybir.ActivationFunctionType
ALU = mybir.AluOpType

D = 2048
NCH = 16
DFF = 5632
FCH = 44
DEPTH = 4
SEQ = 4096
CTX = 256
NTOK = 1088
HW_ = 1092
EPS = 1e-6
NTILE = 34
TB = 4352


class Tok:
    __slots__ = ("w", "r", "name")

    def __init__(self, name=""):
        self.w = None
        self.r = {}
        self.name = name


class KB:
    def __init__(self):
        self.nc = bass.Bass("TRN2", target_bir_lowering=False)
        nc = self.nc
        self.engs = {"pe": nc.tensor, "dve": nc.vector, "act": nc.scalar, "pool": nc.gpsimd, "sp": nc.sync}
        self.sem = {}
        self.cnt = {}
        self.seen = {e: {} for e in self.engs}
        for e in ("pe", "dve", "act", "pool"):
            self.sem[e] = nc.alloc_semaphore(f"s_{e}")
            self.cnt[e] = 0
        self.semobj = {("eng", e): self.sem[e] for e in self.sem}
        self.dpool = {}
        for q, n in (("sp", 12), ("pool", 12), ("act", 4)):
            lst = []
            for i in range(n):
                s = nc.alloc_semaphore(f"d_{q}{i}")
                key = ("dma", q, i)
                self.semobj[key] = s
                lst.append([key, 0])
            self.dpool[q] = [lst, 0]
        self.out_events = []
        self._n = 0
        self.io = None
        self.scope = None
        self.uid = 0
        self._consts = None

    def sb(self, name, shape, dt):
        if self.scope is not None:
            return self.scope.enter_context(self.nc.sbuf_tensor(f"{name}_u{self.uid}", list(shape), dt)).ap()
        return self.nc.alloc_sbuf_tensor(name, list(shape), dt).ap()

    def ps(self, name, shape, dt):
        if self.scope is not None:
            return self.scope.enter_context(self.nc.psum_tensor(f"{name}_u{self.uid}", list(shape), dt)).ap()
        return self.nc.alloc_psum_tensor(name, list(shape), dt).ap()

    def dram(self, name, shape, dt, kind):
        if self.io is not None:
            ap = self.io[name]
            assert list(ap.shape) == list(shape), (name, ap.shape, shape)
            return ap
        return self.nc.dram_tensor(name, list(shape), dt, kind=kind).ap()

    def get_consts(self):
        if self._consts is None:
            sc, self.scope = self.scope, None
            self._consts = Consts(self)
            self.scope = sc
        return self._consts

    def phase(self, io):
        import contextlib
        kb = self

        @contextlib.contextmanager
        def cm():
            kb.uid += 1
            kb.io = io
            with contextlib.ExitStack() as st:
                kb.scope = st
                yield
                kb.barrier()
            kb.scope = None
            kb.io = None
        return cm()

    def end_phase(self):
        if self.io is not None:
            return None
        return self.finish()

    def _wait(self, eng, deps):
        seen = self.seen[eng]
        for key, val in deps.items():
            if eng == "pe" and key == ("eng", "pe"):
                continue
            if seen.get(key, 0) < val:
                self.engs[eng].wait_ge(self.semobj[key], val)
                seen[key] = val

    @staticmethod
    def _add(d, ev):
        if ev is None:
            return
        k, v = ev
        if d.get(k, 0) < v:
            d[k] = v

    def _deps(self, reads, writes):
        deps = {}
        for b in reads:
            self._add(deps, b.w)
        for b in writes:
            self._add(deps, b.w)
            for k, v in b.r.items():
                if deps.get(k, 0) < v:
                    deps[k] = v
        return deps

    def _commit(self, ev, reads, writes):
        for b in reads:
            self._add(b.r, ev)
        for b in writes:
            b.w = ev
            b.r = {}

    def op(self, eng, fn, reads=(), writes=()):
        self._wait(eng, self._deps(reads, writes))
        ins = fn(self.engs[eng])
        self.cnt[eng] += 1
        ins.then_inc(self.sem[eng], 1)
        ev = (("eng", eng), self.cnt[eng])
        self._commit(ev, reads, writes)
        return ev

    def group(self, eng, fns, reads=(), writes=()):
        self._wait(eng, self._deps(reads, writes))
        ins = None
        for fn in fns:
            ins = fn(self.engs[eng])
        self.cnt[eng] += 1
        ins.then_inc(self.sem[eng], 1)
        ev = (("eng", eng), self.cnt[eng])
        self._commit(ev, reads, writes)
        return ev

    def dma(self, q, out, in_, reads=(), writes=(), is_output=False, slow=False):
        lst, idx = self.dpool[q]
        ent = lst[idx]
        self.dpool[q][1] = (idx + 1) % len(lst)
        deps = self._deps(reads, writes)
        if ent[1] > 0:
            self._add(deps, (ent[0], ent[1]))
        self._wait(q, deps)
        ent[1] += 16
        kw = {"allow_slow_non_contiguous": True} if slow else {}
        self.engs[q].dma_start(out=out, in_=in_, **kw).then_inc(self.semobj[ent[0]], 16)
        ev = (ent[0], ent[1])
        self._commit(ev, reads, writes)
        if is_output:
            self.out_events.append(ev)
        return ev

    def finish(self):
        deps = {}
        for ev in self.out_events:
            self._add(deps, ev)
        self._wait("sp", deps)
        fin = {("eng", e): self.cnt[e] for e in self.cnt if self.cnt[e] > 0}
        self._wait("sp", fin)
        return self.nc


class Consts:
    def __init__(self, kb):
        self.ones = kb.sb("c_ones", [128, 128], F32)
        self.eps = kb.sb("c_eps", [128, 1], F32)
        self.t = Tok("consts")
        kb.op("pool", lambda e: e.memset(self.ones[:], 1.0), writes=[self.t])
        kb.op("pool", lambda e: e.memset(self.eps[:], EPS), writes=[self.t])


TOKBLK = [(0, 64, 0), (64, 512, 1), (576, 512, 1)]


def emit_modnorm(kb, cs, x, xt, acoef, bcoef, mt, h, ht, colmap, psA, psA_t, scr):
    sq, sq_t = scr["sq"], scr["sq_t"]
    rstd, rstd_t = scr["rstd"], scr["rstd_t"]
    tmp, tmp_t = scr["tmp"], scr["tmp_t"]
    for (t0, n, seg) in TOKBLK:
        fns = []
        for c in range(NCH):
            i = c % 2
            kb.op("act", lambda e, c=c, i=i: e.activation(out=sq[i][:, 0:n], in_=x[:, c, t0:t0 + n], func=AF.Square),
                  reads=[xt], writes=[sq_t[i]])
            kb.op("pe", lambda e, c=c, i=i: e.matmul(psA[:, 0:n], lhsT=cs.ones[:], rhs=sq[i][:, 0:n],
                                                      start=(c == 0), stop=(c == NCH - 1)),
                  reads=[sq_t[i], cs.t], writes=[psA_t])
        kb.op("act", lambda e: e.activation(out=rstd[:, 0:n], in_=psA[:, 0:n], func=AF.Sqrt,
                                             scale=1.0 / D, bias=cs.eps[:, 0:1]),
              reads=[psA_t, cs.t], writes=[rstd_t])
        kb.op("dve", lambda e: e.reciprocal(out=rstd[:, 0:n], in_=rstd[:, 0:n]), reads=[rstd_t], writes=[rstd_t])
        c0 = colmap(t0)
        for c in range(NCH):
            i = c % 2
            kb.op("dve", lambda e, c=c, i=i: e.tensor_tensor(out=tmp[i][:, 0:n], in0=x[:, c, t0:t0 + n],
                                                             in1=rstd[:, 0:n], op=ALU.mult),
                  reads=[xt, rstd_t], writes=[tmp_t[i]])
            kb.op("act", lambda e, c=c, i=i: e.activation(out=h[:, c, c0:c0 + n], in_=tmp[i][:, 0:n], func=AF.Identity,
                                                           scale=acoef[:, c, seg:seg + 1], bias=bcoef[:, c, seg:seg + 1]),
                  reads=[tmp_t[i], mt], writes=[ht])


def alloc_norm_scratch(kb):
    scr = {}
    scr["sq"] = [kb.sb(f"n_sq{i}", [128, 512], F32) for i in range(2)]
    scr["sq_t"] = [Tok() for _ in range(2)]
    scr["rstd"] = kb.sb("n_rstd", [128, 512], F32)
    scr["rstd_t"] = Tok()
    scr["tmp"] = [kb.sb(f"n_tmp{i}", [128, 512], F32) for i in range(2)]
    scr["tmp_t"] = [Tok() for _ in range(2)]
    return scr


def emit_coefs(kb, mods, mods_t, normg, normg_t, si_sh, si_sc, acoef, bcoef, ct):
    for seg in range(2):
        kb.op("dve", lambda e, seg=seg: e.scalar_tensor_tensor(out=acoef[:, :, seg], in0=mods[:, si_sc, :, seg], scalar=1.0,
                                                               in1=normg[:, :], op0=ALU.add, op1=ALU.mult),
              reads=[mods_t, normg_t], writes=[ct])
        kb.op("dve", lambda e, seg=seg: e.tensor_copy(out=bcoef[:, :, seg], in_=mods[:, si_sh, :, seg]),
              reads=[mods_t], writes=[ct])


def build_M(kb=None):
    kb = kb or KB()
    condT = kb.dram("condT", [128, NCH, 3], F32, "ExternalInput")
    adaw = kb.dram("adaw", [DEPTH, 128, NCH, 1536], F32, "ExternalInput")
    adab = kb.dram("adab", [128, DEPTH, 12], F32, "ExternalInput")
    out = kb.dram("modT", [128, DEPTH, 12, 3], F32, "ExternalOutput")
    sc = kb.sb("sc", [128, NCH, 4], F32); sc_t = Tok()
    bsb = kb.sb("bsb", [128, DEPTH, 12], F32); b_t = Tok()
    res = kb.sb("res", [128, DEPTH, 12, 3], F32); res_t = Tok()
    wb = [kb.sb(f"w{i}", [128, NCH, 768], F32) for i in range(2)]
    w_t = [Tok() for _ in range(2)]
    pst = [kb.ps(f"ps{i}", [128, 512], F32) for i in range(2)]
    ps_t = [Tok() for _ in range(2)]
    kb.dma("sp", sc[:, :, 0:3], condT, writes=[sc_t])
    kb.dma("sp", bsb[:], adab, writes=[b_t])
    kb.op("act", lambda e: e.activation(out=sc[:, :, 0:3], in_=sc[:, :, 0:3], func=AF.Silu), reads=[sc_t], writes=[sc_t])
    it = 0
    for l in range(DEPTH):
        for hf in range(2):
            i = it % 2
            it += 1
            kb.dma("sp", wb[i][:], adaw[l, :, :, hf * 768:(hf + 1) * 768], writes=[w_t[i]])
            for mm in range(6):
                m = hf * 6 + mm
                j = m % 2
                kb.group("pe", [lambda e, k=k, mm=mm, i=i, j=j: e.matmul(pst[j][:, 0:3], lhsT=wb[i][:, k, mm * 128:(mm + 1) * 128],
                                                                          rhs=sc[:, k, 0:3], start=(k == 0), stop=(k == NCH - 1))
                                 for k in range(NCH)],
                         reads=[w_t[i], sc_t], writes=[ps_t[j]])
                kb.op("dve", lambda e, l=l, m=m, j=j: e.tensor_scalar(out=res[:, l, m, :], in0=pst[j][:, 0:3],
                                                                       scalar1=bsb[:, l, m:m + 1], scalar2=None, op0=ALU.add),
                      reads=[ps_t[j], b_t], writes=[res_t])
    kb.dma("sp", out, res[:], reads=[res_t], is_output=True)
    return kb.end_phase()


def build_P0(kb=None):
    kb = kb or KB()
    xin = kb.dram("xT", [128, NCH, NTOK], F32, "ExternalInput")
    mods_d = kb.dram("mods", [128, 6, NCH, 2], F32, "ExternalInput")
    ng_d = kb.dram("normg", [128, NCH], F32, "ExternalInput")
    h_d = kb.dram("h1T", [128, NCH, NTOK], BF16, "ExternalOutput")
    cs = kb.get_consts()
    x = kb.sb("x", [128, NCH, NTOK], F32); xt = Tok()
    h = kb.sb("h", [128, NCH, NTOK], BF16); ht = Tok()
    mods = kb.sb("mods_s", [128, 6, NCH, 2], F32); mt = Tok()
    ng = kb.sb("ng", [128, NCH], F32); ngt = Tok()
    ac = kb.sb("ac", [128, NCH, 2], F32); bc = kb.sb("bc", [128, NCH, 2], F32); ct = Tok()
    psA = kb.ps("psA", [128, 512], F32); psA_t = Tok()
    scr = alloc_norm_scratch(kb)
    for c in range(0, NCH, 4):
        kb.dma("sp", x[:, c:c + 4, :], xin[:, c:c + 4, :], writes=[xt])
    kb.dma("sp", mods[:], mods_d, writes=[mt])
    kb.dma("sp", ng[:], ng_d, writes=[ngt])
    emit_coefs(kb, mods, mt, ng, ngt, 0, 1, ac, bc, ct)
    emit_modnorm(kb, cs, x, xt, ac, bc, ct, h, ht, lambda t: t, psA, psA_t, scr)
    for c in range(0, NCH, 4):
        kb.dma("sp", h_d[:, c:c + 4, :], h[:, c:c + 4, :], reads=[ht], is_output=True)
    return kb.end_phase()


def hcol(t):
    return 1 + t if t < 64 else 67 + (t - 64)


def build_B1(kb=None):
    kb = kb or KB()
    xin = kb.dram("xT", [128, NCH, NTOK], F32, "ExternalInput")
    mix_d = kb.dram("mixT", [128, NCH, NTOK], BF16, "ExternalInput")
    wout_d = kb.dram("wout", [NCH, 128, NCH, 128], F32, "ExternalInput")
    mods_d = kb.dram("mods", [128, 6, NCH, 2], F32, "ExternalInput")
    ng_d = kb.dram("normg", [128, NCH], F32, "ExternalInput")
    xo_d = kb.dram("xmidT", [128, NCH, NTOK], F32, "ExternalOutput")
    h_d = kb.dram("h2T", [128, NCH, NTOK], BF16, "ExternalOutput")
    cs = kb.get_consts()
    x = kb.sb("x", [128, NCH, NTOK], F32); xt = [Tok() for _ in range(NCH)]
    mix = kb.sb("mix", [128, NCH, NTOK], BF16); mixt = Tok()
    h = kb.sb("h", [128, NCH, NTOK], BF16); ht = Tok()
    mods = kb.sb("mods_s", [128, 6, NCH, 2], F32); mt = Tok()
    ng = kb.sb("ng", [128, NCH], F32); ngt = Tok()
    ac = kb.sb("ac", [128, NCH, 2], F32); bc = kb.sb("bc", [128, NCH, 2], F32); ct = Tok()
    wb = [kb.sb(f"w{i}", [128, NCH, 128], BF16) for i in range(3)]
    w_t = [Tok() for _ in range(3)]
    pss = [kb.ps(f"ps{i}", [128, 512], F32) for i in range(6)]
    ps_t = [Tok() for _ in range(6)]
    psA = kb.ps("psA", [128, 512], F32); psA_t = Tok()
    scr = alloc_norm_scratch(kb)
    for c in range(0, NCH, 4):
        kb.dma("sp", x[:, c:c + 4, :], xin[:, c:c + 4, :], writes=xt[c:c + 4])
        kb.dma("sp", mix[:, c:c + 4, :], mix_d[:, c:c + 4, :], writes=[mixt])
    kb.dma("sp", mods[:], mods_d, writes=[mt])
    kb.dma("sp", ng[:], ng_d, writes=[ngt])
    emit_coefs(kb, mods, mt, ng, ngt, 3, 4, ac, bc, ct)
    pi = 0
    for m in range(NCH):
        wi = m % 3
        kb.dma("pool", wb[wi][:], wout_d[m], writes=[w_t[wi]])
        for (t0, n, seg) in TOKBLK:
            j = pi % 6
            pi += 1
            kb.group("pe", [lambda e, k=k, wi=wi, j=j, t0=t0, n=n: e.matmul(pss[j][:, 0:n], lhsT=wb[wi][:, k, :], rhs=mix[:, k, t0:t0 + n],
                                                                            start=(k == 0), stop=(k == NCH - 1))
                             for k in range(NCH)],
                     reads=[w_t[wi], mixt], writes=[ps_t[j]])
            kb.op("dve", lambda e, m=m, j=j, t0=t0, n=n, seg=seg: e.scalar_tensor_tensor(
                out=x[:, m, t0:t0 + n], in0=pss[j][:, 0:n], scalar=mods[:, 2, m, seg:seg + 1],
                in1=x[:, m, t0:t0 + n], op0=ALU.mult, op1=ALU.add),
                reads=[ps_t[j], mt, xt[m]], writes=[xt[m]])
    deps_w = {}
    for m in range(NCH):
        KB._add(deps_w, xt[m].w)
    xm = Tok()
    for e in ("act", "dve", "pe", "sp"):
        kb._wait(e, deps_w)
    emit_modnorm(kb, cs, x, xm, ac, bc, ct, h, ht, lambda t: t, psA, psA_t, scr)
    for c in range(0, NCH, 4):
        kb.dma("sp", xo_d[:, c:c + 4, :], x[:, c:c + 4, :], reads=[xm], is_output=True)
        kb.dma("sp", h_d[:, c:c + 4, :], h[:, c:c + 4, :], reads=[ht], is_output=True)
    return kb.end_phase()


NQ = 4
QF = FCH // NQ
GBLK = [(0, 364), (364, 364), (728, 364)]
DBLK = [(1, 0, 64, 0), (67, 64, 512, 1), (579, 576, 512, 1)]


def build_B2(last, kb=None):
    kb = kb or KB()
    xin = kb.dram("xmidT", [128, NCH, NTOK], F32, "ExternalInput")
    h2_d = kb.dram("h2hT", [128, NCH, HW_], BF16, "ExternalInput")
    wup_d = kb.dram("wup", [FCH, 128, NCH, 256], F32, "ExternalInput")
    wdn_d = kb.dram("wdn", [NQ, NCH, 128, QF, 128], F32, "ExternalInput")
    cw_d = kb.dram("convw", [128, FCH, 4], F32, "ExternalInput")
    mods_d = kb.dram("mods", [128, 6, NCH, 2], F32, "ExternalInput")
    xo_d = kb.dram("xoT", [128, NCH, NTOK], F32, "ExternalOutput")
    cs = kb.get_consts()
    x = kb.sb("x", [128, NCH, NTOK], F32); xt = [Tok() for _ in range(NCH)]
    h2 = kb.sb("h2", [128, NCH, HW_], BF16); h2t = Tok()
    hid = kb.sb("hid", [128, QF, HW_], BF16); hid_t = [Tok() for _ in range(QF)]
    G = kb.sb("G", [128, HW_], F32); Gt = Tok()
    C = kb.sb("C", [128, HW_], F32); Ct = Tok()
    cw = kb.sb("cw", [128, FCH, 4], F32); cwt = Tok()
    mods = kb.sb("mods_s", [128, 6, NCH, 2], F32); mt = Tok()
    wu = [kb.sb(f"wu{i}", [128, NCH, 256], BF16) for i in range(2)]; wu_t = [Tok() for _ in range(2)]
    wd = [kb.sb(f"wd{i}", [128, QF, 128], BF16) for i in range(2)]; wd_t = [Tok() for _ in range(2)]
    pss = [kb.ps(f"ps{i}", [128, 512], F32) for i in range(8)]
    ps_t = [Tok() for _ in range(8)]
    for c in range(0, NCH, 4):
        kb.dma("sp", x[:, c:c + 4, :], xin[:, c:c + 4, :], writes=xt[c:c + 4])
        kb.dma("sp", h2[:, c:c + 4, :], h2_d[:, c:c + 4, :], writes=[h2t])
    kb.dma("sp", mods[:], mods_d, writes=[mt])
    kb.dma("sp", cw[:], cw_d, writes=[cwt])
    if not last:
        ng_d = kb.dram("normg", [128, NCH], F32, "ExternalInput")
        modsn_d = kb.dram("modsn", [128, 6, NCH, 2], F32, "ExternalInput")
        h1_d = kb.dram("h1T", [128, NCH, NTOK], BF16, "ExternalOutput")
        ng = kb.sb("ng", [128, NCH], F32); ngt = Tok()
        modsn = kb.sb("modsn_s", [128, 6, NCH, 2], F32); mnt = Tok()
        ac = kb.sb("ac", [128, NCH, 2], F32); bc = kb.sb("bc", [128, NCH, 2], F32); ct = Tok()
        kb.dma("sp", ng[:], ng_d, writes=[ngt])
        kb.dma("sp", modsn[:], modsn_d, writes=[mnt])
        emit_coefs(kb, modsn, mnt, ng, ngt, 0, 1, ac, bc, ct)
    wdi = 0
    for q in range(NQ):
        for mm in range(QF):
            m = q * QF + mm
            wi = m % 2
            kb.dma("pool", wu[wi][:], wup_d[m], writes=[wu_t[wi]])
            for half in range(2):
                for bi, (c0, n) in enumerate(GBLK):
                    j = half * 3 + bi
                    kb.group("pe", [lambda e, k=k, wi=wi, j=j, c0=c0, n=n, half=half: e.matmul(
                        pss[j][:, 0:n], lhsT=wu[wi][:, k, half * 128:(half + 1) * 128], rhs=h2[:, k, c0:c0 + n],
                        start=(k == 0), stop=(k == NCH - 1)) for k in range(NCH)],
                        reads=[wu_t[wi], h2t], writes=[ps_t[j]])
            for bi, (c0, n) in enumerate(GBLK):
                kb.op("act", lambda e, bi=bi, c0=c0, n=n: e.activation(out=G[:, c0:c0 + n], in_=pss[bi][:, 0:n], func=AF.Identity),
                      reads=[ps_t[bi]], writes=[Gt])
            W = HW_ - 2
            kb.op("dve", lambda e, m=m: e.tensor_scalar(out=C[:, 1:1 + W], in0=G[:, 1:1 + W], scalar1=cw[:, m, 1:2],
                                                        scalar2=cw[:, m, 3:4], op0=ALU.mult, op1=ALU.add),
                  reads=[Gt, cwt], writes=[Ct])
            kb.op("dve", lambda e, m=m: e.scalar_tensor_tensor(out=C[:, 1:1 + W], in0=G[:, 0:W], scalar=cw[:, m, 0:1],
                                                               in1=C[:, 1:1 + W], op0=ALU.mult, op1=ALU.add),
                  reads=[Gt, cwt, Ct], writes=[Ct])
            kb.op("dve", lambda e, m=m: e.scalar_tensor_tensor(out=C[:, 1:1 + W], in0=G[:, 2:2 + W], scalar=cw[:, m, 2:3],
                                                               in1=C[:, 1:1 + W], op0=ALU.mult, op1=ALU.add),
                  reads=[Gt, cwt, Ct], writes=[Ct])
            kb.op("act", lambda e: e.activation(out=C[:, 1:1 + W], in_=C[:, 1:1 + W], func=AF.Silu), reads=[Ct], writes=[Ct])
            for bi, (c0, n) in enumerate(GBLK):
                a = max(c0, 1)
                b_ = min(c0 + n, HW_ - 1)
                kb.op("dve", lambda e, bi=bi, a=a, b_=b_, c0=c0, mm=mm: e.tensor_tensor(
                    out=hid[:, mm, a:b_], in0=pss[3 + bi][:, a - c0:b_ - c0], in1=C[:, a:b_], op=ALU.mult),
                    reads=[ps_t[3 + bi], Ct], writes=[hid_t[mm]])
        for mo in range(NCH):
            wi = wdi % 2
            wdi += 1
            kb.dma("pool", wd[wi][:], wdn_d[q, mo], writes=[wd_t[wi]])
            for bi, (hc0, t0, n, seg) in enumerate(DBLK):
                j = 6 + (bi % 2)
                kb.group("pe", [lambda e, kk=kk, wi=wi, j=j, hc0=hc0, n=n: e.matmul(
                    pss[j][:, 0:n], lhsT=wd[wi][:, kk, :], rhs=hid[:, kk, hc0:hc0 + n],
                    start=(kk == 0), stop=(kk == QF - 1)) for kk in range(QF)],
                    reads=[wd_t[wi]] + hid_t, writes=[ps_t[j]])
                kb.op("dve", lambda e, mo=mo, j=j, t0=t0, n=n, seg=seg: e.scalar_tensor_tensor(
                    out=x[:, mo, t0:t0 + n], in0=pss[j][:, 0:n], scalar=mods[:, 5, mo, seg:seg + 1],
                    in1=x[:, mo, t0:t0 + n], op0=ALU.mult, op1=ALU.add),
                    reads=[ps_t[j], mt, xt[mo]], writes=[xt[mo]])
    deps_w = {}
    for m in range(NCH):
        KB._add(deps_w, xt[m].w)
    for e in ("act", "dve", "pe", "sp"):
        kb._wait(e, deps_w)
    xm = Tok()
    for c in range(0, NCH, 4):
        kb.dma("sp", xo_d[:, c:c + 4, :], x[:, c:c + 4, :], reads=[xm], is_output=True)
    if not last:
        scr = alloc_norm_scratch(kb)
        h1v = h2[:, :, 0:NTOK]
        emit_modnorm(kb, cs, x, xm, ac, bc, ct, h2, h2t, lambda t: t, pss[0], ps_t[0], scr)
        for c in range(0, NCH, 4):
            kb.dma("sp", h1_d[:, c:c + 4, :], h1v[:, c:c + 4, :], reads=[h2t], is_output=True)
    return kb.end_phase()


def fm(a):
    t, f = a.shape
    return np.ascontiguousarray(a.reshape(t, f // 128, 128).transpose(2, 1, 0))


def unfm(a):
    p, c, t = a.shape
    return np.ascontiguousarray(a.transpose(2, 1, 0).reshape(t, c * 128))


def vec_fm(v):
    return np.ascontiguousarray(v.reshape(-1, 128).T)


def core_tokens(x_b, ctx_b, j):
    return np.concatenate([ctx_b[64 * j:64 * j + 64], x_b[1024 * j:1024 * j + 1024]], axis=0)


def run(nc, in_maps):
    res = run_bass_kernel_spmd(nc, in_maps, core_ids=list(range(8)))
    return res.results


_CACHE = {}


def get_nc(name, builder, *args):
    key = (name,) + args
    if key not in _CACHE:
        _CACHE[key] = builder(*args)
    return _CACHE[key]


def host_M(c, c_ctx, ada_w, ada_b):
    cond = np.stack([c[0], c[1], c_ctx], axis=0)
    condT = np.ascontiguousarray(cond.reshape(3, NCH, 128).transpose(2, 1, 0))
    in_maps = []
    for core in range(8):
        sl = slice(core * 1536, (core + 1) * 1536)
        aw = np.ascontiguousarray(ada_w[:, :, sl].reshape(DEPTH, NCH, 128, 1536).transpose(0, 2, 1, 3))
        ab = np.ascontiguousarray(ada_b[:, sl].reshape(DEPTH, 12, 128).transpose(2, 0, 1))
        in_maps.append({"condT": condT, "adaw": aw, "adab": ab})
    res = run(get_nc("M", build_M), in_maps)
    full = np.zeros((DEPTH, 6 * D, 3), np.float32)
    for core in range(8):
        r = res[core]["modT"]
        full[:, core * 1536:(core + 1) * 1536, :] = r.transpose(1, 2, 0, 3).reshape(DEPTH, 1536, 3)
    mods = [[None, None] for _ in range(DEPTH)]
    for l in range(DEPTH):
        for b in range(2):
            m = full[l][:, [2, b]]
            mods[l][b] = np.ascontiguousarray(m.reshape(6, NCH, 128, 2).transpose(2, 0, 1, 3))
    return mods, full


def host_wout(w):
    rows = []
    for i in range(4):
        for cc in range(4):
            r0 = 256 * i + 128 * cc if cc < 2 else 1024 + 256 * i + 128 * (cc - 2)
            rows.append(w[r0:r0 + 128])
    wp = np.stack(rows, axis=0)
    return np.ascontiguousarray(wp.reshape(NCH, 128, NCH, 128).transpose(2, 1, 0, 3))


def host_wup(w):
    wk = w.reshape(NCH, 128, 2, FCH, 128)
    return np.ascontiguousarray(wk.transpose(3, 1, 0, 2, 4).reshape(FCH, 128, NCH, 256))


def host_wdn(w):
    wk = w.reshape(NQ, QF, 128, NCH, 128)
    return np.ascontiguousarray(wk.transpose(0, 3, 2, 1, 4))


def host_convw(cw, cb):
    a = np.concatenate([cw, cb[None]], axis=0)
    return np.ascontiguousarray(a.reshape(4, FCH, 128).transpose(2, 1, 0))


def host_halo(h2_cores):
    outs = []
    for core in range(8):
        b, j = divmod(core, 4)
        h = h2_cores[core]
        o = np.zeros((128, NCH, HW_), h.dtype)
        o[:, :, 1:65] = h[:, :, 0:64]
        o[:, :, 67:1091] = h[:, :, 64:1088]
        if j > 0:
            hp = h2_cores[core - 1]
            o[:, :, 0] = hp[:, :, 63]
            o[:, :, 66] = hp[:, :, 1087]
        if j < 3:
            hn = h2_cores[core + 1]
            o[:, :, 65] = hn[:, :, 0]
            o[:, :, 1091] = hn[:, :, 64]
        outs.append(o)
    return outs


def kb_barrier(kb):
    deps = {("eng", e): kb.cnt[e] for e in kb.cnt if kb.cnt[e] > 0}
    for q in kb.dpool:
        for key, val in kb.dpool[q][0]:
            if val > 0:
                deps[key] = val
    for e in kb.engs:
        kb._wait(e, dict(deps))


KB.barrier = kb_barrier

ABLK = [(0, 2)] + [(2 + 4 * i, 4) for i in range(8)]
BWD_ORDER = [1, 0] + list(range(33, 1, -1))
QSCALE = 128 ** -0.5


def build_Aeven(kb=None):
    import contextlib
    kb = kb or KB()
    nc = kb.nc
    h1_d = kb.dram("h1T", [128, NCH, TB], BF16, "ExternalInput")
    wA_d = kb.dram("wA", [128, NCH, 896], F32, "ExternalInput")
    wC_d = kb.dram("wC", [128, NCH, 768], F32, "ExternalInput")
    w2_d = kb.dram("w2", [128, 2, 128], F32, "ExternalInput")
    gb_d = kb.dram("gbias", [128, 256], F32, "ExternalInput")
    gn_d = kb.dram("gnorm", [128, 256], F32, "ExternalInput")
    scw_d = kb.dram("scw", [128, 2, 3], F32, "ExternalInput")
    msk_d = kb.dram("masks", [128, 4, 128], F32, "ExternalInput")
    id_d = kb.dram("ident", [128, 128], BF16, "ExternalInput")
    out_d = kb.dram("mixT", [128, 4, TB], BF16, "ExternalOutput")
    cs = kb.get_consts()
    pss = [kb.ps(f"ps{i}", [128, 512], F32) for i in range(7)]
    ps_t = [Tok() for _ in range(7)]
    psT = kb.ps("psT", [128, 1024], BF16); psT_t = Tok()
    hb = [kb.sb(f"hb{i}", [128, NCH, 512], BF16) for i in range(1)]; hb_t = [Tok()]

    with contextlib.ExitStack() as st:
        def sbx(name, shape, dt):
            return st.enter_context(nc.sbuf_tensor(f"{name}_u{kb.uid}", list(shape), dt)).ap()
        wC = sbx("wC_s", [128, NCH, 768], BF16); wC_t = Tok()
        U = sbx("u_s", [128, 2, TB + 4], F32); U_t = Tok()
        SBf = sbx("sb_s", [128, 2, TB], F32); SB_t = Tok()
        SX = sbx("sx_s", [128, 512], F32); SX_t = Tok()
        CO = sbx("co_s", [128, 2, TB], F32); CO_t = Tok()
        COb = sbx("cob_s", [128, 2, TB], BF16); COb_t = Tok()
        scw = sbx("scw_s", [128, 2, 3], F32); scw_t = Tok()
        kb.dma("pool", wC[:], wC_d, writes=[wC_t])
        kb.dma("sp", scw[:], scw_d, writes=[scw_t])
        kb.op("pool", lambda e: e.memset(U[:], 0.0), writes=[U_t])

        def ucol(tok):
            return 1 + tok if tok < CTX else 3 + tok
        pi = 0
        for (t0, nt) in ABLK:
            n = nt * 128
            g0 = t0 * 128
            kb.dma("sp", hb[0][:, :, 0:n], h1_d[:, :, g0:g0 + n], writes=[hb_t[0]])
            for c2 in range(2):
                for which in range(3):
                    j = pi % 4
                    pi += 1
                    col = which * 256 + c2 * 128
                    kb.group("pe", [lambda e, k=k, j=j, col=col, n=n: e.matmul(
                        pss[j][:, 0:n], lhsT=wC[:, k, col:col + 128], rhs=hb[0][:, k, 0:n],
                        start=(k == 0), stop=(k == NCH - 1)) for k in range(NCH)],
                        reads=[wC_t, hb_t[0]], writes=[ps_t[j]])
                    if which == 0:
                        kb.op("act", lambda e, j=j, n=n: e.activation(out=SX[:, 0:n], in_=pss[j][:, 0:n], func=AF.Identity),
                              reads=[ps_t[j]], writes=[SX_t])
                    elif which == 1:
                        kb.op("act", lambda e, j=j, n=n, c2=c2, g0=g0: e.activation(out=SBf[:, c2, g0:g0 + n], in_=pss[j][:, 0:n],
                                                                                  func=AF.Identity),
                              reads=[ps_t[j]], writes=[SB_t])
                    else:
                        uc = ucol(g0)
                        kb.op("dve", lambda e, j=j, n=n, c2=c2, uc=uc: e.tensor_tensor(out=U[:, c2, uc:uc + n], in0=pss[j][:, 0:n],
                                                                                     in1=SX[:, 0:n], op=ALU.mult),
                              reads=[ps_t[j], SX_t], writes=[U_t])
        for c2 in range(2):
            for (g0, n) in ((0, CTX), (CTX, SEQ)):
                uc = ucol(g0)
                kb.op("dve", lambda e, c2=c2, g0=g0, n=n, uc=uc: e.tensor_scalar(
                    out=CO[:, c2, g0:g0 + n], in0=U[:, c2, uc:uc + n], scalar1=scw[:, c2, 1:2], scalar2=None, op0=ALU.mult),
                    reads=[U_t, scw_t], writes=[CO_t])
                kb.op("dve", lambda e, c2=c2, g0=g0, n=n, uc=uc: e.scalar_tensor_tensor(
                    out=CO[:, c2, g0:g0 + n], in0=U[:, c2, uc - 1:uc - 1 + n], scalar=scw[:, c2, 0:1],
                    in1=CO[:, c2, g0:g0 + n], op0=ALU.mult, op1=ALU.add),
                    reads=[U_t, scw_t, CO_t], writes=[CO_t])
                kb.op("dve", lambda e, c2=c2, g0=g0, n=n, uc=uc: e.scalar_tensor_tensor(
                    out=CO[:, c2, g0:g0 + n], in0=U[:, c2, uc + 1:uc + 1 + n], scalar=scw[:, c2, 2:3],
                    in1=CO[:, c2, g0:g0 + n], op0=ALU.mult, op1=ALU.add),
                    reads=[U_t, scw_t, CO_t], writes=[CO_t])
                kb.op("dve", lambda e, c2=c2, g0=g0, n=n: e.tensor_tensor(
                    out=COb[:, c2, g0:g0 + n], in0=CO[:, c2, g0:g0 + n], in1=SBf[:, c2, g0:g0 + n], op=ALU.mult),
                    reads=[CO_t, SB_t], writes=[COb_t])
        kb.dma("sp", out_d[:, 2:4, :], COb[:], reads=[COb_t], is_output=True)
        kb.barrier()

    wA = kb.sb("wA_s", [128, NCH, 896], BF16); wA_t = Tok()
    QF = kb.sb("QF", [128, TB], BF16); QB = kb.sb("QB", [128, TB], BF16)
    KF = kb.sb("KF", [128, TB], BF16); KBk = kb.sb("KBk", [128, TB], BF16)
    KTF = kb.sb("KTF", [128, NTILE, 128], BF16); KTB = kb.sb("KTB", [128, NTILE, 128], BF16)
    V = kb.sb("V", [128, NTILE, 256], BF16)
    SR = kb.sb("SR", [128, NTILE, 256], BF16)
    AT = kb.sb("AT", [128, NTILE, 128], BF16)
    SBF = kb.sb("SBF", [128, NTILE, 256], BF16)
    per_t = [Tok() for _ in range(NTILE)]
    qTb = kb.sb("qTb", [128, 512], F32); kTb = kb.sb("kTb", [128, 512], F32); qk_t = Tok()
    GL = kb.sb("GL", [128, 512], F32); GL_t = Tok()
    w2 = kb.sb("w2_s", [128, 2, 128], F32); gb = kb.sb("gb_s", [128, 256], F32); gn = kb.sb("gn_s", [128, 256], F32)
    msk = kb.sb("msk_s", [128, 4, 128], F32); ident = kb.sb("id_s", [128, 128], BF16)
    k_t = Tok()
    el = kb.sb("el", [128, 2, NTILE], F32); el_t = Tok()
    tg = kb.sb("tg", [128, 256], F32); tg_t = Tok()
    sg = kb.sb("sg", [128, 256], F32); sg_t = Tok()
    E1 = kb.sb("E1", [128, 256], F32); E2 = kb.sb("E2", [128, 256], F32); E2t = kb.sb("E2t", [128, 256], F32)
    E1_t, E2_t, E2t_t = Tok(), Tok(), Tok()
    a1 = kb.sb("a1", [128, 128], F32); a2 = kb.sb("a2", [128, 128], F32); a_t = Tok()
    srt = kb.sb("srt", [128, 256], F32); srt_t = Tok()
    kb.dma("pool", wA[:], wA_d, writes=[wA_t])
    for dst, src in ((w2, w2_d), (gb, gb_d), (gn, gn_d), (msk, msk_d), (ident, id_d)):
        kb.dma("sp", dst[:], src, writes=[k_t])

    for (t0, nt) in ABLK:
        n = nt * 128
        g0 = t0 * 128
        kb.dma("sp", hb[0][:, :, 0:n], h1_d[:, :, g0:g0 + n], writes=[hb_t[0]])
        for j, (col, m) in enumerate(((0, 128), (128, 128), (768, 128))):
            kb.group("pe", [lambda e, k=k, j=j, col=col, m=m, n=n: e.matmul(
                pss[j][0:m, 0:n], lhsT=wA[:, k, col:col + m], rhs=hb[0][:, k, 0:n],
                start=(k == 0), stop=(k == NCH - 1)) for k in range(NCH)],
                reads=[wA_t, hb_t[0]], writes=[ps_t[j]])
        kb.op("act", lambda e, n=n: e.activation(out=qTb[:, 0:n], in_=pss[0][:, 0:n], func=AF.Identity, scale=QSCALE),
              reads=[ps_t[0]], writes=[qk_t])
        kb.op("act", lambda e, n=n: e.activation(out=kTb[:, 0:n], in_=pss[1][:, 0:n], func=AF.Identity),
              reads=[ps_t[1]], writes=[qk_t])
        kb.op("dve", lambda e, n=n: e.tensor_copy(out=GL[:, 0:n], in_=pss[2][:, 0:n]), reads=[ps_t[2]], writes=[GL_t])
        for ti in range(nt):
            tl = t0 + ti
            c0 = ti * 128
            gc = tl * 128
            pt = per_t[tl]
            kb.group("pe", [lambda e, k=k, c0=c0: e.matmul(pss[4][:, 0:384], lhsT=hb[0][:, k, c0:c0 + 128], rhs=wA[:, k, 128:512],
                                                          start=(k == 0), stop=(k == NCH - 1)) for k in range(NCH)],
                     reads=[wA_t, hb_t[0]], writes=[ps_t[4]])
            kb.group("pe", [lambda e, k=k, c0=c0: e.matmul(pss[5][:, 0:256], lhsT=hb[0][:, k, c0:c0 + 128], rhs=wA[:, k, 512:768],
                                                          start=(k == 0), stop=(k == NCH - 1)) for k in range(NCH)],
                     reads=[wA_t, hb_t[0]], writes=[ps_t[5]])
            kb.group("pe", [lambda e, d=d, c0=c0: e.matmul(pss[6][:, d * 128:(d + 1) * 128], lhsT=GL[:, c0:c0 + 128], rhs=w2[:, d, :],
                                                          start=True, stop=True) for d in range(2)],
                     reads=[GL_t, k_t], writes=[ps_t[6]])
            kb.op("dve", lambda e: e.tensor_tensor(out=tg[:], in0=pss[6][:, 0:256], in1=gb[:], op=ALU.add),
                  reads=[ps_t[6], k_t], writes=[tg_t])
            kb.op("act", lambda e: e.activation(out=tg[:], in_=tg[:], func=AF.Exp, scale=-1.0), reads=[tg_t], writes=[tg_t])
            kb.op("act", lambda e: e.activation(out=sg[:], in_=tg[:], func=AF.Ln, bias=1.0, scale=1.0), reads=[tg_t], writes=[sg_t])
            kb.group("pe", [lambda e, d=d: e.matmul(pss[6][:, 256 + d * 128:256 + (d + 1) * 128], lhsT=sg[:, d * 128:(d + 1) * 128],
                                                   rhs=msk[:, d, :], start=True, stop=True) for d in range(2)],
                     reads=[sg_t, k_t], writes=[ps_t[6]])
            kb.group("pe", [lambda e, d=d: e.matmul(pss[3][:, d * 128:(d + 1) * 128], lhsT=msk[:, d, :],
                                                   rhs=sg[:, d * 128:(d + 1) * 128], start=True, stop=True) for d in range(2)],
                     reads=[sg_t, k_t], writes=[ps_t[3]])
            kb.op("act", lambda e: e.activation(out=E1[:], in_=pss[6][:, 256:512], func=AF.Exp), reads=[ps_t[6]], writes=[E1_t])
            kb.op("act", lambda e: e.activation(out=E2[:], in_=pss[6][:, 256:512], func=AF.Exp, scale=-1.0),
                  reads=[ps_t[6]], writes=[E2_t])
            kb.op("act", lambda e: e.activation(out=E2t[:], in_=pss[3][:, 0:256], func=AF.Exp, scale=-1.0),
                  reads=[ps_t[3]], writes=[E2t_t])
            kb.op("pool", lambda e, tl=tl: e.tensor_copy(out=el[:, 0, tl:tl + 1], in_=E1[:, 127:128]), reads=[E1_t], writes=[el_t])
            kb.op("pool", lambda e, tl=tl: e.tensor_copy(out=el[:, 1, tl:tl + 1], in_=E1[:, 128:129]), reads=[E1_t], writes=[el_t])
            kb.op("dve", lambda e, c0=c0, gc=gc: e.tensor_tensor(out=QF[:, gc:gc + 128], in0=qTb[:, c0:c0 + 128], in1=E1[:, 0:128], op=ALU.mult),
                  reads=[qk_t, E1_t], writes=[pt])
            kb.op("dve", lambda e, c0=c0, gc=gc: e.tensor_tensor(out=QB[:, gc:gc + 128], in0=qTb[:, c0:c0 + 128], in1=E1[:, 128:256], op=ALU.mult),
                  reads=[qk_t, E1_t], writes=[pt])
            kb.op("dve", lambda e, c0=c0, gc=gc: e.tensor_tensor(out=KF[:, gc:gc + 128], in0=kTb[:, c0:c0 + 128], in1=E2[:, 0:128], op=ALU.mult),
                  reads=[qk_t, E2_t], writes=[pt])
            kb.op("dve", lambda e, c0=c0, gc=gc: e.tensor_tensor(out=KBk[:, gc:gc + 128], in0=kTb[:, c0:c0 + 128], in1=E2[:, 128:256], op=ALU.mult),
                  reads=[qk_t, E2_t], writes=[pt])
            kb.op("dve", lambda e, tl=tl: e.tensor_tensor(out=KTF[:, tl, :], in0=pss[4][:, 0:128], in1=E2t[:, 0:128], op=ALU.mult),
                  reads=[ps_t[4], E2t_t], writes=[pt])
            kb.op("dve", lambda e, tl=tl: e.tensor_tensor(out=KTB[:, tl, :], in0=pss[4][:, 0:128], in1=E2t[:, 128:256], op=ALU.mult),
                  reads=[ps_t[4], E2t_t], writes=[pt])
            kb.op("act", lambda e, tl=tl: e.activation(out=V[:, tl, :], in_=pss[4][:, 128:384], func=AF.Identity),
                  reads=[ps_t[4]], writes=[pt])
            kb.op("act", lambda e: e.activation(out=srt[:], in_=pss[5][:, 0:256], func=AF.Silu), reads=[ps_t[5]], writes=[srt_t])
            kb.op("pool", lambda e, tl=tl: e.tensor_tensor(out=SR[:, tl, :], in0=srt[:], in1=gn[:], op=ALU.mult),
                  reads=[srt_t, k_t], writes=[pt])
            kb.group("pe", [lambda e, gc=gc: e.matmul(pss[5][:, 256:384], lhsT=KF[:, gc:gc + 128], rhs=QF[:, gc:gc + 128], start=True, stop=True),
                            lambda e, gc=gc: e.matmul(pss[5][:, 384:512], lhsT=KBk[:, gc:gc + 128], rhs=QB[:, gc:gc + 128], start=True, stop=True)],
                     reads=[pt], writes=[ps_t[5]])
            kb.op("dve", lambda e: e.tensor_tensor(out=a1[:], in0=pss[5][:, 256:384], in1=msk[:, 2, :], op=ALU.mult),
                  reads=[ps_t[5], k_t], writes=[a_t])
            kb.op("dve", lambda e: e.tensor_tensor(out=a2[:], in0=pss[5][:, 384:512], in1=msk[:, 3, :], op=ALU.mult),
                  reads=[ps_t[5], k_t, a_t], writes=[a_t])
            kb.op("pool", lambda e, tl=tl: e.tensor_tensor(out=AT[:, tl, :], in0=a1[:], in1=a2[:], op=ALU.add),
                  reads=[a_t], writes=[pt])

    S = kb.sb("S", [128, 256], F32); S_t = Tok()
    S2 = kb.sb("S2", [128, 256], F32); S2_t = Tok()
    sbf_t = [Tok() for _ in range(NTILE)]
    kb.op("pool", lambda e: e.memset(S[:], 0.0), writes=[S_t])
    for c in range(NTILE):
        kb.op("act", lambda e, c=c: e.activation(out=SBF[:, c, :], in_=S[:], func=AF.Identity), reads=[S_t], writes=[sbf_t[c]])
        if c == NTILE - 1:
            break
        kb.op("pe", lambda e, c=c: e.matmul(pss[0][:, 0:256], lhsT=KTF[:, c, :], rhs=V[:, c, :], start=True, stop=True),
              reads=[per_t[c]], writes=[ps_t[0]])
        kb.op("dve", lambda e: e.tensor_tensor(out=S2[:], in0=pss[0][:, 0:256], in1=S[:], op=ALU.add),
              reads=[ps_t[0], S_t], writes=[S2_t])
        kb.op("dve", lambda e, c=c: e.tensor_scalar(out=S[:], in0=S2[:], scalar1=el[:, 0, c:c + 1], scalar2=None, op0=ALU.mult),
              reads=[S2_t, el_t], writes=[S_t])

    Sb = kb.sb("Sb", [128, 256], F32); Sb_t = Tok()
    Sbb = kb.sb("Sbb", [128, 256], BF16); Sbb_t = Tok()
    ss = kb.sb("ss", [128, 2], F32); ss_t = Tok()
    junk = kb.sb("junk", [128, 256], F32); junk_t = Tok()
    on = kb.sb("on", [128, 256], BF16); on_t = Tok()
    ost = [kb.sb(f"ost{i}", [128, 2, 128], BF16) for i in range(2)]; ost_t = [Tok(), Tok()]
    kb.op("pool", lambda e: e.memset(Sb[:], 0.0), writes=[Sb_t])
    kb.op("pool", lambda e: e.memset(Sbb[:], 0.0), writes=[Sbb_t])
    for oi, c in enumerate(BWD_ORDER):
        gc = c * 128
        kb.group("pe", [lambda e, c=c, gc=gc: e.matmul(pss[1][:, 0:256], lhsT=QF[:, gc:gc + 128], rhs=SBF[:, c, :], start=True, stop=False),
                        lambda e, c=c, gc=gc: e.matmul(pss[1][:, 0:256], lhsT=QB[:, gc:gc + 128], rhs=Sbb[:], start=False, stop=False),
                        lambda e, c=c, gc=gc: e.matmul(pss[1][:, 0:256], lhsT=AT[:, c, :], rhs=V[:, c, :], start=False, stop=True)],
                 reads=[per_t[c], sbf_t[c], Sbb_t], writes=[ps_t[1]])
        if oi < NTILE - 1 and c != 0:
            pass
        kb.op("pe", lambda e, c=c: e.matmul(pss[2][:, 0:256], lhsT=KTB[:, c, :], rhs=V[:, c, :], start=True, stop=True),
              reads=[per_t[c]], writes=[ps_t[2]])
        kb.op("dve", lambda e: e.tensor_tensor(out=S2[:], in0=pss[2][:, 0:256], in1=Sb[:], op=ALU.add),
              reads=[ps_t[2], Sb_t], writes=[S2_t])
        kb.op("dve", lambda e, c=c: e.tensor_scalar(out=Sb[:], in0=S2[:], scalar1=el[:, 1, c:c + 1], scalar2=None, op0=ALU.mult),
              reads=[S2_t, el_t], writes=[Sb_t])
        kb.op("act", lambda e: e.activation(out=Sbb[:], in_=Sb[:], func=AF.Identity), reads=[Sb_t], writes=[Sbb_t])
        kb.op("act", lambda e: e.activation(out=junk[:], in_=pss[1][:, 0:256], func=AF.Square, accum_out=ss[:, 0:1]),
              reads=[ps_t[1]], writes=[junk_t, ss_t])
        kb.op("act", lambda e: e.activation(out=ss[:, 1:2], in_=ss[:, 0:1], func=AF.Sqrt, scale=1.0 / 256, bias=cs.eps[:, 0:1]),
              reads=[ss_t, cs.t], writes=[ss_t])
        kb.op("dve", lambda e: e.reciprocal(out=ss[:, 1:2], in_=ss[:, 1:2]), reads=[ss_t], writes=[ss_t])
        kb.op("dve", lambda e, c=c: e.scalar_tensor_tensor(out=on[:], in0=pss[1][:, 0:256], scalar=ss[:, 1:2], in1=SR[:, c, :],
                                                          op0=ALU.mult, op1=ALU.mult),
              reads=[ps_t[1], ss_t, per_t[c]], writes=[on_t])
        kb.group("pe", [lambda e, hh=hh: e.transpose(psT[:, hh * 128:(hh + 1) * 128], on[:, hh * 128:(hh + 1) * 128], ident[:])
                        for hh in range(2)],
                 reads=[on_t, k_t], writes=[psT_t])
        oj = oi % 2
        kb.op("act", lambda e, oj=oj: e.activation(out=ost[oj][:].rearrange("p a b -> p (a b)"), in_=psT[:, 0:256], func=AF.Identity),
              reads=[psT_t], writes=[ost_t[oj]])
        kb.dma("sp", out_d[:, 0:2, gc:gc + 128], ost[oj][:], reads=[ost_t[oj]], is_output=True)
    return kb.end_phase()


def host_masks():
    j = np.arange(128)[:, None]
    i = np.arange(128)[None, :]
    mf = (j <= i).astype(np.float32)
    mb = (j >= i).astype(np.float32)
    return np.ascontiguousarray(np.stack([mf * (-1.0 / 16.0), mb * (-1.0 / 16.0), mf, mb], axis=1))


def wfm(w):
    return np.ascontiguousarray(w.reshape(NCH, 128, -1).transpose(1, 0, 2))


def host_Aeven_inputs(g, w_in, gate_w2, gate_b, gla_norm_g, sc_conv_w):
    o = np.cumsum((0,) + (512, 512, 1024, 1024, 32, 1024, 1024, 1024))
    qs, ks, vs, rs, gls, sxs, sbs, scgs = [int(v) for v in o[:8]]
    colsA = np.concatenate([np.arange(qs + 128 * g, qs + 128 * g + 128), np.arange(ks + 128 * g, ks + 128 * g + 128),
                            np.arange(vs + 256 * g, vs + 256 * g + 256), np.arange(rs + 256 * g, rs + 256 * g + 256),
                            np.arange(gls, gls + 32)])
    wa = w_in[:, colsA]
    pad = np.zeros((wa.shape[0], 128), wa.dtype)
    pad[:, 0:16] = wa[:, 768:784]
    pad[:, 32:48] = wa[:, 784:800]
    wa = np.concatenate([wa[:, 0:768], pad], axis=1)
    colsC = np.concatenate([np.arange(sxs + 256 * g, sxs + 256 * g + 256), np.arange(sbs + 256 * g, sbs + 256 * g + 256),
                            np.arange(scgs + 256 * g, scgs + 256 * g + 256)])
    hs = slice(128 * g, 128 * g + 128)
    w2 = np.zeros((128, 2, 128), np.float32)
    w2[0:16, 0, :] = gate_w2[0][:, hs]
    w2[32:48, 1, :] = gate_w2[1][:, hs]
    gbias = np.ascontiguousarray(np.broadcast_to(gate_b[:, hs].reshape(1, 256), (128, 256)))
    gnorm = np.ascontiguousarray(np.broadcast_to(gla_norm_g.reshape(1, 256), (128, 256)))
    scw = np.ascontiguousarray(sc_conv_w[:, 256 * g:256 * g + 256].reshape(3, 2, 128).transpose(2, 1, 0))
    return {"wA": wfm(wa), "wC": wfm(w_in[:, colsC]), "w2": w2, "gbias": gbias, "gnorm": gnorm, "scw": scw,
            "masks": host_masks(), "ident": np.eye(128, dtype=np.float32).astype(ml_dtypes.bfloat16)}


def assemble_h1(h1_cores, b):
    ctxp = [h1_cores[4 * b + j][:, :, 0:64] for j in range(4)]
    latp = [h1_cores[4 * b + j][:, :, 64:1088] for j in range(4)]
    return np.ascontiguousarray(np.concatenate(ctxp + latp, axis=2))


def scatter_mix(mix_cores, b, j):
    parts = []
    for g in range(4):
        m = mix_cores[4 * b + g]
        parts.append(np.concatenate([m[:, :, 64 * j:64 * j + 64], m[:, :, CTX + 1024 * j:CTX + 1024 * j + 1024]], axis=2))
    return np.ascontiguousarray(np.concatenate(parts, axis=1))


QBLK2 = [(0, 256, [0, 1])] + [(CTX + 512 * i, 512, list(range(NTILE))) for i in range(8)]
KSCALE = 128 ** -0.5


def build_Aodd(layer, kb=None):
    import contextlib
    import math
    lam_init = 0.8 - 0.6 * math.exp(-0.3 * layer)
    kb = kb or KB()
    nc = kb.nc
    h1_d = kb.dram("h1T", [128, NCH, TB], BF16, "ExternalInput")
    wD_d = kb.dram("wD", [128, NCH, 768], F32, "ExternalInput")
    wM_d = kb.dram("wM", [128, NCH, 772], F32, "ExternalInput")
    qkg_d = kb.dram("qkg", [128, 512], F32, "ExternalInput")
    rope_d = kb.dram("rope", [128, 32, 2, 64], F32, "ExternalInput")
    lamp_d = kb.dram("lamp", [128, 4, 64], F32, "ExternalInput")
    subg_d = kb.dram("subg", [128, 128], F32, "ExternalInput")
    subgc_d = kb.dram("subgc", [128, 1], F32, "ExternalInput")
    mlcw_d = kb.dram("mlcw", [128, 2, 3], F32, "ExternalInput")
    mlgb_d = kb.dram("mlgb", [128, 4], F32, "ExternalInput")
    mlng_d = kb.dram("mlng", [128, 256], F32, "ExternalInput")
    msk_d = kb.dram("masks", [128, 4, 128], F32, "ExternalInput")
    id_d = kb.dram("ident", [128, 128], BF16, "ExternalInput")
    out_d = kb.dram("mixT", [128, 4, TB], BF16, "ExternalOutput")
    cs = kb.get_consts()
    pss = [kb.ps(f"ps{i}", [128, 512], F32) for i in range(7)]
    ps_t = [Tok() for _ in range(7)]
    psT = kb.ps("psT", [128, 1024], BF16); psT_t = Tok()
    hb = kb.sb("hb", [128, NCH, 512], BF16); hb_t = Tok()
    ident = kb.sb("id_s", [128, 128], BF16)
    msk = kb.sb("msk_s", [128, 4, 128], F32)
    k_t = Tok()
    kb.dma("sp", ident[:], id_d, writes=[k_t])
    kb.dma("sp", msk[:], msk_d, writes=[k_t])
    ss = kb.sb("ss", [128, 4], F32); ss_t = Tok()
    junk = kb.sb("junk", [128, 256], F32); junk_t = Tok()
    ost = [kb.sb(f"ost{i}", [128, 2, 128], BF16) for i in range(2)]; ost_t = [Tok(), Tok()]
    onb = kb.sb("onb", [128, 256], BF16); onb_t = Tok()
    T1 = kb.sb("T1", [128, 256], F32); T1_t = Tok()
    T2 = kb.sb("T2", [128, 256], F32); T2_t = Tok()

    with contextlib.ExitStack() as st:
        def sbx(name, shape, dt):
            return st.enter_context(nc.sbuf_tensor(f"{name}_u{kb.uid}", list(shape), dt)).ap()
        wD = sbx("wD_s", [128, NCH, 768], BF16); wD_t = Tok()
        QKT = sbx("QKT", [128, 2, TB], BF16); qkt_t = [Tok() for _ in range(NTILE)]
        KTZ = sbx("KTZ", [128, 4, TB], BF16)
        VA = sbx("VA", [128, NTILE, 2, 132], BF16); va_t = [Tok() for _ in range(NTILE)]
        rope = sbx("rope_s", [128, 32, 2, 64], F32)
        qkg = sbx("qkg_s", [128, 512], F32)
        lamp = sbx("lamp_s", [128, 4, 64], F32)
        subg = sbx("subg_s", [128, 128], F32)
        SQ = sbx("SQ", [128, 512], F32); SQ_t = Tok()
        XN = sbx("XN", [128, 512], F32); XN_t = Tok()
        Y1 = sbx("Y1", [128, 512], F32); Y1_t = Tok()
        Y2 = sbx("Y2", [128, 512], F32); Y2_t = Tok()
        YB = sbx("YB", [128, 512], BF16); YB_t = Tok()
        rs8 = sbx("rs8", [128, 8], F32); rs8_t = Tok()
        PT = [sbx(f"PT{i}", [128, 512], BF16) for i in range(3)]; PT_t = [Tok(), Tok(), Tok()]
        lam = sbx("lam", [128, 8], F32); lam_t = Tok()
        rz = sbx("rz", [128, 4], F32); rz_t = Tok()
        kc_t = Tok()
        kb.dma("pool", wD[:], wD_d, writes=[wD_t])
        for dst, src in ((rope, rope_d), (qkg, qkg_d), (lamp, lamp_d), (subg, subg_d)):
            kb.dma("sp", dst[:], src, writes=[kc_t])
        kb.op("pool", lambda e: e.memset(VA[:], 1.0), writes=va_t)
        kb.op("pool", lambda e: e.memset(KTZ[:], 0.0), writes=qkt_t)
        kb.op("dve", lambda e: e.tensor_tensor(out=SQ[:, 0:64], in0=lamp[:, 0, :], in1=lamp[:, 1, :], op=ALU.mult),
              reads=[kc_t], writes=[SQ_t])
        kb.op("dve", lambda e: e.tensor_tensor(out=SQ[:, 64:128], in0=lamp[:, 2, :], in1=lamp[:, 3, :], op=ALU.mult),
              reads=[kc_t], writes=[SQ_t])
        kb.op("dve", lambda e: e.tensor_reduce(out=lam[:, 0:2], in_=SQ[:, 0:128].rearrange("p (a b) -> p a b", a=2),
                                               axis=mybir.AxisListType.X, op=ALU.add), reads=[SQ_t], writes=[lam_t])
        kb.op("act", lambda e: e.activation(out=lam[:, 2:4], in_=lam[:, 0:2], func=AF.Exp), reads=[lam_t], writes=[lam_t])
        kb.op("dve", lambda e: e.tensor_tensor(out=lam[:, 4:5], in0=lam[:, 3:4], in1=lam[:, 2:3], op=ALU.subtract),
              reads=[lam_t], writes=[lam_t])
        kb.op("dve", lambda e: e.tensor_scalar(out=lam[:, 4:5], in0=lam[:, 4:5], scalar1=-lam_init, scalar2=None, op0=ALU.add),
              reads=[lam_t], writes=[lam_t])
        kb.op("dve", lambda e: e.tensor_scalar(out=subg[:], in0=subg[:], scalar1=(1.0 - lam_init), scalar2=None, op0=ALU.mult),
              reads=[kc_t], writes=[kc_t])

        for (t0, nt) in ABLK:
            n = nt * 128
            g0 = t0 * 128
            kb.dma("sp", hb[:, :, 0:n], h1_d[:, :, g0:g0 + n], writes=[hb_t])
            for ti in range(nt):
                tl = t0 + ti
                c0 = ti * 128
                gc = tl * 128
                pa = 2 * (tl % 2)
                pb = pa + 1
                kb.group("pe", [lambda e, k=k, c0=c0: e.matmul(pss[pa][:, 0:512], lhsT=hb[:, k, c0:c0 + 128], rhs=wD[:, k, 0:512],
                                                              start=(k == 0), stop=(k == NCH - 1)) for k in range(NCH)],
                         reads=[wD_t, hb_t], writes=[ps_t[pa]])
                kb.group("pe", [lambda e, k=k, c0=c0: e.matmul(pss[pb][:, 0:256], lhsT=hb[:, k, c0:c0 + 128], rhs=wD[:, k, 512:768],
                                                              start=(k == 0), stop=(k == NCH - 1)) for k in range(NCH)],
                         reads=[wD_t, hb_t], writes=[ps_t[pb]])
                kb.op("act", lambda e: e.activation(out=SQ[:], in_=pss[pa][:, 0:512], func=AF.Square), reads=[ps_t[pa]], writes=[SQ_t])
                kb.op("dve", lambda e: e.tensor_reduce(out=rs8[:], in_=SQ[:].rearrange("p (a b) -> p a b", a=8),
                                                       axis=mybir.AxisListType.X, op=ALU.add), reads=[SQ_t], writes=[rs8_t])
                kb.op("act", lambda e: e.activation(out=rs8[:], in_=rs8[:], func=AF.Sqrt, scale=1.0 / 64, bias=cs.eps[:, 0:1]),
                      reads=[rs8_t, cs.t], writes=[rs8_t])
                kb.op("dve", lambda e: e.reciprocal(out=rs8[:], in_=rs8[:]), reads=[rs8_t], writes=[rs8_t])
                kb.op("dve", lambda e: e.tensor_tensor(out=XN[:].rearrange("p (a b) -> p a b", a=8),
                                                       in0=pss[pa][:, 0:512].rearrange("p (a b) -> p a b", a=8),
                                                       in1=rs8[:].unsqueeze(2).broadcast_to([128, 8, 64]), op=ALU.mult),
                      reads=[ps_t[pa], rs8_t], writes=[XN_t])
                kb.op("pool", lambda e: e.tensor_tensor(out=XN[:], in0=XN[:], in1=qkg[:], op=ALU.mult), reads=[XN_t, kc_t], writes=[XN_t])
                if tl >= 2:
                    lt = tl - 2
                    xv = XN[:].rearrange("p (g h x d) -> p g h x d", g=8, h=2, x=2)
                    yv = Y2[:].rearrange("p (g h x d) -> p g h x d", g=8, h=2, x=2)
                    sv = rope[:, lt, 1, :].rearrange("p (h x d) -> p h x d", h=2, x=2)
                    kb.op("dve", lambda e, lt=lt: e.tensor_tensor(out=Y1[:].rearrange("p (a b) -> p a b", a=8),
                                                                 in0=XN[:].rearrange("p (a b) -> p a b", a=8),
                                                                 in1=rope[:, lt:lt + 1, 0, :].broadcast_to([128, 8, 64]), op=ALU.mult),
                          reads=[XN_t, kc_t], writes=[Y1_t])
                    kb.op("dve", lambda e, xv=xv, yv=yv, sv=sv: e.tensor_tensor(
                        out=yv[:, :, :, 0, :], in0=xv[:, :, :, 1, :],
                        in1=sv[:, :, 0, :].unsqueeze(1).broadcast_to([128, 8, 2, 16]), op=ALU.mult),
                        reads=[XN_t, kc_t], writes=[Y2_t])
                    kb.op("dve", lambda e, xv=xv, yv=yv, sv=sv: e.tensor_tensor(
                        out=yv[:, :, :, 1, :], in0=xv[:, :, :, 0, :],
                        in1=sv[:, :, 1, :].unsqueeze(1).broadcast_to([128, 8, 2, 16]), op=ALU.mult),
                        reads=[XN_t, kc_t], writes=[Y2_t])
                    kb.op("pool", lambda e: e.tensor_tensor(out=YB[:], in0=Y1[:], in1=Y2[:], op=ALU.add),
                          reads=[Y1_t, Y2_t], writes=[YB_t])
                else:
                    kb.op("act", lambda e: e.activation(out=YB[:], in_=XN[:], func=AF.Identity), reads=[XN_t], writes=[YB_t])
                kb.group("pe", [lambda e, a=a: e.transpose(psT[:, a * 128:(a + 1) * 128], YB[:, a * 128:(a + 1) * 128], ident[:])
                                for a in range(4)], reads=[YB_t, k_t], writes=[psT_t])
                kb.op("act", lambda e, gc=gc: e.activation(out=QKT[:, :, gc:gc + 128],
                                                           in_=psT[:, 0:256].rearrange("p (a b) -> p a b", a=2), func=AF.Identity),
                      reads=[psT_t], writes=[qkt_t[tl]])
                for hh in range(2):
                    for mm in range(2):
                        kb.op("dve" if (hh + mm) % 2 else "act",
                              (lambda e, hh=hh, mm=mm, gc=gc: e.tensor_copy(
                                  out=KTZ[mm * 64:(mm + 1) * 64, hh * 2 + mm, gc:gc + 128],
                                  in_=psT[mm * 64:(mm + 1) * 64, (2 + hh) * 128:(3 + hh) * 128])) if (hh + mm) % 2 else
                              (lambda e, hh=hh, mm=mm, gc=gc: e.activation(
                                  out=KTZ[mm * 64:(mm + 1) * 64, hh * 2 + mm, gc:gc + 128],
                                  in_=psT[mm * 64:(mm + 1) * 64, (2 + hh) * 128:(3 + hh) * 128], func=AF.Identity)),
                              reads=[psT_t], writes=[qkt_t[tl]])
                kb.op("act", lambda e, tl=tl: e.activation(out=VA[:, tl, :, 0:128],
                                                           in_=pss[pb][:, 0:256].rearrange("p (a b) -> p a b", a=2), func=AF.Identity),
                      reads=[ps_t[pb]], writes=[va_t[tl]])

        T0 = sbx("T0", [128, 4, 128], F32); T0_t = Tok()
        OSB = [sbx(f"OSB{i}", [128, 4, 132], F32) for i in range(2)]; OSB_t = [Tok(), Tok()]
        oi = 0
        for h in range(2):
            for (q0, nq, ktiles) in QBLK2:
                nqt = nq // 128
                qtl = [q0 // 128 + i for i in range(nqt)]
                nk = len(ktiles)
                for m in range(2):
                    SB = [0, 1, 6]

                    def st_fn(ii, h=h, q0=q0, nq=nq, ktiles=ktiles, m=m):
                        kt = ktiles[ii]
                        bnk = SB[ii % 3]
                        return lambda e: e.matmul(
                            pss[bnk][:, 0:nq], lhsT=KTZ[:, h * 2 + m, kt * 128:(kt + 1) * 128],
                            rhs=QKT[:, h, q0:q0 + nq], start=True, stop=True)
                    qreads = [qkt_t[x] for x in qtl]
                    for pre in range(min(2, nk)):
                        kb.op("pe", st_fn(pre), reads=[qkt_t[ktiles[pre]]] + qreads, writes=[ps_t[SB[pre % 3]]])
                    for ii in range(nk):
                        kt = ktiles[ii]
                        bnk = SB[ii % 3]
                        pb = ii % 3
                        kb.op("act", lambda e, bnk=bnk, pb=pb, nq=nq: e.activation(out=PT[pb][:, 0:nq], in_=pss[bnk][:, 0:nq], func=AF.Exp, scale=0.125),
                              reads=[ps_t[bnk]], writes=[PT_t[pb]])
                        fns = [lambda e, qt=qt, pb=pb, kt=kt, ii=ii, nk=nk, h=h: e.matmul(
                            pss[2 + qt][:, 0:129], lhsT=PT[pb][:, qt * 128:(qt + 1) * 128],
                            rhs=VA[:, kt, h, 0:129], start=(ii == 0), stop=(ii == nk - 1)) for qt in range(nqt)]
                        rd = [PT_t[pb], va_t[kt]]
                        wr = [ps_t[2 + qt] for qt in range(nqt)]
                        if ii + 2 < nk:
                            fns.append(st_fn(ii + 2))
                            rd += [qkt_t[ktiles[ii + 2]]] + qreads
                            wr.append(ps_t[SB[(ii + 2) % 3]])
                        kb.group("pe", fns, reads=rd, writes=wr)
                    for qt in range(nqt):
                        kb.op("dve", lambda e, qt=qt, m=m: e.tensor_copy(out=OSB[m][:, qt, 0:129], in_=pss[2 + qt][:, 0:129]),
                              reads=[ps_t[2 + qt]], writes=[OSB_t[m]])
                    for qt in range(nqt):
                        kb.op("dve", lambda e, qt=qt, m=m: e.reciprocal(out=rz[:, 0:1], in_=OSB[m][:, qt, 128:129]), reads=[OSB_t[m]], writes=[rz_t])
                        if m == 0:
                            kb.op("pool", lambda e, qt=qt, m=m: e.tensor_scalar(out=T0[:, qt, :], in0=OSB[m][:, qt, 0:128], scalar1=rz[:, 0:1], scalar2=None,
                                                                                op0=ALU.mult),
                                  reads=[OSB_t[m], rz_t], writes=[T0_t])
                            continue
                        kb.op("dve", lambda e: e.tensor_tensor(out=rz[:, 2:3], in0=rz[:, 0:1], in1=lam[:, 4:5], op=ALU.mult),
                              reads=[rz_t, lam_t], writes=[rz_t])
                        kb.op("dve", lambda e, qt=qt, m=m: e.scalar_tensor_tensor(out=T2[:, 0:128], in0=OSB[m][:, qt, 0:128], scalar=rz[:, 2:3],
                                                                                 in1=T0[:, qt, :], op0=ALU.mult, op1=ALU.add),
                              reads=[OSB_t[m], rz_t, T0_t], writes=[T2_t])
                        kb.op("act", lambda e: e.activation(out=junk[:, 0:128], in_=T2[:, 0:128], func=AF.Square, accum_out=ss[:, 0:1]),
                              reads=[T2_t], writes=[junk_t, ss_t])
                        kb.op("act", lambda e: e.activation(out=ss[:, 1:2], in_=ss[:, 0:1], func=AF.Sqrt, scale=1.0 / 128, bias=cs.eps[:, 0:1]),
                              reads=[ss_t, cs.t], writes=[ss_t])
                        kb.op("dve", lambda e: e.reciprocal(out=ss[:, 1:2], in_=ss[:, 1:2]), reads=[ss_t], writes=[ss_t])
                        kb.op("dve", lambda e: e.scalar_tensor_tensor(out=onb[:, 0:128], in0=T2[:, 0:128], scalar=ss[:, 1:2], in1=subg[:],
                                                                      op0=ALU.mult, op1=ALU.mult),
                              reads=[T2_t, ss_t, kc_t], writes=[onb_t])
                        kb.op("pe", lambda e: e.transpose(psT[:, 0:128], onb[:, 0:128], ident[:]), reads=[onb_t, k_t], writes=[psT_t])
                        oj = oi % 2
                        oi += 1
                        kb.op("act", lambda e, oj=oj: e.activation(out=ost[oj][:, 0, :], in_=psT[:, 0:128], func=AF.Identity),
                              reads=[psT_t], writes=[ost_t[oj]])
                        qa = q0 + qt * 128
                        kb.dma("sp", out_d[:, h, qa:qa + 128], ost[oj][:, 0, :], reads=[ost_t[oj]], is_output=True)
        kb.barrier()

    QM = kb.sb("QM", [128, TB], BF16); KM = kb.sb("KM", [128, TB], BF16); qm_t = Tok()
    VA2 = kb.sb("VA2", [128, NTILE, 260], BF16); va2_t = [Tok() for _ in range(NTILE)]
    SO = kb.sb("SO", [128, NTILE, 256], BF16); so_t = [Tok() for _ in range(NTILE)]
    G4 = kb.sb("G4", [128, NTILE, 4], F32); g4_t = Tok()
    mlgb = kb.sb("mlgb_s", [128, 4], F32); mlng = kb.sb("mlng_s", [128, 256], F32); mlcw = kb.sb("mlcw_s", [128, 2, 3], F32)
    km_t = Tok()
    for dst, src in ((mlgb, mlgb_d), (mlng, mlng_d), (mlcw, mlcw_d)):
        kb.dma("sp", dst[:], src, writes=[km_t])
    kb.op("pool", lambda e: e.memset(VA2[:], 1.0), writes=va2_t)
    with contextlib.ExitStack() as st:
        def sbx(name, shape, dt):
            return st.enter_context(nc.sbuf_tensor(f"{name}_u{kb.uid}", list(shape), dt)).ap()
        wM = sbx("wM_s", [128, NCH, 772], BF16); wM_t = Tok()
        RQ = sbx("RQ", [128, 2, TB + 4], F32); RQ_t = Tok()
        CV = sbx("CV", [128, SEQ], F32); CV_t = Tok()
        g4s = sbx("g4s", [128, 4], F32); g4s_t = Tok()
        kb.dma("pool", wM[:], wM_d, writes=[wM_t])
        kb.op("pool", lambda e: e.memset(RQ[:], 0.0), writes=[RQ_t])

        def ucol(tok):
            return 1 + tok if tok < CTX else 3 + tok
        for (t0, nt) in ABLK:
            n = nt * 128
            g0 = t0 * 128
            kb.dma("sp", hb[:, :, 0:n], h1_d[:, :, g0:g0 + n], writes=[hb_t])
            for w in range(2):
                kb.group("pe", [lambda e, k=k, w=w, n=n: e.matmul(pss[w][:, 0:n], lhsT=wM[:, k, w * 128:(w + 1) * 128], rhs=hb[:, k, 0:n],
                                                                 start=(k == 0), stop=(k == NCH - 1)) for k in range(NCH)],
                         reads=[wM_t, hb_t], writes=[ps_t[w]])
                uc = ucol(g0)
                kb.op("act", lambda e, w=w, n=n, uc=uc: e.activation(out=RQ[:, w, uc:uc + n], in_=pss[w][:, 0:n], func=AF.Identity),
                      reads=[ps_t[w]], writes=[RQ_t])
            for ti in range(nt):
                tl = t0 + ti
                c0 = ti * 128
                kb.group("pe", [lambda e, k=k, c0=c0: e.matmul(pss[2][:, 0:512], lhsT=hb[:, k, c0:c0 + 128], rhs=wM[:, k, 256:768],
                                                              start=(k == 0), stop=(k == NCH - 1)) for k in range(NCH)],
                         reads=[wM_t, hb_t], writes=[ps_t[2]])
                kb.group("pe", [lambda e, k=k, c0=c0: e.matmul(pss[3][:, 0:4], lhsT=hb[:, k, c0:c0 + 128], rhs=wM[:, k, 768:772],
                                                              start=(k == 0), stop=(k == NCH - 1)) for k in range(NCH)],
                         reads=[wM_t, hb_t], writes=[ps_t[3]])
                kb.op("act", lambda e, tl=tl: e.activation(out=VA2[:, tl, 0:256], in_=pss[2][:, 0:256], func=AF.Identity),
                      reads=[ps_t[2]], writes=[va2_t[tl]])
                kb.op("act", lambda e: e.activation(out=T1[:], in_=pss[2][:, 256:512], func=AF.Sigmoid), reads=[ps_t[2]], writes=[T1_t])
                kb.op("pool", lambda e, tl=tl: e.tensor_tensor(out=SO[:, tl, :], in0=T1[:], in1=mlng[:], op=ALU.mult),
                      reads=[T1_t, km_t], writes=[so_t[tl]])
                kb.op("dve", lambda e, tl=tl: e.tensor_tensor(out=G4[:, tl, :], in0=pss[3][:, 0:4], in1=mlgb[:], op=ALU.add),
                      reads=[ps_t[3], km_t], writes=[g4_t])
                kb.op("act", lambda e, tl=tl: e.activation(out=g4s[:], in_=G4[:, tl, :], func=AF.Exp, scale=-1.0), reads=[g4_t], writes=[g4s_t])
                kb.op("act", lambda e: e.activation(out=g4s[:], in_=g4s[:], func=AF.Ln, bias=1.0, scale=1.0), reads=[g4s_t], writes=[g4s_t])
                for col in (1, 3):
                    kb.op("dve", lambda e, tl=tl, col=col: e.tensor_scalar(out=G4[:, tl, col:col + 1], in0=g4s[:, col:col + 1], scalar1=-1.0,
                                                                          scalar2=None, op0=ALU.mult),
                          reads=[g4s_t, g4_t], writes=[g4_t])
        for w, dst, scl in ((0, QM, 1.0), (1, KM, KSCALE)):
            for (g0, n) in ((0, CTX), (CTX, SEQ)):
                uc = ucol(g0)
                kb.op("dve", lambda e, w=w, n=n, uc=uc: e.tensor_scalar(out=CV[:, 0:n], in0=RQ[:, w, uc:uc + n], scalar1=mlcw[:, w, 1:2],
                                                                       scalar2=None, op0=ALU.mult), reads=[RQ_t, km_t], writes=[CV_t])
                kb.op("dve", lambda e, w=w, n=n, uc=uc: e.scalar_tensor_tensor(out=CV[:, 0:n], in0=RQ[:, w, uc - 1:uc - 1 + n],
                                                                              scalar=mlcw[:, w, 0:1], in1=CV[:, 0:n], op0=ALU.mult, op1=ALU.add),
                      reads=[RQ_t, km_t, CV_t], writes=[CV_t])
                kb.op("dve", lambda e, w=w, n=n, uc=uc: e.scalar_tensor_tensor(out=CV[:, 0:n], in0=RQ[:, w, uc + 1:uc + 1 + n],
                                                                              scalar=mlcw[:, w, 2:3], in1=CV[:, 0:n], op0=ALU.mult, op1=ALU.add),
                      reads=[RQ_t, km_t, CV_t], writes=[CV_t])
                kb.op("act", lambda e, n=n: e.activation(out=CV[:, 0:n], in_=CV[:, 0:n], func=AF.Silu), reads=[CV_t], writes=[CV_t])
                kb.op("dve", lambda e, dst=dst, g0=g0, n=n, scl=scl: e.tensor_scalar(out=dst[:, g0:g0 + n], in0=CV[:, 0:n], scalar1=scl,
                                                                                    scalar2=None, op0=ALU.mult), reads=[CV_t], writes=[qm_t])
        kb.barrier()

    QP = [kb.sb(f"QP{d}", [128, TB], BF16) for d in range(2)]
    STm = [kb.sb(f"STm{d}", [128, NTILE, 128], BF16) for d in range(2)]
    KP = [kb.sb(f"KP{d}", [128, NTILE, 128], BF16) for d in range(2)]
    CF = kb.sb("CF", [128, NTILE, 260], BF16); cf_t = [Tok() for _ in range(NTILE)]
    eL = kb.sb("eL", [128, 2, NTILE], F32); eL_t = Tok()
    ch_t = [Tok() for _ in range(NTILE)]
    LF = [kb.sb(f"LF{d}", [128, 128], F32) for d in range(2)]; LF_t = [Tok(), Tok()]
    TM = [kb.sb(f"TM{d}", [128, 128], F32) for d in range(2)]; TM_t = [Tok(), Tok()]
    EB = [kb.sb(f"EB{d}", [128, 128], F32) for d in range(2)]; EB_t = [Tok(), Tok()]
    cj = kb.sb("cj", [128, 8], F32); cj_t = Tok()
    for c in range(NTILE):
        gc = c * 128
        for d in range(2):
            kb.op("dve", lambda e, d=d, c=c: e.tensor_scalar(out=LF[d][:], in0=cs.ones[:], scalar1=G4[:, c, 2 * d + 1:2 * d + 2], scalar2=None,
                                                            op0=ALU.mult), reads=[g4_t, cs.t], writes=[LF_t[d]])
        kb.group("pe", [lambda e, d=d: e.matmul(pss[4][:, d * 128:(d + 1) * 128], lhsT=LF[d][:], rhs=msk[:, d, :], start=True, stop=True)
                        for d in range(2)] +
                 [lambda e, d=d, c=c: e.matmul(pss[4][:, 256 + 4 * d:260 + 4 * d], lhsT=msk[:, d, :], rhs=G4[:, c, :], start=True, stop=True)
                  for d in range(2)],
                 reads=LF_t + [k_t, g4_t], writes=[ps_t[4]])
        kb.op("pe", lambda e, gc=gc: e.matmul(pss[5][:, 0:128], lhsT=KM[:, gc:gc + 128], rhs=QM[:, gc:gc + 128], start=True, stop=True),
              reads=[qm_t], writes=[ps_t[5]])
        kb.op("pe", lambda e, gc=gc: e.transpose(psT[:, 0:128], KM[:, gc:gc + 128], ident[:]), reads=[qm_t, k_t], writes=[psT_t])
        for d in range(2):
            last = 127 if d == 0 else 0
            kb.op("dve", lambda e, d=d, c=c: e.tensor_tensor(out=cj[:, d:d + 1], in0=G4[:, c, 2 * d:2 * d + 1],
                                                            in1=pss[4][:, 256 + 4 * d + 2 * d + 1:256 + 4 * d + 2 * d + 2], op=ALU.subtract),
                  reads=[ps_t[4], g4_t], writes=[cj_t])
            kb.op("dve", lambda e, d=d, last=last: e.tensor_copy(out=cj[:, 2 + d:3 + d], in_=pss[4][:, d * 128 + last:d * 128 + last + 1]),
                  reads=[ps_t[4]], writes=[cj_t])
            kb.op("dve", lambda e, d=d: e.tensor_tensor(out=TM[d][:], in0=pss[4][:, d * 128:(d + 1) * 128], in1=msk[:, 2 + d, :], op=ALU.add),
                  reads=[ps_t[4], k_t], writes=[TM_t[d]])
            kb.op("act", lambda e, d=d: e.activation(out=TM[d][:], in_=TM[d][:], func=AF.Exp, bias=cj[:, d:d + 1], scale=1.0),
                  reads=[TM_t[d], cj_t], writes=[TM_t[d]])
            kb.op("act", lambda e, d=d: e.activation(out=EB[d][:], in_=pss[4][:, d * 128:(d + 1) * 128], func=AF.Exp),
                  reads=[ps_t[4]], writes=[EB_t[d]])
            kb.op("act", lambda e, d=d: e.activation(out=cj[:, 4 + d:5 + d], in_=cj[:, d:d + 1], func=AF.Exp, bias=cj[:, 2 + d:3 + d], scale=1.0),
                  reads=[cj_t], writes=[cj_t])
            kb.op("pool", lambda e, d=d, c=c, last=last: e.tensor_copy(out=eL[:, d, c:c + 1], in_=EB[d][:, last:last + 1]),
                  reads=[EB_t[d]], writes=[eL_t])
            kb.op("dve", lambda e, d=d, c=c: e.tensor_tensor(out=STm[d][:, c, :], in0=pss[5][:, 0:128], in1=TM[d][:], op=ALU.mult),
                  reads=[ps_t[5], TM_t[d]], writes=[ch_t[c]])
            kb.op("dve", lambda e, d=d, gc=gc: e.tensor_tensor(out=QP[d][:, gc:gc + 128], in0=QM[:, gc:gc + 128], in1=EB[d][:], op=ALU.mult),
                  reads=[qm_t, EB_t[d]], writes=[ch_t[c]])
            kb.op("dve", lambda e, d=d, c=c: e.tensor_scalar(out=KP[d][:, c, :], in0=psT[:, 0:128], scalar1=cj[:, 4 + d:5 + d], scalar2=None,
                                                            op0=ALU.mult), reads=[psT_t, cj_t], writes=[ch_t[c]])

    Cst = kb.sb("Cst", [128, 260], F32); Cst_t = Tok()
    kb.op("pool", lambda e: e.memset(Cst[:], 0.0), writes=[Cst_t])
    for c in range(NTILE):
        kb.op("act", lambda e, c=c: e.activation(out=CF[:, c, :], in_=Cst[:], func=AF.Identity), reads=[Cst_t], writes=[cf_t[c]])
        if c == NTILE - 1:
            break
        kb.op("pe", lambda e, c=c: e.matmul(pss[0][:, 0:257], lhsT=KP[0][:, c, :], rhs=VA2[:, c, 0:257], start=True, stop=True),
              reads=[ch_t[c], va2_t[c]], writes=[ps_t[0]])
        kb.op("dve", lambda e, c=c: e.scalar_tensor_tensor(out=Cst[:, 0:257], in0=Cst[:, 0:257], scalar=eL[:, 0, c:c + 1], in1=pss[0][:, 0:257],
                                                          op0=ALU.mult, op1=ALU.add), reads=[ps_t[0], eL_t, Cst_t], writes=[Cst_t])
    Cb = kb.sb("Cb", [128, 260], F32); Cb_t = Tok()
    Cbb = kb.sb("Cbb", [128, 260], BF16); Cbb_t = Tok()
    kb.op("pool", lambda e: e.memset(Cb[:], 0.0), writes=[Cb_t])
    kb.op("pool", lambda e: e.memset(Cbb[:], 0.0), writes=[Cbb_t])
    for oi, c in enumerate(BWD_ORDER):
        gc = c * 128
        kb.group("pe", [lambda e, c=c, gc=gc: e.matmul(pss[1][:, 0:257], lhsT=QP[0][:, gc:gc + 128], rhs=CF[:, c, 0:257], start=True, stop=False),
                        lambda e, c=c, gc=gc: e.matmul(pss[1][:, 0:257], lhsT=STm[0][:, c, :], rhs=VA2[:, c, 0:257], start=False, stop=True)],
                 reads=[ch_t[c], cf_t[c], va2_t[c]], writes=[ps_t[1]])
        kb.group("pe", [lambda e, c=c, gc=gc: e.matmul(pss[2][:, 0:257], lhsT=QP[1][:, gc:gc + 128], rhs=Cbb[:, 0:257], start=True, stop=False),
                        lambda e, c=c, gc=gc: e.matmul(pss[2][:, 0:257], lhsT=STm[1][:, c, :], rhs=VA2[:, c, 0:257], start=False, stop=True)],
                 reads=[ch_t[c], Cbb_t, va2_t[c]], writes=[ps_t[2]])
        kb.op("pe", lambda e, c=c: e.matmul(pss[3][:, 0:257], lhsT=KP[1][:, c, :], rhs=VA2[:, c, 0:257], start=True, stop=True),
              reads=[ch_t[c], va2_t[c]], writes=[ps_t[3]])
        kb.op("dve", lambda e, c=c: e.scalar_tensor_tensor(out=Cb[:, 0:257], in0=Cb[:, 0:257], scalar=eL[:, 1, c:c + 1], in1=pss[3][:, 0:257],
                                                          op0=ALU.mult, op1=ALU.add), reads=[ps_t[3], eL_t, Cb_t], writes=[Cb_t])
        kb.op("act", lambda e: e.activation(out=Cbb[:], in_=Cb[:], func=AF.Identity), reads=[Cb_t], writes=[Cbb_t])
        for d, bk in ((0, 1), (1, 2)):
            kb.op("act", lambda e, d=d, bk=bk: e.activation(out=ss[:, 2 + d:3 + d], in_=pss[bk][:, 256:257], func=AF.Abs),
                  reads=[ps_t[bk]], writes=[ss_t])
            kb.op("dve", lambda e, d=d: e.tensor_scalar(out=ss[:, 2 + d:3 + d], in0=ss[:, 2 + d:3 + d], scalar1=1.0, scalar2=None,
                                                       op0=ALU.max), reads=[ss_t], writes=[ss_t])
        kb.op("dve", lambda e: e.reciprocal(out=ss[:, 2:4], in_=ss[:, 2:4]), reads=[ss_t], writes=[ss_t])
        kb.op("act", lambda e: e.activation(out=T1[:], in_=pss[1][:, 0:256], func=AF.Identity, scale=ss[:, 2:3]),
              reads=[ps_t[1], ss_t], writes=[T1_t])
        kb.op("dve", lambda e: e.scalar_tensor_tensor(out=T2[:], in0=pss[2][:, 0:256], scalar=ss[:, 3:4], in1=T1[:], op0=ALU.mult, op1=ALU.add),
              reads=[ps_t[2], ss_t, T1_t], writes=[T2_t])
        kb.op("act", lambda e: e.activation(out=junk[:], in_=T2[:], func=AF.Square, accum_out=ss[:, 0:1]), reads=[T2_t], writes=[junk_t, ss_t])
        kb.op("act", lambda e: e.activation(out=ss[:, 1:2], in_=ss[:, 0:1], func=AF.Sqrt, scale=1.0 / 256, bias=cs.eps[:, 0:1]),
              reads=[ss_t, cs.t], writes=[ss_t])
        kb.op("dve", lambda e: e.reciprocal(out=ss[:, 1:2], in_=ss[:, 1:2]), reads=[ss_t], writes=[ss_t])
        kb.op("dve", lambda e, c=c: e.scalar_tensor_tensor(out=onb[:], in0=T2[:], scalar=ss[:, 1:2], in1=SO[:, c, :], op0=ALU.mult, op1=ALU.mult),
              reads=[T2_t, ss_t, so_t[c]], writes=[onb_t])
        kb.group("pe", [lambda e, hh=hh: e.transpose(psT[:, hh * 128:(hh + 1) * 128], onb[:, hh * 128:(hh + 1) * 128], ident[:])
                        for hh in range(2)], reads=[onb_t, k_t], writes=[psT_t])
        oj = oi % 2
        kb.op("act", lambda e, oj=oj: e.activation(out=ost[oj][:].rearrange("p a b -> p (a b)"), in_=psT[:, 0:256], func=AF.Identity),
              reads=[psT_t], writes=[ost_t[oj]])
        kb.dma("sp", out_d[:, 2:4, gc:gc + 128], ost[oj][:], reads=[ost_t[oj]], is_output=True)
    return kb.end_phase()


def host_rope():
    t = np.arange(SEQ)
    row = (t // 64).astype(np.float32)
    col = (t % 64).astype(np.float32)
    inv = np.power(np.float32(10000.0), -np.arange(16, dtype=np.float32) / np.float32(16)).astype(np.float32)
    ar = (row[:, None] * inv).astype(np.float32)
    ac = (col[:, None] * inv).astype(np.float32)
    cr, sr, cc, sc = np.cos(ar), np.sin(ar), np.cos(ac), np.sin(ac)
    C = np.concatenate([cr, cr, cc, cc], axis=1)
    S = np.concatenate([-sr, sr, -sc, sc], axis=1)
    tab = np.stack([C, S], axis=1).astype(np.float32)
    return np.ascontiguousarray(tab.reshape(32, 128, 2, 64).transpose(1, 0, 2, 3))


def host_masks_odd():
    j = np.arange(128)[:, None]
    i = np.arange(128)[None, :]
    mf = (j <= i).astype(np.float32)
    mb = (j >= i).astype(np.float32)
    return np.ascontiguousarray(np.stack([mf, mb, (1 - mf) * -30000.0, (1 - mb) * -30000.0], axis=1).astype(np.float32))


def bc(v, n=128):
    return np.ascontiguousarray(np.broadcast_to(np.asarray(v, np.float32).reshape(1, -1), (n, v.size)))


def host_Aodd_inputs(g, layer, w_in, qn_g, kn_g, lam_p, subln_g, ml_conv_w, ml_gate_b, ml_norm_g):
    colsD = np.concatenate([np.arange(256 * g, 256 * g + 256), np.arange(1024 + 256 * g, 1024 + 256 * g + 256),
                            np.arange(2048 + 256 * g, 2048 + 256 * g + 256)])
    gidx = [0 * 8 + 0 * 4 + g, 0 * 8 + 1 * 4 + g, 1 * 8 + 0 * 4 + g, 1 * 8 + 1 * 4 + g]
    colsM = np.concatenate([np.arange(3072 + 128 * g, 3072 + 128 * g + 128), np.arange(3584 + 128 * g, 3584 + 128 * g + 128),
                            np.arange(4096 + 256 * g, 4096 + 256 * g + 256), np.arange(5120 + 256 * g, 5120 + 256 * g + 256),
                            6144 + np.array(gidx)])
    qkg = bc(np.concatenate([np.tile(qn_g, 4), np.tile(kn_g, 4)]))
    mlcw = np.stack([ml_conv_w[:, 128 * g:128 * g + 128], ml_conv_w[:, 512 + 128 * g:512 + 128 * g + 128]], axis=0)
    mlcw = np.ascontiguousarray(mlcw.transpose(2, 0, 1))
    return {"wD": wfm(w_in[:, colsD]), "wM": wfm(w_in[:, colsM]), "qkg": qkg, "rope": host_rope(),
            "lamp": np.ascontiguousarray(np.broadcast_to(lam_p[None], (128, 4, 64))), "subg": bc(subln_g),
            "subgc": np.ascontiguousarray(subln_g.reshape(128, 1).astype(np.float32)),
            "mlcw": mlcw, "mlgb": bc(ml_gate_b[gidx]), "mlng": bc(ml_norm_g), "masks": host_masks_odd(),
            "ident": np.eye(128, dtype=np.float32).astype(ml_dtypes.bfloat16)}


def emit_Mfull(kb, condT2, adaw, adab, mods_out):
    sc = kb.sb("mf_sc", [128, NCH, 2], F32); sc_t = Tok()
    bsb = kb.sb("mf_b", [128, DEPTH, 96], F32); b_t = Tok()
    res = kb.sb("mf_res", [128, DEPTH, 96, 2], F32); res_t = Tok()
    wb = [kb.sb(f"mf_w{i}", [128, NCH, 768], F32) for i in range(2)]
    w_t = [Tok() for _ in range(2)]
    pst = [kb.ps(f"mf_ps{i}", [128, 512], F32) for i in range(2)]
    ps_t = [Tok() for _ in range(2)]
    kb.dma("sp", sc[:], condT2, writes=[sc_t])
    kb.dma("sp", bsb[:], adab, writes=[b_t])
    kb.op("act", lambda e: e.activation(out=sc[:], in_=sc[:], func=AF.Silu), reads=[sc_t], writes=[sc_t])
    it = 0
    for l in range(DEPTH):
        for blk in range(16):
            i = it % 2
            it += 1
            kb.dma("sp", wb[i][:], adaw[l, blk], writes=[w_t[i]])
            for mm in range(6):
                m = blk * 6 + mm
                j = m % 2
                kb.group("pe", [lambda e, k=k, mm=mm, i=i, j=j: e.matmul(pst[j][:, 0:2], lhsT=wb[i][:, k, mm * 128:(mm + 1) * 128],
                                                                          rhs=sc[:, k, 0:2], start=(k == 0), stop=(k == NCH - 1))
                                 for k in range(NCH)],
                         reads=[w_t[i], sc_t], writes=[ps_t[j]])
                kb.op("dve", lambda e, l=l, m=m, j=j: e.tensor_scalar(out=res[:, l, m, :], in0=pst[j][:, 0:2],
                                                                       scalar1=bsb[:, l, m:m + 1], scalar2=None, op0=ALU.add),
                      reads=[ps_t[j], b_t], writes=[res_t])
    for l in range(DEPTH):
        kb.dma("sp", mods_out[l], res[:, l, :, :].rearrange("p (s c) r -> p s c r", s=6), reads=[res_t])


def build_fused():
    kb = KB()
    nc = kb.nc

    def din(name, shape, dt=F32):
        return nc.dram_tensor(name, list(shape), dt, kind="ExternalInput").ap()

    def dint(name, shape, dt=F32):
        return nc.dram_tensor(name, list(shape), dt, kind="Internal").ap()

    xT = din("xT", [4, 128, NCH, NTOK])
    condT2 = din("condT2", [128, NCH, 2])
    adaw = din("adaw", [DEPTH, 16, 128, NCH, 768])
    adab = din("adab", [128, DEPTH, 96])
    ng1 = din("ng1", [DEPTH, 128, NCH])
    ng2 = din("ng2", [DEPTH, 128, NCH])
    zcol = din("zcol", [128, NCH, 1], BF16)
    wout = din("wout", [DEPTH, NCH, 128, NCH, 128])
    wup = din("wup", [DEPTH, FCH, 128, NCH, 256])
    wdn = din("wdn", [DEPTH, NQ, NCH, 128, QF, 128])
    convw = din("convw", [DEPTH, 128, FCH, 4])
    ident = din("ident", [128, 128], BF16)
    e_wA = din("e_wA", [2, 4, 128, NCH, 896]); e_wC = din("e_wC", [2, 4, 128, NCH, 768])
    e_w2 = din("e_w2", [2, 4, 128, 2, 128]); e_gb = din("e_gb", [2, 4, 128, 256]); e_gn = din("e_gn", [2, 128, 256])
    e_scw = din("e_scw", [2, 4, 128, 2, 3]); e_msk = din("e_msk", [128, 4, 128])
    o_wD = din("o_wD", [2, 4, 128, NCH, 768]); o_wM = din("o_wM", [2, 4, 128, NCH, 772])
    o_qkg = din("o_qkg", [2, 128, 512]); o_rope = din("o_rope", [128, 32, 2, 64]); o_lamp = din("o_lamp", [2, 128, 4, 64])
    o_subg = din("o_subg", [2, 128, 128]); o_subgc = din("o_subgc", [2, 128, 1]); o_mlcw = din("o_mlcw", [2, 4, 128, 2, 3]); o_mlgb = din("o_mlgb", [2, 4, 128, 4])
    o_mlng = din("o_mlng", [2, 128, 256]); o_msk = din("o_msk", [128, 4, 128])
    out = nc.dram_tensor("xoutT", [1, 128, NCH, NTOK], F32, kind="ExternalOutput").ap()
    MODS = dint("s_mods", [DEPTH, 128, 6, NCH, 2])
    X = dint("s_x", [4, 128, NCH, NTOK])
    XM = dint("s_xm", [4, 128, NCH, NTOK])
    H1L = dint("s_h1l", [4, 128, NCH, NTOK], BF16)
    H1G = dint("s_h1g", [128, NCH, TB], BF16)
    MIXG = dint("s_mixg", [4, 128, 4, TB], BF16)
    MIXL = dint("s_mixl", [128, NCH, NTOK], BF16)
    H2P = dint("s_h2p", [6, 128, NCH, NTOK], BF16)
    H2L = [H2P[j + 1] for j in range(4)]
    H2S = dint("s_h2s", [3, 128, NCH, NTOK], BF16)
    XMS = dint("s_xms", [1, 128, NCH, NTOK])
    zslot = din("zslot", [128, NCH, NTOK], BF16)
    H2H = dint("s_h2h", [128, NCH, HW_], BF16)

    def copies(pairs):
        for (dst, src) in pairs:
            kb.dma("sp", dst, src, slow=(dst.shape[-1] == 1))
        kb.barrier()

    kb.dma("sp", H2P[0], zslot)
    kb.dma("sp", H2P[5], zslot)
    with kb.phase({}):
        emit_Mfull(kb, condT2, adaw, adab, MODS)
    for j in range(4):
        with kb.phase({"xT": xT[j], "mods": MODS[0], "normg": ng1[0], "h1T": H1L[j]}):
            build_P0(kb=kb)
    for layer in range(DEPTH):
        i2 = layer // 2
        last = layer == DEPTH - 1
        copies([(H1G[:, :, 64 * j:64 * j + 64], H1L[j][:, :, 0:64]) for j in range(4)] +
               [(H1G[:, :, CTX + 1024 * j:CTX + 1024 * j + 1024], H1L[j][:, :, 64:NTOK]) for j in range(4)])
        for g in range(4):
            if layer % 2 == 0:
                io = {"h1T": H1G, "wA": e_wA[i2, g], "wC": e_wC[i2, g], "w2": e_w2[i2, g], "gbias": e_gb[i2, g], "gnorm": e_gn[i2],
                      "scw": e_scw[i2, g], "masks": e_msk, "ident": ident, "mixT": MIXG[g]}
                with kb.phase(io):
                    build_Aeven(kb=kb)
            else:
                io = {"h1T": H1G, "wD": o_wD[i2, g], "wM": o_wM[i2, g], "qkg": o_qkg[i2], "rope": o_rope, "lamp": o_lamp[i2],
                      "subg": o_subg[i2], "subgc": o_subgc[i2], "mlcw": o_mlcw[i2, g], "mlgb": o_mlgb[i2, g], "mlng": o_mlng[i2], "masks": o_msk,
                      "ident": ident, "mixT": MIXG[g]}
                with kb.phase(io):
                    build_Aodd(layer, kb=kb)
        for j in range(4):
            copies([(MIXL[:, 4 * g:4 * g + 4, 0:64], MIXG[g][:, :, 64 * j:64 * j + 64]) for g in range(4)] +
                   [(MIXL[:, 4 * g:4 * g + 4, 64:NTOK], MIXG[g][:, :, CTX + 1024 * j:CTX + 1024 * j + 1024]) for g in range(4)])
            xin = xT[j] if layer == 0 else X[j]
            with kb.phase({"xT": xin, "mixT": MIXL, "wout": wout[layer], "mods": MODS[layer], "normg": ng2[layer],
                           "xmidT": XM[j], "h2T": H2L[j]}):
                build_B1(kb=kb)
        if last:
            jv = nc.sync.snap(nc.sync.partition_id() % 4, min_val=0, max_val=3)
            kb.dma("sp", XMS, XM[bass.ds(jv, 1)])
            kb.dma("sp", H2S, H2P[bass.ds(jv, 3)])
            kb.barrier()
            copies([(H2H[:, :, 1:65], H2S[1][:, :, 0:64]), (H2H[:, :, 67:1091], H2S[1][:, :, 64:NTOK]),
                    (H2H[:, :, 0:1], H2S[0][:, :, 63:64]), (H2H[:, :, 66:67], H2S[0][:, :, NTOK - 1:NTOK]),
                    (H2H[:, :, 65:66], H2S[2][:, :, 0:1]), (H2H[:, :, 1091:1092], H2S[2][:, :, 64:65])])
            io = {"xmidT": XMS[0], "h2hT": H2H, "wup": wup[layer], "wdn": wdn[layer], "convw": convw[layer], "mods": MODS[layer],
                  "xoT": X[0]}
            with kb.phase(io):
                build_B2(True, kb=kb)
            continue
        for j in range(4):
            cp = [(H2H[:, :, 1:65], H2L[j][:, :, 0:64]), (H2H[:, :, 67:1091], H2L[j][:, :, 64:NTOK])]
            if j > 0:
                cp += [(H2H[:, :, 0:1], H2L[j - 1][:, :, 63:64]), (H2H[:, :, 66:67], H2L[j - 1][:, :, NTOK - 1:NTOK])]
            else:
                cp += [(H2H[:, :, 0:1], zcol), (H2H[:, :, 66:67], zcol)]
            if j < 3:
                cp += [(H2H[:, :, 65:66], H2L[j + 1][:, :, 0:1]), (H2H[:, :, 1091:1092], H2L[j + 1][:, :, 64:65])]
            else:
                cp += [(H2H[:, :, 65:66], zcol), (H2H[:, :, 1091:1092], zcol)]
            copies(cp)
            io = {"xmidT": XM[j], "h2hT": H2H, "wup": wup[layer], "wdn": wdn[layer], "convw": convw[layer], "mods": MODS[layer],
                  "xoT": X[j]}
            if not last:
                io.update({"normg": ng1[layer + 1], "modsn": MODS[layer + 1], "h1T": H1L[j]})
            with kb.phase(io):
                build_B2(last, kb=kb)
    kb.dma("sp", out[0], X[0], is_output=True)
    return kb.finish()


def kernel_fused(**inputs):
    I = {k: np.asarray(v) for k, v in inputs.items()}
    x, ctx = I["x"], I["ctx"]
    bf = ml_dtypes.bfloat16
    shared = {}
    shared["adaw"] = np.ascontiguousarray(I["ada_w"].reshape(DEPTH, NCH, 128, 16, 768).transpose(0, 3, 2, 1, 4))
    shared["adab"] = np.ascontiguousarray(I["ada_b"].reshape(DEPTH, 96, 128).transpose(2, 0, 1))
    shared["ng1"] = np.stack([vec_fm(I["norm1_g"][l]) for l in range(DEPTH)])
    shared["ng2"] = np.stack([vec_fm(I["norm2_g"][l]) for l in range(DEPTH)])
    shared["zcol"] = np.zeros((128, NCH, 1), bf)
    shared["zslot"] = np.zeros((128, NCH, NTOK), bf)
    shared["wout"] = np.stack([host_wout((I["ev_w_out"] if l % 2 == 0 else I["od_w_out"])[l // 2]) for l in range(DEPTH)])
    shared["wup"] = np.stack([host_wup(I["ffn_w_up"][l]) for l in range(DEPTH)])
    shared["wdn"] = np.stack([host_wdn(I["ffn_w_down"][l]) for l in range(DEPTH)])
    shared["convw"] = np.stack([host_convw(I["ffn_conv_w"][l], I["ffn_conv_b"][l]) for l in range(DEPTH)])
    shared["ident"] = np.eye(128, dtype=np.float32).astype(bf)
    ev = [[host_Aeven_inputs(g, I["ev_w_in"][i], I["gla_gate_w2"][i], I["gla_gate_b"][i], I["gla_norm_g"][i], I["sc_conv_w"][i])
           for g in range(4)] for i in range(2)]
    for key, src in (("e_wA", "wA"), ("e_wC", "wC"), ("e_w2", "w2"), ("e_gb", "gbias"), ("e_scw", "scw")):
        shared[key] = np.stack([np.stack([ev[i][g][src] for g in range(4)]) for i in range(2)])
    shared["e_gn"] = np.stack([ev[i][0]["gnorm"] for i in range(2)])
    shared["e_msk"] = host_masks()
    od = [[host_Aodd_inputs(g, 2 * i + 1, I["od_w_in"][i], I["da_qnorm_g"][i], I["da_knorm_g"][i], I["da_lambda"][i],
                            I["da_subln_g"][i], I["ml_conv_w"][i], I["ml_gate_b"][i], I["ml_norm_g"][i])
           for g in range(4)] for i in range(2)]
    for key, src in (("o_wD", "wD"), ("o_wM", "wM"), ("o_mlcw", "mlcw"), ("o_mlgb", "mlgb")):
        shared[key] = np.stack([np.stack([od[i][g][src] for g in range(4)]) for i in range(2)])
    for key, src in (("o_qkg", "qkg"), ("o_lamp", "lamp"), ("o_subg", "subg"), ("o_subgc", "subgc"), ("o_mlng", "mlng")):
        shared[key] = np.stack([od[i][0][src] for i in range(2)])
    shared["o_rope"] = od[0][0]["rope"]
    shared["o_msk"] = host_masks_odd()
    in_maps = []
    for core in range(8):
        b, j = divmod(core, 4)
        d = dict(shared)
        d["xT"] = np.stack([fm(core_tokens(x[b], ctx[b], jj)) for jj in range(4)])
        cond = np.stack([I["c_ctx"], I["c"][b]], axis=0)
        d["condT2"] = np.ascontiguousarray(cond.reshape(2, NCH, 128).transpose(2, 1, 0))
        in_maps.append(d)
    res = run(get_nc("fused", build_fused), in_maps)
    out = np.zeros((2, SEQ, D), np.float32)
    for core in range(8):
        b, j = divmod(core, 4)
        out[b, 1024 * j:1024 * j + 1024, :] = unfm(res[core]["xoutT"][0])[64:, :]
    return out
def kernel_unfused(**inputs):
    I = {k: np.asarray(v) for k, v in inputs.items()}
    x, ctx = I["x"], I["ctx"]
    mods, _ = host_M(I["c"], I["c_ctx"], I["ada_w"], I["ada_b"])
    cores = [divmod(c, 4) for c in range(8)]
    x_cores = [fm(core_tokens(x[b], ctx[b], j)) for (b, j) in cores]
    res = run(get_nc("P0", build_P0), [{"xT": x_cores[c], "mods": mods[0][cores[c][0]], "normg": vec_fm(I["norm1_g"][0])}
                                       for c in range(8)])
    h1 = [res[c]["h1T"] for c in range(8)]
    for layer in range(DEPTH):
        i2 = layer // 2
        last = layer == DEPTH - 1
        h1_b = [assemble_h1(h1, b) for b in range(2)]
        in_maps = []
        for c, (b, g) in enumerate(cores):
            if layer % 2 == 0:
                d = host_Aeven_inputs(g, I["ev_w_in"][i2], I["gla_gate_w2"][i2], I["gla_gate_b"][i2], I["gla_norm_g"][i2],
                                      I["sc_conv_w"][i2])
            else:
                d = host_Aodd_inputs(g, layer, I["od_w_in"][i2], I["da_qnorm_g"][i2], I["da_knorm_g"][i2], I["da_lambda"][i2],
                                     I["da_subln_g"][i2], I["ml_conv_w"][i2], I["ml_gate_b"][i2], I["ml_norm_g"][i2])
            d["h1T"] = h1_b[b]
            in_maps.append(d)
        nc = get_nc("Ae", build_Aeven) if layer % 2 == 0 else get_nc("Ao", build_Aodd, layer)
        res = run(nc, in_maps)
        mixes = [res[c]["mixT"] for c in range(8)]
        wout = host_wout((I["ev_w_out"] if layer % 2 == 0 else I["od_w_out"])[i2])
        in_maps = [{"xT": x_cores[c], "mixT": scatter_mix(mixes, b, j), "wout": wout, "mods": mods[layer][b],
                    "normg": vec_fm(I["norm2_g"][layer])} for c, (b, j) in enumerate(cores)]
        res = run(get_nc("B1", build_B1), in_maps)
        halo = host_halo([res[c]["h2T"] for c in range(8)])
        xm = [res[c]["xmidT"] for c in range(8)]
        wup = host_wup(I["ffn_w_up"][layer])
        wdn = host_wdn(I["ffn_w_down"][layer])
        cw = host_convw(I["ffn_conv_w"][layer], I["ffn_conv_b"][layer])
        in_maps = []
        for c, (b, j) in enumerate(cores):
            d = {"xmidT": xm[c], "h2hT": halo[c], "wup": wup, "wdn": wdn, "convw": cw, "mods": mods[layer][b]}
            if not last:
                d["normg"] = vec_fm(I["norm1_g"][layer + 1])
                d["modsn"] = mods[layer + 1][b]
            in_maps.append(d)
        res = run(get_nc("B2", build_B2, last), in_maps)
        x_cores = [res[c]["xoT"] for c in range(8)]
        if not last:
            h1 = [res[c]["h1T"] for c in range(8)]
    out = np.zeros((2, SEQ, D), np.float32)
    for c, (b, j) in enumerate(cores):
        out[b, 1024 * j:1024 * j + 1024, :] = unfm(x_cores[c])[64:, :]
    return out


FUSED = True


def kernel(**inputs):
    return kernel_fused(**inputs) if FUSED else kernel_unfused(**inputs)
```

```python
import numpy as np
import ml_dtypes
import concourse.bass as bass
import concourse.mybir as mybir
from concourse.bass_utils import run_bass_kernel_spmd

F32 = mybir.dt.float32
BF16 = mybir.dt.bfloat16
AF = mybir.ActivationFunctionType
ALU = mybir.AluOpType

D = 2048
NCH = 16
DFF = 5632
FCH = 44
DEPTH = 4
SEQ = 4096
CTX = 256
NTOK = 1088
HW_ = 1092
EPS = 1e-6
NTILE = 34
TB = 4352


class Tok:
    __slots__ = ("w", "r", "name")

    def __init__(self, name=""):
        self.w = None
        self.r = {}
        self.name = name


class KB:
    def __init__(self):
        self.nc = bass.Bass("TRN2", target_bir_lowering=False)
        nc = self.nc
        self.engs = {"pe": nc.tensor, "dve": nc.vector, "act": nc.scalar, "pool": nc.gpsimd, "sp": nc.sync}
        self.sem = {}
        self.cnt = {}
        self.seen = {e: {} for e in self.engs}
        for e in ("pe", "dve", "act", "pool"):
            self.sem[e] = nc.alloc_semaphore(f"s_{e}")
            self.cnt[e] = 0
        self.semobj = {("eng", e): self.sem[e] for e in self.sem}
        self.dpool = {}
        for q, n in (("sp", 12), ("pool", 12), ("act", 4)):
            lst = []
            for i in range(n):
                s = nc.alloc_semaphore(f"d_{q}{i}")
                key = ("dma", q, i)
                self.semobj[key] = s
                lst.append([key, 0])
            self.dpool[q] = [lst, 0]
        self.out_events = []
        self._n = 0
        self.io = None
        self.scope = None
        self.uid = 0
        self._consts = None

    def sb(self, name, shape, dt):
        if self.scope is not None:
            return self.scope.enter_context(self.nc.sbuf_tensor(f"{name}_u{self.uid}", list(shape), dt)).ap()
        return self.nc.alloc_sbuf_tensor(name, list(shape), dt).ap()

    def ps(self, name, shape, dt):
        if self.scope is not None:
            return self.scope.enter_context(self.nc.psum_tensor(f"{name}_u{self.uid}", list(shape), dt)).ap()
        return self.nc.alloc_psum_tensor(name, list(shape), dt).ap()

    def dram(self, name, shape, dt, kind):
        if self.io is not None:
            ap = self.io[name]
            assert list(ap.shape) == list(shape), (name, ap.shape, shape)
            return ap
        return self.nc.dram_tensor(name, list(shape), dt, kind=kind).ap()

    def get_consts(self):
        if self._consts is None:
            sc, self.scope = self.scope, None
            self._consts = Consts(self)
            self.scope = sc
        return self._consts

    def phase(self, io):
        import contextlib
        kb = self

        @contextlib.contextmanager
        def cm():
            kb.uid += 1
            kb.io = io
            with contextlib.ExitStack() as st:
                kb.scope = st
                yield
                kb.barrier()
            kb.scope = None
            kb.io = None
        return cm()

    def end_phase(self):
        if self.io is not None:
            return None
        return self.finish()

    def _wait(self, eng, deps):
        seen = self.seen[eng]
        for key, val in deps.items():
            if eng == "pe" and key == ("eng", "pe"):
                continue
            if seen.get(key, 0) < val:
                self.engs[eng].wait_ge(self.semobj[key], val)
                seen[key] = val

    @staticmethod
    def _add(d, ev):
        if ev is None:
            return
        k, v = ev
        if d.get(k, 0) < v:
            d[k] = v

    def _deps(self, reads, writes):
        deps = {}
        for b in reads:
            self._add(deps, b.w)
        for b in writes:
            self._add(deps, b.w)
            for k, v in b.r.items():
                if deps.get(k, 0) < v:
                    deps[k] = v
        return deps

    def _commit(self, ev, reads, writes):
        for b in reads:
            self._add(b.r, ev)
        for b in writes:
            b.w = ev
            b.r = {}

    def op(self, eng, fn, reads=(), writes=()):
        self._wait(eng, self._deps(reads, writes))
        ins = fn(self.engs[eng])
        self.cnt[eng] += 1
        ins.then_inc(self.sem[eng], 1)
        ev = (("eng", eng), self.cnt[eng])
        self._commit(ev, reads, writes)
        return ev

    def group(self, eng, fns, reads=(), writes=()):
        self._wait(eng, self._deps(reads, writes))
        ins = None
        for fn in fns:
            ins = fn(self.engs[eng])
        self.cnt[eng] += 1
        ins.then_inc(self.sem[eng], 1)
        ev = (("eng", eng), self.cnt[eng])
        self._commit(ev, reads, writes)
        return ev

    def dma(self, q, out, in_, reads=(), writes=(), is_output=False, slow=False):
        lst, idx = self.dpool[q]
        ent = lst[idx]
        self.dpool[q][1] = (idx + 1) % len(lst)
        deps = self._deps(reads, writes)
        if ent[1] > 0:
            self._add(deps, (ent[0], ent[1]))
        self._wait(q, deps)
        ent[1] += 16
        kw = {"allow_slow_non_contiguous": True} if slow else {}
        self.engs[q].dma_start(out=out, in_=in_, **kw).then_inc(self.semobj[ent[0]], 16)
        ev = (ent[0], ent[1])
        self._commit(ev, reads, writes)
        if is_output:
            self.out_events.append(ev)
        return ev

    def finish(self):
        deps = {}
        for ev in self.out_events:
            self._add(deps, ev)
        self._wait("sp", deps)
        fin = {("eng", e): self.cnt[e] for e in self.cnt if self.cnt[e] > 0}
        self._wait("sp", fin)
        return self.nc


class Consts:
    def __init__(self, kb):
        self.ones = kb.sb("c_ones", [128, 128], F32)
        self.eps = kb.sb("c_eps", [128, 1], F32)
        self.t = Tok("consts")
        kb.op("pool", lambda e: e.memset(self.ones[:], 1.0), writes=[self.t])
        kb.op("pool", lambda e: e.memset(self.eps[:], EPS), writes=[self.t])


TOKBLK = [(0, 64, 0), (64, 512, 1), (576, 512, 1)]


def emit_modnorm(kb, cs, x, xt, acoef, bcoef, mt, h, ht, colmap, psA, psA_t, scr):
    sq, sq_t = scr["sq"], scr["sq_t"]
    rstd, rstd_t = scr["rstd"], scr["rstd_t"]
    tmp, tmp_t = scr["tmp"], scr["tmp_t"]
    for (t0, n, seg) in TOKBLK:
        fns = []
        for c in range(NCH):
            i = c % 2
            kb.op("act", lambda e, c=c, i=i: e.activation(out=sq[i][:, 0:n], in_=x[:, c, t0:t0 + n], func=AF.Square),
                  reads=[xt], writes=[sq_t[i]])
            kb.op("pe", lambda e, c=c, i=i: e.matmul(psA[:, 0:n], lhsT=cs.ones[:], rhs=sq[i][:, 0:n],
                                                      start=(c == 0), stop=(c == NCH - 1)),
                  reads=[sq_t[i], cs.t], writes=[psA_t])
        kb.op("act", lambda e: e.activation(out=rstd[:, 0:n], in_=psA[:, 0:n], func=AF.Sqrt,
                                             scale=1.0 / D, bias=cs.eps[:, 0:1]),
              reads=[psA_t, cs.t], writes=[rstd_t])
        kb.op("dve", lambda e: e.reciprocal(out=rstd[:, 0:n], in_=rstd[:, 0:n]), reads=[rstd_t], writes=[rstd_t])
        c0 = colmap(t0)
        for c in range(NCH):
            i = c % 2
            kb.op("dve", lambda e, c=c, i=i: e.tensor_tensor(out=tmp[i][:, 0:n], in0=x[:, c, t0:t0 + n],
                                                             in1=rstd[:, 0:n], op=ALU.mult),
                  reads=[xt, rstd_t], writes=[tmp_t[i]])
            kb.op("act", lambda e, c=c, i=i: e.activation(out=h[:, c, c0:c0 + n], in_=tmp[i][:, 0:n], func=AF.Identity,
                                                           scale=acoef[:, c, seg:seg + 1], bias=bcoef[:, c, seg:seg + 1]),
                  reads=[tmp_t[i], mt], writes=[ht])


def alloc_norm_scratch(kb):
    scr = {}
    scr["sq"] = [kb.sb(f"n_sq{i}", [128, 512], F32) for i in range(2)]
    scr["sq_t"] = [Tok() for _ in range(2)]
    scr["rstd"] = kb.sb("n_rstd", [128, 512], F32)
    scr["rstd_t"] = Tok()
    scr["tmp"] = [kb.sb(f"n_tmp{i}", [128, 512], F32) for i in range(2)]
    scr["tmp_t"] = [Tok() for _ in range(2)]
    return scr


def emit_coefs(kb, mods, mods_t, normg, normg_t, si_sh, si_sc, acoef, bcoef, ct):
    for seg in range(2):
        kb.op("dve", lambda e, seg=seg: e.scalar_tensor_tensor(out=acoef[:, :, seg], in0=mods[:, si_sc, :, seg], scalar=1.0,
                                                               in1=normg[:, :], op0=ALU.add, op1=ALU.mult),
              reads=[mods_t, normg_t], writes=[ct])
        kb.op("dve", lambda e, seg=seg: e.tensor_copy(out=bcoef[:, :, seg], in_=mods[:, si_sh, :, seg]),
              reads=[mods_t], writes=[ct])


def build_M(kb=None):
    kb = kb or KB()
    condT = kb.dram("condT", [128, NCH, 3], F32, "ExternalInput")
    adaw = kb.dram("adaw", [DEPTH, 128, NCH, 1536], F32, "ExternalInput")
    adab = kb.dram("adab", [128, DEPTH, 12], F32, "ExternalInput")
    out = kb.dram("modT", [128, DEPTH, 12, 3], F32, "ExternalOutput")
    sc = kb.sb("sc", [128, NCH, 4], F32); sc_t = Tok()
    bsb = kb.sb("bsb", [128, DEPTH, 12], F32); b_t = Tok()
    res = kb.sb("res", [128, DEPTH, 12, 3], F32); res_t = Tok()
    wb = [kb.sb(f"w{i}", [128, NCH, 768], F32) for i in range(2)]
    w_t = [Tok() for _ in range(2)]
    pst = [kb.ps(f"ps{i}", [128, 512], F32) for i in range(2)]
    ps_t = [Tok() for _ in range(2)]
    kb.dma("sp", sc[:, :, 0:3], condT, writes=[sc_t])
    kb.dma("sp", bsb[:], adab, writes=[b_t])
    kb.op("act", lambda e: e.activation(out=sc[:, :, 0:3], in_=sc[:, :, 0:3], func=AF.Silu), reads=[sc_t], writes=[sc_t])
    it = 0
    for l in range(DEPTH):
        for hf in range(2):
            i = it % 2
            it += 1
            kb.dma("sp", wb[i][:], adaw[l, :, :, hf * 768:(hf + 1) * 768], writes=[w_t[i]])
            for mm in range(6):
                m = hf * 6 + mm
                j = m % 2
                kb.group("pe", [lambda e, k=k, mm=mm, i=i, j=j: e.matmul(pst[j][:, 0:3], lhsT=wb[i][:, k, mm * 128:(mm + 1) * 128],
                                                                          rhs=sc[:, k, 0:3], start=(k == 0), stop=(k == NCH - 1))
                                 for k in range(NCH)],
                         reads=[w_t[i], sc_t], writes=[ps_t[j]])
                kb.op("dve", lambda e, l=l, m=m, j=j: e.tensor_scalar(out=res[:, l, m, :], in0=pst[j][:, 0:3],
                                                                       scalar1=bsb[:, l, m:m + 1], scalar2=None, op0=ALU.add),
                      reads=[ps_t[j], b_t], writes=[res_t])
    kb.dma("sp", out, res[:], reads=[res_t], is_output=True)
    return kb.end_phase()


def build_P0(kb=None):
    kb = kb or KB()
    xin = kb.dram("xT", [128, NCH, NTOK], F32, "ExternalInput")
    mods_d = kb.dram("mods", [128, 6, NCH, 2], F32, "ExternalInput")
    ng_d = kb.dram("normg", [128, NCH], F32, "ExternalInput")
    h_d = kb.dram("h1T", [128, NCH, NTOK], BF16, "ExternalOutput")
    cs = kb.get_consts()
    x = kb.sb("x", [128, NCH, NTOK], F32); xt = Tok()
    h = kb.sb("h", [128, NCH, NTOK], BF16); ht = Tok()
    mods = kb.sb("mods_s", [128, 6, NCH, 2], F32); mt = Tok()
    ng = kb.sb("ng", [128, NCH], F32); ngt = Tok()
    ac = kb.sb("ac", [128, NCH, 2], F32); bc = kb.sb("bc", [128, NCH, 2], F32); ct = Tok()
    psA = kb.ps("psA", [128, 512], F32); psA_t = Tok()
    scr = alloc_norm_scratch(kb)
    for c in range(0, NCH, 4):
        kb.dma("sp", x[:, c:c + 4, :], xin[:, c:c + 4, :], writes=[xt])
    kb.dma("sp", mods[:], mods_d, writes=[mt])
    kb.dma("sp", ng[:], ng_d, writes=[ngt])
    emit_coefs(kb, mods, mt, ng, ngt, 0, 1, ac, bc, ct)
    emit_modnorm(kb, cs, x, xt, ac, bc, ct, h, ht, lambda t: t, psA, psA_t, scr)
    for c in range(0, NCH, 4):
        kb.dma("sp", h_d[:, c:c + 4, :], h[:, c:c + 4, :], reads=[ht], is_output=True)
    return kb.end_phase()


def hcol(t):
    return 1 + t if t < 64 else 67 + (t - 64)


def build_B1(kb=None):
    kb = kb or KB()
    xin = kb.dram("xT", [128, NCH, NTOK], F32, "ExternalInput")
    mix_d = kb.dram("mixT", [128, NCH, NTOK], BF16, "ExternalInput")
    wout_d = kb.dram("wout", [NCH, 128, NCH, 128], F32, "ExternalInput")
    mods_d = kb.dram("mods", [128, 6, NCH, 2], F32, "ExternalInput")
    ng_d = kb.dram("normg", [128, NCH], F32, "ExternalInput")
    xo_d = kb.dram("xmidT", [128, NCH, NTOK], F32, "ExternalOutput")
    h_d = kb.dram("h2T", [128, NCH, NTOK], BF16, "ExternalOutput")
    cs = kb.get_consts()
    x = kb.sb("x", [128, NCH, NTOK], F32); xt = [Tok() for _ in range(NCH)]
    mix = kb.sb("mix", [128, NCH, NTOK], BF16); mixt = Tok()
    h = kb.sb("h", [128, NCH, NTOK], BF16); ht = Tok()
    mods = kb.sb("mods_s", [128, 6, NCH, 2], F32); mt = Tok()
    ng = kb.sb("ng", [128, NCH], F32); ngt = Tok()
    ac = kb.sb("ac", [128, NCH, 2], F32); bc = kb.sb("bc", [128, NCH, 2], F32); ct = Tok()
    wb = [kb.sb(f"w{i}", [128, NCH, 128], BF16) for i in range(3)]
    w_t = [Tok() for _ in range(3)]
    pss = [kb.ps(f"ps{i}", [128, 512], F32) for i in range(6)]
    ps_t = [Tok() for _ in range(6)]
    psA = kb.ps("psA", [128, 512], F32); psA_t = Tok()
    scr = alloc_norm_scratch(kb)
    for c in range(0, NCH, 4):
        kb.dma("sp", x[:, c:c + 4, :], xin[:, c:c + 4, :], writes=xt[c:c + 4])
        kb.dma("sp", mix[:, c:c + 4, :], mix_d[:, c:c + 4, :], writes=[mixt])
    kb.dma("sp", mods[:], mods_d, writes=[mt])
    kb.dma("sp", ng[:], ng_d, writes=[ngt])
    emit_coefs(kb, mods, mt, ng, ngt, 3, 4, ac, bc, ct)
    pi = 0
    for m in range(NCH):
        wi = m % 3
        kb.dma("pool", wb[wi][:], wout_d[m], writes=[w_t[wi]])
        for (t0, n, seg) in TOKBLK:
            j = pi % 6
            pi += 1
            kb.group("pe", [lambda e, k=k, wi=wi, j=j, t0=t0, n=n: e.matmul(pss[j][:, 0:n], lhsT=wb[wi][:, k, :], rhs=mix[:, k, t0:t0 + n],
                                                                            start=(k == 0), stop=(k == NCH - 1))
                             for k in range(NCH)],
                     reads=[w_t[wi], mixt], writes=[ps_t[j]])
            kb.op("dve", lambda e, m=m, j=j, t0=t0, n=n, seg=seg: e.scalar_tensor_tensor(
                out=x[:, m, t0:t0 + n], in0=pss[j][:, 0:n], scalar=mods[:, 2, m, seg:seg + 1],
                in1=x[:, m, t0:t0 + n], op0=ALU.mult, op1=ALU.add),
                reads=[ps_t[j], mt, xt[m]], writes=[xt[m]])
    deps_w = {}
    for m in range(NCH):
        KB._add(deps_w, xt[m].w)
    xm = Tok()
    for e in ("act", "dve", "pe", "sp"):
        kb._wait(e, deps_w)
    emit_modnorm(kb, cs, x, xm, ac, bc, ct, h, ht, lambda t: t, psA, psA_t, scr)
    for c in range(0, NCH, 4):
        kb.dma("sp", xo_d[:, c:c + 4, :], x[:, c:c + 4, :], reads=[xm], is_output=True)
        kb.dma("sp", h_d[:, c:c + 4, :], h[:, c:c + 4, :], reads=[ht], is_output=True)
    return kb.end_phase()


NQ = 4
QF = FCH // NQ
GBLK = [(0, 364), (364, 364), (728, 364)]
DBLK = [(1, 0, 64, 0), (67, 64, 512, 1), (579, 576, 512, 1)]


def build_B2(last, kb=None):
    kb = kb or KB()
    xin = kb.dram("xmidT", [128, NCH, NTOK], F32, "ExternalInput")
    h2_d = kb.dram("h2hT", [128, NCH, HW_], BF16, "ExternalInput")
    wup_d = kb.dram("wup", [FCH, 128, NCH, 256], F32, "ExternalInput")
    wdn_d = kb.dram("wdn", [NQ, NCH, 128, QF, 128], F32, "ExternalInput")
    cw_d = kb.dram("convw", [128, FCH, 4], F32, "ExternalInput")
    mods_d = kb.dram("mods", [128, 6, NCH, 2], F32, "ExternalInput")
    xo_d = kb.dram("xoT", [128, NCH, NTOK], F32, "ExternalOutput")
    cs = kb.get_consts()
    x = kb.sb("x", [128, NCH, NTOK], F32); xt = [Tok() for _ in range(NCH)]
    h2 = kb.sb("h2", [128, NCH, HW_], BF16); h2t = Tok()
    hid = kb.sb("hid", [128, QF, HW_], BF16); hid_t = [Tok() for _ in range(QF)]
    G = kb.sb("G", [128, HW_], F32); Gt = Tok()
    C = kb.sb("C", [128, HW_], F32); Ct = Tok()
    cw = kb.sb("cw", [128, FCH, 4], F32); cwt = Tok()
    mods = kb.sb("mods_s", [128, 6, NCH, 2], F32); mt = Tok()
    wu = [kb.sb(f"wu{i}", [128, NCH, 256], BF16) for i in range(2)]; wu_t = [Tok() for _ in range(2)]
    wd = [kb.sb(f"wd{i}", [128, QF, 128], BF16) for i in range(2)]; wd_t = [Tok() for _ in range(2)]
    pss = [kb.ps(f"ps{i}", [128, 512], F32) for i in range(8)]
    ps_t = [Tok() for _ in range(8)]
    for c in range(0, NCH, 4):
        kb.dma("sp", x[:, c:c + 4, :], xin[:, c:c + 4, :], writes=xt[c:c + 4])
        kb.dma("sp", h2[:, c:c + 4, :], h2_d[:, c:c + 4, :], writes=[h2t])
    kb.dma("sp", mods[:], mods_d, writes=[mt])
    kb.dma("sp", cw[:], cw_d, writes=[cwt])
    if not last:
        ng_d = kb.dram("normg", [128, NCH], F32, "ExternalInput")
        modsn_d = kb.dram("modsn", [128, 6, NCH, 2], F32, "ExternalInput")
        h1_d = kb.dram("h1T", [128, NCH, NTOK], BF16, "ExternalOutput")
        ng = kb.sb("ng", [128, NCH], F32); ngt = Tok()
        modsn = kb.sb("modsn_s", [128, 6, NCH, 2], F32); mnt = Tok()
        ac = kb.sb("ac", [128, NCH, 2], F32); bc = kb.sb("bc", [128, NCH, 2], F32); ct = Tok()
        kb.dma("sp", ng[:], ng_d, writes=[ngt])
        kb.dma("sp", modsn[:], modsn_d, writes=[mnt])
        emit_coefs(kb, modsn, mnt, ng, ngt, 0, 1, ac, bc, ct)
    wdi = 0
    for q in range(NQ):
        for mm in range(QF):
            m = q * QF + mm
            wi = m % 2
            kb.dma("pool", wu[wi][:], wup_d[m], writes=[wu_t[wi]])
            for half in range(2):
                for bi, (c0, n) in enumerate(GBLK):
                    j = half * 3 + bi
                    kb.group("pe", [lambda e, k=k, wi=wi, j=j, c0=c0, n=n, half=half: e.matmul(
                        pss[j][:, 0:n], lhsT=wu[wi][:, k, half * 128:(half + 1) * 128], rhs=h2[:, k, c0:c0 + n],
                        start=(k == 0), stop=(k == NCH - 1)) for k in range(NCH)],
                        reads=[wu_t[wi], h2t], writes=[ps_t[j]])
            for bi, (c0, n) in enumerate(GBLK):
                kb.op("act", lambda e, bi=bi, c0=c0, n=n: e.activation(out=G[:, c0:c0 + n], in_=pss[bi][:, 0:n], func=AF.Identity),
                      reads=[ps_t[bi]], writes=[Gt])
            W = HW_ - 2
            kb.op("dve", lambda e, m=m: e.tensor_scalar(out=C[:, 1:1 + W], in0=G[:, 1:1 + W], scalar1=cw[:, m, 1:2],
                                                        scalar2=cw[:, m, 3:4], op0=ALU.mult, op1=ALU.add),
                  reads=[Gt, cwt], writes=[Ct])
            kb.op("dve", lambda e, m=m: e.scalar_tensor_tensor(out=C[:, 1:1 + W], in0=G[:, 0:W], scalar=cw[:, m, 0:1],
                                                               in1=C[:, 1:1 + W], op0=ALU.mult, op1=ALU.add),
                  reads=[Gt, cwt, Ct], writes=[Ct])
            kb.op("dve", lambda e, m=m: e.scalar_tensor_tensor(out=C[:, 1:1 + W], in0=G[:, 2:2 + W], scalar=cw[:, m, 2:3],
                                                               in1=C[:, 1:1 + W], op0=ALU.mult, op1=ALU.add),
                  reads=[Gt, cwt, Ct], writes=[Ct])
            kb.op("act", lambda e: e.activation(out=C[:, 1:1 + W], in_=C[:, 1:1 + W], func=AF.Silu), reads=[Ct], writes=[Ct])
            for bi, (c0, n) in enumerate(GBLK):
                a = max(c0, 1)
                b_ = min(c0 + n, HW_ - 1)
                kb.op("dve", lambda e, bi=bi, a=a, b_=b_, c0=c0, mm=mm: e.tensor_tensor(
                    out=hid[:, mm, a:b_], in0=pss[3 + bi][:, a - c0:b_ - c0], in1=C[:, a:b_], op=ALU.mult),
                    reads=[ps_t[3 + bi], Ct], writes=[hid_t[mm]])
        for mo in range(NCH):
            wi = wdi % 2
            wdi += 1
            kb.dma("pool", wd[wi][:], wdn_d[q, mo], writes=[wd_t[wi]])
            for bi, (hc0, t0, n, seg) in enumerate(DBLK):
                j = 6 + (bi % 2)
                kb.group("pe", [lambda e, kk=kk, wi=wi, j=j, hc0=hc0, n=n: e.matmul(
                    pss[j][:, 0:n], lhsT=wd[wi][:, kk, :], rhs=hid[:, kk, hc0:hc0 + n],
                    start=(kk == 0), stop=(kk == QF - 1)) for kk in range(QF)],
                    reads=[wd_t[wi]] + hid_t, writes=[ps_t[j]])
                kb.op("dve", lambda e, mo=mo, j=j, t0=t0, n=n, seg=seg: e.scalar_tensor_tensor(
                    out=x[:, mo, t0:t0 + n], in0=pss[j][:, 0:n], scalar=mods[:, 5, mo, seg:seg + 1],
                    in1=x[:, mo, t0:t0 + n], op0=ALU.mult, op1=ALU.add),
                    reads=[ps_t[j], mt, xt[mo]], writes=[xt[mo]])
    deps_w = {}
    for m in range(NCH):
        KB._add(deps_w, xt[m].w)
    for e in ("act", "dve", "pe", "sp"):
        kb._wait(e, deps_w)
    xm = Tok()
    for c in range(0, NCH, 4):
        kb.dma("sp", xo_d[:, c:c + 4, :], x[:, c:c + 4, :], reads=[xm], is_output=True)
    if not last:
        scr = alloc_norm_scratch(kb)
        h1v = h2[:, :, 0:NTOK]
        emit_modnorm(kb, cs, x, xm, ac, bc, ct, h2, h2t, lambda t: t, pss[0], ps_t[0], scr)
        for c in range(0, NCH, 4):
            kb.dma("sp", h1_d[:, c:c + 4, :], h1v[:, c:c + 4, :], reads=[h2t], is_output=True)
    return kb.end_phase()


def fm(a):
    t, f = a.shape
    return np.ascontiguousarray(a.reshape(t, f // 128, 128).transpose(2, 1, 0))


def unfm(a):
    p, c, t = a.shape
    return np.ascontiguousarray(a.transpose(2, 1, 0).reshape(t, c * 128))


def vec_fm(v):
    return np.ascontiguousarray(v.reshape(-1, 128).T)


def core_tokens(x_b, ctx_b, j):
    return np.concatenate([ctx_b[64 * j:64 * j + 64], x_b[1024 * j:1024 * j + 1024]], axis=0)


def run(nc, in_maps):
    res = run_bass_kernel_spmd(nc, in_maps, core_ids=list(range(8)))
    return res.results


_CACHE = {}


def get_nc(name, builder, *args):
    key = (name,) + args
    if key not in _CACHE:
        _CACHE[key] = builder(*args)
    return _CACHE[key]


def host_M(c, c_ctx, ada_w, ada_b):
    cond = np.stack([c[0], c[1], c_ctx], axis=0)
    condT = np.ascontiguousarray(cond.reshape(3, NCH, 128).transpose(2, 1, 0))
    in_maps = []
    for core in range(8):
        sl = slice(core * 1536, (core + 1) * 1536)
        aw = np.ascontiguousarray(ada_w[:, :, sl].reshape(DEPTH, NCH, 128, 1536).transpose(0, 2, 1, 3))
        ab = np.ascontiguousarray(ada_b[:, sl].reshape(DEPTH, 12, 128).transpose(2, 0, 1))
        in_maps.append({"condT": condT, "adaw": aw, "adab": ab})
    res = run(get_nc("M", build_M), in_maps)
    full = np.zeros((DEPTH, 6 * D, 3), np.float32)
    for core in range(8):
        r = res[core]["modT"]
        full[:, core * 1536:(core + 1) * 1536, :] = r.transpose(1, 2, 0, 3).reshape(DEPTH, 1536, 3)
    mods = [[None, None] for _ in range(DEPTH)]
    for l in range(DEPTH):
        for b in range(2):
            m = full[l][:, [2, b]]
            mods[l][b] = np.ascontiguousarray(m.reshape(6, NCH, 128, 2).transpose(2, 0, 1, 3))
    return mods, full


def host_wout(w):
    rows = []
    for i in range(4):
        for cc in range(4):
            r0 = 256 * i + 128 * cc if cc < 2 else 1024 + 256 * i + 128 * (cc - 2)
            rows.append(w[r0:r0 + 128])
    wp = np.stack(rows, axis=0)
    return np.ascontiguousarray(wp.reshape(NCH, 128, NCH, 128).transpose(2, 1, 0, 3))


def host_wup(w):
    wk = w.reshape(NCH, 128, 2, FCH, 128)
    return np.ascontiguousarray(wk.transpose(3, 1, 0, 2, 4).reshape(FCH, 128, NCH, 256))


def host_wdn(w):
    wk = w.reshape(NQ, QF, 128, NCH, 128)
    return np.ascontiguousarray(wk.transpose(0, 3, 2, 1, 4))


def host_convw(cw, cb):
    a = np.concatenate([cw, cb[None]], axis=0)
    return np.ascontiguousarray(a.reshape(4, FCH, 128).transpose(2, 1, 0))


def host_halo(h2_cores):
    outs = []
    for core in range(8):
        b, j = divmod(core, 4)
        h = h2_cores[core]
        o = np.zeros((128, NCH, HW_), h.dtype)
        o[:, :, 1:65] = h[:, :, 0:64]
        o[:, :, 67:1091] = h[:, :, 64:1088]
        if j > 0:
            hp = h2_cores[core - 1]
            o[:, :, 0] = hp[:, :, 63]
            o[:, :, 66] = hp[:, :, 1087]
        if j < 3:
            hn = h2_cores[core + 1]
            o[:, :, 65] = hn[:, :, 0]
            o[:, :, 1091] = hn[:, :, 64]
        outs.append(o)
    return outs


def kb_barrier(kb):
    deps = {("eng", e): kb.cnt[e] for e in kb.cnt if kb.cnt[e] > 0}
    for q in kb.dpool:
        for key, val in kb.dpool[q][0]:
            if val > 0:
                deps[key] = val
    for e in kb.engs:
        kb._wait(e, dict(deps))


KB.barrier = kb_barrier

ABLK = [(0, 2)] + [(2 + 4 * i, 4) for i in range(8)]
BWD_ORDER = [1, 0] + list(range(33, 1, -1))
QSCALE = 128 ** -0.5


def build_Aeven(kb=None):
    import contextlib
    kb = kb or KB()
    nc = kb.nc
    h1_d = kb.dram("h1T", [128, NCH, TB], BF16, "ExternalInput")
    wA_d = kb.dram("wA", [128, NCH, 896], F32, "ExternalInput")
    wC_d = kb.dram("wC", [128, NCH, 768], F32, "ExternalInput")
    w2_d = kb.dram("w2", [128, 2, 128], F32, "ExternalInput")
    gb_d = kb.dram("gbias", [128, 256], F32, "ExternalInput")
    gn_d = kb.dram("gnorm", [128, 256], F32, "ExternalInput")
    scw_d = kb.dram("scw", [128, 2, 3], F32, "ExternalInput")
    msk_d = kb.dram("masks", [128, 4, 128], F32, "ExternalInput")
    id_d = kb.dram("ident", [128, 128], BF16, "ExternalInput")
    out_d = kb.dram("mixT", [128, 4, TB], BF16, "ExternalOutput")
    cs = kb.get_consts()
    pss = [kb.ps(f"ps{i}", [128, 512], F32) for i in range(7)]
    ps_t = [Tok() for _ in range(7)]
    psT = kb.ps("psT", [128, 1024], BF16); psT_t = Tok()
    hb = [kb.sb(f"hb{i}", [128, NCH, 512], BF16) for i in range(1)]; hb_t = [Tok()]

    with contextlib.ExitStack() as st:
        def sbx(name, shape, dt):
            return st.enter_context(nc.sbuf_tensor(f"{name}_u{kb.uid}", list(shape), dt)).ap()
        wC = sbx("wC_s", [128, NCH, 768], BF16); wC_t = Tok()
        U = sbx("u_s", [128, 2, TB + 4], F32); U_t = Tok()
        SBf = sbx("sb_s", [128, 2, TB], F32); SB_t = Tok()
        SX = sbx("sx_s", [128, 512], F32); SX_t = Tok()
        CO = sbx("co_s", [128, 2, TB], F32); CO_t = Tok()
        COb = sbx("cob_s", [128, 2, TB], BF16); COb_t = Tok()
        scw = sbx("scw_s", [128, 2, 3], F32); scw_t = Tok()
        kb.dma("pool", wC[:], wC_d, writes=[wC_t])
        kb.dma("sp", scw[:], scw_d, writes=[scw_t])
        kb.op("pool", lambda e: e.memset(U[:], 0.0), writes=[U_t])

        def ucol(tok):
            return 1 + tok if tok < CTX else 3 + tok
        pi = 0
        for (t0, nt) in ABLK:
            n = nt * 128
            g0 = t0 * 128
            kb.dma("sp", hb[0][:, :, 0:n], h1_d[:, :, g0:g0 + n], writes=[hb_t[0]])
            for c2 in range(2):
                for which in range(3):
                    j = pi % 4
                    pi += 1
                    col = which * 256 + c2 * 128
                    kb.group("pe", [lambda e, k=k, j=j, col=col, n=n: e.matmul(
                        pss[j][:, 0:n], lhsT=wC[:, k, col:col + 128], rhs=hb[0][:, k, 0:n],
                        start=(k == 0), stop=(k == NCH - 1)) for k in range(NCH)],
                        reads=[wC_t, hb_t[0]], writes=[ps_t[j]])
                    if which == 0:
                        kb.op("act", lambda e, j=j, n=n: e.activation(out=SX[:, 0:n], in_=pss[j][:, 0:n], func=AF.Identity),
                              reads=[ps_t[j]], writes=[SX_t])
                    elif which == 1:
                        kb.op("act", lambda e, j=j, n=n, c2=c2, g0=g0: e.activation(out=SBf[:, c2, g0:g0 + n], in_=pss[j][:, 0:n],
                                                                                  func=AF.Identity),
                              reads=[ps_t[j]], writes=[SB_t])
                    else:
                        uc = ucol(g0)
                        kb.op("dve", lambda e, j=j, n=n, c2=c2, uc=uc: e.tensor_tensor(out=U[:, c2, uc:uc + n], in0=pss[j][:, 0:n],
                                                                                     in1=SX[:, 0:n], op=ALU.mult),
                              reads=[ps_t[j], SX_t], writes=[U_t])
        for c2 in range(2):
            for (g0, n) in ((0, CTX), (CTX, SEQ)):
                uc = ucol(g0)
                kb.op("dve", lambda e, c2=c2, g0=g0, n=n, uc=uc: e.tensor_scalar(
                    out=CO[:, c2, g0:g0 + n], in0=U[:, c2, uc:uc + n], scalar1=scw[:, c2, 1:2], scalar2=None, op0=ALU.mult),
                    reads=[U_t, scw_t], writes=[CO_t])
                kb.op("dve", lambda e, c2=c2, g0=g0, n=n, uc=uc: e.scalar_tensor_tensor(
                    out=CO[:, c2, g0:g0 + n], in0=U[:, c2, uc - 1:uc - 1 + n], scalar=scw[:, c2, 0:1],
                    in1=CO[:, c2, g0:g0 + n], op0=ALU.mult, op1=ALU.add),
                    reads=[U_t, scw_t, CO_t], writes=[CO_t])
                kb.op("dve", lambda e, c2=c2, g0=g0, n=n, uc=uc: e.scalar_tensor_tensor(
                    out=CO[:, c2, g0:g0 + n], in0=U[:, c2, uc + 1:uc + 1 + n], scalar=scw[:, c2, 2:3],
                    in1=CO[:, c2, g0:g0 + n], op0=ALU.mult, op1=ALU.add),
                    reads=[U_t, scw_t, CO_t], writes=[CO_t])
                kb.op("dve", lambda e, c2=c2, g0=g0, n=n: e.tensor_tensor(
                    out=COb[:, c2, g0:g0 + n], in0=CO[:, c2, g0:g0 + n], in1=SBf[:, c2, g0:g0 + n], op=ALU.mult),
                    reads=[CO_t, SB_t], writes=[COb_t])
        kb.dma("sp", out_d[:, 2:4, :], COb[:], reads=[COb_t], is_output=True)
        kb.barrier()

    wA = kb.sb("wA_s", [128, NCH, 896], BF16); wA_t = Tok()
    QF = kb.sb("QF", [128, TB], BF16); QB = kb.sb("QB", [128, TB], BF16)
    KF = kb.sb("KF", [128, TB], BF16); KBk = kb.sb("KBk", [128, TB], BF16)
    KTF = kb.sb("KTF", [128, NTILE, 128], BF16); KTB = kb.sb("KTB", [128, NTILE, 128], BF16)
    V = kb.sb("V", [128, NTILE, 256], BF16)
    SR = kb.sb("SR", [128, NTILE, 256], BF16)
    AT = kb.sb("AT", [128, NTILE, 128], BF16)
    SBF = kb.sb("SBF", [128, NTILE, 256], BF16)
    per_t = [Tok() for _ in range(NTILE)]
    qTb = kb.sb("qTb", [128, 512], F32); kTb = kb.sb("kTb", [128, 512], F32); qk_t = Tok()
    GL = kb.sb("GL", [128, 512], F32); GL_t = Tok()
    w2 = kb.sb("w2_s", [128, 2, 128], F32); gb = kb.sb("gb_s", [128, 256], F32); gn = kb.sb("gn_s", [128, 256], F32)
    msk = kb.sb("msk_s", [128, 4, 128], F32); ident = kb.sb("id_s", [128, 128], BF16)
    k_t = Tok()
    el = kb.sb("el", [128, 2, NTILE], F32); el_t = Tok()
    tg = kb.sb("tg", [128, 256], F32); tg_t = Tok()
    sg = kb.sb("sg", [128, 256], F32); sg_t = Tok()
    E1 = kb.sb("E1", [128, 256], F32); E2 = kb.sb("E2", [128, 256], F32); E2t = kb.sb("E2t", [128, 256], F32)
    E1_t, E2_t, E2t_t = Tok(), Tok(), Tok()
    a1 = kb.sb("a1", [128, 128], F32); a2 = kb.sb("a2", [128, 128], F32); a_t = Tok()
    srt = kb.sb("srt", [128, 256], F32); srt_t = Tok()
    kb.dma("pool", wA[:], wA_d, writes=[wA_t])
    for dst, src in ((w2, w2_d), (gb, gb_d), (gn, gn_d), (msk, msk_d), (ident, id_d)):
        kb.dma("sp", dst[:], src, writes=[k_t])

    for (t0, nt) in ABLK:
        n = nt * 128
        g0 = t0 * 128
        kb.dma("sp", hb[0][:, :, 0:n], h1_d[:, :, g0:g0 + n], writes=[hb_t[0]])
        for j, (col, m) in enumerate(((0, 128), (128, 128), (768, 128))):
            kb.group("pe", [lambda e, k=k, j=j, col=col, m=m, n=n: e.matmul(
                pss[j][0:m, 0:n], lhsT=wA[:, k, col:col + m], rhs=hb[0][:, k, 0:n],
                start=(k == 0), stop=(k == NCH - 1)) for k in range(NCH)],
                reads=[wA_t, hb_t[0]], writes=[ps_t[j]])
        kb.op("act", lambda e, n=n: e.activation(out=qTb[:, 0:n], in_=pss[0][:, 0:n], func=AF.Identity, scale=QSCALE),
              reads=[ps_t[0]], writes=[qk_t])
        kb.op("act", lambda e, n=n: e.activation(out=kTb[:, 0:n], in_=pss[1][:, 0:n], func=AF.Identity),
              reads=[ps_t[1]], writes=[qk_t])
        kb.op("dve", lambda e, n=n: e.tensor_copy(out=GL[:, 0:n], in_=pss[2][:, 0:n]), reads=[ps_t[2]], writes=[GL_t])
        for ti in range(nt):
            tl = t0 + ti
            c0 = ti * 128
            gc = tl * 128
            pt = per_t[tl]
            kb.group("pe", [lambda e, k=k, c0=c0: e.matmul(pss[4][:, 0:384], lhsT=hb[0][:, k, c0:c0 + 128], rhs=wA[:, k, 128:512],
                                                          start=(k == 0), stop=(k == NCH - 1)) for k in range(NCH)],
                     reads=[wA_t, hb_t[0]], writes=[ps_t[4]])
            kb.group("pe", [lambda e, k=k, c0=c0: e.matmul(pss[5][:, 0:256], lhsT=hb[0][:, k, c0:c0 + 128], rhs=wA[:, k, 512:768],
                                                          start=(k == 0), stop=(k == NCH - 1)) for k in range(NCH)],
                     reads=[wA_t, hb_t[0]], writes=[ps_t[5]])
            kb.group("pe", [lambda e, d=d, c0=c0: e.matmul(pss[6][:, d * 128:(d + 1) * 128], lhsT=GL[:, c0:c0 + 128], rhs=w2[:, d, :],
                                                          start=True, stop=True) for d in range(2)],
                     reads=[GL_t, k_t], writes=[ps_t[6]])
            kb.op("dve", lambda e: e.tensor_tensor(out=tg[:], in0=pss[6][:, 0:256], in1=gb[:], op=ALU.add),
                  reads=[ps_t[6], k_t], writes=[tg_t])
            kb.op("act", lambda e: e.activation(out=tg[:], in_=tg[:], func=AF.Exp, scale=-1.0), reads=[tg_t], writes=[tg_t])
            kb.op("act", lambda e: e.activation(out=sg[:], in_=tg[:], func=AF.Ln, bias=1.0, scale=1.0), reads=[tg_t], writes=[sg_t])
            kb.group("pe", [lambda e, d=d: e.matmul(pss[6][:, 256 + d * 128:256 + (d + 1) * 128], lhsT=sg[:, d * 128:(d + 1) * 128],
                                                   rhs=msk[:, d, :], start=True, stop=True) for d in range(2)],
                     reads=[sg_t, k_t], writes=[ps_t[6]])
            kb.group("pe", [lambda e, d=d: e.matmul(pss[3][:, d * 128:(d + 1) * 128], lhsT=msk[:, d, :],
                                                   rhs=sg[:, d * 128:(d + 1) * 128], start=True, stop=True) for d in range(2)],
                     reads=[sg_t, k_t], writes=[ps_t[3]])
            kb.op("act", lambda e: e.activation(out=E1[:], in_=pss[6][:, 256:512], func=AF.Exp), reads=[ps_t[6]], writes=[E1_t])
            kb.op("act", lambda e: e.activation(out=E2[:], in_=pss[6][:, 256:512], func=AF.Exp, scale=-1.0),
                  reads=[ps_t[6]], writes=[E2_t])
            kb.op("act", lambda e: e.activation(out=E2t[:], in_=pss[3][:, 0:256], func=AF.Exp, scale=-1.0),
                  reads=[ps_t[3]], writes=[E2t_t])
            kb.op("pool", lambda e, tl=tl: e.tensor_copy(out=el[:, 0, tl:tl + 1], in_=E1[:, 127:128]), reads=[E1_t], writes=[el_t])
            kb.op("pool", lambda e, tl=tl: e.tensor_copy(out=el[:, 1, tl:tl + 1], in_=E1[:, 128:129]), reads=[E1_t], writes=[el_t])
            kb.op("dve", lambda e, c0=c0, gc=gc: e.tensor_tensor(out=QF[:, gc:gc + 128], in0=qTb[:, c0:c0 + 128], in1=E1[:, 0:128], op=ALU.mult),
                  reads=[qk_t, E1_t], writes=[pt])
            kb.op("dve", lambda e, c0=c0, gc=gc: e.tensor_tensor(out=QB[:, gc:gc + 128], in0=qTb[:, c0:c0 + 128], in1=E1[:, 128:256], op=ALU.mult),
                  reads=[qk_t, E1_t], writes=[pt])
            kb.op("dve", lambda e, c0=c0, gc=gc: e.tensor_tensor(out=KF[:, gc:gc + 128], in0=kTb[:, c0:c0 + 128], in1=E2[:, 0:128], op=ALU.mult),
                  reads=[qk_t, E2_t], writes=[pt])
            kb.op("dve", lambda e, c0=c0, gc=gc: e.tensor_tensor(out=KBk[:, gc:gc + 128], in0=kTb[:, c0:c0 + 128], in1=E2[:, 128:256], op=ALU.mult),
                  reads=[qk_t, E2_t], writes=[pt])
            kb.op("dve", lambda e, tl=tl: e.tensor_tensor(out=KTF[:, tl, :], in0=pss[4][:, 0:128], in1=E2t[:, 0:128], op=ALU.mult),
                  reads=[ps_t[4], E2t_t], writes=[pt])
            kb.op("dve", lambda e, tl=tl: e.tensor_tensor(out=KTB[:, tl, :], in0=pss[4][:, 0:128], in1=E2t[:, 128:256], op=ALU.mult),
                  reads=[ps_t[4], E2t_t], writes=[pt])
            kb.op("act", lambda e, tl=tl: e.activation(out=V[:, tl, :], in_=pss[4][:, 128:384], func=AF.Identity),
                  reads=[ps_t[4]], writes=[pt])
            kb.op("act", lambda e: e.activation(out=srt[:], in_=pss[5][:, 0:256], func=AF.Silu), reads=[ps_t[5]], writes=[srt_t])
            kb.op("pool", lambda e, tl=tl: e.tensor_tensor(out=SR[:, tl, :], in0=srt[:], in1=gn[:], op=ALU.mult),
                  reads=[srt_t, k_t], writes=[pt])
            kb.group("pe", [lambda e, gc=gc: e.matmul(pss[5][:, 256:384], lhsT=KF[:, gc:gc + 128], rhs=QF[:, gc:gc + 128], start=True, stop=True),
                            lambda e, gc=gc: e.matmul(pss[5][:, 384:512], lhsT=KBk[:, gc:gc + 128], rhs=QB[:, gc:gc + 128], start=True, stop=True)],
                     reads=[pt], writes=[ps_t[5]])
            kb.op("dve", lambda e: e.tensor_tensor(out=a1[:], in0=pss[5][:, 256:384], in1=msk[:, 2, :], op=ALU.mult),
                  reads=[ps_t[5], k_t], writes=[a_t])
            kb.op("dve", lambda e: e.tensor_tensor(out=a2[:], in0=pss[5][:, 384:512], in1=msk[:, 3, :], op=ALU.mult),
                  reads=[ps_t[5], k_t, a_t], writes=[a_t])
            kb.op("pool", lambda e, tl=tl: e.tensor_tensor(out=AT[:, tl, :], in0=a1[:], in1=a2[:], op=ALU.add),
                  reads=[a_t], writes=[pt])

    S = kb.sb("S", [128, 256], F32); S_t = Tok()
    S2 = kb.sb("S2", [128, 256], F32); S2_t = Tok()
    sbf_t = [Tok() for _ in range(NTILE)]
    kb.op("pool", lambda e: e.memset(S[:], 0.0), writes=[S_t])
    for c in range(NTILE):
        kb.op("act", lambda e, c=c: e.activation(out=SBF[:, c, :], in_=S[:], func=AF.Identity), reads=[S_t], writes=[sbf_t[c]])
        if c == NTILE - 1:
            break
        kb.op("pe", lambda e, c=c: e.matmul(pss[0][:, 0:256], lhsT=KTF[:, c, :], rhs=V[:, c, :], start=True, stop=True),
              reads=[per_t[c]], writes=[ps_t[0]])
        kb.op("dve", lambda e: e.tensor_tensor(out=S2[:], in0=pss[0][:, 0:256], in1=S[:], op=ALU.add),
              reads=[ps_t[0], S_t], writes=[S2_t])
        kb.op("dve", lambda e, c=c: e.tensor_scalar(out=S[:], in0=S2[:], scalar1=el[:, 0, c:c + 1], scalar2=None, op0=ALU.mult),
              reads=[S2_t, el_t], writes=[S_t])

    Sb = kb.sb("Sb", [128, 256], F32); Sb_t = Tok()
    Sbb = kb.sb("Sbb", [128, 256], BF16); Sbb_t = Tok()
    ss = kb.sb("ss", [128, 2], F32); ss_t = Tok()
    junk = kb.sb("junk", [128, 256], F32); junk_t = Tok()
    on = kb.sb("on", [128, 256], BF16); on_t = Tok()
    ost = [kb.sb(f"ost{i}", [128, 2, 128], BF16) for i in range(2)]; ost_t = [Tok(), Tok()]
    kb.op("pool", lambda e: e.memset(Sb[:], 0.0), writes=[Sb_t])
    kb.op("pool", lambda e: e.memset(Sbb[:], 0.0), writes=[Sbb_t])
    for oi, c in enumerate(BWD_ORDER):
        gc = c * 128
        kb.group("pe", [lambda e, c=c, gc=gc: e.matmul(pss[1][:, 0:256], lhsT=QF[:, gc:gc + 128], rhs=SBF[:, c, :], start=True, stop=False),
                        lambda e, c=c, gc=gc: e.matmul(pss[1][:, 0:256], lhsT=QB[:, gc:gc + 128], rhs=Sbb[:], start=False, stop=False),
                        lambda e, c=c, gc=gc: e.matmul(pss[1][:, 0:256], lhsT=AT[:, c, :], rhs=V[:, c, :], start=False, stop=True)],
                 reads=[per_t[c], sbf_t[c], Sbb_t], writes=[ps_t[1]])
        if oi < NTILE - 1 and c != 0:
            pass
        kb.op("pe", lambda e, c=c: e.matmul(pss[2][:, 0:256], lhsT=KTB[:, c, :], rhs=V[:, c, :], start=True, stop=True),
              reads=[per_t[c]], writes=[ps_t[2]])
        kb.op("dve", lambda e: e.tensor_tensor(out=S2[:], in0=pss[2][:, 0:256], in1=Sb[:], op=ALU.add),
              reads=[ps_t[2], Sb_t], writes=[S2_t])
        kb.op("dve", lambda e, c=c: e.tensor_scalar(out=Sb[:], in0=S2[:], scalar1=el[:, 1, c:c + 1], scalar2=None, op0=ALU.mult),
              reads=[S2_t, el_t], writes=[Sb_t])
        kb.op("act", lambda e: e.activation(out=Sbb[:], in_=Sb[:], func=AF.Identity), reads=[Sb_t], writes=[Sbb_t])
        kb.op("act", lambda e: e.activation(out=junk[:], in_=pss[1][:, 0:256], func=AF.Square, accum_out=ss[:, 0:1]),
              reads=[ps_t[1]], writes=[junk_t, ss_t])
        kb.op("act", lambda e: e.activation(out=ss[:, 1:2], in_=ss[:, 0:1], func=AF.Sqrt, scale=1.0 / 256, bias=cs.eps[:, 0:1]),
              reads=[ss_t, cs.t], writes=[ss_t])
        kb.op("dve", lambda e: e.reciprocal(out=ss[:, 1:2], in_=ss[:, 1:2]), reads=[ss_t], writes=[ss_t])
        kb.op("dve", lambda e, c=c: e.scalar_tensor_tensor(out=on[:], in0=pss[1][:, 0:256], scalar=ss[:, 1:2], in1=SR[:, c, :],
                                                          op0=ALU.mult, op1=ALU.mult),
              reads=[ps_t[1], ss_t, per_t[c]], writes=[on_t])
        kb.group("pe", [lambda e, hh=hh: e.transpose(psT[:, hh * 128:(hh + 1) * 128], on[:, hh * 128:(hh + 1) * 128], ident[:])
                        for hh in range(2)],
                 reads=[on_t, k_t], writes=[psT_t])
        oj = oi % 2
        kb.op("act", lambda e, oj=oj: e.activation(out=ost[oj][:].rearrange("p a b -> p (a b)"), in_=psT[:, 0:256], func=AF.Identity),
              reads=[psT_t], writes=[ost_t[oj]])
        kb.dma("sp", out_d[:, 0:2, gc:gc + 128], ost[oj][:], reads=[ost_t[oj]], is_output=True)
    return kb.end_phase()


def host_masks():
    j = np.arange(128)[:, None]
    i = np.arange(128)[None, :]
    mf = (j <= i).astype(np.float32)
    mb = (j >= i).astype(np.float32)
    return np.ascontiguousarray(np.stack([mf * (-1.0 / 16.0), mb * (-1.0 / 16.0), mf, mb], axis=1))


def wfm(w):
    return np.ascontiguousarray(w.reshape(NCH, 128, -1).transpose(1, 0, 2))


def host_Aeven_inputs(g, w_in, gate_w2, gate_b, gla_norm_g, sc_conv_w):
    o = np.cumsum((0,) + (512, 512, 1024, 1024, 32, 1024, 1024, 1024))
    qs, ks, vs, rs, gls, sxs, sbs, scgs = [int(v) for v in o[:8]]
    colsA = np.concatenate([np.arange(qs + 128 * g, qs + 128 * g + 128), np.arange(ks + 128 * g, ks + 128 * g + 128),
                            np.arange(vs + 256 * g, vs + 256 * g + 256), np.arange(rs + 256 * g, rs + 256 * g + 256),
                            np.arange(gls, gls + 32)])
    wa = w_in[:, colsA]
    pad = np.zeros((wa.shape[0], 128), wa.dtype)
    pad[:, 0:16] = wa[:, 768:784]
    pad[:, 32:48] = wa[:, 784:800]
    wa = np.concatenate([wa[:, 0:768], pad], axis=1)
    colsC = np.concatenate([np.arange(sxs + 256 * g, sxs + 256 * g + 256), np.arange(sbs + 256 * g, sbs + 256 * g + 256),
                            np.arange(scgs + 256 * g, scgs + 256 * g + 256)])
    hs = slice(128 * g, 128 * g + 128)
    w2 = np.zeros((128, 2, 128), np.float32)
    w2[0:16, 0, :] = gate_w2[0][:, hs]
    w2[32:48, 1, :] = gate_w2[1][:, hs]
    gbias = np.ascontiguousarray(np.broadcast_to(gate_b[:, hs].reshape(1, 256), (128, 256)))
    gnorm = np.ascontiguousarray(np.broadcast_to(gla_norm_g.reshape(1, 256), (128, 256)))
    scw = np.ascontiguousarray(sc_conv_w[:, 256 * g:256 * g + 256].reshape(3, 2, 128).transpose(2, 1, 0))
    return {"wA": wfm(wa), "wC": wfm(w_in[:, colsC]), "w2": w2, "gbias": gbias, "gnorm": gnorm, "scw": scw,
            "masks": host_masks(), "ident": np.eye(128, dtype=np.float32).astype(ml_dtypes.bfloat16)}


def assemble_h1(h1_cores, b):
    ctxp = [h1_cores[4 * b + j][:, :, 0:64] for j in range(4)]
    latp = [h1_cores[4 * b + j][:, :, 64:1088] for j in range(4)]
    return np.ascontiguousarray(np.concatenate(ctxp + latp, axis=2))


def scatter_mix(mix_cores, b, j):
    parts = []
    for g in range(4):
        m = mix_cores[4 * b + g]
        parts.append(np.concatenate([m[:, :, 64 * j:64 * j + 64], m[:, :, CTX + 1024 * j:CTX + 1024 * j + 1024]], axis=2))
    return np.ascontiguousarray(np.concatenate(parts, axis=1))


QBLK2 = [(0, 256, [0, 1])] + [(CTX + 512 * i, 512, list(range(NTILE))) for i in range(8)]
KSCALE = 128 ** -0.5


def build_Aodd(layer, kb=None):
    import contextlib
    import math
    lam_init = 0.8 - 0.6 * math.exp(-0.3 * layer)
    kb = kb or KB()
    nc = kb.nc
    h1_d = kb.dram("h1T", [128, NCH, TB], BF16, "ExternalInput")
    wD_d = kb.dram("wD", [128, NCH, 768], F32, "ExternalInput")
    wM_d = kb.dram("wM", [128, NCH, 772], F32, "ExternalInput")
    qkg_d = kb.dram("qkg", [128, 512], F32, "ExternalInput")
    rope_d = kb.dram("rope", [128, 32, 2, 64], F32, "ExternalInput")
    lamp_d = kb.dram("lamp", [128, 4, 64], F32, "ExternalInput")
    subg_d = kb.dram("subg", [128, 128], F32, "ExternalInput")
    subgc_d = kb.dram("subgc", [128, 1], F32, "ExternalInput")
    mlcw_d = kb.dram("mlcw", [128, 2, 3], F32, "ExternalInput")
    mlgb_d = kb.dram("mlgb", [128, 4], F32, "ExternalInput")
    mlng_d = kb.dram("mlng", [128, 256], F32, "ExternalInput")
    msk_d = kb.dram("masks", [128, 4, 128], F32, "ExternalInput")
    id_d = kb.dram("ident", [128, 128], BF16, "ExternalInput")
    out_d = kb.dram("mixT", [128, 4, TB], BF16, "ExternalOutput")
    cs = kb.get_consts()
    pss = [kb.ps(f"ps{i}", [128, 512], F32) for i in range(7)]
    ps_t = [Tok() for _ in range(7)]
    psT = kb.ps("psT", [128, 1024], BF16); psT_t = Tok()
    hbs = [kb.sb(f"hb{i}", [128, NCH, 512], BF16) for i in range(2)]; hbts = [Tok(), Tok()]
    ident = kb.sb("id_s", [128, 128], BF16)
    msk = kb.sb("msk_s", [128, 4, 128], F32)
    k_t = Tok()
    kb.dma("sp", ident[:], id_d, writes=[k_t])
    kb.dma("sp", msk[:], msk_d, writes=[k_t])
    ss = kb.sb("ss", [128, 4], F32); ss_t = Tok()
    junk = kb.sb("junk", [128, 256], F32); junk_t = Tok()
    ost = [kb.sb(f"ost{i}", [128, 2, 128], BF16) for i in range(2)]; ost_t = [Tok(), Tok()]
    onb = kb.sb("onb", [128, 256], BF16); onb_t = Tok()
    T1 = kb.sb("T1", [128, 256], F32); T1_t = Tok()
    T2 = kb.sb("T2", [128, 256], F32); T2_t = Tok()

    with contextlib.ExitStack() as st:
        def sbx(name, shape, dt):
            return st.enter_context(nc.sbuf_tensor(f"{name}_u{kb.uid}", list(shape), dt)).ap()
        wD = sbx("wD_s", [128, NCH, 768], BF16); wD_t = Tok()
        QKT = sbx("QKT", [128, 2, TB], BF16); qkt_t = [Tok() for _ in range(NTILE)]
        KTZ = sbx("KTZ", [128, 4, TB], BF16)
        VA = sbx("VA", [128, NTILE, 2, 132], BF16); va_t = [Tok() for _ in range(NTILE)]
        rope = sbx("rope_s", [128, 32, 2, 64], F32)
        qkg = sbx("qkg_s", [128, 512], F32)
        lamp = sbx("lamp_s", [128, 4, 64], F32)
        subg = sbx("subg_s", [128, 128], F32)
        SQ = sbx("SQ", [128, 512], F32); SQ_t = Tok()
        XN = sbx("XN", [128, 512], F32); XN_t = Tok()
        Y1 = sbx("Y1", [128, 512], F32); Y1_t = Tok()
        Y2 = sbx("Y2", [128, 512], F32); Y2_t = Tok()
        YB = sbx("YB", [128, 512], BF16); YB_t = Tok()
        rs8 = sbx("rs8", [128, 8], F32); rs8_t = Tok()
        PT = [sbx(f"PT{i}", [128, 512], BF16) for i in range(3)]; PT_t = [Tok(), Tok(), Tok()]
        lam = sbx("lam", [128, 8], F32); lam_t = Tok()
        rz = sbx("rz", [128, 4], F32); rz_t = Tok()
        kc_t = Tok()
        kb.dma("pool", wD[:], wD_d, writes=[wD_t])
        for dst, src in ((rope, rope_d), (qkg, qkg_d), (lamp, lamp_d), (subg, subg_d)):
            kb.dma("sp", dst[:], src, writes=[kc_t])
        kb.op("pool", lambda e: e.memset(VA[:], 1.0), writes=va_t)
        kb.op("pool", lambda e: e.memset(KTZ[:], 0.0), writes=qkt_t)
        kb.op("dve", lambda e: e.tensor_tensor(out=SQ[:, 0:64], in0=lamp[:, 0, :], in1=lamp[:, 1, :], op=ALU.mult),
              reads=[kc_t], writes=[SQ_t])
        kb.op("dve", lambda e: e.tensor_tensor(out=SQ[:, 64:128], in0=lamp[:, 2, :], in1=lamp[:, 3, :], op=ALU.mult),
              reads=[kc_t], writes=[SQ_t])
        kb.op("dve", lambda e: e.tensor_reduce(out=lam[:, 0:2], in_=SQ[:, 0:128].rearrange("p (a b) -> p a b", a=2),
                                               axis=mybir.AxisListType.X, op=ALU.add), reads=[SQ_t], writes=[lam_t])
        kb.op("act", lambda e: e.activation(out=lam[:, 2:4], in_=lam[:, 0:2], func=AF.Exp), reads=[lam_t], writes=[lam_t])
        kb.op("dve", lambda e: e.tensor_tensor(out=lam[:, 4:5], in0=lam[:, 3:4], in1=lam[:, 2:3], op=ALU.subtract),
              reads=[lam_t], writes=[lam_t])
        kb.op("dve", lambda e: e.tensor_scalar(out=lam[:, 4:5], in0=lam[:, 4:5], scalar1=-lam_init, scalar2=None, op0=ALU.add),
              reads=[lam_t], writes=[lam_t])
        kb.op("dve", lambda e: e.tensor_scalar(out=subg[:], in0=subg[:], scalar1=(1.0 - lam_init), scalar2=None, op0=ALU.mult),
              reads=[kc_t], writes=[kc_t])

        for bidx, (t0, nt) in enumerate(ABLK):
            n = nt * 128
            g0 = t0 * 128
            hb, hb_t = hbs[bidx % 2], hbts[bidx % 2]
            kb.dma("sp", hb[:, :, 0:n], h1_d[:, :, g0:g0 + n], writes=[hb_t])
            for ti in range(nt):
                tl = t0 + ti
                c0 = ti * 128
                gc = tl * 128
                pa = 2 * (tl % 2)
                pb = pa + 1
                kb.group("pe", [lambda e, k=k, c0=c0: e.matmul(pss[pa][:, 0:512], lhsT=hb[:, k, c0:c0 + 128], rhs=wD[:, k, 0:512],
                                                              start=(k == 0), stop=(k == NCH - 1)) for k in range(NCH)],
                         reads=[wD_t, hb_t], writes=[ps_t[pa]])
                kb.group("pe", [lambda e, k=k, c0=c0: e.matmul(pss[pb][:, 0:256], lhsT=hb[:, k, c0:c0 + 128], rhs=wD[:, k, 512:768],
                                                              start=(k == 0), stop=(k == NCH - 1)) for k in range(NCH)],
                         reads=[wD_t, hb_t], writes=[ps_t[pb]])
                kb.op("act", lambda e: e.activation(out=SQ[:], in_=pss[pa][:, 0:512], func=AF.Square), reads=[ps_t[pa]], writes=[SQ_t])
                kb.op("dve", lambda e: e.tensor_reduce(out=rs8[:], in_=SQ[:].rearrange("p (a b) -> p a b", a=8),
                                                       axis=mybir.AxisListType.X, op=ALU.add), reads=[SQ_t], writes=[rs8_t])
                kb.op("act", lambda e: e.activation(out=rs8[:], in_=rs8[:], func=AF.Sqrt, scale=1.0 / 64, bias=cs.eps[:, 0:1]),
                      reads=[rs8_t, cs.t], writes=[rs8_t])
                kb.op("dve", lambda e: e.reciprocal(out=rs8[:], in_=rs8[:]), reads=[rs8_t], writes=[rs8_t])
                kb.op("dve", lambda e: e.tensor_tensor(out=XN[:].rearrange("p (a b) -> p a b", a=8),
                                                       in0=pss[pa][:, 0:512].rearrange("p (a b) -> p a b", a=8),
                                                       in1=rs8[:].unsqueeze(2).broadcast_to([128, 8, 64]), op=ALU.mult),
                      reads=[ps_t[pa], rs8_t], writes=[XN_t])
                kb.op("pool", lambda e: e.tensor_tensor(out=XN[:], in0=XN[:], in1=qkg[:], op=ALU.mult), reads=[XN_t, kc_t], writes=[XN_t])
                if tl >= 2:
                    lt = tl - 2
                    xv = XN[:].rearrange("p (g h x d) -> p g h x d", g=8, h=2, x=2)
                    yv = Y2[:].rearrange("p (g h x d) -> p g h x d", g=8, h=2, x=2)
                    sv = rope[:, lt, 1, :].rearrange("p (h x d) -> p h x d", h=2, x=2)
                    kb.op("dve", lambda e, lt=lt: e.tensor_tensor(out=Y1[:].rearrange("p (a b) -> p a b", a=8),
                                                                 in0=XN[:].rearrange("p (a b) -> p a b", a=8),
                                                                 in1=rope[:, lt:lt + 1, 0, :].broadcast_to([128, 8, 64]), op=ALU.mult),
                          reads=[XN_t, kc_t], writes=[Y1_t])
                    kb.op("dve", lambda e, xv=xv, yv=yv, sv=sv: e.tensor_tensor(
                        out=yv[:, :, :, 0, :], in0=xv[:, :, :, 1, :],
                        in1=sv[:, :, 0, :].unsqueeze(1).broadcast_to([128, 8, 2, 16]), op=ALU.mult),
                        reads=[XN_t, kc_t], writes=[Y2_t])
                    kb.op("dve", lambda e, xv=xv, yv=yv, sv=sv: e.tensor_tensor(
                        out=yv[:, :, :, 1, :], in0=xv[:, :, :, 0, :],
                        in1=sv[:, :, 1, :].unsqueeze(1).broadcast_to([128, 8, 2, 16]), op=ALU.mult),
                        reads=[XN_t, kc_t], writes=[Y2_t])
                    kb.op("pool", lambda e: e.tensor_tensor(out=YB[:], in0=Y1[:], in1=Y2[:], op=ALU.add),
                          reads=[Y1_t, Y2_t], writes=[YB_t])
                else:
                    kb.op("act", lambda e: e.activation(out=YB[:], in_=XN[:], func=AF.Identity), reads=[XN_t], writes=[YB_t])
                kb.group("pe", [lambda e, a=a: e.transpose(psT[:, a * 128:(a + 1) * 128], YB[:, a * 128:(a + 1) * 128], ident[:])
                                for a in range(4)], reads=[YB_t, k_t], writes=[psT_t])
                kb.op("act", lambda e, gc=gc: e.activation(out=QKT[:, :, gc:gc + 128],
                                                           in_=psT[:, 0:256].rearrange("p (a b) -> p a b", a=2), func=AF.Identity),
                      reads=[psT_t], writes=[qkt_t[tl]])
                for hh in range(2):
                    for mm in range(2):
                        kb.op("dve" if (hh + mm) % 2 else "act",
                              (lambda e, hh=hh, mm=mm, gc=gc: e.tensor_copy(
                                  out=KTZ[mm * 64:(mm + 1) * 64, hh * 2 + mm, gc:gc + 128],
                                  in_=psT[mm * 64:(mm + 1) * 64, (2 + hh) * 128:(3 + hh) * 128])) if (hh + mm) % 2 else
                              (lambda e, hh=hh, mm=mm, gc=gc: e.activation(
                                  out=KTZ[mm * 64:(mm + 1) * 64, hh * 2 + mm, gc:gc + 128],
                                  in_=psT[mm * 64:(mm + 1) * 64, (2 + hh) * 128:(3 + hh) * 128], func=AF.Identity)),
                              reads=[psT_t], writes=[qkt_t[tl]])
                kb.op("act", lambda e, tl=tl: e.activation(out=VA[:, tl, :, 0:128],
                                                           in_=pss[pb][:, 0:256].rearrange("p (a b) -> p a b", a=2), func=AF.Identity),
                      reads=[ps_t[pb]], writes=[va_t[tl]])

        T0 = sbx("T0", [128, 4, 128], F32); T0_t = Tok()
        OSB = [sbx(f"OSB{i}", [128, 4, 132], F32) for i in range(2)]; OSB_t = [Tok(), Tok()]
        oi = 0
        for h in range(2):
            for (q0, nq, ktiles) in QBLK2:
                nqt = nq // 128
                qtl = [q0 // 128 + i for i in range(nqt)]
                nk = len(ktiles)
                for m in range(2):
                    SB = [0, 1, 6]

                    def st_fn(ii, h=h, q0=q0, nq=nq, ktiles=ktiles, m=m):
                        kt = ktiles[ii]
                        bnk = SB[ii % 3]
                        return lambda e: e.matmul(
                            pss[bnk][:, 0:nq], lhsT=KTZ[:, h * 2 + m, kt * 128:(kt + 1) * 128],
                            rhs=QKT[:, h, q0:q0 + nq], start=True, stop=True)
                    qreads = [qkt_t[x] for x in qtl]
                    for pre in range(min(2, nk)):
                        kb.op("pe", st_fn(pre), reads=[qkt_t[ktiles[pre]]] + qreads, writes=[ps_t[SB[pre % 3]]])
                    for ii in range(nk):
                        kt = ktiles[ii]
                        bnk = SB[ii % 3]
                        pb = ii % 3
                        kb.op("act", lambda e, bnk=bnk, pb=pb, nq=nq: e.activation(out=PT[pb][:, 0:nq], in_=pss[bnk][:, 0:nq], func=AF.Exp, scale=0.125),
                              reads=[ps_t[bnk]], writes=[PT_t[pb]])
                        fns = [lambda e, qt=qt, pb=pb, kt=kt, ii=ii, nk=nk, h=h: e.matmul(
                            pss[2 + qt][:, 0:129], lhsT=PT[pb][:, qt * 128:(qt + 1) * 128],
                            rhs=VA[:, kt, h, 0:129], start=(ii == 0), stop=(ii == nk - 1)) for qt in range(nqt)]
                        rd = [PT_t[pb], va_t[kt]]
                        wr = [ps_t[2 + qt] for qt in range(nqt)]
                        if ii + 2 < nk:
                            fns.append(st_fn(ii + 2))
                            rd += [qkt_t[ktiles[ii + 2]]] + qreads
                            wr.append(ps_t[SB[(ii + 2) % 3]])
                        kb.group("pe", fns, reads=rd, writes=wr)
                    for qt in range(nqt):
                        kb.op("dve", lambda e, qt=qt, m=m: e.tensor_copy(out=OSB[m][:, qt, 0:129], in_=pss[2 + qt][:, 0:129]),
                              reads=[ps_t[2 + qt]], writes=[OSB_t[m]])
                    for qt in range(nqt):
                        kb.op("dve", lambda e, qt=qt, m=m: e.reciprocal(out=rz[:, 0:1], in_=OSB[m][:, qt, 128:129]), reads=[OSB_t[m]], writes=[rz_t])
                        if m == 0:
                            kb.op("pool", lambda e, qt=qt, m=m: e.tensor_scalar(out=T0[:, qt, :], in0=OSB[m][:, qt, 0:128], scalar1=rz[:, 0:1], scalar2=None,
                                                                                op0=ALU.mult),
                                  reads=[OSB_t[m], rz_t], writes=[T0_t])
                            continue
                        kb.op("dve", lambda e: e.tensor_tensor(out=rz[:, 2:3], in0=rz[:, 0:1], in1=lam[:, 4:5], op=ALU.mult),
                              reads=[rz_t, lam_t], writes=[rz_t])
                        kb.op("dve", lambda e, qt=qt, m=m: e.scalar_tensor_tensor(out=T2[:, 0:128], in0=OSB[m][:, qt, 0:128], scalar=rz[:, 2:3],
                                                                                 in1=T0[:, qt, :], op0=ALU.mult, op1=ALU.add),
                              reads=[OSB_t[m], rz_t, T0_t], writes=[T2_t])
                        kb.op("act", lambda e: e.activation(out=junk[:, 0:128], in_=T2[:, 0:128], func=AF.Square, accum_out=ss[:, 0:1]),
                              reads=[T2_t], writes=[junk_t, ss_t])
                        kb.op("act", lambda e: e.activation(out=ss[:, 1:2], in_=ss[:, 0:1], func=AF.Sqrt, scale=1.0 / 128, bias=cs.eps[:, 0:1]),
                              reads=[ss_t, cs.t], writes=[ss_t])
                        kb.op("dve", lambda e: e.reciprocal(out=ss[:, 1:2], in_=ss[:, 1:2]), reads=[ss_t], writes=[ss_t])
                        kb.op("dve", lambda e: e.scalar_tensor_tensor(out=onb[:, 0:128], in0=T2[:, 0:128], scalar=ss[:, 1:2], in1=subg[:],
                                                                      op0=ALU.mult, op1=ALU.mult),
                              reads=[T2_t, ss_t, kc_t], writes=[onb_t])
                        kb.op("pe", lambda e: e.transpose(psT[:, 0:128], onb[:, 0:128], ident[:]), reads=[onb_t, k_t], writes=[psT_t])
                        oj = oi % 2
                        oi += 1
                        kb.op("act", lambda e, oj=oj: e.activation(out=ost[oj][:, 0, :], in_=psT[:, 0:128], func=AF.Identity),
                              reads=[psT_t], writes=[ost_t[oj]])
                        qa = q0 + qt * 128
                        kb.dma("sp", out_d[:, h, qa:qa + 128], ost[oj][:, 0, :], reads=[ost_t[oj]], is_output=True)
        kb.barrier()

    QM = kb.sb("QM", [128, TB], BF16); KM = kb.sb("KM", [128, TB], BF16); qm_t = Tok()
    VA2 = kb.sb("VA2", [128, NTILE, 260], BF16); va2_t = [Tok() for _ in range(NTILE)]
    SO = kb.sb("SO", [128, NTILE, 256], BF16); so_t = [Tok() for _ in range(NTILE)]
    G4 = kb.sb("G4", [128, NTILE, 4], F32); g4_t = Tok()
    mlgb = kb.sb("mlgb_s", [128, 4], F32); mlng = kb.sb("mlng_s", [128, 256], F32); mlcw = kb.sb("mlcw_s", [128, 2, 3], F32)
    km_t = Tok()
    for dst, src in ((mlgb, mlgb_d), (mlng, mlng_d), (mlcw, mlcw_d)):
        kb.dma("sp", dst[:], src, writes=[km_t])
    kb.op("pool", lambda e: e.memset(VA2[:], 1.0), writes=va2_t)
    with contextlib.ExitStack() as st:
        def sbx(name, shape, dt):
            return st.enter_context(nc.sbuf_tensor(f"{name}_u{kb.uid}", list(shape), dt)).ap()
        wM = sbx("wM_s", [128, NCH, 772], BF16); wM_t = Tok()
        RQ = sbx("RQ", [128, 2, TB + 4], F32); RQ_t = Tok()
        CV = sbx("CV", [128, SEQ], F32); CV_t = Tok()
        g4s = sbx("g4s", [128, 4], F32); g4s_t = Tok()
        kb.dma("pool", wM[:], wM_d, writes=[wM_t])
        kb.op("pool", lambda e: e.memset(RQ[:], 0.0), writes=[RQ_t])

        def ucol(tok):
            return 1 + tok if tok < CTX else 3 + tok
        for bidx, (t0, nt) in enumerate(ABLK):
            n = nt * 128
            g0 = t0 * 128
            hb, hb_t = hbs[bidx % 2], hbts[bidx % 2]
            kb.dma("sp", hb[:, :, 0:n], h1_d[:, :, g0:g0 + n], writes=[hb_t])
            for w in range(2):
                kb.group("pe", [lambda e, k=k, w=w, n=n: e.matmul(pss[w][:, 0:n], lhsT=wM[:, k, w * 128:(w + 1) * 128], rhs=hb[:, k, 0:n],
                                                                 start=(k == 0), stop=(k == NCH - 1)) for k in range(NCH)],
                         reads=[wM_t, hb_t], writes=[ps_t[w]])
                uc = ucol(g0)
                kb.op("act", lambda e, w=w, n=n, uc=uc: e.activation(out=RQ[:, w, uc:uc + n], in_=pss[w][:, 0:n], func=AF.Identity),
                      reads=[ps_t[w]], writes=[RQ_t])
            for ti in range(nt):
                tl = t0 + ti
                c0 = ti * 128
                kb.group("pe", [lambda e, k=k, c0=c0: e.matmul(pss[2][:, 0:512], lhsT=hb[:, k, c0:c0 + 128], rhs=wM[:, k, 256:768],
                                                              start=(k == 0), stop=(k == NCH - 1)) for k in range(NCH)],
                         reads=[wM_t, hb_t], writes=[ps_t[2]])
                kb.group("pe", [lambda e, k=k, c0=c0: e.matmul(pss[3][:, 0:4], lhsT=hb[:, k, c0:c0 + 128], rhs=wM[:, k, 768:772],
                                                              start=(k == 0), stop=(k == NCH - 1)) for k in range(NCH)],
                         reads=[wM_t, hb_t], writes=[ps_t[3]])
                kb.op("act", lambda e, tl=tl: e.activation(out=VA2[:, tl, 0:256], in_=pss[2][:, 0:256], func=AF.Identity),
                      reads=[ps_t[2]], writes=[va2_t[tl]])
                kb.op("act", lambda e: e.activation(out=T1[:], in_=pss[2][:, 256:512], func=AF.Sigmoid), reads=[ps_t[2]], writes=[T1_t])
                kb.op("pool", lambda e, tl=tl: e.tensor_tensor(out=SO[:, tl, :], in0=T1[:], in1=mlng[:], op=ALU.mult),
                      reads=[T1_t, km_t], writes=[so_t[tl]])
                kb.op("dve", lambda e, tl=tl: e.tensor_tensor(out=G4[:, tl, :], in0=pss[3][:, 0:4], in1=mlgb[:], op=ALU.add),
                      reads=[ps_t[3], km_t], writes=[g4_t])
                kb.op("act", lambda e, tl=tl: e.activation(out=g4s[:], in_=G4[:, tl, :], func=AF.Exp, scale=-1.0), reads=[g4_t], writes=[g4s_t])
                kb.op("act", lambda e: e.activation(out=g4s[:], in_=g4s[:], func=AF.Ln, bias=1.0, scale=1.0), reads=[g4s_t], writes=[g4s_t])
                for col in (1, 3):
                    kb.op("dve", lambda e, tl=tl, col=col: e.tensor_scalar(out=G4[:, tl, col:col + 1], in0=g4s[:, col:col + 1], scalar1=-1.0,
                                                                          scalar2=None, op0=ALU.mult),
                          reads=[g4s_t, g4_t], writes=[g4_t])
        for w, dst, scl in ((0, QM, 1.0), (1, KM, KSCALE)):
            for (g0, n) in ((0, CTX), (CTX, SEQ)):
                uc = ucol(g0)
                kb.op("dve", lambda e, w=w, n=n, uc=uc: e.tensor_scalar(out=CV[:, 0:n], in0=RQ[:, w, uc:uc + n], scalar1=mlcw[:, w, 1:2],
                                                                       scalar2=None, op0=ALU.mult), reads=[RQ_t, km_t], writes=[CV_t])
                kb.op("dve", lambda e, w=w, n=n, uc=uc: e.scalar_tensor_tensor(out=CV[:, 0:n], in0=RQ[:, w, uc - 1:uc - 1 + n],
                                                                              scalar=mlcw[:, w, 0:1], in1=CV[:, 0:n], op0=ALU.mult, op1=ALU.add),
                      reads=[RQ_t, km_t, CV_t], writes=[CV_t])
                kb.op("dve", lambda e, w=w, n=n, uc=uc: e.scalar_tensor_tensor(out=CV[:, 0:n], in0=RQ[:, w, uc + 1:uc + 1 + n],
                                                                              scalar=mlcw[:, w, 2:3], in1=CV[:, 0:n], op0=ALU.mult, op1=ALU.add),
                      reads=[RQ_t, km_t, CV_t], writes=[CV_t])
                kb.op("act", lambda e, n=n: e.activation(out=CV[:, 0:n], in_=CV[:, 0:n], func=AF.Silu), reads=[CV_t], writes=[CV_t])
                kb.op("dve", lambda e, dst=dst, g0=g0, n=n, scl=scl: e.tensor_scalar(out=dst[:, g0:g0 + n], in0=CV[:, 0:n], scalar1=scl,
                                                                                    scalar2=None, op0=ALU.mult), reads=[CV_t], writes=[qm_t])
        kb.barrier()

    QP = [kb.sb(f"QP{d}", [128, TB], BF16) for d in range(2)]
    STm = [kb.sb(f"STm{d}", [128, NTILE, 128], BF16) for d in range(2)]
    KP = [kb.sb(f"KP{d}", [128, NTILE, 128], BF16) for d in range(2)]
    CF = kb.sb("CF", [128, NTILE, 260], BF16); cf_t = [Tok() for _ in range(NTILE)]
    eL = kb.sb("eL", [128, 2, NTILE], F32); eL_t = Tok()
    ch_t = [Tok() for _ in range(NTILE)]
    LF = [kb.sb(f"LF{d}", [128, 128], F32) for d in range(2)]; LF_t = [Tok(), Tok()]
    TM = [kb.sb(f"TM{d}", [128, 128], F32) for d in range(2)]; TM_t = [Tok(), Tok()]
    EB = [kb.sb(f"EB{d}", [128, 128], F32) for d in range(2)]; EB_t = [Tok(), Tok()]
    cj = kb.sb("cj", [128, 8], F32); cj_t = Tok()
    for c in range(NTILE):
        gc = c * 128
        for d in range(2):
            kb.op("dve", lambda e, d=d, c=c: e.tensor_scalar(out=LF[d][:], in0=cs.ones[:], scalar1=G4[:, c, 2 * d + 1:2 * d + 2], scalar2=None,
                                                            op0=ALU.mult), reads=[g4_t, cs.t], writes=[LF_t[d]])
        kb.group("pe", [lambda e, d=d: e.matmul(pss[4][:, d * 128:(d + 1) * 128], lhsT=LF[d][:], rhs=msk[:, d, :], start=True, stop=True)
                        for d in range(2)] +
                 [lambda e, d=d, c=c: e.matmul(pss[4][:, 256 + 4 * d:260 + 4 * d], lhsT=msk[:, d, :], rhs=G4[:, c, :], start=True, stop=True)
                  for d in range(2)],
                 reads=LF_t + [k_t, g4_t], writes=[ps_t[4]])
        kb.op("pe", lambda e, gc=gc: e.matmul(pss[5][:, 0:128], lhsT=KM[:, gc:gc + 128], rhs=QM[:, gc:gc + 128], start=True, stop=True),
              reads=[qm_t], writes=[ps_t[5]])
        kb.op("pe", lambda e, gc=gc: e.transpose(psT[:, 0:128], KM[:, gc:gc + 128], ident[:]), reads=[qm_t, k_t], writes=[psT_t])
        for d in range(2):
            last = 127 if d == 0 else 0
            kb.op("dve", lambda e, d=d, c=c: e.tensor_tensor(out=cj[:, d:d + 1], in0=G4[:, c, 2 * d:2 * d + 1],
                                                            in1=pss[4][:, 256 + 4 * d + 2 * d + 1:256 + 4 * d + 2 * d + 2], op=ALU.subtract),
                  reads=[ps_t[4], g4_t], writes=[cj_t])
            kb.op("dve", lambda e, d=d, last=last: e.tensor_copy(out=cj[:, 2 + d:3 + d], in_=pss[4][:, d * 128 + last:d * 128 + last + 1]),
                  reads=[ps_t[4]], writes=[cj_t])
            kb.op("dve", lambda e, d=d: e.tensor_tensor(out=TM[d][:], in0=pss[4][:, d * 128:(d + 1) * 128], in1=msk[:, 2 + d, :], op=ALU.add),
                  reads=[ps_t[4], k_t], writes=[TM_t[d]])
            kb.op("act", lambda e, d=d: e.activation(out=TM[d][:], in_=TM[d][:], func=AF.Exp, bias=cj[:, d:d + 1], scale=1.0),
                  reads=[TM_t[d], cj_t], writes=[TM_t[d]])
            kb.op("act", lambda e, d=d: e.activation(out=EB[d][:], in_=pss[4][:, d * 128:(d + 1) * 128], func=AF.Exp),
                  reads=[ps_t[4]], writes=[EB_t[d]])
            kb.op("act", lambda e, d=d: e.activation(out=cj[:, 4 + d:5 + d], in_=cj[:, d:d + 1], func=AF.Exp, bias=cj[:, 2 + d:3 + d], scale=1.0),
                  reads=[cj_t], writes=[cj_t])
            kb.op("pool", lambda e, d=d, c=c, last=last: e.tensor_copy(out=eL[:, d, c:c + 1], in_=EB[d][:, last:last + 1]),
                  reads=[EB_t[d]], writes=[eL_t])
            kb.op("dve", lambda e, d=d, c=c: e.tensor_tensor(out=STm[d][:, c, :], in0=pss[5][:, 0:128], in1=TM[d][:], op=ALU.mult),
                  reads=[ps_t[5], TM_t[d]], writes=[ch_t[c]])
            kb.op("dve", lambda e, d=d, gc=gc: e.tensor_tensor(out=QP[d][:, gc:gc + 128], in0=QM[:, gc:gc + 128], in1=EB[d][:], op=ALU.mult),
                  reads=[qm_t, EB_t[d]], writes=[ch_t[c]])
            kb.op("dve", lambda e, d=d, c=c: e.tensor_scalar(out=KP[d][:, c, :], in0=psT[:, 0:128], scalar1=cj[:, 4 + d:5 + d], scalar2=None,
                                                            op0=ALU.mult), reads=[psT_t, cj_t], writes=[ch_t[c]])

    Cst = kb.sb("Cst", [128, 260], F32); Cst_t = Tok()
    kb.op("pool", lambda e: e.memset(Cst[:], 0.0), writes=[Cst_t])
    for c in range(NTILE):
        kb.op("act", lambda e, c=c: e.activation(out=CF[:, c, :], in_=Cst[:], func=AF.Identity), reads=[Cst_t], writes=[cf_t[c]])
        if c == NTILE - 1:
            break
        kb.op("pe", lambda e, c=c: e.matmul(pss[0][:, 0:257], lhsT=KP[0][:, c, :], rhs=VA2[:, c, 0:257], start=True, stop=True),
              reads=[ch_t[c], va2_t[c]], writes=[ps_t[0]])
        kb.op("dve", lambda e, c=c: e.scalar_tensor_tensor(out=Cst[:, 0:257], in0=Cst[:, 0:257], scalar=eL[:, 0, c:c + 1], in1=pss[0][:, 0:257],
                                                          op0=ALU.mult, op1=ALU.add), reads=[ps_t[0], eL_t, Cst_t], writes=[Cst_t])
    Cb = kb.sb("Cb", [128, 260], F32); Cb_t = Tok()
    Cbb = kb.sb("Cbb", [128, 260], BF16); Cbb_t = Tok()
    kb.op("pool", lambda e: e.memset(Cb[:], 0.0), writes=[Cb_t])
    kb.op("pool", lambda e: e.memset(Cbb[:], 0.0), writes=[Cbb_t])
    for oi, c in enumerate(BWD_ORDER):
        gc = c * 128
        kb.group("pe", [lambda e, c=c, gc=gc: e.matmul(pss[1][:, 0:257], lhsT=QP[0][:, gc:gc + 128], rhs=CF[:, c, 0:257], start=True, stop=False),
                        lambda e, c=c, gc=gc: e.matmul(pss[1][:, 0:257], lhsT=STm[0][:, c, :], rhs=VA2[:, c, 0:257], start=False, stop=True)],
                 reads=[ch_t[c], cf_t[c], va2_t[c]], writes=[ps_t[1]])
        kb.group("pe", [lambda e, c=c, gc=gc: e.matmul(pss[2][:, 0:257], lhsT=QP[1][:, gc:gc + 128], rhs=Cbb[:, 0:257], start=True, stop=False),
                        lambda e, c=c, gc=gc: e.matmul(pss[2][:, 0:257], lhsT=STm[1][:, c, :], rhs=VA2[:, c, 0:257], start=False, stop=True)],
                 reads=[ch_t[c], Cbb_t, va2_t[c]], writes=[ps_t[2]])
        kb.op("pe", lambda e, c=c: e.matmul(pss[3][:, 0:257], lhsT=KP[1][:, c, :], rhs=VA2[:, c, 0:257], start=True, stop=True),
              reads=[ch_t[c], va2_t[c]], writes=[ps_t[3]])
        kb.op("dve", lambda e, c=c: e.scalar_tensor_tensor(out=Cb[:, 0:257], in0=Cb[:, 0:257], scalar=eL[:, 1, c:c + 1], in1=pss[3][:, 0:257],
                                                          op0=ALU.mult, op1=ALU.add), reads=[ps_t[3], eL_t, Cb_t], writes=[Cb_t])
        kb.op("act", lambda e: e.activation(out=Cbb[:], in_=Cb[:], func=AF.Identity), reads=[Cb_t], writes=[Cbb_t])
        for d, bk in ((0, 1), (1, 2)):
            kb.op("act", lambda e, d=d, bk=bk: e.activation(out=ss[:, 2 + d:3 + d], in_=pss[bk][:, 256:257], func=AF.Abs),
                  reads=[ps_t[bk]], writes=[ss_t])
            kb.op("dve", lambda e, d=d: e.tensor_scalar(out=ss[:, 2 + d:3 + d], in0=ss[:, 2 + d:3 + d], scalar1=1.0, scalar2=None,
                                                       op0=ALU.max), reads=[ss_t], writes=[ss_t])
        kb.op("dve", lambda e: e.reciprocal(out=ss[:, 2:4], in_=ss[:, 2:4]), reads=[ss_t], writes=[ss_t])
        kb.op("act", lambda e: e.activation(out=T1[:], in_=pss[1][:, 0:256], func=AF.Identity, scale=ss[:, 2:3]),
              reads=[ps_t[1], ss_t], writes=[T1_t])
        kb.op("dve", lambda e: e.scalar_tensor_tensor(out=T2[:], in0=pss[2][:, 0:256], scalar=ss[:, 3:4], in1=T1[:], op0=ALU.mult, op1=ALU.add),
              reads=[ps_t[2], ss_t, T1_t], writes=[T2_t])
        kb.op("act", lambda e: e.activation(out=junk[:], in_=T2[:], func=AF.Square, accum_out=ss[:, 0:1]), reads=[T2_t], writes=[junk_t, ss_t])
        kb.op("act", lambda e: e.activation(out=ss[:, 1:2], in_=ss[:, 0:1], func=AF.Sqrt, scale=1.0 / 256, bias=cs.eps[:, 0:1]),
              reads=[ss_t, cs.t], writes=[ss_t])
        kb.op("dve", lambda e: e.reciprocal(out=ss[:, 1:2], in_=ss[:, 1:2]), reads=[ss_t], writes=[ss_t])
        kb.op("dve", lambda e, c=c: e.scalar_tensor_tensor(out=onb[:], in0=T2[:], scalar=ss[:, 1:2], in1=SO[:, c, :], op0=ALU.mult, op1=ALU.mult),
              reads=[T2_t, ss_t, so_t[c]], writes=[onb_t])
        kb.group("pe", [lambda e, hh=hh: e.transpose(psT[:, hh * 128:(hh + 1) * 128], onb[:, hh * 128:(hh + 1) * 128], ident[:])
                        for hh in range(2)], reads=[onb_t, k_t], writes=[psT_t])
        oj = oi % 2
        kb.op("act", lambda e, oj=oj: e.activation(out=ost[oj][:].rearrange("p a b -> p (a b)"), in_=psT[:, 0:256], func=AF.Identity),
              reads=[psT_t], writes=[ost_t[oj]])
        kb.dma("sp", out_d[:, 2:4, gc:gc + 128], ost[oj][:], reads=[ost_t[oj]], is_output=True)
    return kb.end_phase()


def host_rope():
    t = np.arange(SEQ)
    row = (t // 64).astype(np.float32)
    col = (t % 64).astype(np.float32)
    inv = np.power(np.float32(10000.0), -np.arange(16, dtype=np.float32) / np.float32(16)).astype(np.float32)
    ar = (row[:, None] * inv).astype(np.float32)
    ac = (col[:, None] * inv).astype(np.float32)
    cr, sr, cc, sc = np.cos(ar), np.sin(ar), np.cos(ac), np.sin(ac)
    C = np.concatenate([cr, cr, cc, cc], axis=1)
    S = np.concatenate([-sr, sr, -sc, sc], axis=1)
    tab = np.stack([C, S], axis=1).astype(np.float32)
    return np.ascontiguousarray(tab.reshape(32, 128, 2, 64).transpose(1, 0, 2, 3))


def host_masks_odd():
    j = np.arange(128)[:, None]
    i = np.arange(128)[None, :]
    mf = (j <= i).astype(np.float32)
    mb = (j >= i).astype(np.float32)
    return np.ascontiguousarray(np.stack([mf, mb, (1 - mf) * -30000.0, (1 - mb) * -30000.0], axis=1).astype(np.float32))


def bc(v, n=128):
    return np.ascontiguousarray(np.broadcast_to(np.asarray(v, np.float32).reshape(1, -1), (n, v.size)))


def host_Aodd_inputs(g, layer, w_in, qn_g, kn_g, lam_p, subln_g, ml_conv_w, ml_gate_b, ml_norm_g):
    colsD = np.concatenate([np.arange(256 * g, 256 * g + 256), np.arange(1024 + 256 * g, 1024 + 256 * g + 256),
                            np.arange(2048 + 256 * g, 2048 + 256 * g + 256)])
    gidx = [0 * 8 + 0 * 4 + g, 0 * 8 + 1 * 4 + g, 1 * 8 + 0 * 4 + g, 1 * 8 + 1 * 4 + g]
    colsM = np.concatenate([np.arange(3072 + 128 * g, 3072 + 128 * g + 128), np.arange(3584 + 128 * g, 3584 + 128 * g + 128),
                            np.arange(4096 + 256 * g, 4096 + 256 * g + 256), np.arange(5120 + 256 * g, 5120 + 256 * g + 256),
                            6144 + np.array(gidx)])
    qkg = bc(np.concatenate([np.tile(qn_g, 4), np.tile(kn_g, 4)]))
    mlcw = np.stack([ml_conv_w[:, 128 * g:128 * g + 128], ml_conv_w[:, 512 + 128 * g:512 + 128 * g + 128]], axis=0)
    mlcw = np.ascontiguousarray(mlcw.transpose(2, 0, 1))
    return {"wD": wfm(w_in[:, colsD]), "wM": wfm(w_in[:, colsM]), "qkg": qkg, "rope": host_rope(),
            "lamp": np.ascontiguousarray(np.broadcast_to(lam_p[None], (128, 4, 64))), "subg": bc(subln_g),
            "subgc": np.ascontiguousarray(subln_g.reshape(128, 1).astype(np.float32)),
            "mlcw": mlcw, "mlgb": bc(ml_gate_b[gidx]), "mlng": bc(ml_norm_g), "masks": host_masks_odd(),
            "ident": np.eye(128, dtype=np.float32).astype(ml_dtypes.bfloat16)}


def emit_Mfull(kb, condT2, adaw, adab, mods_out):
    sc = kb.sb("mf_sc", [128, NCH, 2], F32); sc_t = Tok()
    bsb = kb.sb("mf_b", [128, DEPTH, 96], F32); b_t = Tok()
    res = kb.sb("mf_res", [128, DEPTH, 96, 2], F32); res_t = Tok()
    wb = [kb.sb(f"mf_w{i}", [128, NCH, 768], F32) for i in range(2)]
    w_t = [Tok() for _ in range(2)]
    pst = [kb.ps(f"mf_ps{i}", [128, 512], F32) for i in range(2)]
    ps_t = [Tok() for _ in range(2)]
    kb.dma("sp", sc[:], condT2, writes=[sc_t])
    kb.dma("sp", bsb[:], adab, writes=[b_t])
    kb.op("act", lambda e: e.activation(out=sc[:], in_=sc[:], func=AF.Silu), reads=[sc_t], writes=[sc_t])
    it = 0
    for l in range(DEPTH):
        for blk in range(16):
            i = it % 2
            it += 1
            kb.dma("sp", wb[i][:], adaw[l, blk], writes=[w_t[i]])
            for mm in range(6):
                m = blk * 6 + mm
                j = m % 2
                kb.group("pe", [lambda e, k=k, mm=mm, i=i, j=j: e.matmul(pst[j][:, 0:2], lhsT=wb[i][:, k, mm * 128:(mm + 1) * 128],
                                                                          rhs=sc[:, k, 0:2], start=(k == 0), stop=(k == NCH - 1))
                                 for k in range(NCH)],
                         reads=[w_t[i], sc_t], writes=[ps_t[j]])
                kb.op("dve", lambda e, l=l, m=m, j=j: e.tensor_scalar(out=res[:, l, m, :], in0=pst[j][:, 0:2],
                                                                       scalar1=bsb[:, l, m:m + 1], scalar2=None, op0=ALU.add),
                      reads=[ps_t[j], b_t], writes=[res_t])
    for l in range(DEPTH):
        kb.dma("sp", mods_out[l], res[:, l, :, :].rearrange("p (s c) r -> p s c r", s=6), reads=[res_t])


def build_fused():
    kb = KB()
    nc = kb.nc

    def din(name, shape, dt=F32):
        return nc.dram_tensor(name, list(shape), dt, kind="ExternalInput").ap()

    def dint(name, shape, dt=F32):
        return nc.dram_tensor(name, list(shape), dt, kind="Internal").ap()

    xT = din("xT", [4, 128, NCH, NTOK])
    condT2 = din("condT2", [128, NCH, 2])
    adaw = din("adaw", [DEPTH, 16, 128, NCH, 768])
    adab = din("adab", [128, DEPTH, 96])
    ng1 = din("ng1", [DEPTH, 128, NCH])
    ng2 = din("ng2", [DEPTH, 128, NCH])
    zcol = din("zcol", [128, NCH, 1], BF16)
    wout = din("wout", [DEPTH, NCH, 128, NCH, 128])
    wup = din("wup", [DEPTH, FCH, 128, NCH, 256])
    wdn = din("wdn", [DEPTH, NQ, NCH, 128, QF, 128])
    convw = din("convw", [DEPTH, 128, FCH, 4])
    ident = din("ident", [128, 128], BF16)
    e_wA = din("e_wA", [2, 4, 128, NCH, 896]); e_wC = din("e_wC", [2, 4, 128, NCH, 768])
    e_w2 = din("e_w2", [2, 4, 128, 2, 128]); e_gb = din("e_gb", [2, 4, 128, 256]); e_gn = din("e_gn", [2, 128, 256])
    e_scw = din("e_scw", [2, 4, 128, 2, 3]); e_msk = din("e_msk", [128, 4, 128])
    o_wD = din("o_wD", [2, 4, 128, NCH, 768]); o_wM = din("o_wM", [2, 4, 128, NCH, 772])
    o_qkg = din("o_qkg", [2, 128, 512]); o_rope = din("o_rope", [128, 32, 2, 64]); o_lamp = din("o_lamp", [2, 128, 4, 64])
    o_subg = din("o_subg", [2, 128, 128]); o_subgc = din("o_subgc", [2, 128, 1]); o_mlcw = din("o_mlcw", [2, 4, 128, 2, 3]); o_mlgb = din("o_mlgb", [2, 4, 128, 4])
    o_mlng = din("o_mlng", [2, 128, 256]); o_msk = din("o_msk", [128, 4, 128])
    out = nc.dram_tensor("xoutT", [1, 128, NCH, NTOK], F32, kind="ExternalOutput").ap()
    MODS = dint("s_mods", [DEPTH, 128, 6, NCH, 2])
    X = dint("s_x", [4, 128, NCH, NTOK])
    XM = dint("s_xm", [4, 128, NCH, NTOK])
    H1L = dint("s_h1l", [4, 128, NCH, NTOK], BF16)
    H1G = dint("s_h1g", [128, NCH, TB], BF16)
    MIXG = dint("s_mixg", [4, 128, 4, TB], BF16)
    MIXL = dint("s_mixl", [128, NCH, NTOK], BF16)
    H2P = dint("s_h2p", [6, 128, NCH, NTOK], BF16)
    H2L = [H2P[j + 1] for j in range(4)]
    H2S = dint("s_h2s", [3, 128, NCH, NTOK], BF16)
    XMS = dint("s_xms", [1, 128, NCH, NTOK])
    zslot = din("zslot", [128, NCH, NTOK], BF16)
    H2H = dint("s_h2h", [128, NCH, HW_], BF16)

    def copies(pairs):
        for (dst, src) in pairs:
            kb.dma("sp", dst, src, slow=(dst.shape[-1] == 1))
        kb.barrier()

    kb.dma("sp", H2P[0], zslot)
    kb.dma("sp", H2P[5], zslot)
    with kb.phase({}):
        emit_Mfull(kb, condT2, adaw, adab, MODS)
    for j in range(4):
        with kb.phase({"xT": xT[j], "mods": MODS[0], "normg": ng1[0], "h1T": H1L[j]}):
            build_P0(kb=kb)
    for layer in range(DEPTH):
        i2 = layer // 2
        last = layer == DEPTH - 1
        copies([(H1G[:, :, 64 * j:64 * j + 64], H1L[j][:, :, 0:64]) for j in range(4)] +
               [(H1G[:, :, CTX + 1024 * j:CTX + 1024 * j + 1024], H1L[j][:, :, 64:NTOK]) for j in range(4)])
        for g in range(4):
            if layer % 2 == 0:
                io = {"h1T": H1G, "wA": e_wA[i2, g], "wC": e_wC[i2, g], "w2": e_w2[i2, g], "gbias": e_gb[i2, g], "gnorm": e_gn[i2],
                      "scw": e_scw[i2, g], "masks": e_msk, "ident": ident, "mixT": MIXG[g]}
                with kb.phase(io):
                    build_Aeven(kb=kb)
            else:
                io = {"h1T": H1G, "wD": o_wD[i2, g], "wM": o_wM[i2, g], "qkg": o_qkg[i2], "rope": o_rope, "lamp": o_lamp[i2],
                      "subg": o_subg[i2], "subgc": o_subgc[i2], "mlcw": o_mlcw[i2, g], "mlgb": o_mlgb[i2, g], "mlng": o_mlng[i2], "masks": o_msk,
                      "ident": ident, "mixT": MIXG[g]}
                with kb.phase(io):
                    build_Aodd(layer, kb=kb)
        for j in range(4):
            copies([(MIXL[:, 4 * g:4 * g + 4, 0:64], MIXG[g][:, :, 64 * j:64 * j + 64]) for g in range(4)] +
                   [(MIXL[:, 4 * g:4 * g + 4, 64:NTOK], MIXG[g][:, :, CTX + 1024 * j:CTX + 1024 * j + 1024]) for g in range(4)])
            xin = xT[j] if layer == 0 else X[j]
            with kb.phase({"xT": xin, "mixT": MIXL, "wout": wout[layer], "mods": MODS[layer], "normg": ng2[layer],
                           "xmidT": XM[j], "h2T": H2L[j]}):
                build_B1(kb=kb)
        if last:
            jv = nc.sync.snap(nc.sync.partition_id() % 4, min_val=0, max_val=3)
            kb.dma("sp", XMS, XM[bass.ds(jv, 1)])
            kb.dma("sp", H2S, H2P[bass.ds(jv, 3)])
            kb.barrier()
            copies([(H2H[:, :, 1:65], H2S[1][:, :, 0:64]), (H2H[:, :, 67:1091], H2S[1][:, :, 64:NTOK]),
                    (H2H[:, :, 0:1], H2S[0][:, :, 63:64]), (H2H[:, :, 66:67], H2S[0][:, :, NTOK - 1:NTOK]),
                    (H2H[:, :, 65:66], H2S[2][:, :, 0:1]), (H2H[:, :, 1091:1092], H2S[2][:, :, 64:65])])
            io = {"xmidT": XMS[0], "h2hT": H2H, "wup": wup[layer], "wdn": wdn[layer], "convw": convw[layer], "mods": MODS[layer],
                  "xoT": X[0]}
            with kb.phase(io):
                build_B2(True, kb=kb)
            continue
        for j in range(4):
            cp = [(H2H[:, :, 1:65], H2L[j][:, :, 0:64]), (H2H[:, :, 67:1091], H2L[j][:, :, 64:NTOK])]
            if j > 0:
                cp += [(H2H[:, :, 0:1], H2L[j - 1][:, :, 63:64]), (H2H[:, :, 66:67], H2L[j - 1][:, :, NTOK - 1:NTOK])]
            else:
                cp += [(H2H[:, :, 0:1], zcol), (H2H[:, :, 66:67], zcol)]
            if j < 3:
                cp += [(H2H[:, :, 65:66], H2L[j + 1][:, :, 0:1]), (H2H[:, :, 1091:1092], H2L[j + 1][:, :, 64:65])]
            else:
                cp += [(H2H[:, :, 65:66], zcol), (H2H[:, :, 1091:1092], zcol)]
            copies(cp)
            io = {"xmidT": XM[j], "h2hT": H2H, "wup": wup[layer], "wdn": wdn[layer], "convw": convw[layer], "mods": MODS[layer],
                  "xoT": X[j]}
            if not last:
                io.update({"normg": ng1[layer + 1], "modsn": MODS[layer + 1], "h1T": H1L[j]})
            with kb.phase(io):
                build_B2(last, kb=kb)
    kb.dma("sp", out[0], X[0], is_output=True)
    return kb.finish()


def kernel_fused(**inputs):
    I = {k: np.asarray(v) for k, v in inputs.items()}
    x, ctx = I["x"], I["ctx"]
    bf = ml_dtypes.bfloat16
    shared = {}
    shared["adaw"] = np.ascontiguousarray(I["ada_w"].reshape(DEPTH, NCH, 128, 16, 768).transpose(0, 3, 2, 1, 4))
    shared["adab"] = np.ascontiguousarray(I["ada_b"].reshape(DEPTH, 96, 128).transpose(2, 0, 1))
    shared["ng1"] = np.stack([vec_fm(I["norm1_g"][l]) for l in range(DEPTH)])
    shared["ng2"] = np.stack([vec_fm(I["norm2_g"][l]) for l in range(DEPTH)])
    shared["zcol"] = np.zeros((128, NCH, 1), bf)
    shared["zslot"] = np.zeros((128, NCH, NTOK), bf)
    shared["wout"] = np.stack([host_wout((I["ev_w_out"] if l % 2 == 0 else I["od_w_out"])[l // 2]) for l in range(DEPTH)])
    shared["wup"] = np.stack([host_wup(I["ffn_w_up"][l]) for l in range(DEPTH)])
    shared["wdn"] = np.stack([host_wdn(I["ffn_w_down"][l]) for l in range(DEPTH)])
    shared["convw"] = np.stack([host_convw(I["ffn_conv_w"][l], I["ffn_conv_b"][l]) for l in range(DEPTH)])
    shared["ident"] = np.eye(128, dtype=np.float32).astype(bf)
    ev = [[host_Aeven_inputs(g, I["ev_w_in"][i], I["gla_gate_w2"][i], I["gla_gate_b"][i], I["gla_norm_g"][i], I["sc_conv_w"][i])
           for g in range(4)] for i in range(2)]
    for key, src in (("e_wA", "wA"), ("e_wC", "wC"), ("e_w2", "w2"), ("e_gb", "gbias"), ("e_scw", "scw")):
        shared[key] = np.stack([np.stack([ev[i][g][src] for g in range(4)]) for i in range(2)])
    shared["e_gn"] = np.stack([ev[i][0]["gnorm"] for i in range(2)])
    shared["e_msk"] = host_masks()
    od = [[host_Aodd_inputs(g, 2 * i + 1, I["od_w_in"][i], I["da_qnorm_g"][i], I["da_knorm_g"][i], I["da_lambda"][i],
                            I["da_subln_g"][i], I["ml_conv_w"][i], I["ml_gate_b"][i], I["ml_norm_g"][i])
           for g in range(4)] for i in range(2)]
    for key, src in (("o_wD", "wD"), ("o_wM", "wM"), ("o_mlcw", "mlcw"), ("o_mlgb", "mlgb")):
        shared[key] = np.stack([np.stack([od[i][g][src] for g in range(4)]) for i in range(2)])
    for key, src in (("o_qkg", "qkg"), ("o_lamp", "lamp"), ("o_subg", "subg"), ("o_subgc", "subgc"), ("o_mlng", "mlng")):
        shared[key] = np.stack([od[i][0][src] for i in range(2)])
    shared["o_rope"] = od[0][0]["rope"]
    shared["o_msk"] = host_masks_odd()
    in_maps = []
    for core in range(8):
        b, j = divmod(core, 4)
        d = dict(shared)
        d["xT"] = np.stack([fm(core_tokens(x[b], ctx[b], jj)) for jj in range(4)])
        cond = np.stack([I["c_ctx"], I["c"][b]], axis=0)
        d["condT2"] = np.ascontiguousarray(cond.reshape(2, NCH, 128).transpose(2, 1, 0))
        in_maps.append(d)
    res = run(get_nc("fused", build_fused), in_maps)
    out = np.zeros((2, SEQ, D), np.float32)
    for core in range(8):
        b, j = divmod(core, 4)
        out[b, 1024 * j:1024 * j + 1024, :] = unfm(res[core]["xoutT"][0])[64:, :]
    return out
def kernel_unfused(**inputs):
    I = {k: np.asarray(v) for k, v in inputs.items()}
    x, ctx = I["x"], I["ctx"]
    mods, _ = host_M(I["c"], I["c_ctx"], I["ada_w"], I["ada_b"])
    cores = [divmod(c, 4) for c in range(8)]
    x_cores = [fm(core_tokens(x[b], ctx[b], j)) for (b, j) in cores]
    res = run(get_nc("P0", build_P0), [{"xT": x_cores[c], "mods": mods[0][cores[c][0]], "normg": vec_fm(I["norm1_g"][0])}
                                       for c in range(8)])
    h1 = [res[c]["h1T"] for c in range(8)]
    for layer in range(DEPTH):
        i2 = layer // 2
        last = layer == DEPTH - 1
        h1_b = [assemble_h1(h1, b) for b in range(2)]
        in_maps = []
        for c, (b, g) in enumerate(cores):
            if layer % 2 == 0:
                d = host_Aeven_inputs(g, I["ev_w_in"][i2], I["gla_gate_w2"][i2], I["gla_gate_b"][i2], I["gla_norm_g"][i2],
                                      I["sc_conv_w"][i2])
            else:
                d = host_Aodd_inputs(g, layer, I["od_w_in"][i2], I["da_qnorm_g"][i2], I["da_knorm_g"][i2], I["da_lambda"][i2],
                                     I["da_subln_g"][i2], I["ml_conv_w"][i2], I["ml_gate_b"][i2], I["ml_norm_g"][i2])
            d["h1T"] = h1_b[b]
            in_maps.append(d)
        nc = get_nc("Ae", build_Aeven) if layer % 2 == 0 else get_nc("Ao", build_Aodd, layer)
        res = run(nc, in_maps)
        mixes = [res[c]["mixT"] for c in range(8)]
        wout = host_wout((I["ev_w_out"] if layer % 2 == 0 else I["od_w_out"])[i2])
        in_maps = [{"xT": x_cores[c], "mixT": scatter_mix(mixes, b, j), "wout": wout, "mods": mods[layer][b],
                    "normg": vec_fm(I["norm2_g"][layer])} for c, (b, j) in enumerate(cores)]
        res = run(get_nc("B1", build_B1), in_maps)
        halo = host_halo([res[c]["h2T"] for c in range(8)])
        xm = [res[c]["xmidT"] for c in range(8)]
        wup = host_wup(I["ffn_w_up"][layer])
        wdn = host_wdn(I["ffn_w_down"][layer])
        cw = host_convw(I["ffn_conv_w"][layer], I["ffn_conv_b"][layer])
        in_maps = []
        for c, (b, j) in enumerate(cores):
            d = {"xmidT": xm[c], "h2hT": halo[c], "wup": wup, "wdn": wdn, "convw": cw, "mods": mods[layer][b]}
            if not last:
                d["normg"] = vec_fm(I["norm1_g"][layer + 1])
                d["modsn"] = mods[layer + 1][b]
            in_maps.append(d)
        res = run(get_nc("B2", build_B2, last), in_maps)
        x_cores = [res[c]["xoT"] for c in range(8)]
        if not last:
            h1 = [res[c]["h1T"] for c in range(8)]
    out = np.zeros((2, SEQ, D), np.float32)
    for c, (b, j) in enumerate(cores):
        out[b, 1024 * j:1024 * j + 1024, :] = unfm(x_cores[c])[64:, :]
    return out


FUSED = True


def kernel(**inputs):
    return kernel_fused(**inputs) if FUSED else kernel_unfused(**inputs)
```

```python
import numpy as np
import ml_dtypes
import concourse.bass as bass
import concourse.mybir as mybir
from concourse.bass_utils import run_bass_kernel_spmd

F32 = mybir.dt.float32
BF16 = mybir.dt.bfloat16
AF = mybir.ActivationFunctionType
ALU = mybir.AluOpType

D = 2048
NCH = 16
DFF = 5632
FCH = 44
DEPTH = 4
SEQ = 4096
CTX = 256
NTOK = 1088
HW_ = 1092
EPS = 1e-6
NTILE = 34
TB = 4352


class Tok:
    __slots__ = ("w", "r", "name")

    def __init__(self, name=""):
        self.w = None
        self.r = {}
        self.name = name


class KB:
    def __init__(self):
        self.nc = bass.Bass("TRN2", target_bir_lowering=False)
        nc = self.nc
        self.engs = {"pe": nc.tensor, "dve": nc.vector, "act": nc.scalar, "pool": nc.gpsimd, "sp": nc.sync}
        self.sem = {}
        self.cnt = {}
        self.seen = {e: {} for e in self.engs}
        for e in ("pe", "dve", "act", "pool"):
            self.sem[e] = nc.alloc_semaphore(f"s_{e}")
            self.cnt[e] = 0
        self.semobj = {("eng", e): self.sem[e] for e in self.sem}
        self.dpool = {}
        for q, n in (("sp", 12), ("pool", 12), ("act", 4)):
            lst = []
            for i in range(n):
                s = nc.alloc_semaphore(f"d_{q}{i}")
                key = ("dma", q, i)
                self.semobj[key] = s
                lst.append([key, 0])
            self.dpool[q] = [lst, 0]
        self.out_events = []
        self._n = 0
        self.io = None
        self.scope = None
        self.uid = 0
        self._consts = None

    def sb(self, name, shape, dt):
        if self.scope is not None:
            return self.scope.enter_context(self.nc.sbuf_tensor(f"{name}_u{self.uid}", list(shape), dt)).ap()
        return self.nc.alloc_sbuf_tensor(name, list(shape), dt).ap()

    def ps(self, name, shape, dt):
        if self.scope is not None:
            return self.scope.enter_context(self.nc.psum_tensor(f"{name}_u{self.uid}", list(shape), dt)).ap()
        return self.nc.alloc_psum_tensor(name, list(shape), dt).ap()

    def dram(self, name, shape, dt, kind):
        if self.io is not None:
            ap = self.io[name]
            assert list(ap.shape) == list(shape), (name, ap.shape, shape)
            return ap
        return self.nc.dram_tensor(name, list(shape), dt, kind=kind).ap()

    def get_consts(self):
        if self._consts is None:
            sc, self.scope = self.scope, None
            self._consts = Consts(self)
            self.scope = sc
        return self._consts

    def phase(self, io):
        import contextlib
        kb = self

        @contextlib.contextmanager
        def cm():
            kb.uid += 1
            kb.io = io
            with contextlib.ExitStack() as st:
                kb.scope = st
                yield
                kb.barrier()
            kb.scope = None
            kb.io = None
        return cm()

    def end_phase(self):
        if self.io is not None:
            return None
        return self.finish()

    def _wait(self, eng, deps):
        seen = self.seen[eng]
        for key, val in deps.items():
            if eng == "pe" and key == ("eng", "pe"):
                continue
            if seen.get(key, 0) < val:
                self.engs[eng].wait_ge(self.semobj[key], val)
                seen[key] = val

    @staticmethod
    def _add(d, ev):
        if ev is None:
            return
        k, v = ev
        if d.get(k, 0) < v:
            d[k] = v

    def _deps(self, reads, writes):
        deps = {}
        for b in reads:
            self._add(deps, b.w)
        for b in writes:
            self._add(deps, b.w)
            for k, v in b.r.items():
                if deps.get(k, 0) < v:
                    deps[k] = v
        return deps

    def _commit(self, ev, reads, writes):
        for b in reads:
            self._add(b.r, ev)
        for b in writes:
            b.w = ev
            b.r = {}

    def op(self, eng, fn, reads=(), writes=()):
        self._wait(eng, self._deps(reads, writes))
        ins = fn(self.engs[eng])
        self.cnt[eng] += 1
        ins.then_inc(self.sem[eng], 1)
        ev = (("eng", eng), self.cnt[eng])
        self._commit(ev, reads, writes)
        return ev

    def group(self, eng, fns, reads=(), writes=()):
        self._wait(eng, self._deps(reads, writes))
        ins = None
        for fn in fns:
            ins = fn(self.engs[eng])
        self.cnt[eng] += 1
        ins.then_inc(self.sem[eng], 1)
        ev = (("eng", eng), self.cnt[eng])
        self._commit(ev, reads, writes)
        return ev

    def dma(self, q, out, in_, reads=(), writes=(), is_output=False, slow=False):
        lst, idx = self.dpool[q]
        ent = lst[idx]
        self.dpool[q][1] = (idx + 1) % len(lst)
        deps = self._deps(reads, writes)
        if ent[1] > 0:
            self._add(deps, (ent[0], ent[1]))
        self._wait(q, deps)
        ent[1] += 16
        kw = {"allow_slow_non_contiguous": True} if slow else {}
        self.engs[q].dma_start(out=out, in_=in_, **kw).then_inc(self.semobj[ent[0]], 16)
        ev = (ent[0], ent[1])
        self._commit(ev, reads, writes)
        if is_output:
            self.out_events.append(ev)
        return ev

    def finish(self):
        deps = {}
        for ev in self.out_events:
            self._add(deps, ev)
        self._wait("sp", deps)
        fin = {("eng", e): self.cnt[e] for e in self.cnt if self.cnt[e] > 0}
        self._wait("sp", fin)
        return self.nc


class Consts:
    def __init__(self, kb):
        self.ones = kb.sb("c_ones", [128, 128], F32)
        self.eps = kb.sb("c_eps", [128, 1], F32)
        self.t = Tok("consts")
        kb.op("pool", lambda e: e.memset(self.ones[:], 1.0), writes=[self.t])
        kb.op("pool", lambda e: e.memset(self.eps[:], EPS), writes=[self.t])


TOKBLK = [(0, 64, 0), (64, 512, 1), (576, 512, 1)]


def emit_modnorm(kb, cs, x, xt, acoef, bcoef, mt, h, ht, colmap, psA, psA_t, scr):
    sq, sq_t = scr["sq"], scr["sq_t"]
    rstd, rstd_t = scr["rstd"], scr["rstd_t"]
    tmp, tmp_t = scr["tmp"], scr["tmp_t"]
    for (t0, n, seg) in TOKBLK:
        fns = []
        for c in range(NCH):
            i = c % 2
            kb.op("act", lambda e, c=c, i=i: e.activation(out=sq[i][:, 0:n], in_=x[:, c, t0:t0 + n], func=AF.Square),
                  reads=[xt], writes=[sq_t[i]])
            kb.op("pe", lambda e, c=c, i=i: e.matmul(psA[:, 0:n], lhsT=cs.ones[:], rhs=sq[i][:, 0:n],
                                                      start=(c == 0), stop=(c == NCH - 1)),
                  reads=[sq_t[i], cs.t], writes=[psA_t])
        kb.op("act", lambda e: e.activation(out=rstd[:, 0:n], in_=psA[:, 0:n], func=AF.Sqrt,
                                             scale=1.0 / D, bias=cs.eps[:, 0:1]),
              reads=[psA_t, cs.t], writes=[rstd_t])
        kb.op("dve", lambda e: e.reciprocal(out=rstd[:, 0:n], in_=rstd[:, 0:n]), reads=[rstd_t], writes=[rstd_t])
        c0 = colmap(t0)
        for c in range(NCH):
            i = c % 2
            kb.op("dve", lambda e, c=c, i=i: e.tensor_tensor(out=tmp[i][:, 0:n], in0=x[:, c, t0:t0 + n],
                                                             in1=rstd[:, 0:n], op=ALU.mult),
                  reads=[xt, rstd_t], writes=[tmp_t[i]])
            kb.op("act", lambda e, c=c, i=i: e.activation(out=h[:, c, c0:c0 + n], in_=tmp[i][:, 0:n], func=AF.Identity,
                                                           scale=acoef[:, c, seg:seg + 1], bias=bcoef[:, c, seg:seg + 1]),
                  reads=[tmp_t[i], mt], writes=[ht])


def alloc_norm_scratch(kb):
    scr = {}
    scr["sq"] = [kb.sb(f"n_sq{i}", [128, 512], F32) for i in range(2)]
    scr["sq_t"] = [Tok() for _ in range(2)]
    scr["rstd"] = kb.sb("n_rstd", [128, 512], F32)
    scr["rstd_t"] = Tok()
    scr["tmp"] = [kb.sb(f"n_tmp{i}", [128, 512], F32) for i in range(2)]
    scr["tmp_t"] = [Tok() for _ in range(2)]
    return scr


def emit_coefs(kb, mods, mods_t, normg, normg_t, si_sh, si_sc, acoef, bcoef, ct):
    for seg in range(2):
        kb.op("dve", lambda e, seg=seg: e.scalar_tensor_tensor(out=acoef[:, :, seg], in0=mods[:, si_sc, :, seg], scalar=1.0,
                                                               in1=normg[:, :], op0=ALU.add, op1=ALU.mult),
              reads=[mods_t, normg_t], writes=[ct])
        kb.op("dve", lambda e, seg=seg: e.tensor_copy(out=bcoef[:, :, seg], in_=mods[:, si_sh, :, seg]),
              reads=[mods_t], writes=[ct])


def build_M(kb=None):
    kb = kb or KB()
    condT = kb.dram("condT", [128, NCH, 3], F32, "ExternalInput")
    adaw = kb.dram("adaw", [DEPTH, 128, NCH, 1536], F32, "ExternalInput")
    adab = kb.dram("adab", [128, DEPTH, 12], F32, "ExternalInput")
    out = kb.dram("modT", [128, DEPTH, 12, 3], F32, "ExternalOutput")
    sc = kb.sb("sc", [128, NCH, 4], F32); sc_t = Tok()
    bsb = kb.sb("bsb", [128, DEPTH, 12], F32); b_t = Tok()
    res = kb.sb("res", [128, DEPTH, 12, 3], F32); res_t = Tok()
    wb = [kb.sb(f"w{i}", [128, NCH, 768], F32) for i in range(2)]
    w_t = [Tok() for _ in range(2)]
    pst = [kb.ps(f"ps{i}", [128, 512], F32) for i in range(2)]
    ps_t = [Tok() for _ in range(2)]
    kb.dma("sp", sc[:, :, 0:3], condT, writes=[sc_t])
    kb.dma("sp", bsb[:], adab, writes=[b_t])
    kb.op("act", lambda e: e.activation(out=sc[:, :, 0:3], in_=sc[:, :, 0:3], func=AF.Silu), reads=[sc_t], writes=[sc_t])
    it = 0
    for l in range(DEPTH):
        for hf in range(2):
            i = it % 2
            it += 1
            kb.dma("sp", wb[i][:], adaw[l, :, :, hf * 768:(hf + 1) * 768], writes=[w_t[i]])
            for mm in range(6):
                m = hf * 6 + mm
                j = m % 2
                kb.group("pe", [lambda e, k=k, mm=mm, i=i, j=j: e.matmul(pst[j][:, 0:3], lhsT=wb[i][:, k, mm * 128:(mm + 1) * 128],
                                                                          rhs=sc[:, k, 0:3], start=(k == 0), stop=(k == NCH - 1))
                                 for k in range(NCH)],
                         reads=[w_t[i], sc_t], writes=[ps_t[j]])
                kb.op("dve", lambda e, l=l, m=m, j=j: e.tensor_scalar(out=res[:, l, m, :], in0=pst[j][:, 0:3],
                                                                       scalar1=bsb[:, l, m:m + 1], scalar2=None, op0=ALU.add),
                      reads=[ps_t[j], b_t], writes=[res_t])
    kb.dma("sp", out, res[:], reads=[res_t], is_output=True)
    return kb.end_phase()


def build_P0(kb=None):
    kb = kb or KB()
    xin = kb.dram("xT", [128, NCH, NTOK], F32, "ExternalInput")
    mods_d = kb.dram("mods", [128, 6, NCH, 2], F32, "ExternalInput")
    ng_d = kb.dram("normg", [128, NCH], F32, "ExternalInput")
    h_d = kb.dram("h1T", [128, NCH, NTOK], BF16, "ExternalOutput")
    cs = kb.get_consts()
    x = kb.sb("x", [128, NCH, NTOK], F32); xt = Tok()
    h = kb.sb("h", [128, NCH, NTOK], BF16); ht = Tok()
    mods = kb.sb("mods_s", [128, 6, NCH, 2], F32); mt = Tok()
    ng = kb.sb("ng", [128, NCH], F32); ngt = Tok()
    ac = kb.sb("ac", [128, NCH, 2], F32); bc = kb.sb("bc", [128, NCH, 2], F32); ct = Tok()
    psA = kb.ps("psA", [128, 512], F32); psA_t = Tok()
    scr = alloc_norm_scratch(kb)
    for c in range(0, NCH, 4):
        kb.dma("sp", x[:, c:c + 4, :], xin[:, c:c + 4, :], writes=[xt])
    kb.dma("sp", mods[:], mods_d, writes=[mt])
    kb.dma("sp", ng[:], ng_d, writes=[ngt])
    emit_coefs(kb, mods, mt, ng, ngt, 0, 1, ac, bc, ct)
    emit_modnorm(kb, cs, x, xt, ac, bc, ct, h, ht, lambda t: t, psA, psA_t, scr)
    for c in range(0, NCH, 4):
        kb.dma("sp", h_d[:, c:c + 4, :], h[:, c:c + 4, :], reads=[ht], is_output=True)
    return kb.end_phase()


def hcol(t):
    return 1 + t if t < 64 else 67 + (t - 64)


def build_B1(kb=None):
    kb = kb or KB()
    xin = kb.dram("xT", [128, NCH, NTOK], F32, "ExternalInput")
    mix_d = kb.dram("mixT", [128, NCH, NTOK], BF16, "ExternalInput")
    wout_d = kb.dram("wout", [NCH, 128, NCH, 128], F32, "ExternalInput")
    mods_d = kb.dram("mods", [128, 6, NCH, 2], F32, "ExternalInput")
    ng_d = kb.dram("normg", [128, NCH], F32, "ExternalInput")
    xo_d = kb.dram("xmidT", [128, NCH, NTOK], F32, "ExternalOutput")
    h_d = kb.dram("h2T", [128, NCH, NTOK], BF16, "ExternalOutput")
    cs = kb.get_consts()
    x = kb.sb("x", [128, NCH, NTOK], F32); xt = [Tok() for _ in range(NCH)]
    mix = kb.sb("mix", [128, NCH, NTOK], BF16); mixt = Tok()
    h = kb.sb("h", [128, NCH, NTOK], BF16); ht = Tok()
    mods = kb.sb("mods_s", [128, 6, NCH, 2], F32); mt = Tok()
    ng = kb.sb("ng", [128, NCH], F32); ngt = Tok()
    ac = kb.sb("ac", [128, NCH, 2], F32); bc = kb.sb("bc", [128, NCH, 2], F32); ct = Tok()
    wb = [kb.sb(f"w{i}", [128, NCH, 128], BF16) for i in range(3)]
    w_t = [Tok() for _ in range(3)]
    pss = [kb.ps(f"ps{i}", [128, 512], F32) for i in range(6)]
    ps_t = [Tok() for _ in range(6)]
    psA = kb.ps("psA", [128, 512], F32); psA_t = Tok()
    scr = alloc_norm_scratch(kb)
    for c in range(0, NCH, 4):
        kb.dma("sp", x[:, c:c + 4, :], xin[:, c:c + 4, :], writes=xt[c:c + 4])
        kb.dma("sp", mix[:, c:c + 4, :], mix_d[:, c:c + 4, :], writes=[mixt])
    kb.dma("sp", mods[:], mods_d, writes=[mt])
    kb.dma("sp", ng[:], ng_d, writes=[ngt])
    emit_coefs(kb, mods, mt, ng, ngt, 3, 4, ac, bc, ct)
    pi = 0
    for m in range(NCH):
        wi = m % 3
        kb.dma("pool", wb[wi][:], wout_d[m], writes=[w_t[wi]])
        for (t0, n, seg) in TOKBLK:
            j = pi % 6
            pi += 1
            kb.group("pe", [lambda e, k=k, wi=wi, j=j, t0=t0, n=n: e.matmul(pss[j][:, 0:n], lhsT=wb[wi][:, k, :], rhs=mix[:, k, t0:t0 + n],
                                                                            start=(k == 0), stop=(k == NCH - 1))
                             for k in range(NCH)],
                     reads=[w_t[wi], mixt], writes=[ps_t[j]])
            kb.op("dve", lambda e, m=m, j=j, t0=t0, n=n, seg=seg: e.scalar_tensor_tensor(
                out=x[:, m, t0:t0 + n], in0=pss[j][:, 0:n], scalar=mods[:, 2, m, seg:seg + 1],
                in1=x[:, m, t0:t0 + n], op0=ALU.mult, op1=ALU.add),
                reads=[ps_t[j], mt, xt[m]], writes=[xt[m]])
    deps_w = {}
    for m in range(NCH):
        KB._add(deps_w, xt[m].w)
    xm = Tok()
    for e in ("act", "dve", "pe", "sp"):
        kb._wait(e, deps_w)
    emit_modnorm(kb, cs, x, xm, ac, bc, ct, h, ht, lambda t: t, psA, psA_t, scr)
    for c in range(0, NCH, 4):
        kb.dma("sp", xo_d[:, c:c + 4, :], x[:, c:c + 4, :], reads=[xm], is_output=True)
        kb.dma("sp", h_d[:, c:c + 4, :], h[:, c:c + 4, :], reads=[ht], is_output=True)
    return kb.end_phase()


NQ = 4
QF = FCH // NQ
GBLK = [(0, 364), (364, 364), (728, 364)]
DBLK = [(1, 0, 64, 0), (67, 64, 512, 1), (579, 576, 512, 1)]


def build_B2(last, kb=None):
    kb = kb or KB()
    xin = kb.dram("xmidT", [128, NCH, NTOK], F32, "ExternalInput")
    h2_d = kb.dram("h2hT", [128, NCH, HW_], BF16, "ExternalInput")
    wup_d = kb.dram("wup", [FCH, 128, NCH, 256], F32, "ExternalInput")
    wdn_d = kb.dram("wdn", [NQ, NCH, 128, QF, 128], F32, "ExternalInput")
    cw_d = kb.dram("convw", [128, FCH, 4], F32, "ExternalInput")
    mods_d = kb.dram("mods", [128, 6, NCH, 2], F32, "ExternalInput")
    xo_d = kb.dram("xoT", [128, NCH, NTOK], F32, "ExternalOutput")
    cs = kb.get_consts()
    x = kb.sb("x", [128, NCH, NTOK], F32); xt = [Tok() for _ in range(NCH)]
    h2 = kb.sb("h2", [128, NCH, HW_], BF16); h2t = Tok()
    hid = kb.sb("hid", [128, QF, HW_], BF16); hid_t = [Tok() for _ in range(QF)]
    G = kb.sb("G", [128, HW_], F32); Gt = Tok()
    C = kb.sb("C", [128, HW_], F32); Ct = Tok()
    cw = kb.sb("cw", [128, FCH, 4], F32); cwt = Tok()
    mods = kb.sb("mods_s", [128, 6, NCH, 2], F32); mt = Tok()
    wu = [kb.sb(f"wu{i}", [128, NCH, 256], BF16) for i in range(2)]; wu_t = [Tok() for _ in range(2)]
    wd = [kb.sb(f"wd{i}", [128, QF, 128], BF16) for i in range(2)]; wd_t = [Tok() for _ in range(2)]
    pss = [kb.ps(f"ps{i}", [128, 512], F32) for i in range(8)]
    ps_t = [Tok() for _ in range(8)]
    for c in range(0, NCH, 4):
        kb.dma("sp", x[:, c:c + 4, :], xin[:, c:c + 4, :], writes=xt[c:c + 4])
        kb.dma("sp", h2[:, c:c + 4, :], h2_d[:, c:c + 4, :], writes=[h2t])
    kb.dma("sp", mods[:], mods_d, writes=[mt])
    kb.dma("sp", cw[:], cw_d, writes=[cwt])
    if not last:
        ng_d = kb.dram("normg", [128, NCH], F32, "ExternalInput")
        modsn_d = kb.dram("modsn", [128, 6, NCH, 2], F32, "ExternalInput")
        h1_d = kb.dram("h1T", [128, NCH, NTOK], BF16, "ExternalOutput")
        ng = kb.sb("ng", [128, NCH], F32); ngt = Tok()
        modsn = kb.sb("modsn_s", [128, 6, NCH, 2], F32); mnt = Tok()
        ac = kb.sb("ac", [128, NCH, 2], F32); bc = kb.sb("bc", [128, NCH, 2], F32); ct = Tok()
        kb.dma("sp", ng[:], ng_d, writes=[ngt])
        kb.dma("sp", modsn[:], modsn_d, writes=[mnt])
        emit_coefs(kb, modsn, mnt, ng, ngt, 0, 1, ac, bc, ct)
    wdi = 0
    for q in range(NQ):
        for mm in range(QF):
            m = q * QF + mm
            wi = m % 2
            kb.dma("pool", wu[wi][:], wup_d[m], writes=[wu_t[wi]])
            for half in range(2):
                for bi, (c0, n) in enumerate(GBLK):
                    j = half * 3 + bi
                    kb.group("pe", [lambda e, k=k, wi=wi, j=j, c0=c0, n=n, half=half: e.matmul(
                        pss[j][:, 0:n], lhsT=wu[wi][:, k, half * 128:(half + 1) * 128], rhs=h2[:, k, c0:c0 + n],
                        start=(k == 0), stop=(k == NCH - 1)) for k in range(NCH)],
                        reads=[wu_t[wi], h2t], writes=[ps_t[j]])
            for bi, (c0, n) in enumerate(GBLK):
                kb.op("act", lambda e, bi=bi, c0=c0, n=n: e.activation(out=G[:, c0:c0 + n], in_=pss[bi][:, 0:n], func=AF.Identity),
                      reads=[ps_t[bi]], writes=[Gt])
            W = HW_ - 2
            kb.op("dve", lambda e, m=m: e.tensor_scalar(out=C[:, 1:1 + W], in0=G[:, 1:1 + W], scalar1=cw[:, m, 1:2],
                                                        scalar2=cw[:, m, 3:4], op0=ALU.mult, op1=ALU.add),
                  reads=[Gt, cwt], writes=[Ct])
            kb.op("dve", lambda e, m=m: e.scalar_tensor_tensor(out=C[:, 1:1 + W], in0=G[:, 0:W], scalar=cw[:, m, 0:1],
                                                               in1=C[:, 1:1 + W], op0=ALU.mult, op1=ALU.add),
                  reads=[Gt, cwt, Ct], writes=[Ct])
            kb.op("dve", lambda e, m=m: e.scalar_tensor_tensor(out=C[:, 1:1 + W], in0=G[:, 2:2 + W], scalar=cw[:, m, 2:3],
                                                               in1=C[:, 1:1 + W], op0=ALU.mult, op1=ALU.add),
                  reads=[Gt, cwt, Ct], writes=[Ct])
            kb.op("act", lambda e: e.activation(out=C[:, 1:1 + W], in_=C[:, 1:1 + W], func=AF.Silu), reads=[Ct], writes=[Ct])
            for bi, (c0, n) in enumerate(GBLK):
                a = max(c0, 1)
                b_ = min(c0 + n, HW_ - 1)
                kb.op("dve", lambda e, bi=bi, a=a, b_=b_, c0=c0, mm=mm: e.tensor_tensor(
                    out=hid[:, mm, a:b_], in0=pss[3 + bi][:, a - c0:b_ - c0], in1=C[:, a:b_], op=ALU.mult),
                    reads=[ps_t[3 + bi], Ct], writes=[hid_t[mm]])
        for mo in range(NCH):
            wi = wdi % 2
            wdi += 1
            kb.dma("pool", wd[wi][:], wdn_d[q, mo], writes=[wd_t[wi]])
            for bi, (hc0, t0, n, seg) in enumerate(DBLK):
                j = 6 + (bi % 2)
                kb.group("pe", [lambda e, kk=kk, wi=wi, j=j, hc0=hc0, n=n: e.matmul(
                    pss[j][:, 0:n], lhsT=wd[wi][:, kk, :], rhs=hid[:, kk, hc0:hc0 + n],
                    start=(kk == 0), stop=(kk == QF - 1)) for kk in range(QF)],
                    reads=[wd_t[wi]] + hid_t, writes=[ps_t[j]])
                kb.op("dve", lambda e, mo=mo, j=j, t0=t0, n=n, seg=seg: e.scalar_tensor_tensor(
                    out=x[:, mo, t0:t0 + n], in0=pss[j][:, 0:n], scalar=mods[:, 5, mo, seg:seg + 1],
                    in1=x[:, mo, t0:t0 + n], op0=ALU.mult, op1=ALU.add),
                    reads=[ps_t[j], mt, xt[mo]], writes=[xt[mo]])
    deps_w = {}
    for m in range(NCH):
        KB._add(deps_w, xt[m].w)
    for e in ("act", "dve", "pe", "sp"):
        kb._wait(e, deps_w)
    xm = Tok()
    for c in range(0, NCH, 4):
        kb.dma("sp", xo_d[:, c:c + 4, :], x[:, c:c + 4, :], reads=[xm], is_output=True)
    if not last:
        scr = alloc_norm_scratch(kb)
        h1v = h2[:, :, 0:NTOK]
        emit_modnorm(kb, cs, x, xm, ac, bc, ct, h2, h2t, lambda t: t, pss[0], ps_t[0], scr)
        for c in range(0, NCH, 4):
            kb.dma("sp", h1_d[:, c:c + 4, :], h1v[:, c:c + 4, :], reads=[h2t], is_output=True)
    return kb.end_phase()


def fm(a):
    t, f = a.shape
    return np.ascontiguousarray(a.reshape(t, f // 128, 128).transpose(2, 1, 0))


def unfm(a):
    p, c, t = a.shape
    return np.ascontiguousarray(a.transpose(2, 1, 0).reshape(t, c * 128))


def vec_fm(v):
    return np.ascontiguousarray(v.reshape(-1, 128).T)


def core_tokens(x_b, ctx_b, j):
    return np.concatenate([ctx_b[64 * j:64 * j + 64], x_b[1024 * j:1024 * j + 1024]], axis=0)


def run(nc, in_maps):
    res = run_bass_kernel_spmd(nc, in_maps, core_ids=list(range(8)))
    return res.results


_CACHE = {}


def get_nc(name, builder, *args):
    key = (name,) + args
    if key not in _CACHE:
        _CACHE[key] = builder(*args)
    return _CACHE[key]


def host_M(c, c_ctx, ada_w, ada_b):
    cond = np.stack([c[0], c[1], c_ctx], axis=0)
    condT = np.ascontiguousarray(cond.reshape(3, NCH, 128).transpose(2, 1, 0))
    in_maps = []
    for core in range(8):
        sl = slice(core * 1536, (core + 1) * 1536)
        aw = np.ascontiguousarray(ada_w[:, :, sl].reshape(DEPTH, NCH, 128, 1536).transpose(0, 2, 1, 3))
        ab = np.ascontiguousarray(ada_b[:, sl].reshape(DEPTH, 12, 128).transpose(2, 0, 1))
        in_maps.append({"condT": condT, "adaw": aw, "adab": ab})
    res = run(get_nc("M", build_M), in_maps)
    full = np.zeros((DEPTH, 6 * D, 3), np.float32)
    for core in range(8):
        r = res[core]["modT"]
        full[:, core * 1536:(core + 1) * 1536, :] = r.transpose(1, 2, 0, 3).reshape(DEPTH, 1536, 3)
    mods = [[None, None] for _ in range(DEPTH)]
    for l in range(DEPTH):
        for b in range(2):
            m = full[l][:, [2, b]]
            mods[l][b] = np.ascontiguousarray(m.reshape(6, NCH, 128, 2).transpose(2, 0, 1, 3))
    return mods, full


def host_wout(w):
    rows = []
    for i in range(4):
        for cc in range(4):
            r0 = 256 * i + 128 * cc if cc < 2 else 1024 + 256 * i + 128 * (cc - 2)
            rows.append(w[r0:r0 + 128])
    wp = np.stack(rows, axis=0)
    return np.ascontiguousarray(wp.reshape(NCH, 128, NCH, 128).transpose(2, 1, 0, 3))


def host_wup(w):
    wk = w.reshape(NCH, 128, 2, FCH, 128)
    return np.ascontiguousarray(wk.transpose(3, 1, 0, 2, 4).reshape(FCH, 128, NCH, 256))


def host_wdn(w):
    wk = w.reshape(NQ, QF, 128, NCH, 128)
    return np.ascontiguousarray(wk.transpose(0, 3, 2, 1, 4))


def host_convw(cw, cb):
    a = np.concatenate([cw, cb[None]], axis=0)
    return np.ascontiguousarray(a.reshape(4, FCH, 128).transpose(2, 1, 0))


def host_halo(h2_cores):
    outs = []
    for core in range(8):
        b, j = divmod(core, 4)
        h = h2_cores[core]
        o = np.zeros((128, NCH, HW_), h.dtype)
        o[:, :, 1:65] = h[:, :, 0:64]
        o[:, :, 67:1091] = h[:, :, 64:1088]
        if j > 0:
            hp = h2_cores[core - 1]
            o[:, :, 0] = hp[:, :, 63]
            o[:, :, 66] = hp[:, :, 1087]
        if j < 3:
            hn = h2_cores[core + 1]
            o[:, :, 65] = hn[:, :, 0]
            o[:, :, 1091] = hn[:, :, 64]
        outs.append(o)
    return outs


def kb_barrier(kb):
    deps = {("eng", e): kb.cnt[e] for e in kb.cnt if kb.cnt[e] > 0}
    for q in kb.dpool:
        for key, val in kb.dpool[q][0]:
            if val > 0:
                deps[key] = val
    for e in kb.engs:
        kb._wait(e, dict(deps))


KB.barrier = kb_barrier

ABLK = [(0, 2)] + [(2 + 4 * i, 4) for i in range(8)]
BWD_ORDER = [1, 0] + list(range(33, 1, -1))
QSCALE = 128 ** -0.5


def build_Aeven(kb=None):
    import contextlib
    kb = kb or KB()
    nc = kb.nc
    h1_d = kb.dram("h1T", [128, NCH, TB], BF16, "ExternalInput")
    wA_d = kb.dram("wA", [128, NCH, 896], F32, "ExternalInput")
    wC_d = kb.dram("wC", [128, NCH, 768], F32, "ExternalInput")
    w2_d = kb.dram("w2", [128, 2, 128], F32, "ExternalInput")
    gb_d = kb.dram("gbias", [128, 256], F32, "ExternalInput")
    gn_d = kb.dram("gnorm", [128, 256], F32, "ExternalInput")
    scw_d = kb.dram("scw", [128, 2, 3], F32, "ExternalInput")
    msk_d = kb.dram("masks", [128, 4, 128], F32, "ExternalInput")
    id_d = kb.dram("ident", [128, 128], BF16, "ExternalInput")
    out_d = kb.dram("mixT", [128, 4, TB], BF16, "ExternalOutput")
    cs = kb.get_consts()
    pss = [kb.ps(f"ps{i}", [128, 512], F32) for i in range(7)]
    ps_t = [Tok() for _ in range(7)]
    psT = kb.ps("psT", [128, 1024], BF16); psT_t = Tok()
    hb = [kb.sb(f"hb{i}", [128, NCH, 512], BF16) for i in range(1)]; hb_t = [Tok()]

    with contextlib.ExitStack() as st:
        def sbx(name, shape, dt):
            return st.enter_context(nc.sbuf_tensor(f"{name}_u{kb.uid}", list(shape), dt)).ap()
        wC = sbx("wC_s", [128, NCH, 768], BF16); wC_t = Tok()
        U = sbx("u_s", [128, 2, TB + 4], F32); U_t = Tok()
        SBf = sbx("sb_s", [128, 2, TB], F32); SB_t = Tok()
        SX = sbx("sx_s", [128, 512], F32); SX_t = Tok()
        CO = sbx("co_s", [128, 2, TB], F32); CO_t = Tok()
        COb = sbx("cob_s", [128, 2, TB], BF16); COb_t = Tok()
        scw = sbx("scw_s", [128, 2, 3], F32); scw_t = Tok()
        kb.dma("pool", wC[:], wC_d, writes=[wC_t])
        kb.dma("sp", scw[:], scw_d, writes=[scw_t])
        kb.op("pool", lambda e: e.memset(U[:], 0.0), writes=[U_t])

        def ucol(tok):
            return 1 + tok if tok < CTX else 3 + tok
        pi = 0
        for (t0, nt) in ABLK:
            n = nt * 128
            g0 = t0 * 128
            kb.dma("sp", hb[0][:, :, 0:n], h1_d[:, :, g0:g0 + n], writes=[hb_t[0]])
            for c2 in range(2):
                for which in range(3):
                    j = pi % 4
                    pi += 1
                    col = which * 256 + c2 * 128
                    kb.group("pe", [lambda e, k=k, j=j, col=col, n=n: e.matmul(
                        pss[j][:, 0:n], lhsT=wC[:, k, col:col + 128], rhs=hb[0][:, k, 0:n],
                        start=(k == 0), stop=(k == NCH - 1)) for k in range(NCH)],
                        reads=[wC_t, hb_t[0]], writes=[ps_t[j]])
                    if which == 0:
                        kb.op("act", lambda e, j=j, n=n: e.activation(out=SX[:, 0:n], in_=pss[j][:, 0:n], func=AF.Identity),
                              reads=[ps_t[j]], writes=[SX_t])
                    elif which == 1:
                        kb.op("act", lambda e, j=j, n=n, c2=c2, g0=g0: e.activation(out=SBf[:, c2, g0:g0 + n], in_=pss[j][:, 0:n],
                                                                                  func=AF.Identity),
                              reads=[ps_t[j]], writes=[SB_t])
                    else:
                        uc = ucol(g0)
                        kb.op("dve", lambda e, j=j, n=n, c2=c2, uc=uc: e.tensor_tensor(out=U[:, c2, uc:uc + n], in0=pss[j][:, 0:n],
                                                                                     in1=SX[:, 0:n], op=ALU.mult),
                              reads=[ps_t[j], SX_t], writes=[U_t])
        for c2 in range(2):
            for (g0, n) in ((0, CTX), (CTX, SEQ)):
                uc = ucol(g0)
                kb.op("dve", lambda e, c2=c2, g0=g0, n=n, uc=uc: e.tensor_scalar(
                    out=CO[:, c2, g0:g0 + n], in0=U[:, c2, uc:uc + n], scalar1=scw[:, c2, 1:2], scalar2=None, op0=ALU.mult),
                    reads=[U_t, scw_t], writes=[CO_t])
                kb.op("dve", lambda e, c2=c2, g0=g0, n=n, uc=uc: e.scalar_tensor_tensor(
                    out=CO[:, c2, g0:g0 + n], in0=U[:, c2, uc - 1:uc - 1 + n], scalar=scw[:, c2, 0:1],
                    in1=CO[:, c2, g0:g0 + n], op0=ALU.mult, op1=ALU.add),
                    reads=[U_t, scw_t, CO_t], writes=[CO_t])
                kb.op("dve", lambda e, c2=c2, g0=g0, n=n, uc=uc: e.scalar_tensor_tensor(
                    out=CO[:, c2, g0:g0 + n], in0=U[:, c2, uc + 1:uc + 1 + n], scalar=scw[:, c2, 2:3],
                    in1=CO[:, c2, g0:g0 + n], op0=ALU.mult, op1=ALU.add),
                    reads=[U_t, scw_t, CO_t], writes=[CO_t])
                kb.op("dve", lambda e, c2=c2, g0=g0, n=n: e.tensor_tensor(
                    out=COb[:, c2, g0:g0 + n], in0=CO[:, c2, g0:g0 + n], in1=SBf[:, c2, g0:g0 + n], op=ALU.mult),
                    reads=[CO_t, SB_t], writes=[COb_t])
        kb.dma("sp", out_d[:, 2:4, :], COb[:], reads=[COb_t], is_output=True)
        kb.barrier()

    wA = kb.sb("wA_s", [128, NCH, 896], BF16); wA_t = Tok()
    QF = kb.sb("QF", [128, TB], BF16); QB = kb.sb("QB", [128, TB], BF16)
    KF = kb.sb("KF", [128, TB], BF16); KBk = kb.sb("KBk", [128, TB], BF16)
    KTF = kb.sb("KTF", [128, NTILE, 128], BF16); KTB = kb.sb("KTB", [128, NTILE, 128], BF16)
    V = kb.sb("V", [128, NTILE, 256], BF16)
    SR = kb.sb("SR", [128, NTILE, 256], BF16)
    AT = kb.sb("AT", [128, NTILE, 128], BF16)
    SBF = kb.sb("SBF", [128, NTILE, 256], BF16)
    per_t = [Tok() for _ in range(NTILE)]
    qTb = kb.sb("qTb", [128, 512], F32); kTb = kb.sb("kTb", [128, 512], F32); qk_t = Tok()
    GL = kb.sb("GL", [128, 512], F32); GL_t = Tok()
    w2 = kb.sb("w2_s", [128, 2, 128], F32); gb = kb.sb("gb_s", [128, 256], F32); gn = kb.sb("gn_s", [128, 256], F32)
    msk = kb.sb("msk_s", [128, 4, 128], F32); ident = kb.sb("id_s", [128, 128], BF16)
    k_t = Tok()
    el = kb.sb("el", [128, 2, NTILE], F32); el_t = Tok()
    tg = kb.sb("tg", [128, 256], F32); tg_t = Tok()
    sg = kb.sb("sg", [128, 256], F32); sg_t = Tok()
    E1 = kb.sb("E1", [128, 256], F32); E2 = kb.sb("E2", [128, 256], F32); E2t = kb.sb("E2t", [128, 256], F32)
    E1_t, E2_t, E2t_t = Tok(), Tok(), Tok()
    a1 = kb.sb("a1", [128, 128], F32); a2 = kb.sb("a2", [128, 128], F32); a_t = Tok()
    srt = kb.sb("srt", [128, 256], F32); srt_t = Tok()
    kb.dma("pool", wA[:], wA_d, writes=[wA_t])
    for dst, src in ((w2, w2_d), (gb, gb_d), (gn, gn_d), (msk, msk_d), (ident, id_d)):
        kb.dma("sp", dst[:], src, writes=[k_t])

    for (t0, nt) in ABLK:
        n = nt * 128
        g0 = t0 * 128
        kb.dma("sp", hb[0][:, :, 0:n], h1_d[:, :, g0:g0 + n], writes=[hb_t[0]])
        for j, (col, m) in enumerate(((0, 128), (128, 128), (768, 128))):
            kb.group("pe", [lambda e, k=k, j=j, col=col, m=m, n=n: e.matmul(
                pss[j][0:m, 0:n], lhsT=wA[:, k, col:col + m], rhs=hb[0][:, k, 0:n],
                start=(k == 0), stop=(k == NCH - 1)) for k in range(NCH)],
                reads=[wA_t, hb_t[0]], writes=[ps_t[j]])
        kb.op("act", lambda e, n=n: e.activation(out=qTb[:, 0:n], in_=pss[0][:, 0:n], func=AF.Identity, scale=QSCALE),
              reads=[ps_t[0]], writes=[qk_t])
        kb.op("act", lambda e, n=n: e.activation(out=kTb[:, 0:n], in_=pss[1][:, 0:n], func=AF.Identity),
              reads=[ps_t[1]], writes=[qk_t])
        kb.op("dve", lambda e, n=n: e.tensor_copy(out=GL[:, 0:n], in_=pss[2][:, 0:n]), reads=[ps_t[2]], writes=[GL_t])
        for ti in range(nt):
            tl = t0 + ti
            c0 = ti * 128
            gc = tl * 128
            pt = per_t[tl]
            kb.group("pe", [lambda e, k=k, c0=c0: e.matmul(pss[4][:, 0:384], lhsT=hb[0][:, k, c0:c0 + 128], rhs=wA[:, k, 128:512],
                                                          start=(k == 0), stop=(k == NCH - 1)) for k in range(NCH)],
                     reads=[wA_t, hb_t[0]], writes=[ps_t[4]])
            kb.group("pe", [lambda e, k=k, c0=c0: e.matmul(pss[5][:, 0:256], lhsT=hb[0][:, k, c0:c0 + 128], rhs=wA[:, k, 512:768],
                                                          start=(k == 0), stop=(k == NCH - 1)) for k in range(NCH)],
                     reads=[wA_t, hb_t[0]], writes=[ps_t[5]])
            kb.group("pe", [lambda e, d=d, c0=c0: e.matmul(pss[6][:, d * 128:(d + 1) * 128], lhsT=GL[:, c0:c0 + 128], rhs=w2[:, d, :],
                                                          start=True, stop=True) for d in range(2)],
                     reads=[GL_t, k_t], writes=[ps_t[6]])
            kb.op("dve", lambda e: e.tensor_tensor(out=tg[:], in0=pss[6][:, 0:256], in1=gb[:], op=ALU.add),
                  reads=[ps_t[6], k_t], writes=[tg_t])
            kb.op("act", lambda e: e.activation(out=tg[:], in_=tg[:], func=AF.Exp, scale=-1.0), reads=[tg_t], writes=[tg_t])
            kb.op("act", lambda e: e.activation(out=sg[:], in_=tg[:], func=AF.Ln, bias=1.0, scale=1.0), reads=[tg_t], writes=[sg_t])
            kb.group("pe", [lambda e, d=d: e.matmul(pss[6][:, 256 + d * 128:256 + (d + 1) * 128], lhsT=sg[:, d * 128:(d + 1) * 128],
                                                   rhs=msk[:, d, :], start=True, stop=True) for d in range(2)],
                     reads=[sg_t, k_t], writes=[ps_t[6]])
            kb.group("pe", [lambda e, d=d: e.matmul(pss[3][:, d * 128:(d + 1) * 128], lhsT=msk[:, d, :],
                                                   rhs=sg[:, d * 128:(d + 1) * 128], start=True, stop=True) for d in range(2)],
                     reads=[sg_t, k_t], writes=[ps_t[3]])
            kb.op("act", lambda e: e.activation(out=E1[:], in_=pss[6][:, 256:512], func=AF.Exp), reads=[ps_t[6]], writes=[E1_t])
            kb.op("act", lambda e: e.activation(out=E2[:], in_=pss[6][:, 256:512], func=AF.Exp, scale=-1.0),
                  reads=[ps_t[6]], writes=[E2_t])
            kb.op("act", lambda e: e.activation(out=E2t[:], in_=pss[3][:, 0:256], func=AF.Exp, scale=-1.0),
                  reads=[ps_t[3]], writes=[E2t_t])
            kb.op("pool", lambda e, tl=tl: e.tensor_copy(out=el[:, 0, tl:tl + 1], in_=E1[:, 127:128]), reads=[E1_t], writes=[el_t])
            kb.op("pool", lambda e, tl=tl: e.tensor_copy(out=el[:, 1, tl:tl + 1], in_=E1[:, 128:129]), reads=[E1_t], writes=[el_t])
            kb.op("dve", lambda e, c0=c0, gc=gc: e.tensor_tensor(out=QF[:, gc:gc + 128], in0=qTb[:, c0:c0 + 128], in1=E1[:, 0:128], op=ALU.mult),
                  reads=[qk_t, E1_t], writes=[pt])
            kb.op("dve", lambda e, c0=c0, gc=gc: e.tensor_tensor(out=QB[:, gc:gc + 128], in0=qTb[:, c0:c0 + 128], in1=E1[:, 128:256], op=ALU.mult),
                  reads=[qk_t, E1_t], writes=[pt])
            kb.op("dve", lambda e, c0=c0, gc=gc: e.tensor_tensor(out=KF[:, gc:gc + 128], in0=kTb[:, c0:c0 + 128], in1=E2[:, 0:128], op=ALU.mult),
                  reads=[qk_t, E2_t], writes=[pt])
            kb.op("dve", lambda e, c0=c0, gc=gc: e.tensor_tensor(out=KBk[:, gc:gc + 128], in0=kTb[:, c0:c0 + 128], in1=E2[:, 128:256], op=ALU.mult),
                  reads=[qk_t, E2_t], writes=[pt])
            kb.op("dve", lambda e, tl=tl: e.tensor_tensor(out=KTF[:, tl, :], in0=pss[4][:, 0:128], in1=E2t[:, 0:128], op=ALU.mult),
                  reads=[ps_t[4], E2t_t], writes=[pt])
            kb.op("dve", lambda e, tl=tl: e.tensor_tensor(out=KTB[:, tl, :], in0=pss[4][:, 0:128], in1=E2t[:, 128:256], op=ALU.mult),
                  reads=[ps_t[4], E2t_t], writes=[pt])
            kb.op("act", lambda e, tl=tl: e.activation(out=V[:, tl, :], in_=pss[4][:, 128:384], func=AF.Identity),
                  reads=[ps_t[4]], writes=[pt])
            kb.op("act", lambda e: e.activation(out=srt[:], in_=pss[5][:, 0:256], func=AF.Silu), reads=[ps_t[5]], writes=[srt_t])
            kb.op("pool", lambda e, tl=tl: e.tensor_tensor(out=SR[:, tl, :], in0=srt[:], in1=gn[:], op=ALU.mult),
                  reads=[srt_t, k_t], writes=[pt])
            kb.group("pe", [lambda e, gc=gc: e.matmul(pss[5][:, 256:384], lhsT=KF[:, gc:gc + 128], rhs=QF[:, gc:gc + 128], start=True, stop=True),
                            lambda e, gc=gc: e.matmul(pss[5][:, 384:512], lhsT=KBk[:, gc:gc + 128], rhs=QB[:, gc:gc + 128], start=True, stop=True)],
                     reads=[pt], writes=[ps_t[5]])
            kb.op("dve", lambda e: e.tensor_tensor(out=a1[:], in0=pss[5][:, 256:384], in1=msk[:, 2, :], op=ALU.mult),
                  reads=[ps_t[5], k_t], writes=[a_t])
            kb.op("dve", lambda e: e.tensor_tensor(out=a2[:], in0=pss[5][:, 384:512], in1=msk[:, 3, :], op=ALU.mult),
                  reads=[ps_t[5], k_t, a_t], writes=[a_t])
            kb.op("pool", lambda e, tl=tl: e.tensor_tensor(out=AT[:, tl, :], in0=a1[:], in1=a2[:], op=ALU.add),
                  reads=[a_t], writes=[pt])

    S = kb.sb("S", [128, 256], F32); S_t = Tok()
    S2 = kb.sb("S2", [128, 256], F32); S2_t = Tok()
    sbf_t = [Tok() for _ in range(NTILE)]
    kb.op("pool", lambda e: e.memset(S[:], 0.0), writes=[S_t])
    for c in range(NTILE):
        kb.op("act", lambda e, c=c: e.activation(out=SBF[:, c, :], in_=S[:], func=AF.Identity), reads=[S_t], writes=[sbf_t[c]])
        if c == NTILE - 1:
            break
        kb.op("pe", lambda e, c=c: e.matmul(pss[0][:, 0:256], lhsT=KTF[:, c, :], rhs=V[:, c, :], start=True, stop=True),
              reads=[per_t[c]], writes=[ps_t[0]])
        kb.op("dve", lambda e: e.tensor_tensor(out=S2[:], in0=pss[0][:, 0:256], in1=S[:], op=ALU.add),
              reads=[ps_t[0], S_t], writes=[S2_t])
        kb.op("dve", lambda e, c=c: e.tensor_scalar(out=S[:], in0=S2[:], scalar1=el[:, 0, c:c + 1], scalar2=None, op0=ALU.mult),
              reads=[S2_t, el_t], writes=[S_t])

    Sb = kb.sb("Sb", [128, 256], F32); Sb_t = Tok()
    Sbb = kb.sb("Sbb", [128, 256], BF16); Sbb_t = Tok()
    ss = kb.sb("ss", [128, 2], F32); ss_t = Tok()
    junk = kb.sb("junk", [128, 256], F32); junk_t = Tok()
    on = kb.sb("on", [128, 256], BF16); on_t = Tok()
    ost = [kb.sb(f"ost{i}", [128, 2, 128], BF16) for i in range(2)]; ost_t = [Tok(), Tok()]
    kb.op("pool", lambda e: e.memset(Sb[:], 0.0), writes=[Sb_t])
    kb.op("pool", lambda e: e.memset(Sbb[:], 0.0), writes=[Sbb_t])
    for oi, c in enumerate(BWD_ORDER):
        gc = c * 128
        kb.group("pe", [lambda e, c=c, gc=gc: e.matmul(pss[1][:, 0:256], lhsT=QF[:, gc:gc + 128], rhs=SBF[:, c, :], start=True, stop=False),
                        lambda e, c=c, gc=gc: e.matmul(pss[1][:, 0:256], lhsT=QB[:, gc:gc + 128], rhs=Sbb[:], start=False, stop=False),
                        lambda e, c=c, gc=gc: e.matmul(pss[1][:, 0:256], lhsT=AT[:, c, :], rhs=V[:, c, :], start=False, stop=True)],
                 reads=[per_t[c], sbf_t[c], Sbb_t], writes=[ps_t[1]])
        if oi < NTILE - 1 and c != 0:
            pass
        kb.op("pe", lambda e, c=c: e.matmul(pss[2][:, 0:256], lhsT=KTB[:, c, :], rhs=V[:, c, :], start=True, stop=True),
              reads=[per_t[c]], writes=[ps_t[2]])
        kb.op("dve", lambda e: e.tensor_tensor(out=S2[:], in0=pss[2][:, 0:256], in1=Sb[:], op=ALU.add),
              reads=[ps_t[2], Sb_t], writes=[S2_t])
        kb.op("dve", lambda e, c=c: e.tensor_scalar(out=Sb[:], in0=S2[:], scalar1=el[:, 1, c:c + 1], scalar2=None, op0=ALU.mult),
              reads=[S2_t, el_t], writes=[Sb_t])
        kb.op("act", lambda e: e.activation(out=Sbb[:], in_=Sb[:], func=AF.Identity), reads=[Sb_t], writes=[Sbb_t])
        kb.op("act", lambda e: e.activation(out=junk[:], in_=pss[1][:, 0:256], func=AF.Square, accum_out=ss[:, 0:1]),
              reads=[ps_t[1]], writes=[junk_t, ss_t])
        kb.op("act", lambda e: e.activation(out=ss[:, 1:2], in_=ss[:, 0:1], func=AF.Sqrt, scale=1.0 / 256, bias=cs.eps[:, 0:1]),
              reads=[ss_t, cs.t], writes=[ss_t])
        kb.op("dve", lambda e: e.reciprocal(out=ss[:, 1:2], in_=ss[:, 1:2]), reads=[ss_t], writes=[ss_t])
        kb.op("dve", lambda e, c=c: e.scalar_tensor_tensor(out=on[:], in0=pss[1][:, 0:256], scalar=ss[:, 1:2], in1=SR[:, c, :],
                                                          op0=ALU.mult, op1=ALU.mult),
              reads=[ps_t[1], ss_t, per_t[c]], writes=[on_t])
        kb.group("pe", [lambda e, hh=hh: e.transpose(psT[:, hh * 128:(hh + 1) * 128], on[:, hh * 128:(hh + 1) * 128], ident[:])
                        for hh in range(2)],
                 reads=[on_t, k_t], writes=[psT_t])
        oj = oi % 2
        kb.op("act", lambda e, oj=oj: e.activation(out=ost[oj][:].rearrange("p a b -> p (a b)"), in_=psT[:, 0:256], func=AF.Identity),
              reads=[psT_t], writes=[ost_t[oj]])
        kb.dma("sp", out_d[:, 0:2, gc:gc + 128], ost[oj][:], reads=[ost_t[oj]], is_output=True)
    return kb.end_phase()


def host_masks():
    j = np.arange(128)[:, None]
    i = np.arange(128)[None, :]
    mf = (j <= i).astype(np.float32)
    mb = (j >= i).astype(np.float32)
    return np.ascontiguousarray(np.stack([mf * (-1.0 / 16.0), mb * (-1.0 / 16.0), mf, mb], axis=1))


def wfm(w):
    return np.ascontiguousarray(w.reshape(NCH, 128, -1).transpose(1, 0, 2))


def host_Aeven_inputs(g, w_in, gate_w2, gate_b, gla_norm_g, sc_conv_w):
    o = np.cumsum((0,) + (512, 512, 1024, 1024, 32, 1024, 1024, 1024))
    qs, ks, vs, rs, gls, sxs, sbs, scgs = [int(v) for v in o[:8]]
    colsA = np.concatenate([np.arange(qs + 128 * g, qs + 128 * g + 128), np.arange(ks + 128 * g, ks + 128 * g + 128),
                            np.arange(vs + 256 * g, vs + 256 * g + 256), np.arange(rs + 256 * g, rs + 256 * g + 256),
                            np.arange(gls, gls + 32)])
    wa = w_in[:, colsA]
    pad = np.zeros((wa.shape[0], 128), wa.dtype)
    pad[:, 0:16] = wa[:, 768:784]
    pad[:, 32:48] = wa[:, 784:800]
    wa = np.concatenate([wa[:, 0:768], pad], axis=1)
    colsC = np.concatenate([np.arange(sxs + 256 * g, sxs + 256 * g + 256), np.arange(sbs + 256 * g, sbs + 256 * g + 256),
                            np.arange(scgs + 256 * g, scgs + 256 * g + 256)])
    hs = slice(128 * g, 128 * g + 128)
    w2 = np.zeros((128, 2, 128), np.float32)
    w2[0:16, 0, :] = gate_w2[0][:, hs]
    w2[32:48, 1, :] = gate_w2[1][:, hs]
    gbias = np.ascontiguousarray(np.broadcast_to(gate_b[:, hs].reshape(1, 256), (128, 256)))
    gnorm = np.ascontiguousarray(np.broadcast_to(gla_norm_g.reshape(1, 256), (128, 256)))
    scw = np.ascontiguousarray(sc_conv_w[:, 256 * g:256 * g + 256].reshape(3, 2, 128).transpose(2, 1, 0))
    return {"wA": wfm(wa), "wC": wfm(w_in[:, colsC]), "w2": w2, "gbias": gbias, "gnorm": gnorm, "scw": scw,
            "masks": host_masks(), "ident": np.eye(128, dtype=np.float32).astype(ml_dtypes.bfloat16)}


def assemble_h1(h1_cores, b):
    ctxp = [h1_cores[4 * b + j][:, :, 0:64] for j in range(4)]
    latp = [h1_cores[4 * b + j][:, :, 64:1088] for j in range(4)]
    return np.ascontiguousarray(np.concatenate(ctxp + latp, axis=2))


def scatter_mix(mix_cores, b, j):
    parts = []
    for g in range(4):
        m = mix_cores[4 * b + g]
        parts.append(np.concatenate([m[:, :, 64 * j:64 * j + 64], m[:, :, CTX + 1024 * j:CTX + 1024 * j + 1024]], axis=2))
    return np.ascontiguousarray(np.concatenate(parts, axis=1))


QBLK2 = [(0, 256, [0, 1])] + [(CTX + 512 * i, 512, list(range(NTILE))) for i in range(8)]
KSCALE = 128 ** -0.5


def build_Aodd(layer, kb=None):
    import contextlib
    import math
    lam_init = 0.8 - 0.6 * math.exp(-0.3 * layer)
    kb = kb or KB()
    nc = kb.nc
    h1_d = kb.dram("h1T", [128, NCH, TB], BF16, "ExternalInput")
    wD_d = kb.dram("wD", [128, NCH, 768], F32, "ExternalInput")
    wM_d = kb.dram("wM", [128, NCH, 772], F32, "ExternalInput")
    qkg_d = kb.dram("qkg", [128, 512], F32, "ExternalInput")
    rope_d = kb.dram("rope", [128, 32, 2, 64], F32, "ExternalInput")
    lamp_d = kb.dram("lamp", [128, 4, 64], F32, "ExternalInput")
    subg_d = kb.dram("subg", [128, 128], F32, "ExternalInput")
    subgc_d = kb.dram("subgc", [128, 1], F32, "ExternalInput")
    mlcw_d = kb.dram("mlcw", [128, 2, 3], F32, "ExternalInput")
    mlgb_d = kb.dram("mlgb", [128, 4], F32, "ExternalInput")
    mlng_d = kb.dram("mlng", [128, 256], F32, "ExternalInput")
    msk_d = kb.dram("masks", [128, 4, 128], F32, "ExternalInput")
    id_d = kb.dram("ident", [128, 128], BF16, "ExternalInput")
    out_d = kb.dram("mixT", [128, 4, TB], BF16, "ExternalOutput")
    cs = kb.get_consts()
    pss = [kb.ps(f"ps{i}", [128, 512], F32) for i in range(7)]
    ps_t = [Tok() for _ in range(7)]
    psT = kb.ps("psT", [128, 1024], BF16); psT_t = Tok()
    hbs = [kb.sb(f"hb{i}", [128, NCH, 512], BF16) for i in range(2)]; hbts = [Tok(), Tok()]
    ident = kb.sb("id_s", [128, 128], BF16)
    msk = kb.sb("msk_s", [128, 4, 128], F32)
    k_t = Tok()
    kb.dma("sp", ident[:], id_d, writes=[k_t])
    kb.dma("sp", msk[:], msk_d, writes=[k_t])
    ss = kb.sb("ss", [128, 4], F32); ss_t = Tok()
    junk = kb.sb("junk", [128, 256], F32); junk_t = Tok()
    ost = [kb.sb(f"ost{i}", [128, 2, 128], BF16) for i in range(2)]; ost_t = [Tok(), Tok()]
    onb = kb.sb("onb", [128, 256], BF16); onb_t = Tok()
    T1 = kb.sb("T1", [128, 256], F32); T1_t = Tok()
    T2 = kb.sb("T2", [128, 256], F32); T2_t = Tok()

    with contextlib.ExitStack() as st:
        def sbx(name, shape, dt):
            return st.enter_context(nc.sbuf_tensor(f"{name}_u{kb.uid}", list(shape), dt)).ap()
        wD = sbx("wD_s", [128, NCH, 768], BF16); wD_t = Tok()
        QKT = sbx("QKT", [128, 2, TB], BF16); qkt_t = [Tok() for _ in range(NTILE)]
        KTZ = sbx("KTZ", [128, 4, TB], BF16)
        VA = sbx("VA", [128, NTILE, 2, 132], BF16); va_t = [Tok() for _ in range(NTILE)]
        rope = sbx("rope_s", [128, 32, 2, 64], F32)
        qkg = sbx("qkg_s", [128, 512], F32)
        lamp = sbx("lamp_s", [128, 4, 64], F32)
        subg = sbx("subg_s", [128, 128], F32)
        SQ = sbx("SQ", [128, 512], F32); SQ_t = Tok()
        XN = sbx("XN", [128, 512], F32); XN_t = Tok()
        Y1 = sbx("Y1", [128, 512], F32); Y1_t = Tok()
        Y2 = sbx("Y2", [128, 512], F32); Y2_t = Tok()
        YB = sbx("YB", [128, 512], BF16); YB_t = Tok()
        rs8 = sbx("rs8", [128, 8], F32); rs8_t = Tok()
        PT = [sbx(f"PT{i}", [128, 512], BF16) for i in range(3)]; PT_t = [Tok(), Tok(), Tok()]
        lam = sbx("lam", [128, 8], F32); lam_t = Tok()
        rz = sbx("rz", [128, 4], F32); rz_t = Tok()
        kc_t = Tok()
        kb.dma("pool", wD[:], wD_d, writes=[wD_t])
        for dst, src in ((rope, rope_d), (qkg, qkg_d), (lamp, lamp_d), (subg, subg_d)):
            kb.dma("sp", dst[:], src, writes=[kc_t])
        kb.op("pool", lambda e: e.memset(VA[:], 1.0), writes=va_t)
        kb.op("pool", lambda e: e.memset(KTZ[:], 0.0), writes=qkt_t)
        kb.op("dve", lambda e: e.tensor_tensor(out=SQ[:, 0:64], in0=lamp[:, 0, :], in1=lamp[:, 1, :], op=ALU.mult),
              reads=[kc_t], writes=[SQ_t])
        kb.op("dve", lambda e: e.tensor_tensor(out=SQ[:, 64:128], in0=lamp[:, 2, :], in1=lamp[:, 3, :], op=ALU.mult),
              reads=[kc_t], writes=[SQ_t])
        kb.op("dve", lambda e: e.tensor_reduce(out=lam[:, 0:2], in_=SQ[:, 0:128].rearrange("p (a b) -> p a b", a=2),
                                               axis=mybir.AxisListType.X, op=ALU.add), reads=[SQ_t], writes=[lam_t])
        kb.op("act", lambda e: e.activation(out=lam[:, 2:4], in_=lam[:, 0:2], func=AF.Exp), reads=[lam_t], writes=[lam_t])
        kb.op("dve", lambda e: e.tensor_tensor(out=lam[:, 4:5], in0=lam[:, 3:4], in1=lam[:, 2:3], op=ALU.subtract),
              reads=[lam_t], writes=[lam_t])
        kb.op("dve", lambda e: e.tensor_scalar(out=lam[:, 4:5], in0=lam[:, 4:5], scalar1=-lam_init, scalar2=None, op0=ALU.add),
              reads=[lam_t], writes=[lam_t])
        kb.op("dve", lambda e: e.tensor_scalar(out=subg[:], in0=subg[:], scalar1=(1.0 - lam_init), scalar2=None, op0=ALU.mult),
              reads=[kc_t], writes=[kc_t])

        for bidx, (t0, nt) in enumerate(ABLK):
            n = nt * 128
            g0 = t0 * 128
            hb, hb_t = hbs[bidx % 2], hbts[bidx % 2]
            kb.dma("sp", hb[:, :, 0:n], h1_d[:, :, g0:g0 + n], writes=[hb_t])
            for ti in range(nt):
                tl = t0 + ti
                c0 = ti * 128
                gc = tl * 128
                pa = 2 * (tl % 2)
                pb = pa + 1
                kb.group("pe", [lambda e, k=k, c0=c0: e.matmul(pss[pa][:, 0:512], lhsT=hb[:, k, c0:c0 + 128], rhs=wD[:, k, 0:512],
                                                              start=(k == 0), stop=(k == NCH - 1)) for k in range(NCH)],
                         reads=[wD_t, hb_t], writes=[ps_t[pa]])
                kb.group("pe", [lambda e, k=k, c0=c0: e.matmul(pss[pb][:, 0:256], lhsT=hb[:, k, c0:c0 + 128], rhs=wD[:, k, 512:768],
                                                              start=(k == 0), stop=(k == NCH - 1)) for k in range(NCH)],
                         reads=[wD_t, hb_t], writes=[ps_t[pb]])
                kb.op("act", lambda e: e.activation(out=SQ[:], in_=pss[pa][:, 0:512], func=AF.Square), reads=[ps_t[pa]], writes=[SQ_t])
                kb.op("dve", lambda e: e.tensor_reduce(out=rs8[:], in_=SQ[:].rearrange("p (a b) -> p a b", a=8),
                                                       axis=mybir.AxisListType.X, op=ALU.add), reads=[SQ_t], writes=[rs8_t])
                kb.op("act", lambda e: e.activation(out=rs8[:], in_=rs8[:], func=AF.Sqrt, scale=1.0 / 64, bias=cs.eps[:, 0:1]),
                      reads=[rs8_t, cs.t], writes=[rs8_t])
                kb.op("dve", lambda e: e.reciprocal(out=rs8[:], in_=rs8[:]), reads=[rs8_t], writes=[rs8_t])
                kb.op("dve", lambda e: e.tensor_tensor(out=XN[:].rearrange("p (a b) -> p a b", a=8),
                                                       in0=pss[pa][:, 0:512].rearrange("p (a b) -> p a b", a=8),
                                                       in1=rs8[:].unsqueeze(2).broadcast_to([128, 8, 64]), op=ALU.mult),
                      reads=[ps_t[pa], rs8_t], writes=[XN_t])
                kb.op("pool", lambda e: e.tensor_tensor(out=XN[:], in0=XN[:], in1=qkg[:], op=ALU.mult), reads=[XN_t, kc_t], writes=[XN_t])
                if tl >= 2:
                    lt = tl - 2
                    xv = XN[:].rearrange("p (g h x d) -> p g h x d", g=8, h=2, x=2)
                    yv = Y2[:].rearrange("p (g h x d) -> p g h x d", g=8, h=2, x=2)
                    sv = rope[:, lt, 1, :].rearrange("p (h x d) -> p h x d", h=2, x=2)
                    kb.op("dve", lambda e, lt=lt: e.tensor_tensor(out=Y1[:].rearrange("p (a b) -> p a b", a=8),
                                                                 in0=XN[:].rearrange("p (a b) -> p a b", a=8),
                                                                 in1=rope[:, lt:lt + 1, 0, :].broadcast_to([128, 8, 64]), op=ALU.mult),
                          reads=[XN_t, kc_t], writes=[Y1_t])
                    kb.op("dve", lambda e, xv=xv, yv=yv, sv=sv: e.tensor_tensor(
                        out=yv[:, :, :, 0, :], in0=xv[:, :, :, 1, :],
                        in1=sv[:, :, 0, :].unsqueeze(1).broadcast_to([128, 8, 2, 16]), op=ALU.mult),
                        reads=[XN_t, kc_t], writes=[Y2_t])
                    kb.op("dve", lambda e, xv=xv, yv=yv, sv=sv: e.tensor_tensor(
                        out=yv[:, :, :, 1, :], in0=xv[:, :, :, 0, :],
                        in1=sv[:, :, 1, :].unsqueeze(1).broadcast_to([128, 8, 2, 16]), op=ALU.mult),
                        reads=[XN_t, kc_t], writes=[Y2_t])
                    kb.op("pool", lambda e: e.tensor_tensor(out=YB[:], in0=Y1[:], in1=Y2[:], op=ALU.add),
                          reads=[Y1_t, Y2_t], writes=[YB_t])
                else:
                    kb.op("act", lambda e: e.activation(out=YB[:], in_=XN[:], func=AF.Identity), reads=[XN_t], writes=[YB_t])
                kb.group("pe", [lambda e, a=a: e.transpose(psT[:, a * 128:(a + 1) * 128], YB[:, a * 128:(a + 1) * 128], ident[:])
                                for a in range(4)], reads=[YB_t, k_t], writes=[psT_t])
                kb.op("act", lambda e, gc=gc: e.activation(out=QKT[:, :, gc:gc + 128],
                                                           in_=psT[:, 0:256].rearrange("p (a b) -> p a b", a=2), func=AF.Identity),
                      reads=[psT_t], writes=[qkt_t[tl]])
                for hh in range(2):
                    for mm in range(2):
                        kb.op("dve" if (hh + mm) % 2 else "act",
                              (lambda e, hh=hh, mm=mm, gc=gc: e.tensor_copy(
                                  out=KTZ[mm * 64:(mm + 1) * 64, hh * 2 + mm, gc:gc + 128],
                                  in_=psT[mm * 64:(mm + 1) * 64, (2 + hh) * 128:(3 + hh) * 128])) if (hh + mm) % 2 else
                              (lambda e, hh=hh, mm=mm, gc=gc: e.activation(
                                  out=KTZ[mm * 64:(mm + 1) * 64, hh * 2 + mm, gc:gc + 128],
                                  in_=psT[mm * 64:(mm + 1) * 64, (2 + hh) * 128:(3 + hh) * 128], func=AF.Identity)),
                              reads=[psT_t], writes=[qkt_t[tl]])
                kb.op("act", lambda e, tl=tl: e.activation(out=VA[:, tl, :, 0:128],
                                                           in_=pss[pb][:, 0:256].rearrange("p (a b) -> p a b", a=2), func=AF.Identity),
                      reads=[ps_t[pb]], writes=[va_t[tl]])

        T0 = sbx("T0", [128, 4, 128], F32); T0_t = Tok()
        OSB = [sbx(f"OSB{i}", [128, 4, 132], F32) for i in range(2)]; OSB_t = [Tok(), Tok()]
        oi = 0
        for h in range(2):
            for (q0, nq, ktiles) in QBLK2:
                nqt = nq // 128
                qtl = [q0 // 128 + i for i in range(nqt)]
                nk = len(ktiles)
                for m in range(2):
                    SB = [0, 1, 6]

                    def st_fn(ii, h=h, q0=q0, nq=nq, ktiles=ktiles, m=m):
                        kt = ktiles[ii]
                        bnk = SB[ii % 3]
                        return lambda e: e.matmul(
                            pss[bnk][:, 0:nq], lhsT=KTZ[:, h * 2 + m, kt * 128:(kt + 1) * 128],
                            rhs=QKT[:, h, q0:q0 + nq], start=True, stop=True)
                    qreads = [qkt_t[x] for x in qtl]
                    for pre in range(min(2, nk)):
                        kb.op("pe", st_fn(pre), reads=[qkt_t[ktiles[pre]]] + qreads, writes=[ps_t[SB[pre % 3]]])
                    for ii in range(nk):
                        kt = ktiles[ii]
                        bnk = SB[ii % 3]
                        pb = ii % 3
                        kb.op("act", lambda e, bnk=bnk, pb=pb, nq=nq: e.activation(out=PT[pb][:, 0:nq], in_=pss[bnk][:, 0:nq], func=AF.Exp, scale=0.125),
                              reads=[ps_t[bnk]], writes=[PT_t[pb]])
                        fns = [lambda e, qt=qt, pb=pb, kt=kt, ii=ii, nk=nk, h=h: e.matmul(
                            pss[2 + qt][:, 0:129], lhsT=PT[pb][:, qt * 128:(qt + 1) * 128],
                            rhs=VA[:, kt, h, 0:129], start=(ii == 0), stop=(ii == nk - 1)) for qt in range(nqt)]
                        rd = [PT_t[pb], va_t[kt]]
                        wr = [ps_t[2 + qt] for qt in range(nqt)]
                        if ii + 2 < nk:
                            fns.append(st_fn(ii + 2))
                            rd += [qkt_t[ktiles[ii + 2]]] + qreads
                            wr.append(ps_t[SB[(ii + 2) % 3]])
                        kb.group("pe", fns, reads=rd, writes=wr)
                    for qt in range(nqt):
                        kb.op("dve", lambda e, qt=qt, m=m: e.tensor_copy(out=OSB[m][:, qt, 0:129], in_=pss[2 + qt][:, 0:129]),
                              reads=[ps_t[2 + qt]], writes=[OSB_t[m]])
                    for qt in range(nqt):
                        kb.op("dve", lambda e, qt=qt, m=m: e.reciprocal(out=rz[:, 0:1], in_=OSB[m][:, qt, 128:129]), reads=[OSB_t[m]], writes=[rz_t])
                        if m == 0:
                            kb.op("pool", lambda e, qt=qt, m=m: e.tensor_scalar(out=T0[:, qt, :], in0=OSB[m][:, qt, 0:128], scalar1=rz[:, 0:1], scalar2=None,
                                                                                op0=ALU.mult),
                                  reads=[OSB_t[m], rz_t], writes=[T0_t])
                            continue
                        kb.op("dve", lambda e: e.tensor_tensor(out=rz[:, 2:3], in0=rz[:, 0:1], in1=lam[:, 4:5], op=ALU.mult),
                              reads=[rz_t, lam_t], writes=[rz_t])
                        kb.op("dve", lambda e, qt=qt, m=m: e.scalar_tensor_tensor(out=T2[:, 0:128], in0=OSB[m][:, qt, 0:128], scalar=rz[:, 2:3],
                                                                                 in1=T0[:, qt, :], op0=ALU.mult, op1=ALU.add),
                              reads=[OSB_t[m], rz_t, T0_t], writes=[T2_t])
                        kb.op("act", lambda e: e.activation(out=junk[:, 0:128], in_=T2[:, 0:128], func=AF.Square, accum_out=ss[:, 0:1]),
                              reads=[T2_t], writes=[junk_t, ss_t])
                        kb.op("act", lambda e: e.activation(out=ss[:, 1:2], in_=ss[:, 0:1], func=AF.Sqrt, scale=1.0 / 128, bias=cs.eps[:, 0:1]),
                              reads=[ss_t, cs.t], writes=[ss_t])
                        kb.op("dve", lambda e: e.reciprocal(out=ss[:, 1:2], in_=ss[:, 1:2]), reads=[ss_t], writes=[ss_t])
                        kb.op("dve", lambda e: e.scalar_tensor_tensor(out=onb[:, 0:128], in0=T2[:, 0:128], scalar=ss[:, 1:2], in1=subg[:],
                                                                      op0=ALU.mult, op1=ALU.mult),
                              reads=[T2_t, ss_t, kc_t], writes=[onb_t])
                        kb.op("pe", lambda e: e.transpose(psT[:, 0:128], onb[:, 0:128], ident[:]), reads=[onb_t, k_t], writes=[psT_t])
                        oj = oi % 2
                        oi += 1
                        kb.op("act", lambda e, oj=oj: e.activation(out=ost[oj][:, 0, :], in_=psT[:, 0:128], func=AF.Identity),
                              reads=[psT_t], writes=[ost_t[oj]])
                        qa = q0 + qt * 128
                        kb.dma("sp", out_d[:, h, qa:qa + 128], ost[oj][:, 0, :], reads=[ost_t[oj]], is_output=True)
        kb.barrier()

    QM = kb.sb("QM", [128, TB], BF16); KM = kb.sb("KM", [128, TB], BF16); qm_t = Tok()
    VA2 = kb.sb("VA2", [128, NTILE, 260], BF16); va2_t = [Tok() for _ in range(NTILE)]
    SO = kb.sb("SO", [128, NTILE, 256], BF16); so_t = [Tok() for _ in range(NTILE)]
    G4 = kb.sb("G4", [128, NTILE, 4], F32); g4_t = Tok()
    mlgb = kb.sb("mlgb_s", [128, 4], F32); mlng = kb.sb("mlng_s", [128, 256], F32); mlcw = kb.sb("mlcw_s", [128, 2, 3], F32)
    km_t = Tok()
    for dst, src in ((mlgb, mlgb_d), (mlng, mlng_d), (mlcw, mlcw_d)):
        kb.dma("sp", dst[:], src, writes=[km_t])
    kb.op("pool", lambda e: e.memset(VA2[:], 1.0), writes=va2_t)
    with contextlib.ExitStack() as st:
        def sbx(name, shape, dt):
            return st.enter_context(nc.sbuf_tensor(f"{name}_u{kb.uid}", list(shape), dt)).ap()
        wM = sbx("wM_s", [128, NCH, 772], BF16); wM_t = Tok()
        RQ = sbx("RQ", [128, 2, TB + 4], F32); RQ_t = Tok()
        CV = sbx("CV", [128, SEQ], F32); CV_t = Tok()
        g4s = sbx("g4s", [128, 4], F32); g4s_t = Tok()
        kb.dma("pool", wM[:], wM_d, writes=[wM_t])
        kb.op("pool", lambda e: e.memset(RQ[:], 0.0), writes=[RQ_t])

        def ucol(tok):
            return 1 + tok if tok < CTX else 3 + tok
        for bidx, (t0, nt) in enumerate(ABLK):
            n = nt * 128
            g0 = t0 * 128
            hb, hb_t = hbs[bidx % 2], hbts[bidx % 2]
            kb.dma("sp", hb[:, :, 0:n], h1_d[:, :, g0:g0 + n], writes=[hb_t])
            for w in range(2):
                kb.group("pe", [lambda e, k=k, w=w, n=n: e.matmul(pss[w][:, 0:n], lhsT=wM[:, k, w * 128:(w + 1) * 128], rhs=hb[:, k, 0:n],
                                                                 start=(k == 0), stop=(k == NCH - 1)) for k in range(NCH)],
                         reads=[wM_t, hb_t], writes=[ps_t[w]])
                uc = ucol(g0)
                kb.op("act", lambda e, w=w, n=n, uc=uc: e.activation(out=RQ[:, w, uc:uc + n], in_=pss[w][:, 0:n], func=AF.Identity),
                      reads=[ps_t[w]], writes=[RQ_t])
            for ti in range(nt):
                tl = t0 + ti
                c0 = ti * 128
                kb.group("pe", [lambda e, k=k, c0=c0: e.matmul(pss[2][:, 0:512], lhsT=hb[:, k, c0:c0 + 128], rhs=wM[:, k, 256:768],
                                                              start=(k == 0), stop=(k == NCH - 1)) for k in range(NCH)],
                         reads=[wM_t, hb_t], writes=[ps_t[2]])
                kb.group("pe", [lambda e, k=k, c0=c0: e.matmul(pss[3][:, 0:4], lhsT=hb[:, k, c0:c0 + 128], rhs=wM[:, k, 768:772],
                                                              start=(k == 0), stop=(k == NCH - 1)) for k in range(NCH)],
                         reads=[wM_t, hb_t], writes=[ps_t[3]])
                kb.op("act", lambda e, tl=tl: e.activation(out=VA2[:, tl, 0:256], in_=pss[2][:, 0:256], func=AF.Identity),
                      reads=[ps_t[2]], writes=[va2_t[tl]])
                kb.op("act", lambda e: e.activation(out=T1[:], in_=pss[2][:, 256:512], func=AF.Sigmoid), reads=[ps_t[2]], writes=[T1_t])
                kb.op("pool", lambda e, tl=tl: e.tensor_tensor(out=SO[:, tl, :], in0=T1[:], in1=mlng[:], op=ALU.mult),
                      reads=[T1_t, km_t], writes=[so_t[tl]])
                kb.op("dve", lambda e, tl=tl: e.tensor_tensor(out=G4[:, tl, :], in0=pss[3][:, 0:4], in1=mlgb[:], op=ALU.add),
                      reads=[ps_t[3], km_t], writes=[g4_t])
                kb.op("act", lambda e, tl=tl: e.activation(out=g4s[:], in_=G4[:, tl, :], func=AF.Exp, scale=-1.0), reads=[g4_t], writes=[g4s_t])
                kb.op("act", lambda e: e.activation(out=g4s[:], in_=g4s[:], func=AF.Ln, bias=1.0, scale=1.0), reads=[g4s_t], writes=[g4s_t])
                for col in (1, 3):
                    kb.op("dve", lambda e, tl=tl, col=col: e.tensor_scalar(out=G4[:, tl, col:col + 1], in0=g4s[:, col:col + 1], scalar1=-1.0,
                                                                          scalar2=None, op0=ALU.mult),
                          reads=[g4s_t, g4_t], writes=[g4_t])
        for w, dst, scl in ((0, QM, 1.0), (1, KM, KSCALE)):
            for (g0, n) in ((0, CTX), (CTX, SEQ)):
                uc = ucol(g0)
                kb.op("dve", lambda e, w=w, n=n, uc=uc: e.tensor_scalar(out=CV[:, 0:n], in0=RQ[:, w, uc:uc + n], scalar1=mlcw[:, w, 1:2],
                                                                       scalar2=None, op0=ALU.mult), reads=[RQ_t, km_t], writes=[CV_t])
                kb.op("dve", lambda e, w=w, n=n, uc=uc: e.scalar_tensor_tensor(out=CV[:, 0:n], in0=RQ[:, w, uc - 1:uc - 1 + n],
                                                                              scalar=mlcw[:, w, 0:1], in1=CV[:, 0:n], op0=ALU.mult, op1=ALU.add),
                      reads=[RQ_t, km_t, CV_t], writes=[CV_t])
                kb.op("dve", lambda e, w=w, n=n, uc=uc: e.scalar_tensor_tensor(out=CV[:, 0:n], in0=RQ[:, w, uc + 1:uc + 1 + n],
                                                                              scalar=mlcw[:, w, 2:3], in1=CV[:, 0:n], op0=ALU.mult, op1=ALU.add),
                      reads=[RQ_t, km_t, CV_t], writes=[CV_t])
                kb.op("act", lambda e, n=n: e.activation(out=CV[:, 0:n], in_=CV[:, 0:n], func=AF.Silu), reads=[CV_t], writes=[CV_t])
                kb.op("dve", lambda e, dst=dst, g0=g0, n=n, scl=scl: e.tensor_scalar(out=dst[:, g0:g0 + n], in0=CV[:, 0:n], scalar1=scl,
                                                                                    scalar2=None, op0=ALU.mult), reads=[CV_t], writes=[qm_t])
        kb.barrier()

    QP = [kb.sb(f"QP{d}", [128, TB], BF16) for d in range(2)]
    STm = [kb.sb(f"STm{d}", [128, NTILE, 128], BF16) for d in range(2)]
    KP = [kb.sb(f"KP{d}", [128, NTILE, 128], BF16) for d in range(2)]
    CF = kb.sb("CF", [128, NTILE, 260], BF16); cf_t = [Tok() for _ in range(NTILE)]
    eL = kb.sb("eL", [128, 2, NTILE], F32); eL_t = Tok()
    ch_t = [Tok() for _ in range(NTILE)]
    LF = [kb.sb(f"LF{d}", [128, 128], F32) for d in range(2)]; LF_t = [Tok(), Tok()]
    TM = [kb.sb(f"TM{d}", [128, 128], F32) for d in range(2)]; TM_t = [Tok(), Tok()]
    EB = [kb.sb(f"EB{d}", [128, 128], F32) for d in range(2)]; EB_t = [Tok(), Tok()]
    cj = kb.sb("cj", [128, 8], F32); cj_t = Tok()
    for c in range(NTILE):
        gc = c * 128
        for d in range(2):
            kb.op("dve", lambda e, d=d, c=c: e.tensor_scalar(out=LF[d][:], in0=cs.ones[:], scalar1=G4[:, c, 2 * d + 1:2 * d + 2], scalar2=None,
                                                            op0=ALU.mult), reads=[g4_t, cs.t], writes=[LF_t[d]])
        kb.group("pe", [lambda e, d=d: e.matmul(pss[4][:, d * 128:(d + 1) * 128], lhsT=LF[d][:], rhs=msk[:, d, :], start=True, stop=True)
                        for d in range(2)] +
                 [lambda e, d=d, c=c: e.matmul(pss[4][:, 256 + 4 * d:260 + 4 * d], lhsT=msk[:, d, :], rhs=G4[:, c, :], start=True, stop=True)
                  for d in range(2)],
                 reads=LF_t + [k_t, g4_t], writes=[ps_t[4]])
        kb.op("pe", lambda e, gc=gc: e.matmul(pss[5][:, 0:128], lhsT=KM[:, gc:gc + 128], rhs=QM[:, gc:gc + 128], start=True, stop=True),
              reads=[qm_t], writes=[ps_t[5]])
        kb.op("pe", lambda e, gc=gc: e.transpose(psT[:, 0:128], KM[:, gc:gc + 128], ident[:]), reads=[qm_t, k_t], writes=[psT_t])
        for d in range(2):
            last = 127 if d == 0 else 0
            kb.op("dve", lambda e, d=d, c=c: e.tensor_tensor(out=cj[:, d:d + 1], in0=G4[:, c, 2 * d:2 * d + 1],
                                                            in1=pss[4][:, 256 + 4 * d + 2 * d + 1:256 + 4 * d + 2 * d + 2], op=ALU.subtract),
                  reads=[ps_t[4], g4_t], writes=[cj_t])
            kb.op("dve", lambda e, d=d, last=last: e.tensor_copy(out=cj[:, 2 + d:3 + d], in_=pss[4][:, d * 128 + last:d * 128 + last + 1]),
                  reads=[ps_t[4]], writes=[cj_t])
            kb.op("dve", lambda e, d=d: e.tensor_tensor(out=TM[d][:], in0=pss[4][:, d * 128:(d + 1) * 128], in1=msk[:, 2 + d, :], op=ALU.add),
                  reads=[ps_t[4], k_t], writes=[TM_t[d]])
            kb.op("act", lambda e, d=d: e.activation(out=TM[d][:], in_=TM[d][:], func=AF.Exp, bias=cj[:, d:d + 1], scale=1.0),
                  reads=[TM_t[d], cj_t], writes=[TM_t[d]])
            kb.op("act", lambda e, d=d: e.activation(out=EB[d][:], in_=pss[4][:, d * 128:(d + 1) * 128], func=AF.Exp),
                  reads=[ps_t[4]], writes=[EB_t[d]])
            kb.op("act", lambda e, d=d: e.activation(out=cj[:, 4 + d:5 + d], in_=cj[:, d:d + 1], func=AF.Exp, bias=cj[:, 2 + d:3 + d], scale=1.0),
                  reads=[cj_t], writes=[cj_t])
            kb.op("pool", lambda e, d=d, c=c, last=last: e.tensor_copy(out=eL[:, d, c:c + 1], in_=EB[d][:, last:last + 1]),
                  reads=[EB_t[d]], writes=[eL_t])
            kb.op("dve", lambda e, d=d, c=c: e.tensor_tensor(out=STm[d][:, c, :], in0=pss[5][:, 0:128], in1=TM[d][:], op=ALU.mult),
                  reads=[ps_t[5], TM_t[d]], writes=[ch_t[c]])
            kb.op("dve", lambda e, d=d, gc=gc: e.tensor_tensor(out=QP[d][:, gc:gc + 128], in0=QM[:, gc:gc + 128], in1=EB[d][:], op=ALU.mult),
                  reads=[qm_t, EB_t[d]], writes=[ch_t[c]])
            kb.op("dve", lambda e, d=d, c=c: e.tensor_scalar(out=KP[d][:, c, :], in0=psT[:, 0:128], scalar1=cj[:, 4 + d:5 + d], scalar2=None,
                                                            op0=ALU.mult), reads=[psT_t, cj_t], writes=[ch_t[c]])

    Cst = kb.sb("Cst", [128, 260], F32); Cst_t = Tok()
    kb.op("pool", lambda e: e.memset(Cst[:], 0.0), writes=[Cst_t])
    for c in range(NTILE):
        kb.op("act", lambda e, c=c: e.activation(out=CF[:, c, :], in_=Cst[:], func=AF.Identity), reads=[Cst_t], writes=[cf_t[c]])
        if c == NTILE - 1:
            break
        kb.op("pe", lambda e, c=c: e.matmul(pss[0][:, 0:257], lhsT=KP[0][:, c, :], rhs=VA2[:, c, 0:257], start=True, stop=True),
              reads=[ch_t[c], va2_t[c]], writes=[ps_t[0]])
        kb.op("dve", lambda e, c=c: e.scalar_tensor_tensor(out=Cst[:, 0:257], in0=Cst[:, 0:257], scalar=eL[:, 0, c:c + 1], in1=pss[0][:, 0:257],
                                                          op0=ALU.mult, op1=ALU.add), reads=[ps_t[0], eL_t, Cst_t], writes=[Cst_t])
    Cb = kb.sb("Cb", [128, 260], F32); Cb_t = Tok()
    Cbb = kb.sb("Cbb", [128, 260], BF16); Cbb_t = Tok()
    kb.op("pool", lambda e: e.memset(Cb[:], 0.0), writes=[Cb_t])
    kb.op("pool", lambda e: e.memset(Cbb[:], 0.0), writes=[Cbb_t])
    for oi, c in enumerate(BWD_ORDER):
        gc = c * 128
        kb.group("pe", [lambda e, c=c, gc=gc: e.matmul(pss[1][:, 0:257], lhsT=QP[0][:, gc:gc + 128], rhs=CF[:, c, 0:257], start=True, stop=False),
                        lambda e, c=c, gc=gc: e.matmul(pss[1][:, 0:257], lhsT=STm[0][:, c, :], rhs=VA2[:, c, 0:257], start=False, stop=True)],
                 reads=[ch_t[c], cf_t[c], va2_t[c]], writes=[ps_t[1]])
        kb.group("pe", [lambda e, c=c, gc=gc: e.matmul(pss[2][:, 0:257], lhsT=QP[1][:, gc:gc + 128], rhs=Cbb[:, 0:257], start=True, stop=False),
                        lambda e, c=c, gc=gc: e.matmul(pss[2][:, 0:257], lhsT=STm[1][:, c, :], rhs=VA2[:, c, 0:257], start=False, stop=True)],
                 reads=[ch_t[c], Cbb_t, va2_t[c]], writes=[ps_t[2]])
        kb.op("pe", lambda e, c=c: e.matmul(pss[3][:, 0:257], lhsT=KP[1][:, c, :], rhs=VA2[:, c, 0:257], start=True, stop=True),
              reads=[ch_t[c], va2_t[c]], writes=[ps_t[3]])
        kb.op("dve", lambda e, c=c: e.scalar_tensor_tensor(out=Cb[:, 0:257], in0=Cb[:, 0:257], scalar=eL[:, 1, c:c + 1], in1=pss[3][:, 0:257],
                                                          op0=ALU.mult, op1=ALU.add), reads=[ps_t[3], eL_t, Cb_t], writes=[Cb_t])
        kb.op("act", lambda e: e.activation(out=Cbb[:], in_=Cb[:], func=AF.Identity), reads=[Cb_t], writes=[Cbb_t])
        for d, bk in ((0, 1), (1, 2)):
            kb.op("act", lambda e, d=d, bk=bk: e.activation(out=ss[:, 2 + d:3 + d], in_=pss[bk][:, 256:257], func=AF.Abs),
                  reads=[ps_t[bk]], writes=[ss_t])
            kb.op("dve", lambda e, d=d: e.tensor_scalar(out=ss[:, 2 + d:3 + d], in0=ss[:, 2 + d:3 + d], scalar1=1.0, scalar2=None,
                                                       op0=ALU.max), reads=[ss_t], writes=[ss_t])
        kb.op("dve", lambda e: e.reciprocal(out=ss[:, 2:4], in_=ss[:, 2:4]), reads=[ss_t], writes=[ss_t])
        kb.op("act", lambda e: e.activation(out=T1[:], in_=pss[1][:, 0:256], func=AF.Identity, scale=ss[:, 2:3]),
              reads=[ps_t[1], ss_t], writes=[T1_t])
        kb.op("dve", lambda e: e.scalar_tensor_tensor(out=T2[:], in0=pss[2][:, 0:256], scalar=ss[:, 3:4], in1=T1[:], op0=ALU.mult, op1=ALU.add),
              reads=[ps_t[2], ss_t, T1_t], writes=[T2_t])
        kb.op("act", lambda e: e.activation(out=junk[:], in_=T2[:], func=AF.Square, accum_out=ss[:, 0:1]), reads=[T2_t], writes=[junk_t, ss_t])
        kb.op("act", lambda e: e.activation(out=ss[:, 1:2], in_=ss[:, 0:1], func=AF.Sqrt, scale=1.0 / 256, bias=cs.eps[:, 0:1]),
              reads=[ss_t, cs.t], writes=[ss_t])
        kb.op("dve", lambda e: e.reciprocal(out=ss[:, 1:2], in_=ss[:, 1:2]), reads=[ss_t], writes=[ss_t])
        kb.op("dve", lambda e, c=c: e.scalar_tensor_tensor(out=onb[:], in0=T2[:], scalar=ss[:, 1:2], in1=SO[:, c, :], op0=ALU.mult, op1=ALU.mult),
              reads=[T2_t, ss_t, so_t[c]], writes=[onb_t])
        kb.group("pe", [lambda e, hh=hh: e.transpose(psT[:, hh * 128:(hh + 1) * 128], onb[:, hh * 128:(hh + 1) * 128], ident[:])
                        for hh in range(2)], reads=[onb_t, k_t], writes=[psT_t])
        oj = oi % 2
        kb.op("act", lambda e, oj=oj: e.activation(out=ost[oj][:].rearrange("p a b -> p (a b)"), in_=psT[:, 0:256], func=AF.Identity),
              reads=[psT_t], writes=[ost_t[oj]])
        kb.dma("sp", out_d[:, 2:4, gc:gc + 128], ost[oj][:], reads=[ost_t[oj]], is_output=True)
    return kb.end_phase()


def host_rope():
    t = np.arange(SEQ)
    row = (t // 64).astype(np.float32)
    col = (t % 64).astype(np.float32)
    inv = np.power(np.float32(10000.0), -np.arange(16, dtype=np.float32) / np.float32(16)).astype(np.float32)
    ar = (row[:, None] * inv).astype(np.float32)
    ac = (col[:, None] * inv).astype(np.float32)
    cr, sr, cc, sc = np.cos(ar), np.sin(ar), np.cos(ac), np.sin(ac)
    C = np.concatenate([cr, cr, cc, cc], axis=1)
    S = np.concatenate([-sr, sr, -sc, sc], axis=1)
    tab = np.stack([C, S], axis=1).astype(np.float32)
    return np.ascontiguousarray(tab.reshape(32, 128, 2, 64).transpose(1, 0, 2, 3))


def host_masks_odd():
    j = np.arange(128)[:, None]
    i = np.arange(128)[None, :]
    mf = (j <= i).astype(np.float32)
    mb = (j >= i).astype(np.float32)
    return np.ascontiguousarray(np.stack([mf, mb, (1 - mf) * -30000.0, (1 - mb) * -30000.0], axis=1).astype(np.float32))


def bc(v, n=128):
    return np.ascontiguousarray(np.broadcast_to(np.asarray(v, np.float32).reshape(1, -1), (n, v.size)))


def host_Aodd_inputs(g, layer, w_in, qn_g, kn_g, lam_p, subln_g, ml_conv_w, ml_gate_b, ml_norm_g):
    colsD = np.concatenate([np.arange(256 * g, 256 * g + 256), np.arange(1024 + 256 * g, 1024 + 256 * g + 256),
                            np.arange(2048 + 256 * g, 2048 + 256 * g + 256)])
    gidx = [0 * 8 + 0 * 4 + g, 0 * 8 + 1 * 4 + g, 1 * 8 + 0 * 4 + g, 1 * 8 + 1 * 4 + g]
    colsM = np.concatenate([np.arange(3072 + 128 * g, 3072 + 128 * g + 128), np.arange(3584 + 128 * g, 3584 + 128 * g + 128),
                            np.arange(4096 + 256 * g, 4096 + 256 * g + 256), np.arange(5120 + 256 * g, 5120 + 256 * g + 256),
                            6144 + np.array(gidx)])
    qkg = bc(np.concatenate([np.tile(qn_g, 4), np.tile(kn_g, 4)]))
    mlcw = np.stack([ml_conv_w[:, 128 * g:128 * g + 128], ml_conv_w[:, 512 + 128 * g:512 + 128 * g + 128]], axis=0)
    mlcw = np.ascontiguousarray(mlcw.transpose(2, 0, 1))
    return {"wD": wfm(w_in[:, colsD]), "wM": wfm(w_in[:, colsM]), "qkg": qkg, "rope": host_rope(),
            "lamp": np.ascontiguousarray(np.broadcast_to(lam_p[None], (128, 4, 64))), "subg": bc(subln_g),
            "subgc": np.ascontiguousarray(subln_g.reshape(128, 1).astype(np.float32)),
            "mlcw": mlcw, "mlgb": bc(ml_gate_b[gidx]), "mlng": bc(ml_norm_g), "masks": host_masks_odd(),
            "ident": np.eye(128, dtype=np.float32).astype(ml_dtypes.bfloat16)}


def emit_Mfull(kb, condT2, adaw, adab, mods_out):
    sc = kb.sb("mf_sc", [128, NCH, 2], F32); sc_t = Tok()
    bsb = kb.sb("mf_b", [128, DEPTH, 96], F32); b_t = Tok()
    res = kb.sb("mf_res", [128, DEPTH, 96, 2], F32); res_t = Tok()
    wb = [kb.sb(f"mf_w{i}", [128, NCH, 768], F32) for i in range(2)]
    w_t = [Tok() for _ in range(2)]
    pst = [kb.ps(f"mf_ps{i}", [128, 512], F32) for i in range(2)]
    ps_t = [Tok() for _ in range(2)]
    kb.dma("sp", sc[:], condT2, writes=[sc_t])
    kb.dma("sp", bsb[:], adab, writes=[b_t])
    kb.op("act", lambda e: e.activation(out=sc[:], in_=sc[:], func=AF.Silu), reads=[sc_t], writes=[sc_t])
    it = 0
    for l in range(DEPTH):
        for blk in range(16):
            i = it % 2
            it += 1
            kb.dma("sp", wb[i][:], adaw[l, blk], writes=[w_t[i]])
            for mm in range(6):
                m = blk * 6 + mm
                j = m % 2
                kb.group("pe", [lambda e, k=k, mm=mm, i=i, j=j: e.matmul(pst[j][:, 0:2], lhsT=wb[i][:, k, mm * 128:(mm + 1) * 128],
                                                                          rhs=sc[:, k, 0:2], start=(k == 0), stop=(k == NCH - 1))
                                 for k in range(NCH)],
                         reads=[w_t[i], sc_t], writes=[ps_t[j]])
                kb.op("dve", lambda e, l=l, m=m, j=j: e.tensor_scalar(out=res[:, l, m, :], in0=pst[j][:, 0:2],
                                                                       scalar1=bsb[:, l, m:m + 1], scalar2=None, op0=ALU.add),
                      reads=[ps_t[j], b_t], writes=[res_t])
    for l in range(DEPTH):
        kb.dma("sp", mods_out[l], res[:, l, :, :].rearrange("p (s c) r -> p s c r", s=6), reads=[res_t])


def build_fused():
    kb = KB()
    nc = kb.nc

    def din(name, shape, dt=F32):
        return nc.dram_tensor(name, list(shape), dt, kind="ExternalInput").ap()

    def dint(name, shape, dt=F32):
        return nc.dram_tensor(name, list(shape), dt, kind="Internal").ap()

    xT = din("xT", [4, 128, NCH, NTOK])
    condT2 = din("condT2", [128, NCH, 2])
    adaw = din("adaw", [DEPTH, 16, 128, NCH, 768])
    adab = din("adab", [128, DEPTH, 96])
    ng1 = din("ng1", [DEPTH, 128, NCH])
    ng2 = din("ng2", [DEPTH, 128, NCH])
    zcol = din("zcol", [128, NCH, 1], BF16)
    wout = din("wout", [DEPTH, NCH, 128, NCH, 128])
    wup = din("wup", [DEPTH, FCH, 128, NCH, 256])
    wdn = din("wdn", [DEPTH, NQ, NCH, 128, QF, 128])
    convw = din("convw", [DEPTH, 128, FCH, 4])
    ident = din("ident", [128, 128], BF16)
    e_wA = din("e_wA", [2, 4, 128, NCH, 896]); e_wC = din("e_wC", [2, 4, 128, NCH, 768])
    e_w2 = din("e_w2", [2, 4, 128, 2, 128]); e_gb = din("e_gb", [2, 4, 128, 256]); e_gn = din("e_gn", [2, 128, 256])
    e_scw = din("e_scw", [2, 4, 128, 2, 3]); e_msk = din("e_msk", [128, 4, 128])
    o_wD = din("o_wD", [2, 4, 128, NCH, 768]); o_wM = din("o_wM", [2, 4, 128, NCH, 772])
    o_qkg = din("o_qkg", [2, 128, 512]); o_rope = din("o_rope", [128, 32, 2, 64]); o_lamp = din("o_lamp", [2, 128, 4, 64])
    o_subg = din("o_subg", [2, 128, 128]); o_subgc = din("o_subgc", [2, 128, 1]); o_mlcw = din("o_mlcw", [2, 4, 128, 2, 3]); o_mlgb = din("o_mlgb", [2, 4, 128, 4])
    o_mlng = din("o_mlng", [2, 128, 256]); o_msk = din("o_msk", [128, 4, 128])
    out = nc.dram_tensor("xoutT", [1, 128, NCH, NTOK], F32, kind="ExternalOutput").ap()
    MODS = dint("s_mods", [DEPTH, 128, 6, NCH, 2])
    X = dint("s_x", [4, 128, NCH, NTOK])
    XM = dint("s_xm", [4, 128, NCH, NTOK])
    H1L = dint("s_h1l", [4, 128, NCH, NTOK], BF16)
    H1G = dint("s_h1g", [128, NCH, TB], BF16)
    MIXG = dint("s_mixg", [4, 128, 4, TB], BF16)
    MIXL = dint("s_mixl", [128, NCH, NTOK], BF16)
    H2P = dint("s_h2p", [6, 128, NCH, NTOK], BF16)
    H2L = [H2P[j + 1] for j in range(4)]
    H2S = dint("s_h2s", [3, 128, NCH, NTOK], BF16)
    XMS = dint("s_xms", [1, 128, NCH, NTOK])
    zslot = din("zslot", [128, NCH, NTOK], BF16)
    H2H = dint("s_h2h", [128, NCH, HW_], BF16)

    def copies(pairs):
        for (dst, src) in pairs:
            kb.dma("sp", dst, src, slow=(dst.shape[-1] == 1))
        kb.barrier()

    kb.dma("sp", H2P[0], zslot)
    kb.dma("sp", H2P[5], zslot)
    with kb.phase({}):
        emit_Mfull(kb, condT2, adaw, adab, MODS)
    for j in range(4):
        with kb.phase({"xT": xT[j], "mods": MODS[0], "normg": ng1[0], "h1T": H1L[j]}):
            build_P0(kb=kb)
    for layer in range(DEPTH):
        i2 = layer // 2
        last = layer == DEPTH - 1
        copies([(H1G[:, :, 64 * j:64 * j + 64], H1L[j][:, :, 0:64]) for j in range(4)] +
               [(H1G[:, :, CTX + 1024 * j:CTX + 1024 * j + 1024], H1L[j][:, :, 64:NTOK]) for j in range(4)])
        for g in range(4):
            if layer % 2 == 0:
                io = {"h1T": H1G, "wA": e_wA[i2, g], "wC": e_wC[i2, g], "w2": e_w2[i2, g], "gbias": e_gb[i2, g], "gnorm": e_gn[i2],
                      "scw": e_scw[i2, g], "masks": e_msk, "ident": ident, "mixT": MIXG[g]}
                with kb.phase(io):
                    build_Aeven(kb=kb)
            else:
                io = {"h1T": H1G, "wD": o_wD[i2, g], "wM": o_wM[i2, g], "qkg": o_qkg[i2], "rope": o_rope, "lamp": o_lamp[i2],
                      "subg": o_subg[i2], "subgc": o_subgc[i2], "mlcw": o_mlcw[i2, g], "mlgb": o_mlgb[i2, g], "mlng": o_mlng[i2], "masks": o_msk,
                      "ident": ident, "mixT": MIXG[g]}
                with kb.phase(io):
                    build_Aodd(layer, kb=kb)
        for j in range(4):
            copies([(MIXL[:, 4 * g:4 * g + 4, 0:64], MIXG[g][:, :, 64 * j:64 * j + 64]) for g in range(4)] +
                   [(MIXL[:, 4 * g:4 * g + 4, 64:NTOK], MIXG[g][:, :, CTX + 1024 * j:CTX + 1024 * j + 1024]) for g in range(4)])
            xin = xT[j] if layer == 0 else X[j]
            with kb.phase({"xT": xin, "mixT": MIXL, "wout": wout[layer], "mods": MODS[layer], "normg": ng2[layer],
                           "xmidT": XM[j], "h2T": H2L[j]}):
                build_B1(kb=kb)
        if last:
            jv = nc.sync.snap(nc.sync.partition_id() % 4, min_val=0, max_val=3)
            kb.dma("sp", XMS, XM[bass.ds(jv, 1)])
            kb.dma("sp", H2S, H2P[bass.ds(jv, 3)])
            kb.barrier()
            copies([(H2H[:, :, 1:65], H2S[1][:, :, 0:64]), (H2H[:, :, 67:1091], H2S[1][:, :, 64:NTOK]),
                    (H2H[:, :, 0:1], H2S[0][:, :, 63:64]), (H2H[:, :, 66:67], H2S[0][:, :, NTOK - 1:NTOK]),
                    (H2H[:, :, 65:66], H2S[2][:, :, 0:1]), (H2H[:, :, 1091:1092], H2S[2][:, :, 64:65])])
            io = {"xmidT": XMS[0], "h2hT": H2H, "wup": wup[layer], "wdn": wdn[layer], "convw": convw[layer], "mods": MODS[layer],
                  "xoT": X[0]}
            with kb.phase(io):
                build_B2(True, kb=kb)
            continue
        for j in range(4):
            cp = [(H2H[:, :, 1:65], H2L[j][:, :, 0:64]), (H2H[:, :, 67:1091], H2L[j][:, :, 64:NTOK])]
            if j > 0:
                cp += [(H2H[:, :, 0:1], H2L[j - 1][:, :, 63:64]), (H2H[:, :, 66:67], H2L[j - 1][:, :, NTOK - 1:NTOK])]
            else:
                cp += [(H2H[:, :, 0:1], zcol), (H2H[:, :, 66:67], zcol)]
            if j < 3:
                cp += [(H2H[:, :, 65:66], H2L[j + 1][:, :, 0:1]), (H2H[:, :, 1091:1092], H2L[j + 1][:, :, 64:65])]
            else:
                cp += [(H2H[:, :, 65:66], zcol), (H2H[:, :, 1091:1092], zcol)]
            copies(cp)
            io = {"xmidT": XM[j], "h2hT": H2H, "wup": wup[layer], "wdn": wdn[layer], "convw": convw[layer], "mods": MODS[layer],
                  "xoT": X[j]}
            if not last:
                io.update({"normg": ng1[layer + 1], "modsn": MODS[layer + 1], "h1T": H1L[j]})
            with kb.phase(io):
                build_B2(last, kb=kb)
    kb.dma("sp", out[0], X[0], is_output=True)
    return kb.finish()


def kernel_fused(**inputs):
    I = {k: np.asarray(v) for k, v in inputs.items()}
    x, ctx = I["x"], I["ctx"]
    bf = ml_dtypes.bfloat16
    shared = {}
    shared["adaw"] = np.ascontiguousarray(I["ada_w"].reshape(DEPTH, NCH, 128, 16, 768).transpose(0, 3, 2, 1, 4))
    shared["adab"] = np.ascontiguousarray(I["ada_b"].reshape(DEPTH, 96, 128).transpose(2, 0, 1))
    shared["ng1"] = np.stack([vec_fm(I["norm1_g"][l]) for l in range(DEPTH)])
    shared["ng2"] = np.stack([vec_fm(I["norm2_g"][l]) for l in range(DEPTH)])
    shared["zcol"] = np.zeros((128, NCH, 1), bf)
    shared["zslot"] = np.zeros((128, NCH, NTOK), bf)
    shared["wout"] = np.stack([host_wout((I["ev_w_out"] if l % 2 == 0 else I["od_w_out"])[l // 2]) for l in range(DEPTH)])
    shared["wup"] = np.stack([host_wup(I["ffn_w_up"][l]) for l in range(DEPTH)])
    shared["wdn"] = np.stack([host_wdn(I["ffn_w_down"][l]) for l in range(DEPTH)])
    shared["convw"] = np.stack([host_convw(I["ffn_conv_w"][l], I["ffn_conv_b"][l]) for l in range(DEPTH)])
    shared["ident"] = np.eye(128, dtype=np.float32).astype(bf)
    ev = [[host_Aeven_inputs(g, I["ev_w_in"][i], I["gla_gate_w2"][i], I["gla_gate_b"][i], I["gla_norm_g"][i], I["sc_conv_w"][i])
           for g in range(4)] for i in range(2)]
    for key, src in (("e_wA", "wA"), ("e_wC", "wC"), ("e_w2", "w2"), ("e_gb", "gbias"), ("e_scw", "scw")):
        shared[key] = np.stack([np.stack([ev[i][g][src] for g in range(4)]) for i in range(2)])
    shared["e_gn"] = np.stack([ev[i][0]["gnorm"] for i in range(2)])
    shared["e_msk"] = host_masks()
    od = [[host_Aodd_inputs(g, 2 * i + 1, I["od_w_in"][i], I["da_qnorm_g"][i], I["da_knorm_g"][i], I["da_lambda"][i],
                            I["da_subln_g"][i], I["ml_conv_w"][i], I["ml_gate_b"][i], I["ml_norm_g"][i])
           for g in range(4)] for i in range(2)]
    for key, src in (("o_wD", "wD"), ("o_wM", "wM"), ("o_mlcw", "mlcw"), ("o_mlgb", "mlgb")):
        shared[key] = np.stack([np.stack([od[i][g][src] for g in range(4)]) for i in range(2)])
    for key, src in (("o_qkg", "qkg"), ("o_lamp", "lamp"), ("o_subg", "subg"), ("o_subgc", "subgc"), ("o_mlng", "mlng")):
        shared[key] = np.stack([od[i][0][src] for i in range(2)])
    shared["o_rope"] = od[0][0]["rope"]
    shared["o_msk"] = host_masks_odd()
    in_maps = []
    for core in range(8):
        b, j = divmod(core, 4)
        d = dict(shared)
        d["xT"] = np.stack([fm(core_tokens(x[b], ctx[b], jj)) for jj in range(4)])
        cond = np.stack([I["c_ctx"], I["c"][b]], axis=0)
        d["condT2"] = np.ascontiguousarray(cond.reshape(2, NCH, 128).transpose(2, 1, 0))
        in_maps.append(d)
    res = run(get_nc("fused", build_fused), in_maps)
    out = np.zeros((2, SEQ, D), np.float32)
    for core in range(8):
        b, j = divmod(core, 4)
        out[b, 1024 * j:1024 * j + 1024, :] = unfm(res[core]["xoutT"][0])[64:, :]
    return out
def kernel_unfused(**inputs):
    I = {k: np.asarray(v) for k, v in inputs.items()}
    x, ctx = I["x"], I["ctx"]
    mods, _ = host_M(I["c"], I["c_ctx"], I["ada_w"], I["ada_b"])
    cores = [divmod(c, 4) for c in range(8)]
    x_cores = [fm(core_tokens(x[b], ctx[b], j)) for (b, j) in cores]
    res = run(get_nc("P0", build_P0), [{"xT": x_cores[c], "mods": mods[0][cores[c][0]], "normg": vec_fm(I["norm1_g"][0])}
                                       for c in range(8)])
    h1 = [res[c]["h1T"] for c in range(8)]
    for layer in range(DEPTH):
        i2 = layer // 2
        last = layer == DEPTH - 1
        h1_b = [assemble_h1(h1, b) for b in range(2)]
        in_maps = []
        for c, (b, g) in enumerate(cores):
            if layer % 2 == 0:
                d = host_Aeven_inputs(g, I["ev_w_in"][i2], I["gla_gate_w2"][i2], I["gla_gate_b"][i2], I["gla_norm_g"][i2],
                                      I["sc_conv_w"][i2])
            else:
                d = host_Aodd_inputs(g, layer, I["od_w_in"][i2], I["da_qnorm_g"][i2], I["da_knorm_g"][i2], I["da_lambda"][i2],
                                     I["da_subln_g"][i2], I["ml_conv_w"][i2], I["ml_gate_b"][i2], I["ml_norm_g"][i2])
            d["h1T"] = h1_b[b]
            in_maps.append(d)
        nc = get_nc("Ae", build_Aeven) if layer % 2 == 0 else get_nc("Ao", build_Aodd, layer)
        res = run(nc, in_maps)
        mixes = [res[c]["mixT"] for c in range(8)]
        wout = host_wout((I["ev_w_out"] if layer % 2 == 0 else I["od_w_out"])[i2])
        in_maps = [{"xT": x_cores[c], "mixT": scatter_mix(mixes, b, j), "wout": wout, "mods": mods[layer][b],
                    "normg": vec_fm(I["norm2_g"][layer])} for c, (b, j) in enumerate(cores)]
        res = run(get_nc("B1", build_B1), in_maps)
        halo = host_halo([res[c]["h2T"] for c in range(8)])
        xm = [res[c]["xmidT"] for c in range(8)]
        wup = host_wup(I["ffn_w_up"][layer])
        wdn = host_wdn(I["ffn_w_down"][layer])
        cw = host_convw(I["ffn_conv_w"][layer], I["ffn_conv_b"][layer])
        in_maps = []
        for c, (b, j) in enumerate(cores):
            d = {"xmidT": xm[c], "h2hT": halo[c], "wup": wup, "wdn": wdn, "convw": cw, "mods": mods[layer][b]}
            if not last:
                d["normg"] = vec_fm(I["norm1_g"][layer + 1])
                d["modsn"] = mods[layer + 1][b]
            in_maps.append(d)
        res = run(get_nc("B2", build_B2, last), in_maps)
        x_cores = [res[c]["xoT"] for c in range(8)]
        if not last:
            h1 = [res[c]["h1T"] for c in range(8)]
    out = np.zeros((2, SEQ, D), np.float32)
    for c, (b, j) in enumerate(cores):
        out[b, 1024 * j:1024 * j + 1024, :] = unfm(x_cores[c])[64:, :]
    return out


FUSED = False


def kernel(**inputs):
    return kernel_fused(**inputs) if FUSED else kernel_unfused(**inputs)
```
